# Optimizing a Trainium2 kernel written in Bass

```python
import math
import jax, jax.numpy as jnp
from jax import lax
import numpy as np

D_MODEL = 1024
BATCH = 8
SEQ = 4096
DEPTH = 2

N_EVEN = (DEPTH + 1) // 2
N_ODD = DEPTH // 2

HGRN_DIM = D_MODEL // 2
HGRN_HEAD_DIM = 128
HGRN_HEADS = HGRN_DIM // HGRN_HEAD_DIM
HGRN_CHUNK = 64

S5_DIM = D_MODEL - HGRN_DIM
S5_GROUP = 16
S5_GROUPS = S5_DIM // S5_GROUP
S5_STATE = 64
S5_DT_MIN = 1e-3
S5_DT_MAX = 1e-1

EVEN_IN = 4 * HGRN_DIM + S5_DIM

MLA_HEADS = 8
MLA_Q_RANK = 384
MLA_KV_RANK = 256
MLA_NOPE = 128
MLA_ROPE = 64
MLA_V = 128
MLA_QK = MLA_NOPE + MLA_ROPE
ODD_IN = MLA_Q_RANK + MLA_KV_RANK + MLA_ROPE
ROPE_THETA = 10000.0
Q_BLOCK = 128

D_FF = 2816
CONV_W = 3
EPS = 1e-6

kernel_name = "hybrid_hgrn2_s5_mla_convffn"


def rmsnorm(x, g):
    xf = x.astype(jnp.float32)
    y = xf * lax.rsqrt(jnp.mean(xf * xf, axis=-1, keepdims=True) + EPS)
    return (y * g.astype(jnp.float32)).astype(x.dtype)


def hgrn2(q, f, i, g, lb, norm_g):
    f32 = jnp.float32
    B_, S_, _ = q.shape
    H, Dh, C = HGRN_HEADS, HGRN_HEAD_DIM, HGRN_CHUNK
    N = S_ // C
    lb = lb.astype(f32)
    forget = lb + (1.0 - lb) * jax.nn.sigmoid(f.astype(f32))
    key_in = 1.0 - forget

    def heads(t):
        return t.astype(f32).reshape(B_, N, C, H, Dh).transpose(0, 3, 1, 2, 4)

    qh, kh, vh = heads(q), heads(key_in), heads(i)
    b = jnp.cumsum(heads(jnp.log(forget)), axis=3)
    b_last = b[:, :, :, -1:, :]
    qd = qh * jnp.exp(b)
    kd = kh * jnp.exp(-b)
    causal = jnp.tril(jnp.ones((C, C), dtype=bool))
    att = jnp.where(causal, jnp.einsum('bhntd,bhnsd->bhnts', qd, kd), 0.0)
    o_intra = jnp.einsum('bhnts,bhnsv->bhntv', att, vh)
    d_state = jnp.einsum('bhnsd,bhnsv->bhndv', kh * jnp.exp(b_last - b), vh)
    decay = jnp.exp(b_last[:, :, :, 0, :])

    def step(state, inp):
        ds_n, dec_n = inp
        return dec_n[..., None] * state + ds_n, state

    s0 = jnp.zeros((B_, H, Dh, Dh), f32)
    _, s_start = lax.scan(step, s0, (d_state.transpose(2, 0, 1, 3, 4), decay.transpose(2, 0, 1, 3)))
    s_start = s_start.transpose(1, 2, 0, 3, 4)
    o = o_intra + jnp.einsum('bhntd,bhndv->bhntv', qd, s_start)
    o = o.transpose(0, 2, 3, 1, 4).reshape(B_, S_, H, Dh)
    o = o * lax.rsqrt(jnp.mean(o * o, axis=-1, keepdims=True) + EPS) * norm_g.astype(f32).reshape(H, Dh)
    return o.reshape(B_, S_, HGRN_DIM) * jax.nn.silu(g.astype(f32))


def s5(u, a_re, a_im, log_dt, b_re, b_im, c_re, c_im, d_skip, w_glu, b_glu):
    f32 = jnp.float32
    B_, S_, _ = u.shape
    G, P, Hc = S5_GROUPS, S5_STATE, S5_GROUP
    uf = u.astype(f32).reshape(B_, S_, G, Hc)
    ar, ai = a_re.astype(f32), a_im.astype(f32)
    dt = jnp.exp(log_dt.astype(f32))[:, None]
    mag = jnp.exp(ar * dt)
    abar_re, abar_im = mag * jnp.cos(ai * dt), mag * jnp.sin(ai * dt)
    den = ar * ar + ai * ai
    xr, xi = abar_re - 1.0, abar_im
    coef_re = ((xr * ar + xi * ai) / den)[..., None]
    coef_im = ((xi * ar - xr * ai) / den)[..., None]
    br, bi = b_re.astype(f32), b_im.astype(f32)
    bb_re = coef_re * br - coef_im * bi
    bb_im = coef_re * bi + coef_im * br
    bu_re = jnp.einsum('bsgh,gph->bsgp', uf, bb_re)
    bu_im = jnp.einsum('bsgh,gph->bsgp', uf, bb_im)
    a_seq_re = jnp.broadcast_to(abar_re, (1, S_, G, P))
    a_seq_im = jnp.broadcast_to(abar_im, (1, S_, G, P))

    def combine(left, right):
        a1r, a1i, b1r, b1i = left
        a2r, a2i, b2r, b2i = right
        return (a2r * a1r - a2i * a1i, a2r * a1i + a2i * a1r,
                a2r * b1r - a2i * b1i + b2r, a2r * b1i + a2i * b1r + b2i)

    _, _, s_re, s_im = lax.associative_scan(combine, (a_seq_re, a_seq_im, bu_re, bu_im), axis=1)
    y = (jnp.einsum('bsgp,ghp->bsgh', s_re, c_re.astype(f32))
         - jnp.einsum('bsgp,ghp->bsgh', s_im, c_im.astype(f32))
         + d_skip.astype(f32).reshape(G, Hc) * uf)
    z = jax.nn.gelu(y.reshape(B_, S_, S5_DIM))
    return z * jax.nn.sigmoid(z @ w_glu.astype(f32) + b_glu.astype(f32))


def even_mixer(hn, w_in, lb, hgrn_norm_g, a_re, a_im, log_dt, b_re, b_im, c_re, c_im,
               d_skip, w_glu, b_glu, w_out):
    proj = hn @ w_in
    q, f, i, g, u = jnp.split(proj, [HGRN_DIM, 2 * HGRN_DIM, 3 * HGRN_DIM, 4 * HGRN_DIM], axis=-1)
    y_a = hgrn2(q, f, i, g, lb, hgrn_norm_g).astype(hn.dtype)
    y_b = s5(u, a_re, a_im, log_dt, b_re, b_im, c_re, c_im, d_skip, w_glu, b_glu).astype(hn.dtype)
    return jnp.concatenate([y_a, y_b], axis=-1) @ w_out


def rotate(x, cos, sin):
    x1, x2 = jnp.split(x, 2, axis=-1)
    return jnp.concatenate([x1 * cos - x2 * sin, x1 * sin + x2 * cos], axis=-1)


def mla(hn, positions, w_in, q_norm_g, w_uq, kv_norm_g, w_ukv, w_out):
    B_, S_, _ = hn.shape
    H = MLA_HEADS
    proj = hn @ w_in
    cq, ckv, k_rope = jnp.split(proj, [MLA_Q_RANK, MLA_Q_RANK + MLA_KV_RANK], axis=-1)
    q = (rmsnorm(cq, q_norm_g) @ w_uq).reshape(B_, S_, H, MLA_QK)
    kv = (rmsnorm(ckv, kv_norm_g) @ w_ukv).reshape(B_, S_, H, MLA_NOPE + MLA_V)
    q_nope, q_rope = jnp.split(q, [MLA_NOPE], axis=-1)
    k_nope, v = jnp.split(kv, [MLA_NOPE], axis=-1)
    freqs = ROPE_THETA ** (-jnp.arange(0, MLA_ROPE, 2, dtype=jnp.float32) / MLA_ROPE)
    ang = positions.astype(jnp.float32)[..., None] * freqs
    cos = jnp.cos(ang)[:, :, None, :].astype(hn.dtype)
    sin = jnp.sin(ang)[:, :, None, :].astype(hn.dtype)
    q_rope = rotate(q_rope, cos, sin)
    k_rope = rotate(k_rope[:, :, None, :], cos, sin)
    q = jnp.concatenate([q_nope, q_rope], axis=-1)
    k = jnp.concatenate([k_nope, jnp.broadcast_to(k_rope, (B_, S_, H, MLA_ROPE))], axis=-1)
    scale = MLA_QK ** -0.5
    nb = S_ // Q_BLOCK
    qb = q.reshape(B_, nb, Q_BLOCK, H, MLA_QK).transpose(1, 0, 2, 3, 4)
    kpos = jnp.arange(S_)

    def attend(args):
        q_blk, blk = args
        s = jnp.einsum('bqhd,bkhd->bhqk', q_blk, k).astype(jnp.float32) * scale
        qpos = blk * Q_BLOCK + jnp.arange(Q_BLOCK)
        s = jnp.where(kpos[None, :] <= qpos[:, None], s, -jnp.inf)
        p = jax.nn.softmax(s, axis=-1).astype(v.dtype)
        return jnp.einsum('bhqk,bkhd->bqhd', p, v)

    o = lax.map(attend, (qb, jnp.arange(nb)))
    o = o.transpose(1, 0, 2, 3, 4).reshape(B_, S_, H * MLA_V)
    return o @ w_out


def conv_ffn(hn, w_in, conv_w, conv_b, w_out):
    a, u = jnp.split(hn @ w_in, 2, axis=-1)
    a = lax.conv_general_dilated(a, conv_w[:, None, :], window_strides=(1,),
                                 padding=[(CONV_W - 1, 0)],
                                 dimension_numbers=('NWC', 'WIO', 'NWC'),
                                 feature_group_count=D_FF) + conv_b
    return (jax.nn.silu(a) * u) @ w_out


def setup_inputs(seed: int = 0) -> dict:
    key = jax.random.key(seed)
    ks = iter(jax.random.split(key, 40))
    nrm = lambda shape, s: jax.random.normal(next(ks), shape, jnp.float32) * s
    D = D_MODEL
    G, P, Hc = S5_GROUPS, S5_STATE, S5_GROUP
    x = nrm((BATCH, SEQ, D), 1.0)
    offset = jax.random.randint(next(ks), (BATCH, 1), 0, SEQ)
    positions = (offset + jnp.arange(SEQ)[None, :]).astype(jnp.int32)
    return {
        "x": x,
        "positions": positions,
        "norm_mix_g": 1.0 + nrm((DEPTH, D), 0.02),
        "norm_ffn_g": 1.0 + nrm((DEPTH, D), 0.02),
        "final_norm_g": 1.0 + nrm((D,), 0.02),
        "even_w_in": nrm((N_EVEN, D, EVEN_IN), D ** -0.5),
        "hgrn_lb_logits": nrm((N_EVEN + 1, HGRN_DIM), 0.1),
        "hgrn_norm_g": 1.0 + nrm((N_EVEN, HGRN_DIM), 0.02),
        "s5_a_re": -0.5 + nrm((N_EVEN, G, P), 0.01),
        "s5_a_im": jnp.pi * jnp.arange(P, dtype=jnp.float32) + nrm((N_EVEN, G, P), 0.01),
        "s5_log_dt": jax.random.uniform(next(ks), (N_EVEN, G), jnp.float32,
                                        math.log(S5_DT_MIN), math.log(S5_DT_MAX)),
        "s5_b_re": nrm((N_EVEN, G, P, Hc), (2 * Hc) ** -0.5),
        "s5_b_im": nrm((N_EVEN, G, P, Hc), (2 * Hc) ** -0.5),
        "s5_c_re": nrm((N_EVEN, G, Hc, P), (2 * P) ** -0.5),
        "s5_c_im": nrm((N_EVEN, G, Hc, P), (2 * P) ** -0.5),
        "s5_d": nrm((N_EVEN, S5_DIM), 1.0),
        "s5_w_glu": nrm((N_EVEN, S5_DIM, S5_DIM), S5_DIM ** -0.5),
        "s5_b_glu": nrm((N_EVEN, S5_DIM), 0.02),
        "even_w_out": nrm((N_EVEN, HGRN_DIM + S5_DIM, D), (HGRN_DIM + S5_DIM) ** -0.5),
        "odd_w_in": nrm((N_ODD, D, ODD_IN), D ** -0.5),
        "mla_q_norm_g": 1.0 + nrm((N_ODD, MLA_Q_RANK), 0.02),
        "mla_w_uq": nrm((N_ODD, MLA_Q_RANK, MLA_HEADS * MLA_QK), MLA_Q_RANK ** -0.5),
        "mla_kv_norm_g": 1.0 + nrm((N_ODD, MLA_KV_RANK), 0.02),
        "mla_w_ukv": nrm((N_ODD, MLA_KV_RANK, MLA_HEADS * (MLA_NOPE + MLA_V)), MLA_KV_RANK ** -0.5),
        "odd_w_out": nrm((N_ODD, MLA_HEADS * MLA_V, D), (MLA_HEADS * MLA_V) ** -0.5),
        "ffn_w_in": nrm((DEPTH, D, 2 * D_FF), D ** -0.5),
        "ffn_conv_w": nrm((DEPTH, CONV_W, D_FF), CONV_W ** -0.5),
        "ffn_conv_b": nrm((DEPTH, D_FF), 0.02),
        "ffn_w_out": nrm((DEPTH, D_FF, D), D_FF ** -0.5),
    }


def reference(x, positions, norm_mix_g, norm_ffn_g, final_norm_g,
              even_w_in, hgrn_lb_logits, hgrn_norm_g,
              s5_a_re, s5_a_im, s5_log_dt, s5_b_re, s5_b_im, s5_c_re, s5_c_im,
              s5_d, s5_w_glu, s5_b_glu, even_w_out,
              odd_w_in, mla_q_norm_g, mla_w_uq, mla_kv_norm_g, mla_w_ukv, odd_w_out,
              ffn_w_in, ffn_conv_w, ffn_conv_b, ffn_w_out):
    lower_bounds = jnp.cumsum(jax.nn.softmax(hgrn_lb_logits.astype(jnp.float32), axis=0), axis=0)
    h = x
    for layer in range(DEPTH):
        j = layer // 2
        hn = rmsnorm(h, norm_mix_g[layer])
        if layer % 2 == 0:
            mix = even_mixer(hn, even_w_in[j], lower_bounds[j], hgrn_norm_g[j],
                             s5_a_re[j], s5_a_im[j], s5_log_dt[j], s5_b_re[j], s5_b_im[j],
                             s5_c_re[j], s5_c_im[j], s5_d[j], s5_w_glu[j], s5_b_glu[j],
                             even_w_out[j])
        else:
            mix = mla(hn, positions, odd_w_in[j], mla_q_norm_g[j], mla_w_uq[j],
                      mla_kv_norm_g[j], mla_w_ukv[j], odd_w_out[j])
        h = h + mix
        h = h + conv_ffn(rmsnorm(h, norm_ffn_g[layer]), ffn_w_in[layer], ffn_conv_w[layer],
                         ffn_conv_b[layer], ffn_w_out[layer])
    return rmsnorm(h, final_norm_g)
```

```python
import math
from contextlib import ExitStack
import numpy as np
import concourse.bass as bass
import concourse.mybir as mybir
from concourse.bass_utils import run_bass_kernel_spmd

F32 = mybir.dt.float32
BF16 = mybir.dt.bfloat16
I32 = mybir.dt.int32
AF = mybir.ActivationFunctionType
ALU = mybir.AluOpType
AX = mybir.AxisListType

S = 4096
D = 1024
NB = 8
TB = 512
DFF = 2816
NFC = DFF // 128
EPS = 1e-6

ENGS = ("pe", "act", "dve", "pool", "sp")


class Buf:
    __slots__ = ("name", "w", "r")

    def __init__(self, name):
        self.name = name
        self.w = None
        self.r = []


class Op:
    __slots__ = ("eng", "fn", "deps", "key", "pos", "ndma", "waits", "flag")

    def __init__(self, eng, fn, deps, key, ndma):
        self.eng = eng
        self.fn = fn
        self.deps = deps
        self.key = key
        self.pos = -1
        self.ndma = ndma
        self.waits = []
        self.flag = False


class KB:
    def __init__(self, nc):
        self.nc = nc
        self.ops = []
        self.eng_obj = {"pe": nc.tensor, "act": nc.scalar, "dve": nc.vector,
                        "pool": nc.gpsimd, "sp": nc.sync}
        self.last = {}
        self.dma_out = []
        self.nbuf = 0

    def buf(self, name=None):
        self.nbuf += 1
        return Buf(name or f"b{self.nbuf}")

    def bufs(self, n, name="b"):
        return [self.buf(f"{name}{i}") for i in range(n)]

    def _deps(self, idx, R, W):
        deps = set()
        for b in R:
            if b.w is not None:
                deps.add(b.w)
        for b in W:
            if b.w is not None:
                deps.add(b.w)
            deps.update(b.r)
        for b in W:
            b.w = idx
            b.r = []
        for b in R:
            b.r.append(idx)
        deps.discard(idx)
        return deps

    def op(self, eng, fn, R=(), W=()):
        idx = len(self.ops)
        deps = self._deps(idx, R, W)
        self.ops.append(Op(eng, fn, deps, eng, 0))
        self.last[eng] = idx
        return idx

    def dma(self, eng, pairs, R=(), W=(), key=None, **kw):
        idx = len(self.ops)
        deps = self._deps(idx, R, W)
        if key is None:
            key = "d_" + (W[0].name if W else R[0].name)

        def fn(e, pairs=pairs, kw=kw):
            return [e.dma_start(out=o, in_=i, **kw) for (o, i) in pairs]
        self.ops.append(Op(eng, fn, deps, key, len(pairs)))
        self.dma_out.append(idx)
        return idx

    def barrier(self):
        targets = set(self.last.values()) | set(self.dma_out)
        for e in ENGS:
            idx = len(self.ops)
            self.ops.append(Op(e, None, set(targets), e, 0))
        self.dma_out = []

    def finalize(self, es):
        nc = self.nc
        ops = self.ops
        cnt = {}
        for o in ops:
            if o.ndma:
                cnt[o.key] = cnt.get(o.key, 0) + o.ndma
                o.pos = cnt[o.key]
            elif o.fn is not None:
                cnt[o.key] = cnt.get(o.key, 0) + 1
                o.pos = cnt[o.key]
            else:
                o.pos = cnt.get(o.key, 0)
        seen = {e: {} for e in ENGS}
        flagged = {}
        for o in ops:
            need = {}
            for d in o.deps:
                dop = ops[d]
                if dop.fn is None:
                    continue
                k = dop.key
                if (not dop.ndma) and k == o.eng and k in ("pe", "sp"):
                    continue
                if dop.pos > seen[o.eng].get(k, 0):
                    if dop.pos > need.get(k, (0, None))[0]:
                        need[k] = (dop.pos, dop)
            for k, (p, dop) in need.items():
                seen[o.eng][k] = p
                o.waits.append((k, dop))
                dop.flag = True
        val = {}
        rank = {}
        for i, o in enumerate(ops):
            if o.ndma:
                val[i] = 16 * o.pos
            elif o.fn is not None and o.flag:
                rank[o.key] = rank.get(o.key, 0) + 1
                val[i] = rank[o.key]
        opidx = {id(o): i for i, o in enumerate(ops)}
        keys = set()
        for o in ops:
            if o.ndma or o.flag:
                keys.add(o.key)
        sems = {}
        for kname in sorted(keys):
            sems[kname] = es.enter_context(nc.semaphore("s_" + kname))
        self.nsem = len(sems)
        for o in ops:
            e = self.eng_obj[o.eng]
            for (k, dop) in o.waits:
                e.wait_ge(sems[k], val[opidx[id(dop)]])
            if o.fn is None:
                continue
            ins = o.fn(e)
            if o.ndma:
                for i_ in ins:
                    i_.then_inc(sems[o.key], 16)
            elif o.flag:
                ins.then_inc(sems[o.key], 1)


class Prog:
    def __init__(self, debug=()):
        self.debug = set(debug)
        self.nc = bass.Bass("TRN2", target_bir_lowering=False)
        self.k = KB(self.nc)
        self.es = ExitStack()
        self.dram = {}

    def din(self, name, shape, dtype=F32):
        t = self.nc.dram_tensor(name, list(shape), dtype, kind="ExternalInput").ap()
        self.dram[name] = t
        return t

    def dout(self, name, shape, dtype=F32):
        t = self.nc.dram_tensor(name, list(shape), dtype, kind="ExternalOutput").ap()
        self.dram[name] = t
        return t

    def dint(self, name, shape, dtype=F32):
        kind = "ExternalOutput" if name in self.debug else "Internal"
        t = self.nc.dram_tensor(name, list(shape), dtype, kind=kind).ap()
        self.dram[name] = t
        return t

    def sb(self, es, name, shape, dtype):
        self.nsb = getattr(self, "nsb", 0) + 1
        return es.enter_context(self.nc.sbuf_tensor(f"{name}_{self.nsb}", list(shape), dtype))


def setup_common(P):
    nc, k = P.nc, P.k
    P.ps = [P.es.enter_context(nc.psum_tensor(f"ps{i}", [128, 512], F32)) for i in range(8)]
    P.psb = k.bufs(8, "psb")
    P.ident_f = P.sb(P.es, "ident_f", [128, 128], F32)
    P.ident = P.sb(P.es, "ident", [128, 128], BF16)
    P.epsb = P.sb(P.es, "epsb", [128, 1], F32)
    P.b_const = k.buf("const")
    ones = P.sb(P.es, "ones_f", [128, 128], F32)
    P.ones_f = ones
    k.op("pool", lambda e: e.memset(ones[:], 1.0), W=[P.b_const])
    k.op("pool", lambda e: e.affine_select(out=P.ident_f[:], in_=ones[:], pattern=[[-1, 128]],
                                           compare_op=ALU.is_equal, fill=0.0, base=0,
                                           channel_multiplier=1), R=[P.b_const], W=[P.b_const])
    k.op("pool", lambda e: e.tensor_copy(out=P.ident[:], in_=P.ident_f[:]), R=[P.b_const], W=[P.b_const])
    k.op("pool", lambda e: e.memset(P.epsb[:], EPS), W=[P.b_const])


def cast_weight(P, name, src2d, rows, cols):
    k = P.k
    dst = P.dint(name, [rows, cols], BF16)
    b = k.buf(name)
    bb = cols if cols <= 2048 else 512
    per_row = cols // bb
    rstep = max(1, 4096 // per_row)
    pairs = []
    for r0 in range(0, rows, rstep):
        r1 = min(rows, r0 + rstep)
        pairs.append((dst[r0:r1, :].rearrange("r (a b) -> r a b", b=bb),
                      src2d[r0:r1, :].rearrange("r (a b) -> r a b", b=bb)))
    k.dma("pool", pairs, W=[b], key="wc_" + name)
    return dst, b


def load_vec_fm(P, es, name, src1d, nchunk, b, eng="sp"):
    t = P.sb(es, name, [128, nchunk], F32)
    P.k.dma(eng, [(t[:], src1d.rearrange("(c p) -> p c", p=128))], W=[b],
            allow_slow_non_contiguous=True)
    return t


def norm_block(P, T, blk, src, b_src, gT):
    k, nc = P.k, P.nc
    xres, bx = T["xres"], T["b_xres"]
    for j in range(4):
        r0 = blk * TB + j * 128
        k.dma("sp", [(xres[:, j, :], src[r0:r0 + 128, :])], R=[b_src[blk * 4 + j]], W=[bx[j]], key=f"d_xres{j}")
    rms_transpose(P, T, gT)


def rms_transpose(P, T, gT):
    k = P.k
    xres, bx = T["xres"], T["b_xres"]
    hnT, bh = T["hnT"], T["b_hnT"]
    junk, bj = T["junk"], T["b_junk"]
    ss, bss = T["ss"], T["b_ss"]
    xn, bxn = T["xn"], T["b_xn"]
    for j in range(4):
        pb = T["ps_tr"][j % len(T["ps_tr"])]
        k.op("act", lambda e, j=j: e.activation(out=junk[:], in_=xres[:, j, :], func=AF.Square,
                                                accum_out=ss[:, j:j + 1]),
             R=[bx[j]], W=[bj, bss[j]])
        k.op("act", lambda e, j=j: e.activation(out=ss[:, 4 + j:5 + j], in_=ss[:, j:j + 1], func=AF.Sqrt,
                                                bias=P.epsb[:], scale=1.0 / D),
             R=[bss[j], P.b_const], W=[bss[j]])
        k.op("dve", lambda e, j=j: e.reciprocal(out=ss[:, 8 + j:9 + j], in_=ss[:, 4 + j:5 + j]),
             R=[bss[j]], W=[bss[j]])
        k.op("dve", lambda e, j=j: e.tensor_scalar(out=xn[:, j, :], in0=xres[:, j, :], scalar1=ss[:, 8 + j:9 + j],
                                                   scalar2=None, op0=ALU.mult),
             R=[bx[j], bss[j]], W=[bxn[j]])
        psT = P.ps[pb][:].bitcast(BF16)
        for c in range(8):
            k.op("pe", lambda e, j=j, c=c, psT=psT: e.transpose(out=psT[:, c * 128:(c + 1) * 128],
                                                               in_=xn[:, j, c * 128:(c + 1) * 128],
                                                               identity=P.ident[:]),
                 R=[bxn[j], P.b_const], W=[P.psb[pb]])
        k.op("dve", lambda e, j=j, psT=psT: e.tensor_tensor(
            out=hnT[:, :, j * 128:(j + 1) * 128],
            in0=psT.rearrange("p (c t) -> p c t", c=8),
            in1=gT[:, :].unsqueeze(2).broadcast_to([128, 8, 128]), op=ALU.mult),
            R=[P.psb[pb], T["b_g"]], W=[bh])


def alloc_blockbufs(P, es, pfx, ps_tr):
    k = P.k
    T = {}
    T["xres"] = P.sb(es, pfx + "xres", [128, 4, 1024], F32)
    T["b_xres"] = k.bufs(4, pfx + "xres")
    T["hnT"] = P.sb(es, pfx + "hnT", [128, 8, 512], BF16)
    T["b_hnT"] = k.buf(pfx + "hnT")
    T["junk"] = P.sb(es, pfx + "junk", [128, 1024], BF16)
    T["b_junk"] = k.buf(pfx + "junk")
    T["ss"] = P.sb(es, pfx + "ss", [128, 12], F32)
    T["b_ss"] = k.bufs(4, pfx + "ss")
    T["xn"] = P.sb(es, pfx + "xn", [128, 4, 1024], BF16)
    T["b_xn"] = k.bufs(4, pfx + "xn")
    T["ps_tr"] = ps_tr
    return T


def ffn_phase(P, layer, src, b_src, dst, b_dst, w_in_bf, b_win, w_out_bf, b_wout, final_norm=None, premix=None):
    k, nc = P.k, P.nc
    I = P.inp
    with ExitStack() as es:
        bc = k.buf("ffn_consts")
        gT = load_vec_fm(P, es, "ffn_g", I["norm_ffn_g"][layer], 8, bc)
        cb = load_vec_fm(P, es, "ffn_cb", I["ffn_conv_b"][layer], NFC, bc)
        cw = P.sb(es, "ffn_cw", [128, 3, NFC], F32)
        for t in range(3):
            k.dma("sp", [(cw[:, t, :], I["ffn_conv_w"][layer, t].rearrange("(c p) -> p c", p=128))], W=[bc],
                  allow_slow_non_contiguous=True)
        T = alloc_blockbufs(P, es, "f_", ps_tr=[6, 7])
        T["b_g"] = bc
        wout = P.sb(es, "ffn_wout", [128, NFC, 1024], BF16)
        bwo = k.buf("ffn_wout")
        k.dma("sp", [(wout[:, c0:c0 + 11, :],
                      w_out_bf[c0 * 128:(c0 + 11) * 128, :].rearrange("(c p) n -> p c n", p=128))
                     for c0 in (0, 11)], R=[b_wout], W=[bwo])
        NSL = 2
        wsl = [P.sb(es, f"ffn_wsl{i}", [128, 8, 2, 256], BF16) for i in range(NSL)]
        bws = k.bufs(NSL, "ffn_wsl")
        a_sb = [P.sb(es, f"ffn_a{i}", [128, 516], F32) for i in range(2)]
        ba = k.bufs(2, "ffn_a")
        c_sb = [P.sb(es, f"ffn_c{i}", [128, 512], F32) for i in range(2)]
        bcs = k.bufs(2, "ffn_c")
        s_sb = [P.sb(es, f"ffn_s{i}", [128, 512], F32) for i in range(2)]
        bss = k.bufs(2, "ffn_s")
        halo = P.sb(es, "ffn_halo", [128, NFC, 2], F32)
        bhalo = k.bufs(NFC, "ffn_halo")
        gTt = P.sb(es, "ffn_gT", [128, NFC, 512], BF16)
        bg = k.bufs(NFC, "ffn_gT")
        k.op("pool", lambda e: e.memset(halo[:], 0.0), W=bhalo)
        if final_norm is not None:
            fg = P.sb(es, "fin_g", [128, 1024], F32)
            bfg = k.buf("fin_g")
            k.dma("sp", [(fg[:], final_norm.partition_broadcast(128))], W=[bfg])
            ostage = [P.sb(es, f"fin_o{i}", [128, 1024], F32) for i in range(2)]
            bos = k.bufs(2, "fin_o")
        if premix is not None:
            ycatT, b_ya, b_yb, wo_bf, b_wo = premix
            wo = P.sb(es, "pm_wo", [128, 8, 1024], BF16)
            bwo2 = k.buf("pm_wo")
            k.dma("sp", [(wo[:], wo_bf.rearrange("(c p) n -> p c n", p=128))], R=[b_wo], W=[bwo2])
            yT = P.sb(es, "pm_yT", [128, 8, 512], BF16)
            byT = k.buf("pm_yT")
        nsl = 0
        cnt = 0
        for blk in range(NB):
            xres, bx = T["xres"], T["b_xres"]
            hnT, bh = T["hnT"], T["b_hnT"]
            if premix is None:
                norm_block(P, T, blk, src, b_src, gT)
            else:
                for j in range(4):
                    r0 = blk * TB + j * 128
                    k.dma("sp", [(xres[:, j, :], src[r0:r0 + 128, :])], R=[b_src[blk * 4 + j]], W=[bx[j]], key=f"d_xres{j}")
                c0 = blk * TB
                k.dma("sp", [(yT[:], ycatT[:, c0:c0 + TB].rearrange("(c p) t -> p c t", p=128))], R=[b_ya[blk], b_yb[blk]], W=[byT])
                for j in range(4):
                    for n in range(2):
                        pb = 4 + (j * 2 + n) % 2
                        for kc in range(8):
                            k.op("pe", lambda e, kc=kc, j=j, n=n, pb=pb: e.matmul(
                                P.ps[pb][:], lhsT=yT[:, kc, j * 128:(j + 1) * 128], rhs=wo[:, kc, n * 512:(n + 1) * 512],
                                start=(kc == 0), stop=(kc == 7)), R=[byT, bwo2], W=[P.psb[pb]])
                        k.op("dve", lambda e, j=j, n=n, pb=pb: e.tensor_tensor(
                            out=xres[:, j, n * 512:(n + 1) * 512], in0=xres[:, j, n * 512:(n + 1) * 512],
                            in1=P.ps[pb][:], op=ALU.add), R=[bx[j], P.psb[pb]], W=[bx[j]])
                rms_transpose(P, T, gT)
            for pr in range(NFC // 2):
                sl = nsl % NSL
                nsl += 1
                pairs = []
                for h_ in range(2):
                    c0 = h_ * DFF + pr * 256
                    pairs.append((wsl[sl][:, :, h_, :],
                                  w_in_bf[:, c0:c0 + 256].rearrange("(kc p) n -> p kc n", p=128)))
                k.dma("sp", pairs, R=[b_win], W=[bws[sl]])
                for q in range(2):
                    oc = pr * 2 + q
                    pa, pu = (0, 1) if (cnt % 2 == 0) else (2, 3)
                    i2 = cnt % 2
                    cnt += 1
                    for kc in range(8):
                        k.op("pe", lambda e, kc=kc, sl=sl, q=q, pa=pa: e.matmul(
                            P.ps[pa][:], lhsT=wsl[sl][:, kc, 0, q * 128:(q + 1) * 128], rhs=hnT[:, kc, :],
                            start=(kc == 0), stop=(kc == 7)), R=[bws[sl], bh], W=[P.psb[pa]])
                    for kc in range(8):
                        k.op("pe", lambda e, kc=kc, sl=sl, q=q, pu=pu: e.matmul(
                            P.ps[pu][:], lhsT=wsl[sl][:, kc, 1, q * 128:(q + 1) * 128], rhs=hnT[:, kc, :],
                            start=(kc == 0), stop=(kc == 7)), R=[bws[sl], bh], W=[P.psb[pu]])
                    A = a_sb[i2]
                    k.op("pool", lambda e, A=A, oc=oc: e.tensor_copy(out=A[:, 2:4], in_=halo[:, oc, :]),
                         R=[bhalo[oc]], W=[ba[i2]])
                    k.op("act", lambda e, A=A, pa=pa: e.copy(out=A[:, 4:516], in_=P.ps[pa][:]),
                         R=[P.psb[pa]], W=[ba[i2]])
                    k.op("pool", lambda e, A=A, oc=oc: e.tensor_copy(out=halo[:, oc, :], in_=A[:, 514:516]),
                         R=[ba[i2]], W=[bhalo[oc]])
                    C = c_sb[i2]
                    k.op("act", lambda e, C=C, pa=pa, oc=oc: e.activation(
                        out=C[:], in_=P.ps[pa][:], func=AF.Identity, bias=cb[:, oc:oc + 1], scale=cw[:, 2, oc:oc + 1]),
                        R=[P.psb[pa], bc], W=[bcs[i2]])
                    k.op("dve", lambda e, C=C, A=A, oc=oc: e.scalar_tensor_tensor(
                        out=C[:], in0=A[:, 3:515], scalar=cw[:, 1, oc:oc + 1], in1=C[:], op0=ALU.mult, op1=ALU.add),
                        R=[ba[i2], bc, bcs[i2]], W=[bcs[i2]])
                    k.op("dve", lambda e, C=C, A=A, oc=oc: e.scalar_tensor_tensor(
                        out=C[:], in0=A[:, 2:514], scalar=cw[:, 0, oc:oc + 1], in1=C[:], op0=ALU.mult, op1=ALU.add),
                        R=[ba[i2], bc, bcs[i2]], W=[bcs[i2]])
                    Ssb = s_sb[i2]
                    k.op("act", lambda e, C=C, Ssb=Ssb: e.activation(out=Ssb[:], in_=C[:], func=AF.Silu),
                         R=[bcs[i2]], W=[bss[i2]])
                    k.op("dve", lambda e, Ssb=Ssb, pu=pu, oc=oc: e.tensor_tensor(
                        out=gTt[:, oc, :], in0=Ssb[:], in1=P.ps[pu][:], op=ALU.mult),
                        R=[bss[i2], P.psb[pu]], W=[bg[oc]])
            for j in range(4):
                for n in range(2):
                    pb = 4 + (j * 2 + n) % 2
                    for kc in range(NFC):
                        k.op("pe", lambda e, kc=kc, j=j, n=n, pb=pb: e.matmul(
                            P.ps[pb][:], lhsT=gTt[:, kc, j * 128:(j + 1) * 128],
                            rhs=wout[:, kc, n * 512:(n + 1) * 512], start=(kc == 0), stop=(kc == NFC - 1)),
                            R=[bg[kc], bwo], W=[P.psb[pb]])
                    k.op("dve", lambda e, j=j, n=n, pb=pb: e.tensor_tensor(
                        out=xres[:, j, n * 512:(n + 1) * 512], in0=xres[:, j, n * 512:(n + 1) * 512],
                        in1=P.ps[pb][:], op=ALU.add), R=[bx[j], P.psb[pb]], W=[bx[j]])
                r0 = blk * TB + j * 128
                if final_norm is None:
                    k.dma("sp", [(dst[r0:r0 + 128, :], xres[:, j, :])], R=[bx[j]], W=[b_dst[blk * 4 + j]],
                          key=f"d_xst{j}")
                else:
                    ss, bs4 = T["ss"], T["b_ss"]
                    junk, bj = T["junk"], T["b_junk"]
                    o = ostage[j % 2]
                    k.op("act", lambda e, j=j: e.activation(out=junk[:], in_=xres[:, j, :], func=AF.Square,
                                                            accum_out=ss[:, j:j + 1]),
                         R=[bx[j]], W=[bj, bs4[j]])
                    k.op("act", lambda e, j=j: e.activation(out=ss[:, 4 + j:5 + j], in_=ss[:, j:j + 1], func=AF.Sqrt,
                                                            bias=P.epsb[:], scale=1.0 / D),
                         R=[bs4[j], P.b_const], W=[bs4[j]])
                    k.op("dve", lambda e, j=j: e.reciprocal(out=ss[:, 8 + j:9 + j], in_=ss[:, 4 + j:5 + j]),
                         R=[bs4[j]], W=[bs4[j]])
                    k.op("dve", lambda e, j=j, o=o: e.scalar_tensor_tensor(
                        out=o[:], in0=xres[:, j, :], scalar=ss[:, 8 + j:9 + j], in1=fg[:], op0=ALU.mult, op1=ALU.mult),
                        R=[bx[j], bs4[j], bfg], W=[bos[j % 2]])
                    k.dma("sp", [(dst[r0:r0 + 128, :], o[:])], R=[bos[j % 2]], W=[b_dst[blk * 4 + j]], key=f"d_ost{j % 2}")
        k.barrier()


HQ = 8
VP = 130
SCALE = 192 ** -0.5
TWO_PI = 2.0 * math.pi
CW1 = 6.28125
CW2 = TWO_PI - 6.28125


def mla_proj_phase(P, src, b_src, W):
    k, nc = P.k, P.nc
    I = P.inp
    QTn = P.dint("QTn", [HQ, 128, S], BF16)
    QTr = P.dint("QTr", [HQ, 64, S], BF16)
    KTn = P.dint("KTn", [HQ, 128, S], BF16)
    KTr = P.dint("KTr", [64, S], BF16)
    Va = P.dint("Va", [S, HQ * VP], BF16)
    P.mla_dram = dict(QTn=QTn, QTr=QTr, KTn=KTn, KTr=KTr, Va=Va)
    P.b_qkv = k.bufs(NB, "qkv")
    stats = P.sb(P.es, "mla_stats", [128, 16], F32)
    P.mla_stats = stats
    P.b_stats = k.buf("mla_stats")
    with ExitStack() as es:
        bc = k.buf("pd_consts")
        gT = load_vec_fm(P, es, "pd_g", I["norm_mix_g"][1], 8, bc)
        qg = load_vec_fm(P, es, "pd_qg", I["mla_q_norm_g"], 3, bc)
        kvg = load_vec_fm(P, es, "pd_kvg", I["mla_kv_norm_g"], 2, bc)
        g5 = P.sb(es, "pd_g5", [128, 5], F32)
        k.op("pool", lambda e: e.tensor_copy(out=g5[:, 0:3], in_=qg[:]), R=[bc], W=[bc])
        k.op("pool", lambda e: e.tensor_copy(out=g5[:, 3:5], in_=kvg[:]), R=[bc], W=[bc])
        win = P.sb(es, "pd_win", [128, 8, 704], BF16)
        k.dma("sp", [(win[:], W["odd_w_in"][0].rearrange("(c p) n -> p c n", p=128))], R=[W["odd_w_in"][1]], W=[bc])
        wuq = P.sb(es, "pd_wuq", [128, 3, 1536], BF16)
        k.dma("sp", [(wuq[:], W["mla_w_uq"][0].rearrange("(c p) n -> p c n", p=128))], R=[W["mla_w_uq"][1]], W=[bc])
        wukv = P.sb(es, "pd_wukv", [128, 2, 2048], BF16)
        k.dma("sp", [(wukv[:], W["mla_w_ukv"][0].rearrange("(c p) n -> p c n", p=128))], R=[W["mla_w_ukv"][1]], W=[bc])
        freq = P.sb(es, "pd_freq", [128, 32], F32)
        k.dma("sp", [(freq[:], I["rope_freq"].partition_broadcast(128))], W=[bc])
        posi = P.sb(es, "pd_posi", [128, 32], I32)
        k.dma("sp", [(posi[:, 8 * i:8 * i + 8], I["positions"][1024 * i:1024 * (i + 1)].rearrange("(j p) -> p j", p=128))
                     for i in range(4)], W=[bc], allow_slow_non_contiguous=True)
        posf = P.sb(es, "pd_posf", [128, 32], F32)
        ang = P.sb(es, "pd_ang", [128, 32, 32], F32)
        tmpf = P.sb(es, "pd_tmpf", [128, 32, 32], F32)
        tmpi = P.sb(es, "pd_tmpi", [128, 32, 32], I32)
        sinT = P.sb(es, "pd_sin", [128, 32, 32], F32)
        cosT = P.sb(es, "pd_cos", [128, 32, 32], F32)
        br = k.buf("pd_rope")
        k.op("dve", lambda e: e.tensor_copy(out=posf[:], in_=posi[:]), R=[bc], W=[br])
        k.op("dve", lambda e: e.tensor_tensor(out=ang[:], in0=posf[:, :].unsqueeze(2).broadcast_to([128, 32, 32]),
                                              in1=freq[:, :].unsqueeze(1).broadcast_to([128, 32, 32]), op=ALU.mult),
             R=[br, bc], W=[br])
        k.op("dve", lambda e: e.tensor_scalar(out=tmpf[:], in0=ang[:], scalar1=1.0 / TWO_PI, scalar2=None, op0=ALU.mult),
             R=[br], W=[br])
        k.op("dve", lambda e: e.tensor_copy(out=tmpi[:], in_=tmpf[:]), R=[br], W=[br])
        k.op("dve", lambda e: e.tensor_copy(out=tmpf[:], in_=tmpi[:]), R=[br], W=[br])
        k.op("dve", lambda e: e.scalar_tensor_tensor(out=ang[:], in0=tmpf[:], scalar=-CW1, in1=ang[:],
                                                     op0=ALU.mult, op1=ALU.add), R=[br], W=[br])
        k.op("dve", lambda e: e.scalar_tensor_tensor(out=ang[:], in0=tmpf[:], scalar=-CW2, in1=ang[:],
                                                     op0=ALU.mult, op1=ALU.add), R=[br], W=[br])
        k.op("dve", lambda e: e.tensor_scalar(out=ang[:], in0=ang[:], scalar1=-math.pi, scalar2=math.pi,
                                              op0=ALU.max, op1=ALU.min), R=[br], W=[br])
        k.op("act", lambda e: e.activation(out=sinT[:], in_=ang[:], func=AF.Sin), R=[br], W=[br])
        k.op("dve", lambda e: e.tensor_scalar(out=tmpf[:], in0=ang[:], scalar1=math.pi / 2, scalar2=math.pi,
                                              op0=ALU.add, op1=ALU.is_gt), R=[br], W=[br])
        k.op("dve", lambda e: e.scalar_tensor_tensor(out=tmpf[:], in0=tmpf[:], scalar=-TWO_PI, in1=ang[:],
                                                     op0=ALU.mult, op1=ALU.add), R=[br], W=[br])
        k.op("dve", lambda e: e.tensor_scalar(out=tmpf[:], in0=tmpf[:], scalar1=math.pi / 2, scalar2=math.pi,
                                              op0=ALU.add, op1=ALU.min), R=[br], W=[br])
        k.op("act", lambda e: e.activation(out=cosT[:], in_=tmpf[:], func=AF.Sin), R=[br], W=[br])
        k.op("pool", lambda e: e.memset(stats[:], 0.0), W=[P.b_stats])

        T = alloc_blockbufs(P, es, "d_", ps_tr=[7])
        T["b_g"] = bc
        junk2 = P.sb(es, "pd_junk2", [128, 1536], F32)
        bj2 = k.buf("pd_junk2")
        st = P.sb(es, "pd_st", [128, 16], F32)
        bst = k.buf("pd_st")
        cn = P.sb(es, "pd_cn", [128, 640], BF16)
        bcn = k.buf("pd_cn")
        cT = P.sb(es, "pd_cT", [128, 5, 128], BF16)
        bcT = k.buf("pd_cT")
        kr = P.sb(es, "pd_kr", [128, 64], F32)
        bkr = k.buf("pd_kr")
        krb = P.sb(es, "pd_krb", [128, 64], BF16)
        bkrb = k.buf("pd_krb")
        qsb = P.sb(es, "pd_qsb", [128, 1536], F32)
        bqsb = k.buf("pd_qsb")
        qbf = P.sb(es, "pd_qbf", [128, 8, 192], BF16)
        bqbf = k.buf("pd_qbf")
        rt = [P.sb(es, f"pd_rt{i}", [128, 8, 32], F32) for i in range(4)]
        brt = k.buf("pd_rt")
        kbf = P.sb(es, "pd_kbf", [128, 8, 128], BF16)
        bkbf = k.buf("pd_kbf")
        vst = [P.sb(es, f"pd_vst{i}", [128, 8, VP], BF16) for i in range(2)]
        bvst = k.bufs(2, "pd_vst")
        for i in range(2):
            k.op("pool", lambda e, i=i: e.memset(vst[i][:], 1.0), W=[bvst[i]])
        qTn_st = P.sb(es, "pd_qTn", [128, 8, 512], BF16)
        qTr_st = P.sb(es, "pd_qTr", [64, 8, 512], BF16)
        kTn_st = P.sb(es, "pd_kTn", [128, 8, 512], BF16)
        kTr_st = P.sb(es, "pd_kTr", [64, 512], BF16)
        bqT = k.buf("pd_qT")
        bkT = k.buf("pd_kT")
        nv = 0
        import os
        LIM = float(os.environ.get("PD_LIMIT", "99"))
        for blk in range(NB):
            if LIM < 1 or (LIM < 90 and blk > 0):
                break
            norm_block(P, T, blk, src, b_src, gT)
            hnT, bh = T["hnT"], T["b_hnT"]
            for j in range(4):
                if LIM < 2 or (LIM < 90 and j > 0):
                    break
                sub = blk * 4 + j
                tok = slice(j * 128, (j + 1) * 128)
                for kc in range(8):
                    k.op("pe", lambda e, kc=kc, tok=tok: e.matmul(P.ps[0][:, 0:384], lhsT=hnT[:, kc, tok], rhs=win[:, kc, 0:384],
                                                                  start=(kc == 0), stop=(kc == 7)), R=[bh, bc], W=[P.psb[0]])
                for kc in range(8):
                    k.op("pe", lambda e, kc=kc, tok=tok: e.matmul(P.ps[1][:, 0:320], lhsT=hnT[:, kc, tok], rhs=win[:, kc, 384:704],
                                                                  start=(kc == 0), stop=(kc == 7)), R=[bh, bc], W=[P.psb[1]])
                k.op("act", lambda e: e.activation(out=junk2[:, 0:384], in_=P.ps[0][:, 0:384], func=AF.Square,
                                                   accum_out=st[:, 0:1]), R=[P.psb[0]], W=[bj2, bst])
                k.op("act", lambda e: e.activation(out=junk2[:, 0:256], in_=P.ps[1][:, 0:256], func=AF.Square,
                                                   accum_out=st[:, 1:2]), R=[P.psb[1]], W=[bj2, bst])
                k.op("act", lambda e: e.activation(out=junk2[:, 0:64], in_=P.ps[1][:, 256:320], func=AF.Square,
                                                   accum_out=st[:, 2:3]), R=[P.psb[1]], W=[bj2, bst])
                k.op("act", lambda e: e.activation(out=st[:, 3:4], in_=st[:, 0:1], func=AF.Sqrt, bias=P.epsb[:],
                                                   scale=1.0 / 384), R=[bst, P.b_const], W=[bst])
                k.op("act", lambda e: e.activation(out=st[:, 4:5], in_=st[:, 1:2], func=AF.Sqrt, bias=P.epsb[:],
                                                   scale=1.0 / 256), R=[bst, P.b_const], W=[bst])
                k.op("dve", lambda e: e.reciprocal(out=st[:, 5:7], in_=st[:, 3:5]), R=[bst], W=[bst])
                k.op("dve", lambda e: e.tensor_scalar(out=cn[:, 0:384], in0=P.ps[0][:, 0:384], scalar1=st[:, 5:6],
                                                      scalar2=None, op0=ALU.mult), R=[P.psb[0], bst], W=[bcn])
                k.op("dve", lambda e: e.tensor_scalar(out=cn[:, 384:640], in0=P.ps[1][:, 0:256], scalar1=st[:, 6:7],
                                                      scalar2=None, op0=ALU.mult), R=[P.psb[1], bst], W=[bcn])
                k.op("act", lambda e: e.copy(out=kr[:], in_=P.ps[1][:, 256:320]), R=[P.psb[1]], W=[bkr])
                psT = P.ps[2][:].bitcast(BF16)
                for c in range(5):
                    k.op("pe", lambda e, c=c, psT=psT: e.transpose(out=psT[:, c * 128:(c + 1) * 128],
                                                                   in_=cn[:, c * 128:(c + 1) * 128], identity=P.ident[:]),
                         R=[bcn, P.b_const], W=[P.psb[2]])
                k.op("dve", lambda e, psT=psT: e.tensor_tensor(out=cT[:], in0=psT[:, 0:640].rearrange("p (c t) -> p c t", c=5),
                                                               in1=g5[:, :].unsqueeze(2).broadcast_to([128, 5, 128]), op=ALU.mult),
                     R=[P.psb[2], bc], W=[bcT])
                if LIM < 3:
                    break
                for n in range(3):
                    for kc in range(3):
                        k.op("pe", lambda e, n=n, kc=kc: e.matmul(P.ps[3 + n][:], lhsT=cT[:, kc, :],
                                                                  rhs=wuq[:, kc, n * 512:(n + 1) * 512],
                                                                  start=(kc == 0), stop=(kc == 2)), R=[bcT, bc], W=[P.psb[3 + n]])
                for n in range(3):
                    k.op("act", lambda e, n=n: e.activation(out=qsb[:, n * 512:(n + 1) * 512], in_=P.ps[3 + n][:],
                                                            func=AF.Copy, scale=SCALE), R=[P.psb[3 + n]], W=[bqsb])
                if LIM < 3.1:
                    break
                k.op("pool", lambda e: e.tensor_tensor(out=junk2[:], in0=qsb[:], in1=qsb[:], op=ALU.mult),
                     R=[bqsb], W=[bj2])
                k.op("dve", lambda e: e.tensor_reduce(out=st[:, 8:16], in_=junk2[:, :].rearrange("p (h d) -> p h d", h=8),
                                                      axis=AX.X, op=ALU.add), R=[bj2], W=[bst])
                k.op("dve", lambda e: e.tensor_tensor(out=stats[:, 0:8], in0=stats[:, 0:8], in1=st[:, 8:16], op=ALU.max),
                     R=[bst, P.b_stats], W=[P.b_stats])
                if LIM < 3.2:
                    break
                q3 = qsb[:, :].rearrange("p (h d) -> p h d", h=8)
                cosb = cosT[:, sub, :].unsqueeze(1).broadcast_to([128, 8, 32])
                sinb = sinT[:, sub, :].unsqueeze(1).broadcast_to([128, 8, 32])
                x1 = q3[:, :, 128:160]
                x2 = q3[:, :, 160:192]
                k.op("pool", lambda e, x1=x1, cosb=cosb: e.tensor_tensor(out=rt[0][:], in0=x1, in1=cosb, op=ALU.mult),
                     R=[bqsb, br], W=[brt])
                k.op("pool", lambda e, x2=x2, sinb=sinb: e.tensor_tensor(out=rt[1][:], in0=x2, in1=sinb, op=ALU.mult),
                     R=[bqsb, br], W=[brt])
                k.op("pool", lambda e, x1=x1, sinb=sinb: e.tensor_tensor(out=rt[2][:], in0=x1, in1=sinb, op=ALU.mult),
                     R=[bqsb, br], W=[brt])
                k.op("pool", lambda e, x2=x2, cosb=cosb: e.tensor_tensor(out=rt[3][:], in0=x2, in1=cosb, op=ALU.mult),
                     R=[bqsb, br], W=[brt])
                k.op("dve", lambda e: e.tensor_tensor(out=qbf[:, :, 128:160], in0=rt[0][:], in1=rt[1][:], op=ALU.subtract),
                     R=[brt], W=[bqbf])
                k.op("dve", lambda e: e.tensor_tensor(out=qbf[:, :, 160:192], in0=rt[2][:], in1=rt[3][:], op=ALU.add),
                     R=[brt], W=[bqbf])
                k.op("act", lambda e, q3=q3: e.copy(out=qbf[:, :, 0:128], in_=q3[:, :, 0:128]), R=[bqsb], W=[bqbf])
                if LIM < 3.3:
                    break
                psQn = P.ps[6][:].bitcast(BF16)
                psQr = P.ps[2][:].bitcast(BF16)
                for h in range(8):
                    k.op("pe", lambda e, h=h, psQn=psQn: e.transpose(out=psQn[:, h * 128:(h + 1) * 128], in_=qbf[:, h, 0:128],
                                                                     identity=P.ident[:]), R=[bqbf, P.b_const], W=[P.psb[6]])
                for h in range(8):
                    k.op("pe", lambda e, h=h, psQr=psQr: e.transpose(out=psQr[0:64, h * 128:(h + 1) * 128], in_=qbf[:, h, 128:192],
                                                                     identity=P.ident[:]), R=[bqbf, P.b_const], W=[P.psb[2]])
                k.op("act", lambda e, psQn=psQn, tok=tok: e.copy(out=qTn_st[:, :, tok], in_=psQn.rearrange("p (h t) -> p h t", h=8)),
                     R=[P.psb[6]], W=[bqT])
                k.op("dve", lambda e, psQr=psQr, tok=tok: e.tensor_copy(out=qTr_st[:, :, tok],
                                                                        in_=psQr[0:64, :].rearrange("p (h t) -> p h t", h=8)),
                     R=[P.psb[2]], W=[bqT])
                if LIM < 4:
                    break
                kvb = [3, 4, 5, 0]
                for n in range(4):
                    for kc in range(2):
                        k.op("pe", lambda e, n=n, kc=kc: e.matmul(P.ps[kvb[n]][:], lhsT=cT[:, 3 + kc, :],
                                                                  rhs=wukv[:, kc, n * 512:(n + 1) * 512],
                                                                  start=(kc == 0), stop=(kc == 1)), R=[bcT, bc], W=[P.psb[kvb[n]]])
                if LIM < 4.1:
                    break
                vs = vst[nv % 2]
                bvs = bvst[nv % 2]
                nv += 1
                for n in range(4):
                    pv = P.ps[kvb[n]][:, :].rearrange("p (h c) -> p h c", h=2)
                    if n % 2 == 0:
                        k.op("act", lambda e, n=n, pv=pv: e.copy(out=kbf[:, 2 * n:2 * n + 2, :], in_=pv[:, :, 0:128]),
                             R=[P.psb[kvb[n]]], W=[bkbf])
                        k.op("act", lambda e, n=n, pv=pv, vs=vs: e.copy(out=vs[:, 2 * n:2 * n + 2, 0:128], in_=pv[:, :, 128:256]),
                             R=[P.psb[kvb[n]]], W=[bvs])
                    else:
                        k.op("dve", lambda e, n=n, pv=pv: e.tensor_copy(out=kbf[:, 2 * n:2 * n + 2, :], in_=pv[:, :, 0:128]),
                             R=[P.psb[kvb[n]]], W=[bkbf])
                        k.op("dve", lambda e, n=n, pv=pv, vs=vs: e.tensor_copy(out=vs[:, 2 * n:2 * n + 2, 0:128], in_=pv[:, :, 128:256]),
                             R=[P.psb[kvb[n]]], W=[bvs])
                if LIM < 4.2:
                    break
                r0 = blk * TB + j * 128
                k.dma("sp", [(P.mla_dram["Va"][r0:r0 + 128, :], vs[:, :, :].rearrange("p h c -> p (h c)"))],
                      R=[bvs], W=[P.b_qkv[blk]], key=f"d_vst{(nv - 1) % 2}")
                if LIM < 4.3:
                    break
                k.op("pool", lambda e: e.tensor_tensor(out=junk2[:, 0:1024], in0=kbf[:, :, :].rearrange("p h c -> p (h c)"),
                                                       in1=kbf[:, :, :].rearrange("p h c -> p (h c)"), op=ALU.mult),
                     R=[bkbf], W=[bj2])
                k.op("dve", lambda e: e.tensor_reduce(out=st[:, 8:16], in_=junk2[:, 0:1024].rearrange("p (h d) -> p h d", h=8),
                                                      axis=AX.X, op=ALU.add), R=[bj2], W=[bst])
                k.op("dve", lambda e: e.tensor_scalar(out=st[:, 8:16], in0=st[:, 8:16], scalar1=st[:, 2:3], scalar2=None,
                                                      op0=ALU.add), R=[bst], W=[bst])
                k.op("dve", lambda e: e.tensor_tensor(out=stats[:, 8:16], in0=stats[:, 8:16], in1=st[:, 8:16], op=ALU.max),
                     R=[bst, P.b_stats], W=[P.b_stats])
                if LIM < 4.4:
                    break
                c1 = cosT[:, sub, :]
                s1 = sinT[:, sub, :]
                k.op("pool", lambda e, c1=c1: e.tensor_tensor(out=rt[0][:, 0, :], in0=kr[:, 0:32], in1=c1, op=ALU.mult),
                     R=[bkr, br], W=[brt])
                k.op("pool", lambda e, s1=s1: e.tensor_tensor(out=rt[1][:, 0, :], in0=kr[:, 32:64], in1=s1, op=ALU.mult),
                     R=[bkr, br], W=[brt])
                k.op("pool", lambda e, s1=s1: e.tensor_tensor(out=rt[2][:, 0, :], in0=kr[:, 0:32], in1=s1, op=ALU.mult),
                     R=[bkr, br], W=[brt])
                k.op("pool", lambda e, c1=c1: e.tensor_tensor(out=rt[3][:, 0, :], in0=kr[:, 32:64], in1=c1, op=ALU.mult),
                     R=[bkr, br], W=[brt])
                k.op("dve", lambda e: e.tensor_tensor(out=krb[:, 0:32], in0=rt[0][:, 0, :], in1=rt[1][:, 0, :], op=ALU.subtract),
                     R=[brt], W=[bkrb])
                k.op("dve", lambda e: e.tensor_tensor(out=krb[:, 32:64], in0=rt[2][:, 0, :], in1=rt[3][:, 0, :], op=ALU.add),
                     R=[brt], W=[bkrb])
                if LIM < 4.5:
                    break
                psKn = P.ps[6][:].bitcast(BF16)
                psKr = P.ps[2][:].bitcast(BF16)
                for h in range(8):
                    k.op("pe", lambda e, h=h, psKn=psKn: e.transpose(out=psKn[:, h * 128:(h + 1) * 128], in_=kbf[:, h, :],
                                                                     identity=P.ident[:]), R=[bkbf, P.b_const], W=[P.psb[6]])
                k.op("pe", lambda e, psKr=psKr: e.transpose(out=psKr[0:64, 0:128], in_=krb[:], identity=P.ident[:]),
                     R=[bkrb, P.b_const], W=[P.psb[2]])
                k.op("act", lambda e, psKn=psKn, tok=tok: e.copy(out=kTn_st[:, :, tok], in_=psKn.rearrange("p (h t) -> p h t", h=8)),
                     R=[P.psb[6]], W=[bkT])
                k.op("dve", lambda e, psKr=psKr, tok=tok: e.tensor_copy(out=kTr_st[:, tok], in_=psKr[0:64, 0:128]),
                     R=[P.psb[2]], W=[bkT])
            if LIM < 5:
                break
            c0 = blk * TB
            k.dma("sp", [(P.mla_dram["QTn"][:, :, c0:c0 + TB].rearrange("h d t -> d h t"), qTn_st[:]),
                         (P.mla_dram["QTr"][:, :, c0:c0 + TB].rearrange("h d t -> d h t"), qTr_st[:])],
                  R=[bqT], W=[P.b_qkv[blk]], key="d_qT")
            k.dma("sp", [(P.mla_dram["KTn"][:, :, c0:c0 + TB].rearrange("h d t -> d h t"), kTn_st[:]),
                         (P.mla_dram["KTr"][:, c0:c0 + TB], kTr_st[:])],
                  R=[bkT], W=[P.b_qkv[blk]], key="d_kT")
        k.barrier()


def mla_attn_phase(P, src, b_src, dst, b_dst, W):
    k, nc = P.k, P.nc
    I = P.inp
    Dm = P.mla_dram
    stats = P.mla_stats
    with ExitStack() as es:
        bc = k.buf("pe_consts")
        wo = P.sb(es, "pe_wo", [128, 8, 1024], BF16)
        k.dma("sp", [(wo[:], W["odd_w_out"][0].rearrange("(c p) n -> p c n", p=128))], R=[W["odd_w_out"][1]], W=[bc])
        tps = P.ps[7]
        sm = P.sb(es, "pe_sm", [16, 4], F32)
        dg = P.sb(es, "pe_dg", [8, 8], F32)
        negc = P.sb(es, "pe_negc", [128, 8], F32)
        k.op("pe", lambda e: e.transpose(out=tps[0:16, 0:128], in_=stats[:, 0:16], identity=P.ident_f[:]),
             R=[P.b_stats, P.b_const], W=[P.psb[7]])
        k.op("dve", lambda e: e.tensor_reduce(out=sm[:, 0:1], in_=tps[0:16, 0:128], axis=AX.X, op=ALU.max),
             R=[P.psb[7]], W=[bc])
        k.op("pe", lambda e: e.transpose(out=tps[0:1, 128:144], in_=sm[:, 0:1], identity=P.ident_f[0:16, 0:16]),
             R=[bc, P.b_const], W=[P.psb[7]])
        rowv = P.sb(es, "pe_rowv", [1, 24], F32)
        k.op("dve", lambda e: e.tensor_copy(out=rowv[:, 0:16], in_=tps[0:1, 128:144]), R=[P.psb[7]], W=[bc])
        k.op("dve", lambda e: e.tensor_tensor(out=rowv[:, 16:24], in0=rowv[:, 0:8], in1=rowv[:, 8:16], op=ALU.mult),
             R=[bc], W=[bc])
        k.op("act", lambda e: e.activation(out=rowv[:, 16:24], in_=rowv[:, 16:24], func=AF.Sqrt), R=[bc], W=[bc])
        k.op("dve", lambda e: e.tensor_scalar(out=rowv[:, 16:24], in0=rowv[:, 16:24], scalar1=-1.0, scalar2=None, op0=ALU.mult),
             R=[bc], W=[bc])
        k.op("pe", lambda e: e.matmul(tps[:, 160:168], lhsT=P.ones_f[0:1, :], rhs=rowv[:, 16:24], start=True, stop=True),
             R=[bc, P.b_const], W=[P.psb[7]])
        k.op("dve", lambda e: e.tensor_copy(out=negc[:], in_=tps[:, 160:168]), R=[P.psb[7]], W=[bc])
        tri = P.sb(es, "pe_tri", [128, 128], BF16)
        k.op("pool", lambda e: e.affine_select(out=tri[:], in_=P.ones_f[:], pattern=[[1, 128]], compare_op=ALU.is_ge,
                                               fill=0.0, base=0, channel_multiplier=-1), R=[P.b_const], W=[bc])
        KTn = P.sb(es, "pe_KTn", [128, 8, S], BF16)
        KTr = P.sb(es, "pe_KTr", [64, S], BF16)
        Vs = P.sb(es, "pe_V", [128, 32, HQ * VP], BF16)
        bK = k.bufs(8, "pe_K")
        bV = k.bufs(8, "pe_V")
        for b in range(NB):
            c0 = b * TB
            k.dma("sp", [(KTn[:, :, c0:c0 + TB], Dm["KTn"][:, :, c0:c0 + TB].rearrange("h d t -> d h t")),
                         (KTr[:, c0:c0 + TB], Dm["KTr"][:, c0:c0 + TB])], R=[P.b_qkv[b]], W=[bK[b]])
            k.dma("sp", [(Vs[:, 4 * b:4 * b + 4, :], Dm["Va"][c0:c0 + TB, :].rearrange("(j p) c -> p j c", p=128))],
                  R=[P.b_qkv[b]], W=[bV[b]])
        NQ = 1
        Qn = [P.sb(es, f"pe_Qn{i}", [128, 8, TB], BF16) for i in range(NQ)]
        Qr = [P.sb(es, f"pe_Qr{i}", [64, 8, TB], BF16) for i in range(NQ)]
        bQ = k.bufs(NQ, "pe_Q")
        NPT = 3
        PT = [P.sb(es, f"pe_PT{i}", [128, TB], BF16) for i in range(NPT)]
        bPT = k.bufs(NPT, "pe_PT")
        att = P.sb(es, "pe_att", [128, 4, 1024], BF16)
        batt = k.bufs(4, "pe_att")
        aT = P.sb(es, "pe_aT", [128, 8, TB], BF16)
        baT = k.buf("pe_aT")
        rden = P.sb(es, "pe_rden", [128, 4], F32)
        brden = k.buf("pe_rden")
        hres = [P.sb(es, f"pe_hres{i}", [128, 1024], F32) for i in range(2)]
        bhres = k.bufs(2, "pe_hres")
        npt = 0
        nsc = 0
        for qb in range(NB):
            qi = qb % NQ
            c0 = qb * TB
            k.dma("sp", [(Qn[qi][:], Dm["QTn"][:, :, c0:c0 + TB].rearrange("h d t -> d h t")),
                         (Qr[qi][:], Dm["QTr"][:, :, c0:c0 + TB].rearrange("h d t -> d h t"))],
                  R=[P.b_qkv[qb]], W=[bQ[qi]])
            for h in range(8):
                ob = (3, 4) if h % 2 == 0 else (5, 6)
                nkt = 4 * qb + 4
                first = [True, True]
                for kt in range(nkt):
                    r = kt - 4 * qb
                    q0 = max(r, 0) * 128
                    sb_ = nsc % 3
                    nsc += 1
                    pi = npt % NPT
                    npt += 1
                    kb = kt // 4
                    k.op("pe", lambda e, h=h, kt=kt, q0=q0, sb_=sb_, qi=qi: e.matmul(
                        P.ps[sb_][:, q0:TB], lhsT=KTn[:, h, kt * 128:(kt + 1) * 128], rhs=Qn[qi][:, h, q0:TB],
                        start=True, stop=False), R=[bK[kb], bQ[qi]], W=[P.psb[sb_]])
                    k.op("pe", lambda e, h=h, kt=kt, q0=q0, sb_=sb_, qi=qi: e.matmul(
                        P.ps[sb_][:, q0:TB], lhsT=KTr[:, kt * 128:(kt + 1) * 128], rhs=Qr[qi][:, h, q0:TB],
                        start=False, stop=True), R=[bK[kb], bQ[qi]], W=[P.psb[sb_]])
                    k.op("act", lambda e, h=h, q0=q0, sb_=sb_, pi=pi: e.activation(
                        out=PT[pi][:, q0:TB], in_=P.ps[sb_][:, q0:TB], func=AF.Exp, bias=negc[:, h:h + 1], scale=1.0),
                        R=[P.psb[sb_], bc], W=[bPT[pi]])
                    if r >= 0:
                        k.op("pool", lambda e, q0=q0, pi=pi: e.tensor_tensor(out=PT[pi][:, q0:q0 + 128], in0=PT[pi][:, q0:q0 + 128],
                                                                             in1=tri[:], op=ALU.mult), R=[bPT[pi], bc], W=[bPT[pi]])
                    for js in range(max(r, 0), 4):
                        bank = ob[js // 2]
                        off = (js % 2) * 129
                        last = (kt == 4 * qb + js)
                        st_ = first[js // 2]
                        first[js // 2] = False
                        k.op("pe", lambda e, js=js, bank=bank, off=off, pi=pi, kt=kt, h=h, st_=st_, last=last: e.matmul(
                            P.ps[bank][:, off:off + 129], lhsT=PT[pi][:, js * 128:(js + 1) * 128],
                            rhs=Vs[:, kt, h * VP:h * VP + 129], start=st_, stop=last, skip_group_check=True),
                            R=[bPT[pi], bV[kb]], W=[P.psb[bank]])
                for js in range(4):
                    bank = ob[js // 2]
                    off = (js % 2) * 129
                    k.op("dve", lambda e, js=js, bank=bank, off=off: e.reciprocal(out=rden[:, js:js + 1],
                                                                                  in_=P.ps[bank][:, off + 128:off + 129]),
                         R=[P.psb[bank]], W=[brden])
                    k.op("dve", lambda e, js=js, bank=bank, off=off, h=h: e.tensor_scalar(
                        out=att[:, js, h * 128:(h + 1) * 128], in0=P.ps[bank][:, off:off + 128], scalar1=rden[:, js:js + 1],
                        scalar2=None, op0=ALU.mult), R=[P.psb[bank], brden], W=[batt[js]])
            for js in range(4):
                psA = P.ps[7][:].bitcast(BF16)
                for h in range(8):
                    k.op("pe", lambda e, js=js, h=h, psA=psA: e.transpose(out=psA[:, h * 128:(h + 1) * 128],
                                                                          in_=att[:, js, h * 128:(h + 1) * 128], identity=P.ident[:]),
                         R=[batt[js], P.b_const], W=[P.psb[7]])
                k.op("dve", lambda e, js=js, psA=psA: e.tensor_copy(out=aT[:, :, js * 128:(js + 1) * 128],
                                                                    in_=psA.rearrange("p (h t) -> p h t", h=8)),
                     R=[P.psb[7]], W=[baT])
            for js in range(4):
                r0 = qb * TB + js * 128
                hr = hres[js % 2]
                bhr = bhres[js % 2]
                k.dma("sp", [(hr[:], src[r0:r0 + 128, :])], R=[b_src[qb * 4 + js]], W=[bhr], key=f"d_hres{js % 2}")
                for n in range(2):
                    bank = (3, 5)[n]
                    for h in range(8):
                        k.op("pe", lambda e, js=js, n=n, h=h, bank=bank: e.matmul(
                            P.ps[bank][:], lhsT=aT[:, h, js * 128:(js + 1) * 128], rhs=wo[:, h, n * 512:(n + 1) * 512],
                            start=(h == 0), stop=(h == 7)), R=[baT, bc], W=[P.psb[bank]])
                    k.op("dve", lambda e, n=n, bank=bank, hr=hr: e.tensor_tensor(
                        out=hr[:, n * 512:(n + 1) * 512], in0=hr[:, n * 512:(n + 1) * 512], in1=P.ps[bank][:],
                        op=ALU.add), R=[bhr, P.psb[bank]], W=[bhr])
                k.dma("sp", [(dst[r0:r0 + 128, :], hr[:])], R=[bhr], W=[b_dst[qb * 4 + js]], key=f"d_hst{js % 2}")
        k.barrier()


def hgrn_phase(P, src, b_src, W):
    k, nc = P.k, P.nc
    I = P.inp
    uT_d = P.dint("uT", [512, S], BF16)
    ycatT = P.dint("ycatT", [1024, S], BF16)
    P.uT_d, P.ycatT = uT_d, ycatT
    P.b_uT = k.bufs(NB, "uT")
    P.b_ya = k.bufs(NB, "ya")
    P.b_yb = k.bufs(NB, "yb")
    with ExitStack() as es:
        bc = k.buf("pa_consts")
        gT = load_vec_fm(P, es, "pa_g", I["norm_mix_g"][0], 8, bc)
        win = P.sb(es, "pa_win", [128, 8, 2560], BF16)
        k.dma("sp", [(win[:, 4 * i:4 * i + 4, :], W["even_w_in"][0][512 * i:512 * (i + 1), :].rearrange("(c p) n -> p c n", p=128))
                     for i in range(2)], R=[W["even_w_in"][1]], W=[bc])
        l0 = load_vec_fm(P, es, "pa_l0", I["hgrn_lb_logits"][0], 4, bc)
        l1 = load_vec_fm(P, es, "pa_l1", I["hgrn_lb_logits"][1], 4, bc)
        lb = P.sb(es, "pa_lb", [128, 4], F32)
        oml = P.sb(es, "pa_oml", [128, 4], F32)
        noml = P.sb(es, "pa_noml", [128, 4], F32)
        k.op("dve", lambda e: e.tensor_tensor(out=lb[:], in0=l0[:], in1=l1[:], op=ALU.subtract), R=[bc], W=[bc])
        k.op("act", lambda e: e.activation(out=lb[:], in_=lb[:], func=AF.Sigmoid), R=[bc], W=[bc])
        k.op("dve", lambda e: e.tensor_scalar(out=oml[:], in0=lb[:], scalar1=-1.0, scalar2=1.0, op0=ALU.mult, op1=ALU.add),
             R=[bc], W=[bc])
        k.op("dve", lambda e: e.tensor_scalar(out=noml[:], in0=oml[:], scalar1=-1.0, scalar2=None, op0=ALU.mult),
             R=[bc], W=[bc])
        ng = P.sb(es, "pa_ng", [128, 512], F32)
        k.dma("sp", [(ng[:], I["hgrn_norm_g"].partition_broadcast(128))], W=[bc])
        msk = P.sb(es, "pa_msk", [128, 128], F32)
        k.op("pool", lambda e: e.affine_select(out=msk[:], in_=P.ones_f[:], pattern=[[1, 128]], compare_op=ALU.is_ge,
                                               fill=0.0, base=0, channel_multiplier=-1), R=[P.b_const], W=[bc])
        k.op("pool", lambda e: e.memset(msk[0:64, 64:128], 0.0), R=[bc], W=[bc])
        rst = P.sb(es, "pa_rst", [128, 512], F32)
        k.op("pool", lambda e: e.memset(rst[:], 1.0), W=[bc])
        k.op("pool", lambda e: e.memset(rst[:, :].rearrange("p (n c) -> p n c", c=64)[:, :, 0:1], 0.0), R=[bc], W=[bc])
        mA = P.sb(es, "pa_mA", [128, 512], BF16)
        mB = P.sb(es, "pa_mB", [128, 512], BF16)
        k.op("pool", lambda e: e.memset(mA[:], 1.0), W=[bc])
        k.op("pool", lambda e: e.memset(mA[:, :].rearrange("p (n c) -> p n c", c=128)[:, :, 64:128], 0.0), R=[bc], W=[bc])
        k.op("pool", lambda e: e.memset(mB[:], 1.0), W=[bc])
        k.op("pool", lambda e: e.memset(mB[:, :].rearrange("p (n c) -> p n c", c=128)[:, :, 0:64], 0.0), R=[bc], W=[bc])
        rmA = P.sb(es, "pa_rmA", [128, 1], F32)
        rmB = P.sb(es, "pa_rmB", [128, 1], F32)
        k.op("pool", lambda e: e.memset(rmA[0:64, :], 1.0), W=[bc])
        k.op("pool", lambda e: e.memset(rmA[64:128, :], 0.0), W=[bc])
        k.op("pool", lambda e: e.memset(rmB[0:64, :], 0.0), W=[bc])
        k.op("pool", lambda e: e.memset(rmB[64:128, :], 1.0), W=[bc])
        St = P.sb(es, "pa_S", [128, 4, 128], F32)
        Sbf = [P.sb(es, f"pa_Sbf{i}", [128, 4, 128], BF16) for i in range(2)]
        bS = k.bufs(4, "pa_S")
        bSbf = [k.bufs(4, "pa_SbfA"), k.bufs(4, "pa_SbfB")]
        k.op("pool", lambda e: e.memset(St[:], 0.0), W=bS)
        k.op("pool", lambda e: e.memset(Sbf[0][:], 0.0), W=bSbf[0])
        k.op("pool", lambda e: e.memset(Sbf[1][:], 0.0), W=bSbf[1])

        T = alloc_blockbufs(P, es, "a_", ps_tr=[7])
        T["b_g"] = bc
        sig = P.sb(es, "pa_sig", [128, 512], F32); bsig = k.buf("pa_sig")
        logf = P.sb(es, "pa_logf", [128, 512], F32); blogf = k.buf("pa_logf")
        bb = P.sb(es, "pa_b", [128, 512], F32); bbb = k.buf("pa_b")
        eb = P.sb(es, "pa_eb", [128, 4, 512], F32); beb = k.bufs(4, "pa_eb")
        enb = P.sb(es, "pa_enb", [128, 512], F32); benb = k.buf("pa_enb")
        kk = P.sb(es, "pa_kk", [128, 512], F32); bkk = k.buf("pa_kk")
        kd32 = P.sb(es, "pa_kd32", [128, 512], F32); bkd32 = k.buf("pa_kd32")
        qd = P.sb(es, "pa_qd", [128, 4, 512], BF16); bqd = k.bufs(4, "pa_qd")
        kd = P.sb(es, "pa_kd", [128, 4, 512], BF16); bkd = k.bufs(4, "pa_kd")
        qdA = P.sb(es, "pa_qdA", [128, 4, 512], BF16); qdB = P.sb(es, "pa_qdB", [128, 4, 512], BF16)
        kbT = P.sb(es, "pa_kbT", [128, 4, 512], BF16); bkbT = k.bufs(4, "pa_kbT")
        uTs = P.sb(es, "pa_uT", [128, 4, 512], BF16); buTs = k.buf("pa_uTs")
        kbtm = P.sb(es, "pa_kbtm", [128, 4, 128], BF16); bkbtm = k.buf("pa_kbtm")
        kbtmB = P.sb(es, "pa_kbtmB", [128, 4, 128], BF16)
        vtm = P.sb(es, "pa_vtm", [128, 512], BF16); bvtm = k.buf("pa_vtm")
        ngsg = P.sb(es, "pa_ngsg", [128, 512], F32); bngsg = k.buf("pa_ngsg")
        attm = P.sb(es, "pa_attm", [128, 4, 128], BF16); battm = k.bufs(4, "pa_attm")
        yatm = P.sb(es, "pa_yatm", [128, 512], BF16); byatm = k.buf("pa_yatm")
        yaT = P.sb(es, "pa_yaT", [128, 4, 512], BF16); byaT = k.buf("pa_yaT")
        sst = P.sb(es, "pa_sst", [128, 16], F32); bsst = k.bufs(4, "pa_sst")
        junk = P.sb(es, "pa_junk", [128, 128], F32); bjunk = k.buf("pa_junk")
        import os
        HL = float(os.environ.get("HG_LIMIT", "99"))
        for blk in range(NB):
            if HL < 1 or (HL < 90 and blk > 0):
                break
            norm_block(P, T, blk, src, b_src, gT)
            hnT, bh = T["hnT"], T["b_hnT"]
            c0 = blk * TB
            for c in range(4):
                pb = 4 + c % 2
                for kc in range(8):
                    k.op("pe", lambda e, c=c, kc=kc, pb=pb: e.matmul(P.ps[pb][:], lhsT=win[:, kc, 2048 + c * 128:2048 + (c + 1) * 128],
                                                                     rhs=hnT[:, kc, :], start=(kc == 0), stop=(kc == 7)),
                         R=[bc, bh], W=[P.psb[pb]])
                k.op("act", lambda e, c=c, pb=pb: e.copy(out=uTs[:, c, :], in_=P.ps[pb][:]), R=[P.psb[pb]], W=[buTs])
            k.dma("sp", [(uT_d[:, c0:c0 + TB].rearrange("(c p) t -> p c t", p=128), uTs[:])], R=[buTs], W=[P.b_uT[blk]])
            if HL < 1.1:
                break
            for h in range(4):
                if HL < 1.2 and h > 0:
                    break
                pq, pf = 4, 5
                for kc in range(8):
                    k.op("pe", lambda e, h=h, kc=kc: e.matmul(P.ps[pf][:], lhsT=win[:, kc, 512 + h * 128:512 + (h + 1) * 128],
                                                              rhs=hnT[:, kc, :], start=(kc == 0), stop=(kc == 7)),
                         R=[bc, bh], W=[P.psb[pf]])
                for kc in range(8):
                    k.op("pe", lambda e, h=h, kc=kc: e.matmul(P.ps[pq][:], lhsT=win[:, kc, h * 128:(h + 1) * 128],
                                                              rhs=hnT[:, kc, :], start=(kc == 0), stop=(kc == 7)),
                         R=[bc, bh], W=[P.psb[pq]])
                k.op("act", lambda e: e.activation(out=sig[:], in_=P.ps[pf][:], func=AF.Sigmoid), R=[P.psb[pf]], W=[bsig])
                k.op("act", lambda e, h=h: e.activation(out=logf[:], in_=sig[:], func=AF.Ln, bias=lb[:, h:h + 1],
                                                        scale=oml[:, h:h + 1]), R=[bsig, bc], W=[blogf])
                if HL < 1.15:
                    break
                k.op("dve", lambda e: e.tensor_tensor_scan(out=bb[:], data0=rst[:], data1=logf[:], initial=0.0,
                                                           op0=ALU.mult, op1=ALU.add), R=[blogf, bc], W=[bbb])
                if HL < 1.17:
                    break
                k.op("act", lambda e, h=h: e.activation(out=eb[:, h, :], in_=bb[:], func=AF.Exp), R=[bbb], W=[beb[h]])
                k.op("act", lambda e: e.activation(out=enb[:], in_=bb[:], func=AF.Exp, scale=-1.0), R=[bbb], W=[benb])
                k.op("dve", lambda e, h=h: e.tensor_scalar(out=kk[:], in0=sig[:], scalar1=noml[:, h:h + 1], scalar2=oml[:, h:h + 1],
                                                           op0=ALU.mult, op1=ALU.add), R=[bsig, bc], W=[bkk])
                k.op("dve", lambda e, h=h: e.tensor_tensor(out=qd[:, h, :], in0=eb[:, h, :], in1=P.ps[pq][:], op=ALU.mult),
                     R=[beb[h], P.psb[pq]], W=[bqd[h]])
                k.op("pool", lambda e, h=h: e.tensor_tensor(out=qdA[:, h, :], in0=qd[:, h, :], in1=mA[:], op=ALU.mult),
                     R=[bqd[h], bc], W=[bqd[h]])
                k.op("pool", lambda e, h=h: e.tensor_tensor(out=qdB[:, h, :], in0=qd[:, h, :], in1=mB[:], op=ALU.mult),
                     R=[bqd[h], bc], W=[bqd[h]])
                k.op("dve", lambda e: e.tensor_tensor(out=kd32[:], in0=kk[:], in1=enb[:], op=ALU.mult), R=[bkk, benb], W=[bkd32])
                if HL < 1.18:
                    break
                k.op("pool", lambda e, h=h: e.tensor_copy(out=kd[:, h, :], in_=kd32[:]), R=[bkd32], W=[bkd[h]])
                k.op("pool", lambda e, h=h: e.tensor_tensor(
                    out=kbT[:, h, :].rearrange("p (n c) -> p n c", c=64), in0=kd32[:, :].rearrange("p (n c) -> p n c", c=64),
                    in1=eb[:, h, :].rearrange("p (n c) -> p n c", c=64)[:, :, 63:64].broadcast_to([128, 8, 64]), op=ALU.mult),
                    R=[bkd32, beb[h]], W=[bkbT[h]])
            for j in range(4):
                if HL < 2 or (HL < 90 and j > 0):
                    break
                tok = slice(j * 128, (j + 1) * 128)
                for kc in range(8):
                    k.op("pe", lambda e, kc=kc, tok=tok: e.matmul(P.ps[4][:], lhsT=hnT[:, kc, tok], rhs=win[:, kc, 1024:1536],
                                                                  start=(kc == 0), stop=(kc == 7)), R=[bc, bh], W=[P.psb[4]])
                for kc in range(8):
                    k.op("pe", lambda e, kc=kc, tok=tok: e.matmul(P.ps[5][:], lhsT=hnT[:, kc, tok], rhs=win[:, kc, 1536:2048],
                                                                  start=(kc == 0), stop=(kc == 7)), R=[bc, bh], W=[P.psb[5]])
                k.op("act", lambda e: e.copy(out=vtm[:], in_=P.ps[4][:]), R=[P.psb[4]], W=[bvtm])
                k.op("act", lambda e: e.activation(out=ngsg[:], in_=P.ps[5][:], func=AF.Silu), R=[P.psb[5]], W=[bngsg])
                k.op("pool", lambda e: e.tensor_tensor(out=ngsg[:], in0=ngsg[:], in1=ng[:], op=ALU.mult), R=[bngsg, bc], W=[bngsg])
                psK = P.ps[6][:].bitcast(BF16)
                for h in range(4):
                    k.op("pe", lambda e, h=h, tok=tok, psK=psK: e.transpose(out=psK[:, h * 128:(h + 1) * 128], in_=kbT[:, h, tok],
                                                                            identity=P.ident[:]), R=[bkbT[h], P.b_const], W=[P.psb[6]])
                k.op("act", lambda e, psK=psK: e.activation(out=kbtm[:], in_=psK[:, 0:512].rearrange("p (h d) -> p h d", h=4),
                                                            func=AF.Copy, scale=rmA[:, 0:1]), R=[P.psb[6], bc], W=[bkbtm])
                k.op("act", lambda e, psK=psK: e.activation(out=kbtmB[:], in_=psK[:, 0:512].rearrange("p (h d) -> p h d", h=4),
                                                            func=AF.Copy, scale=rmB[:, 0:1]), R=[P.psb[6], bc], W=[bkbtm])
                tA = j * 128 + 63
                tB = j * 128 + 127
                if HL < 3:
                    break
                for h in range(4):
                    pb = h
                    vh = vtm[:, h * 128:(h + 1) * 128]
                    k.op("pe", lambda e, h=h, tok=tok, pb=pb: e.matmul(P.ps[pb][:, 0:128], lhsT=kd[:, h, tok], rhs=qd[:, h, tok],
                                                                       start=True, stop=True), R=[bkd[h], bqd[h]], W=[P.psb[pb]])
                    k.op("pe", lambda e, h=h, pb=pb, vh=vh: e.matmul(P.ps[pb][:, 128:256], lhsT=kbtm[:, h, :], rhs=vh,
                                                                     start=True, stop=True), R=[bkbtm, bvtm], W=[P.psb[pb]])
                    k.op("pe", lambda e, h=h, pb=pb, vh=vh: e.matmul(P.ps[pb][:, 256:384], lhsT=kbtmB[:, h, :], rhs=vh,
                                                                     start=True, stop=True), R=[bkbtm, bvtm], W=[P.psb[pb]])
                    k.op("dve", lambda e, h=h, pb=pb: e.tensor_tensor(out=attm[:, h, :], in0=P.ps[pb][:, 0:128], in1=msk[:], op=ALU.mult),
                         R=[P.psb[pb], bc], W=[battm[h]])
                    k.op("dve", lambda e, h=h, pb=pb, tA=tA: e.scalar_tensor_tensor(
                        out=St[:, h, :], in0=St[:, h, :], scalar=eb[:, h, tA:tA + 1], in1=P.ps[pb][:, 128:256], op0=ALU.mult, op1=ALU.add),
                        R=[bS[h], beb[h], P.psb[pb]], W=[bS[h]])
                    k.op("act", lambda e, h=h: e.copy(out=Sbf[1][:, h, :], in_=St[:, h, :]), R=[bS[h]], W=[bSbf[1][h]])
                if HL < 4:
                    break
                for h in range(4):
                    pb = h
                    po = 4 + h % 2
                    vh = vtm[:, h * 128:(h + 1) * 128]
                    tokA = slice(j * 128, j * 128 + 64)
                    tokB = slice(j * 128 + 64, j * 128 + 128)
                    k.op("pe", lambda e, h=h, po=po, vh=vh: e.matmul(P.ps[po][:, 0:128], lhsT=attm[:, h, :], rhs=vh,
                                                                     start=True, stop=False), R=[battm[h], bvtm], W=[P.psb[po]])
                    k.op("pe", lambda e, h=h, po=po, tok=tok: e.matmul(P.ps[po][:, 0:128], lhsT=qdA[:, h, tok], rhs=Sbf[0][:, h, :],
                                                                       start=False, stop=False),
                         R=[bqd[h], bSbf[0][h]], W=[P.psb[po]])
                    k.op("pe", lambda e, h=h, po=po, tok=tok: e.matmul(P.ps[po][:, 0:128], lhsT=qdB[:, h, tok], rhs=Sbf[1][:, h, :],
                                                                       start=False, stop=True),
                         R=[bqd[h], bSbf[1][h]], W=[P.psb[po]])
                    k.op("dve", lambda e, h=h, pb=pb, tB=tB: e.scalar_tensor_tensor(
                        out=St[:, h, :], in0=St[:, h, :], scalar=eb[:, h, tB:tB + 1], in1=P.ps[pb][:, 256:384], op0=ALU.mult, op1=ALU.add),
                        R=[bS[h], beb[h], P.psb[pb]], W=[bS[h]])
                    k.op("act", lambda e, h=h: e.copy(out=Sbf[0][:, h, :], in_=St[:, h, :]), R=[bS[h]], W=[bSbf[0][h]])
                    k.op("act", lambda e, h=h, po=po: e.activation(out=junk[:], in_=P.ps[po][:, 0:128], func=AF.Square,
                                                                   accum_out=sst[:, h:h + 1]), R=[P.psb[po]], W=[bjunk, bsst[h]])
                    k.op("act", lambda e, h=h: e.activation(out=sst[:, 4 + h:5 + h], in_=sst[:, h:h + 1], func=AF.Sqrt, bias=P.epsb[:],
                                                            scale=1.0 / 128), R=[bsst[h], P.b_const], W=[bsst[h]])
                    k.op("dve", lambda e, h=h: e.reciprocal(out=sst[:, 8 + h:9 + h], in_=sst[:, 4 + h:5 + h]), R=[bsst[h]], W=[bsst[h]])
                    k.op("dve", lambda e, h=h, po=po: e.scalar_tensor_tensor(
                        out=yatm[:, h * 128:(h + 1) * 128], in0=P.ps[po][:, 0:128], scalar=sst[:, 8 + h:9 + h],
                        in1=ngsg[:, h * 128:(h + 1) * 128], op0=ALU.mult, op1=ALU.mult),
                        R=[P.psb[po], bsst[h], bngsg], W=[byatm])
                if HL < 5:
                    break
                psY = P.ps[6][:].bitcast(BF16)
                for h in range(4):
                    k.op("pe", lambda e, h=h, psY=psY: e.transpose(out=psY[:, h * 128:(h + 1) * 128], in_=yatm[:, h * 128:(h + 1) * 128],
                                                                   identity=P.ident[:]), R=[byatm, P.b_const], W=[P.psb[6]])
                k.op("dve", lambda e, psY=psY, tok=tok: e.tensor_copy(out=yaT[:, :, tok], in_=psY[:, 0:512].rearrange("p (h t) -> p h t", h=4)),
                     R=[P.psb[6]], W=[byaT])
            if HL < 6:
                break
            k.dma("sp", [(ycatT[0:512, c0:c0 + TB].rearrange("(c p) t -> p c t", p=128), yaT[:])], R=[byaT], W=[P.b_ya[blk]])
        k.barrier()


def s5_phase(P, W):
    k, nc = P.k, P.nc
    I = P.inp
    L = 16
    NCH = S // L
    G = 32
    with ExitStack() as es:
        bs = k.buf("s5_tab")

        def dv(fn, R=(), Wb=()):
            k.op("dve", fn, R=[bs] + list(R), W=[bs] + list(Wb))

        def ac(fn, R=(), Wb=()):
            k.op("act", fn, R=[bs] + list(R), W=[bs] + list(Wb))

        def t32(name, shape=(128, 32)):
            return P.sb(es, "s5_" + name, list(shape), F32)

        AR, AI, LDT = t32("AR"), t32("AI"), t32("LDT")
        k.dma("sp", [(AR[0:64, :], I["s5_a_re"].rearrange("g p -> p g")), (AR[64:128, :], I["s5_a_re"].rearrange("g p -> p g")),
                     (AI[0:64, :], I["s5_a_im"].rearrange("g p -> p g")), (AI[64:128, :], I["s5_a_im"].rearrange("g p -> p g")),
                     (LDT[:], I["s5_log_dt"].partition_broadcast(128))], W=[bs], allow_slow_non_contiguous=True)
        B1 = t32("B1", (128, 32, 16))
        B2 = t32("B2", (128, 32, 16))
        k.dma("sp", [(B1[0:64], I["s5_b_re"].rearrange("g p h -> p g h")), (B1[64:128], I["s5_b_im"].rearrange("g p h -> p g h")),
                     (B2[0:64], I["s5_b_im"].rearrange("g p h -> p g h")), (B2[64:128], I["s5_b_re"].rearrange("g p h -> p g h"))],
              W=[bs], key="d_s5_b")
        CN1 = t32("CN1", (128, 4, 128))
        CN2 = t32("CN2", (128, 4, 128))
        cre = I["s5_c_re"].rearrange("(f a) h p -> (a h) f p", f=4)
        cim = I["s5_c_im"].rearrange("(f a) h p -> (a h) f p", f=4)
        k.dma("sp", [(CN1[:, :, 0:64], cre), (CN1[:, :, 64:128], cim), (CN2[:, :, 0:64], cim), (CN2[:, :, 64:128], cre)],
              W=[bs], key="d_s5_c")
        dT = load_vec_fm(P, es, "s5_dT", I["s5_d"], 4, bs)
        bgl = load_vec_fm(P, es, "s5_bglu", I["s5_b_glu"], 4, bs)
        sgn = t32("sgn", (128, 1))
        dv(lambda e: e.memset(sgn[0:64, :], -1.0))
        dv(lambda e: e.memset(sgn[64:128, :], 1.0))
        DT, XR, TH = t32("DT"), t32("XR"), t32("TH")
        ac(lambda e: e.activation(out=DT[:], in_=LDT[:], func=AF.Exp))
        dv(lambda e: e.tensor_tensor(out=XR[:], in0=AR[:], in1=DT[:], op=ALU.mult))
        dv(lambda e: e.tensor_tensor(out=TH[:], in0=AI[:], in1=DT[:], op=ALU.mult))
        MAG = t32("MAG")
        ac(lambda e: e.activation(out=MAG[:], in_=XR[:], func=AF.Exp))
        tf, r1, r2 = t32("tf"), t32("r1"), t32("r2")
        ti_ = P.sb(es, "s5_ti", [128, 32], I32)
        dv(lambda e: e.tensor_scalar(out=tf[:], in0=TH[:], scalar1=1.0 / TWO_PI, scalar2=None, op0=ALU.mult))
        dv(lambda e: e.tensor_copy(out=ti_[:], in_=tf[:]))
        dv(lambda e: e.tensor_copy(out=tf[:], in_=ti_[:]))
        dv(lambda e: e.scalar_tensor_tensor(out=r1[:], in0=tf[:], scalar=-CW1, in1=TH[:], op0=ALU.mult, op1=ALU.add))
        dv(lambda e: e.scalar_tensor_tensor(out=r1[:], in0=tf[:], scalar=-CW2, in1=r1[:], op0=ALU.mult, op1=ALU.add))
        dv(lambda e: e.tensor_scalar(out=r1[:], in0=r1[:], scalar1=-math.pi, scalar2=math.pi, op0=ALU.max, op1=ALU.min))
        SN, CS = t32("SN"), t32("CS")
        ac(lambda e: e.activation(out=SN[:], in_=r1[:], func=AF.Sin))
        dv(lambda e: e.tensor_scalar(out=tf[:], in0=r1[:], scalar1=math.pi / 2, scalar2=math.pi, op0=ALU.add, op1=ALU.is_gt))
        dv(lambda e: e.scalar_tensor_tensor(out=tf[:], in0=tf[:], scalar=-TWO_PI, in1=r1[:], op0=ALU.mult, op1=ALU.add))
        dv(lambda e: e.tensor_scalar(out=tf[:], in0=tf[:], scalar1=math.pi / 2, scalar2=math.pi, op0=ALU.add, op1=ALU.min))
        ac(lambda e: e.activation(out=CS[:], in_=tf[:], func=AF.Sin))
        pows = list(range(0, 17)) + [16 << i for i in range(1, 8)]
        TR = {n: t32(f"TR{n}") for n in pows}
        TI = {n: t32(f"TI{n}") for n in pows}
        dv(lambda e: e.memset(TR[0][:], 1.0))
        dv(lambda e: e.memset(TI[0][:], 0.0))
        dv(lambda e: e.tensor_tensor(out=TR[1][:], in0=MAG[:], in1=CS[:], op=ALU.mult))
        dv(lambda e: e.tensor_tensor(out=TI[1][:], in0=MAG[:], in1=SN[:], op=ALU.mult))
        ta, tb = t32("ta"), t32("tb")

        def cmul(oR, oI, aR, aI, bR, bI):
            dv(lambda e: e.tensor_tensor(out=ta[:], in0=aI[:], in1=bI[:], op=ALU.mult))
            dv(lambda e: e.tensor_tensor(out=tb[:], in0=aR[:], in1=bI[:], op=ALU.mult))
            dv(lambda e: e.tensor_tensor(out=oR[:], in0=aR[:], in1=bR[:], op=ALU.mult))
            dv(lambda e: e.tensor_tensor(out=oR[:], in0=oR[:], in1=ta[:], op=ALU.subtract))
            dv(lambda e: e.tensor_tensor(out=oI[:], in0=aI[:], in1=bR[:], op=ALU.mult))
            dv(lambda e: e.tensor_tensor(out=oI[:], in0=oI[:], in1=tb[:], op=ALU.add))
        for n in range(2, 17):
            cmul(TR[n], TI[n], TR[n - 1], TI[n - 1], TR[1], TI[1])
        for i in range(1, 8):
            n = 16 << i
            cmul(TR[n], TI[n], TR[n // 2], TI[n // 2], TR[n // 2], TI[n // 2])
        den, xr_, cr_, ci_ = t32("den"), t32("xr"), t32("cr"), t32("ci")
        dv(lambda e: e.tensor_tensor(out=den[:], in0=AR[:], in1=AR[:], op=ALU.mult))
        dv(lambda e: e.tensor_tensor(out=ta[:], in0=AI[:], in1=AI[:], op=ALU.mult))
        dv(lambda e: e.tensor_tensor(out=den[:], in0=den[:], in1=ta[:], op=ALU.add))
        dv(lambda e: e.reciprocal(out=den[:], in_=den[:]))
        dv(lambda e: e.tensor_scalar(out=xr_[:], in0=TR[1][:], scalar1=-1.0, scalar2=None, op0=ALU.add))
        dv(lambda e: e.tensor_tensor(out=cr_[:], in0=xr_[:], in1=AR[:], op=ALU.mult))
        dv(lambda e: e.tensor_tensor(out=ta[:], in0=TI[1][:], in1=AI[:], op=ALU.mult))
        dv(lambda e: e.tensor_tensor(out=cr_[:], in0=cr_[:], in1=ta[:], op=ALU.add))
        dv(lambda e: e.tensor_tensor(out=cr_[:], in0=cr_[:], in1=den[:], op=ALU.mult))
        dv(lambda e: e.tensor_tensor(out=ci_[:], in0=TI[1][:], in1=AR[:], op=ALU.mult))
        dv(lambda e: e.tensor_tensor(out=ta[:], in0=xr_[:], in1=AI[:], op=ALU.mult))
        dv(lambda e: e.tensor_tensor(out=ci_[:], in0=ci_[:], in1=ta[:], op=ALU.subtract))
        dv(lambda e: e.tensor_tensor(out=ci_[:], in0=ci_[:], in1=den[:], op=ALU.mult))
        cis = t32("cis")
        dv(lambda e: e.tensor_scalar(out=cis[:], in0=ci_[:], scalar1=sgn[:, 0:1], scalar2=None, op0=ALU.mult))
        ncis = t32("ncis")
        dv(lambda e: e.tensor_scalar(out=ncis[:], in0=cis[:], scalar1=-1.0, scalar2=None, op0=ALU.mult))

        def bc16(t):
            return t[:, :].unsqueeze(2).broadcast_to([128, 32, 16])
        BB = t32("BB", (128, 32, 16))
        BBs = t32("BBs", (128, 32, 16))
        tmp3 = t32("tmp3", (128, 32, 16))
        dv(lambda e: e.tensor_tensor(out=BB[:], in0=B1[:], in1=bc16(cr_), op=ALU.mult))
        dv(lambda e: e.tensor_tensor(out=tmp3[:], in0=B2[:], in1=bc16(cis), op=ALU.mult))
        dv(lambda e: e.tensor_tensor(out=BB[:], in0=BB[:], in1=tmp3[:], op=ALU.add))
        dv(lambda e: e.tensor_tensor(out=BBs[:], in0=B2[:], in1=bc16(cr_), op=ALU.mult))
        dv(lambda e: e.tensor_tensor(out=tmp3[:], in0=B1[:], in1=bc16(ncis), op=ALU.mult))
        dv(lambda e: e.tensor_tensor(out=BBs[:], in0=BBs[:], in1=tmp3[:], op=ALU.add))
        CT1 = t32("CT1", (128, 32, 16))
        CT2 = t32("CT2", (128, 32, 16))
        for f in range(4):
            for (CN, CT) in ((CN1, CT1), (CN2, CT2)):
                k.op("pe", lambda e, f=f, CN=CN: e.transpose(out=P.ps[0][:, 0:128], in_=CN[:, f, :], identity=P.ident_f[:]),
                     R=[bs, P.b_const], W=[P.psb[0]])
                dv(lambda e, f=f, CT=CT: e.tensor_copy(out=CT[:, 8 * f:8 * f + 8, :].rearrange("p g h -> p (g h)"), in_=P.ps[0][:, 0:128]),
                   R=[P.psb[0]])
        CTn = t32("CTn", (128, 32, 16))
        dv(lambda e: e.tensor_scalar(out=CTn[:], in0=CT1[:], scalar1=sgn[:, 0:1], scalar2=-1.0, op0=ALU.mult, op1=ALU.mult))
        pi_ = P.sb(es, "s5_pi", [128, 1], I32)
        k.op("pool", lambda e: e.iota(out=pi_[:], pattern=[[0, 1]], base=0, channel_multiplier=1), R=[bs], W=[bs])
        pg_i = P.sb(es, "s5_pgi", [128, 1], I32)
        om_i = P.sb(es, "s5_omi", [128, 1], I32)
        dv(lambda e: e.tensor_scalar(out=pg_i[:], in0=pi_[:], scalar1=4, scalar2=None, op0=ALU.arith_shift_right))
        dv(lambda e: e.tensor_scalar(out=om_i[:], in0=pg_i[:], scalar1=1, scalar2=None, op0=ALU.bitwise_and))
        pgf, om, em = t32("pgf", (128, 1)), t32("om", (128, 1)), t32("em", (128, 1))
        dv(lambda e: e.tensor_copy(out=pgf[:], in_=pg_i[:]))
        dv(lambda e: e.tensor_copy(out=om[:], in_=om_i[:]))
        dv(lambda e: e.tensor_scalar(out=em[:], in0=om[:], scalar1=-1.0, scalar2=1.0, op0=ALU.mult, op1=ALU.add))
        ji = P.sb(es, "s5_ji", [128, 128], I32)
        k.op("pool", lambda e: e.iota(out=ji[:], pattern=[[1, 128]], base=0, channel_multiplier=0), R=[bs], W=[bs])
        dv(lambda e: e.tensor_scalar(out=ji[:], in0=ji[:], scalar1=4, scalar2=None, op0=ALU.arith_shift_right))
        bmask = t32("bmask", (128, 128))
        dv(lambda e: e.tensor_copy(out=bmask[:], in_=ji[:]))
        dv(lambda e: e.tensor_scalar(out=bmask[:], in0=bmask[:], scalar1=pgf[:, 0:1], scalar2=None, op0=ALU.is_equal))
        esw = t32("esw", (128, 128))
        dv(lambda e: e.memset(esw[:], 0.0))
        dv(lambda e: e.tensor_copy(out=esw[0:64, 64:128], in_=P.ident_f[0:64, 0:64]), R=[P.b_const])
        dv(lambda e: e.tensor_copy(out=esw[64:128, 0:64], in_=P.ident_f[64:128, 64:128]), R=[P.b_const])

        uT = P.sb(es, "s5_uT", [128, 4, S], BF16)
        buT = k.bufs(NB, "s5_uT")
        for b in range(NB):
            c0 = b * TB
            k.dma("sp", [(uT[:, :, c0:c0 + TB], P.uT_d[:, c0:c0 + TB].rearrange("(c p) t -> p c t", p=128))],
                  R=[P.b_uT[b]], W=[buT[b]])
        Xb = P.sb(es, "s5_Xb", [128, G, NCH], BF16)
        bXb = k.bufs(G, "s5_Xb")
        Kb = P.sb(es, "s5_Kb", [128, 4, L, 128], BF16)
        bKb = k.buf("s5_Kb")

        def mb_table(n, MB):
            tis = t32(f"tis_{n}") if False else ta
            dv(lambda e: e.tensor_scalar(out=ta[:], in0=TI[n][:], scalar1=sgn[:, 0:1], scalar2=None, op0=ALU.mult))
            dv(lambda e: e.tensor_tensor(out=MB[:], in0=BB[:], in1=bc16(TR[n]), op=ALU.mult))
            dv(lambda e: e.tensor_tensor(out=tmp3[:], in0=BBs[:], in1=bc16(ta), op=ALU.mult))
            dv(lambda e: e.tensor_tensor(out=MB[:], in0=MB[:], in1=tmp3[:], op=ALU.add))

        with ExitStack() as es2:
            Pm = P.sb(es2, "s5_Pm", [128, 4, L, 2, 128], BF16)
            bPm = k.buf("s5_Pm")
            MB = P.sb(es2, "s5_MB", [128, 32, 16], F32)
            for n in range(L):
                mb_table(n, MB)
                j = L - 1 - n
                for f in range(4):
                    mbf = MB[:, 8 * f:8 * f + 8, :].rearrange("p g h -> p (g h)")
                    k.op("pe", lambda e, mbf=mbf: e.transpose(out=P.ps[0][:, 0:128], in_=mbf, identity=P.ident_f[:]),
                         R=[bs, P.b_const], W=[P.psb[0]])
                    k.op("dve", lambda e, f=f, j=j: e.tensor_scalar(out=Pm[:, f, j, 0, :], in0=P.ps[0][:, 0:128], scalar1=em[:, 0:1],
                                                                    scalar2=None, op0=ALU.mult), R=[P.psb[0], bs], W=[bPm])
                    k.op("dve", lambda e, f=f, j=j: e.tensor_scalar(out=Pm[:, f, j, 1, :], in0=P.ps[0][:, 0:128], scalar1=om[:, 0:1],
                                                                    scalar2=None, op0=ALU.mult), R=[P.psb[0], bs], W=[bPm])
                    ctf = CTn[:, 8 * f:8 * f + 8, :].rearrange("p g h -> p (g h)")
                    k.op("pe", lambda e, mbf=mbf, ctf=ctf: e.matmul(P.ps[1][:, 0:128], lhsT=mbf, rhs=ctf, start=True, stop=True),
                         R=[bs], W=[P.psb[1]])
                    if n == 0:
                        k.op("dve", lambda e: e.tensor_tensor(out=tmpK[:], in0=P.ps[1][:, 0:128], in1=bmask[:], op=ALU.mult),
                             R=[P.psb[1], bs], W=[bs]) if False else None
                    k.op("dve", lambda e, f=f, n=n: e.tensor_tensor(out=Kb[:, f, n, :], in0=P.ps[1][:, 0:128], in1=bmask[:], op=ALU.mult),
                         R=[P.psb[1], bs], W=[bKb])
                    if n == 0:
                        k.op("dve", lambda e, f=f: e.scalar_tensor_tensor(out=Kb[:, f, 0, :], in0=P.ident_f[:], scalar=dT[:, f:f + 1],
                                                                          in1=Kb[:, f, 0, :], op0=ALU.mult, op1=ALU.add),
                             R=[bKb, bs, P.b_const], W=[bKb])
            Xs = [P.sb(es2, f"s5_Xs{i}", [128, NCH], F32) for i in range(2)]
            bXs = k.bufs(2, "s5_Xs")
            MT = [P.sb(es2, f"s5_MT{i}", [128, 128], F32) for i in range(4)]
            bMT = k.bufs(4, "s5_MT")
            tM = [P.sb(es2, f"s5_tM{i}", [128, 128], F32) for i in range(2)]
            btM = k.bufs(2, "s5_tM")
            w2 = P.sb(es2, "s5_w2", [128, 32, 8], F32)
            for i in range(8):
                n = 16 << i
                dv(lambda e, i=i, n=n: e.tensor_scalar(out=w2[:, :, i], in0=TI[n][:], scalar1=sgn[:, 0:1], scalar2=-1.0,
                                                       op0=ALU.mult, op1=ALU.mult))
            nmt = 0
            for g in range(G):
                f, gq = g // 8, (g % 8) // 2
                m = g % 2
                vb = 2
                for j in range(L):
                    k.op("pe", lambda e, f=f, gq=gq, m=m, j=j, vb=vb: e.matmul(
                        P.ps[vb][:, 0:NCH], lhsT=Pm[32 * gq:32 * gq + 32, f, j, m, :],
                        rhs=uT[32 * gq:32 * gq + 32, f, :].rearrange("p (c l) -> p l c", l=L)[:, j, :],
                        start=(j == 0), stop=(j == L - 1), tile_position=(32 * gq, 0)),
                        R=[bPm] + buT, W=[P.psb[vb]])
                X = Xs[g % 2]
                k.op("act", lambda e, X=X, vb=vb: e.copy(out=X[:], in_=P.ps[vb][:, 0:NCH]), R=[P.psb[vb]], W=[bXs[g % 2]])
                for i in range(8):
                    sh = 1 << i
                    n = 16 << i
                    mt = MT[nmt % 4]
                    bmt = bMT[nmt % 4]
                    tm_ = tM[nmt % 2]
                    btm = btM[nmt % 2]
                    nmt += 1
                    k.op("pool", lambda e, tm_=tm_, g=g, i=i: e.tensor_scalar(out=tm_[:], in0=esw[:], scalar1=w2[:, g, i:i + 1], scalar2=None,
                                                                              op0=ALU.mult), R=[bs], W=[btm])
                    k.op("pool", lambda e, mt=mt, g=g, n=n: e.tensor_scalar(out=mt[:], in0=P.ident_f[:], scalar1=TR[n][:, g:g + 1],
                                                                            scalar2=None, op0=ALU.mult), R=[bs, P.b_const], W=[bmt])
                    k.op("pool", lambda e, mt=mt, tm_=tm_: e.tensor_tensor(out=mt[:], in0=mt[:], in1=tm_[:], op=ALU.add),
                         R=[btm, bmt], W=[bmt])
                    pb = 4 + i % 2
                    k.op("pe", lambda e, mt=mt, X=X, sh=sh, pb=pb: e.matmul(P.ps[pb][:, sh:NCH], lhsT=mt[:], rhs=X[:, 0:NCH - sh],
                                                                            start=True, stop=True), R=[bmt, bXs[g % 2]], W=[P.psb[pb]])
                    k.op("dve", lambda e, X=X, sh=sh, pb=pb: e.tensor_tensor(out=X[:, sh:NCH], in0=X[:, sh:NCH], in1=P.ps[pb][:, sh:NCH],
                                                                             op=ALU.add), R=[P.psb[pb], bXs[g % 2]], W=[bXs[g % 2]])
                k.op("pool", lambda e, g=g: e.memset(Xb[:, g, 0:1], 0.0), W=[bXb[g]])
                k.op("act", lambda e, g=g, X=X: e.copy(out=Xb[:, g, 1:NCH], in_=X[:, 0:NCH - 1]), R=[bXs[g % 2]], W=[bXb[g]])
            k.barrier()
        with ExitStack() as es3:
            Qm = P.sb(es3, "s5_Qm", [128, L, 16, 2, 32], BF16)
            bQm = k.buf("s5_Qm")
            Qt = P.sb(es3, "s5_Qt", [128, 32, 16], F32)
            trs = P.sb(es3, "s5_trs", [128, 32], F32)
            nti = P.sb(es3, "s5_nti", [128, 32], F32)
            mpat = P.sb(es3, "s5_mpat", [128, 2, 2, 16], F32)
            dv(lambda e: e.memset(mpat[:], 0.0))
            dv(lambda e: e.memset(mpat[:, 0, 0, :], 1.0))
            dv(lambda e: e.memset(mpat[:, 1, 1, :], 1.0))
            for tau in range(L):
                n = tau + 1
                dv(lambda e, n=n: e.tensor_scalar(out=trs[:], in0=TR[n][:], scalar1=sgn[:, 0:1], scalar2=-1.0, op0=ALU.mult, op1=ALU.mult))
                dv(lambda e, n=n: e.tensor_scalar(out=nti[:], in0=TI[n][:], scalar1=-1.0, scalar2=None, op0=ALU.mult))
                dv(lambda e: e.tensor_tensor(out=Qt[:], in0=CT1[:], in1=bc16(trs), op=ALU.mult))
                dv(lambda e: e.tensor_tensor(out=tmp3[:], in0=CT2[:], in1=bc16(nti), op=ALU.mult))
                dv(lambda e: e.tensor_tensor(out=Qt[:], in0=Qt[:], in1=tmp3[:], op=ALU.add))
                for mem in range(2):
                    k.op("dve", lambda e, tau=tau, mem=mem: e.tensor_tensor(
                        out=Qm[:, tau, :, mem, :], in0=Qt[:, :, :].rearrange("p (q m) h -> p q (m h)", m=2),
                        in1=mpat[:, mem, :, :].rearrange("p m h -> p (m h)").unsqueeze(1).broadcast_to([128, 16, 32]), op=ALU.mult),
                        R=[bs], W=[bQm])
            zT = P.sb(es3, "s5_zT", [128, 4, S], BF16)
            bz = k.bufs(4, "s5_zT")
            ys = [P.sb(es3, f"s5_ys{i}", [128, NCH], F32) for i in range(2)]
            bys = k.bufs(2, "s5_ys")
            yv = [P.sb(es3, f"s5_yv{i}", [128, NCH], F32) for i in range(2)]
            byv = k.bufs(2, "s5_yv")
            y2 = [P.sb(es3, f"s5_y2{i}", [128, NCH], F32) for i in range(2)]
            by2 = k.bufs(2, "s5_y2")
            cnt = 0
            for f in range(4):
                ul = uT[:, f, :].rearrange("p (c l) -> p l c", l=L)
                for tau in range(L):
                    i2 = cnt % 2
                    pb = cnt % 4
                    cnt += 1
                    for q in range(4):
                        for mem in range(2):
                            g = f * 8 + q * 2 + mem
                            k.op("pe", lambda e, q=q, mem=mem, g=g, tau=tau, pb=pb, f=f: e.matmul(
                                P.ps[pb][32 * q:32 * q + 32, 256:512], lhsT=Qm[:, tau, f * 4 + q, mem, :], rhs=Xb[:, g, :],
                                start=(mem == 0), stop=(mem == 1), tile_position=(0, 32 * q)),
                                R=[bQm, bXb[g]], W=[P.psb[pb]])
                    k.op("act", lambda e, pb=pb, i2=i2: e.copy(out=ys[i2][:], in_=P.ps[pb][:, 256:512]), R=[P.psb[pb]], W=[bys[i2]])
                    for j in range(tau + 1):
                        k.op("pe", lambda e, f=f, tau=tau, j=j, pb=pb, ul=ul: e.matmul(
                            P.ps[pb][:, 0:256], lhsT=Kb[:, f, tau - j, :], rhs=ul[:, j, :], start=(j == 0), stop=(j == tau)),
                            R=[bKb] + buT, W=[P.psb[pb]])
                    Y = yv[i2]
                    k.op("dve", lambda e, pb=pb, i2=i2, Y=Y: e.tensor_tensor(out=Y[:], in0=ys[i2][:], in1=P.ps[pb][:, 0:256], op=ALU.add),
                         R=[bys[i2], P.psb[pb]], W=[byv[i2]])
                    Y2 = y2[i2]
                    k.op("pool", lambda e, Y=Y, Y2=Y2: e.tensor_tensor(out=Y2[:], in0=Y[:], in1=Y[:], op=ALU.mult), R=[byv[i2]], W=[by2[i2]])
                    k.op("pool", lambda e, Y2=Y2: e.tensor_scalar(out=Y2[:], in0=Y2[:], scalar1=0.044715, scalar2=1.0, op0=ALU.mult, op1=ALU.add),
                         R=[by2[i2]], W=[by2[i2]])
                    k.op("dve", lambda e, Y=Y, Y2=Y2: e.tensor_tensor(out=Y2[:], in0=Y2[:], in1=Y[:], op=ALU.mult), R=[by2[i2], byv[i2]], W=[by2[i2]])
                    k.op("act", lambda e, Y2=Y2: e.activation(out=Y2[:], in_=Y2[:], func=AF.Sigmoid, scale=1.5957691216057308),
                         R=[by2[i2]], W=[by2[i2]])
                    k.op("dve", lambda e, Y=Y, Y2=Y2, f=f, tau=tau: e.tensor_tensor(
                        out=zT[:, f, :].rearrange("p (c l) -> p l c", l=L)[:, tau, :], in0=Y[:], in1=Y2[:], op=ALU.mult),
                        R=[by2[i2], byv[i2]], W=[bz[f]])
            wgl = P.sb(es3, "s5_wgl", [128, 4, 512], BF16)
            bwgl = k.buf("s5_wgl")
            k.dma("sp", [(wgl[:], W["s5_w_glu"][0].rearrange("(c p) n -> p c n", p=128))], R=[W["s5_w_glu"][1]], W=[bwgl])
            sg = [P.sb(es3, f"s5_sg{i}", [128, TB], F32) for i in range(2)]
            bsg = k.bufs(2, "s5_sg")
            ybT = [P.sb(es3, f"s5_ybT{i}", [128, 4, TB], BF16) for i in range(2)]
            bybT = k.bufs(2, "s5_ybT")
            cnt = 0
            for blk in range(NB):
                c0 = blk * TB
                for fo in range(4):
                    pb = 4 + cnt % 2
                    i2 = cnt % 2
                    cnt += 1
                    for fi in range(4):
                        k.op("pe", lambda e, fo=fo, fi=fi, pb=pb, c0=c0: e.matmul(
                            P.ps[pb][:], lhsT=wgl[:, fi, fo * 128:(fo + 1) * 128], rhs=zT[:, fi, c0:c0 + TB],
                            start=(fi == 0), stop=(fi == 3)), R=[bwgl] + bz, W=[P.psb[pb]])
                    k.op("act", lambda e, fo=fo, pb=pb, i2=i2: e.activation(out=sg[i2][:], in_=P.ps[pb][:], func=AF.Sigmoid,
                                                                           bias=bgl[:, fo:fo + 1], scale=1.0), R=[P.psb[pb], bs], W=[bsg[i2]])
                    k.op("dve", lambda e, fo=fo, i2=i2, blk=blk, c0=c0: e.tensor_tensor(
                        out=ybT[blk % 2][:, fo, :], in0=sg[i2][:], in1=zT[:, fo, c0:c0 + TB], op=ALU.mult),
                        R=[bsg[i2]] + bz, W=[bybT[blk % 2]])
                k.dma("sp", [(P.ycatT[512:1024, c0:c0 + TB].rearrange("(c p) t -> p c t", p=128), ybT[blk % 2][:])],
                      R=[bybT[blk % 2]], W=[P.b_yb[blk]], key=f"d_ybT{blk % 2}")
            k.barrier()


def build(stages=("all",), debug=()):
    P = Prog(debug)
    k = P.k
    I = {}
    P.inp = I
    I["x"] = P.din("x", [S, D])
    I["positions"] = P.din("positions", [S], I32)
    for nm, shp in [("norm_mix_g", [2, D]), ("norm_ffn_g", [2, D]), ("final_norm_g", [D]),
                    ("even_w_in", [D, 2560]), ("hgrn_lb_logits", [2, 512]), ("hgrn_norm_g", [512]),
                    ("s5_a_re", [32, 64]), ("s5_a_im", [32, 64]), ("s5_log_dt", [32]),
                    ("s5_b_re", [32, 64, 16]), ("s5_b_im", [32, 64, 16]),
                    ("s5_c_re", [32, 16, 64]), ("s5_c_im", [32, 16, 64]),
                    ("s5_d", [512]), ("s5_w_glu", [512, 512]), ("s5_b_glu", [512]),
                    ("even_w_out", [D, D]), ("odd_w_in", [D, 704]), ("mla_q_norm_g", [384]),
                    ("mla_w_uq", [384, 1536]), ("mla_kv_norm_g", [256]), ("mla_w_ukv", [256, 2048]),
                    ("odd_w_out", [D, D]), ("ffn_w_in", [2, D, 2 * DFF]), ("ffn_conv_w", [2, 3, DFF]),
                    ("ffn_conv_b", [2, DFF]), ("ffn_w_out", [2, DFF, D])]:
        I[nm] = P.din(nm, shp)
    I["rope_freq"] = P.din("rope_freq", [32])
    out = P.dout("out", [S, D])
    P.b_out = k.bufs(32, "out")
    P.b_x = k.bufs(32, "x")
    setup_common(P)
    if "ffn1" in stages:
        w_in_bf, b1 = cast_weight(P, "w_ffn_in1", I["ffn_w_in"][1], D, 2 * DFF)
        w_out_bf, b2 = cast_weight(P, "w_ffn_out1", I["ffn_w_out"][1], DFF, D)
        ffn_phase(P, 1, I["x"], P.b_x, out, P.b_out, w_in_bf, b1, w_out_bf, b2, final_norm=I["final_norm_g"])
    W = {}
    if "hgrn" in stages:
        W["even_w_in"] = cast_weight(P, "bf_even_w_in", I["even_w_in"], D, 2560)
        hgrn_phase(P, I["x"], P.b_x, W)
    if "s5" in stages:
        W["s5_w_glu"] = cast_weight(P, "bf_s5_w_glu", I["s5_w_glu"], 512, 512)
        s5_phase(P, W)
    if "mla" in stages:
        for nm, r, c in [("odd_w_in", D, 704), ("mla_w_uq", 384, 1536), ("mla_w_ukv", 256, 2048), ("odd_w_out", D, D)]:
            W[nm] = cast_weight(P, "bf_" + nm, I[nm], r, c)
        mla_proj_phase(P, I["x"], P.b_x, W)
        if "mla_pd_only" not in stages:
            mla_attn_phase(P, I["x"], P.b_x, out, P.b_out, W)
    if "all" in stages:
        W["even_w_in"] = cast_weight(P, "bf_even_w_in", I["even_w_in"], D, 2560)
        W["s5_w_glu"] = cast_weight(P, "bf_s5_w_glu", I["s5_w_glu"], 512, 512)
        W["even_w_out"] = cast_weight(P, "bf_even_w_out", I["even_w_out"], D, D)
        W["ffn_in0"] = cast_weight(P, "bf_ffn_in0", I["ffn_w_in"][0], D, 2 * DFF)
        W["ffn_out0"] = cast_weight(P, "bf_ffn_out0", I["ffn_w_out"][0], DFF, D)
        for nm, r, c in [("odd_w_in", D, 704), ("mla_w_uq", 384, 1536), ("mla_w_ukv", 256, 2048), ("odd_w_out", D, D)]:
            W[nm] = cast_weight(P, "bf_" + nm, I[nm], r, c)
        W["ffn_in1"] = cast_weight(P, "bf_ffn_in1", I["ffn_w_in"][1], D, 2 * DFF)
        W["ffn_out1"] = cast_weight(P, "bf_ffn_out1", I["ffn_w_out"][1], DFF, D)
        h2 = P.dint("h2", [S, D], F32)
        h3 = P.dint("h3", [S, D], F32)
        b_h2 = k.bufs(32, "h2")
        b_h3 = k.bufs(32, "h3")
        hgrn_phase(P, I["x"], P.b_x, W)
        s5_phase(P, W)
        ffn_phase(P, 0, I["x"], P.b_x, h2, b_h2, W["ffn_in0"][0], W["ffn_in0"][1], W["ffn_out0"][0], W["ffn_out0"][1],
                  premix=(P.ycatT, P.b_ya, P.b_yb, W["even_w_out"][0], W["even_w_out"][1]))
        mla_proj_phase(P, h2, b_h2, W)
        mla_attn_phase(P, h2, b_h2, h3, b_h3, W)
        ffn_phase(P, 1, h3, b_h3, out, P.b_out, W["ffn_in1"][0], W["ffn_in1"][1], W["ffn_out1"][0], W["ffn_out1"][1],
                  final_norm=I["final_norm_g"])
    k.barrier()
    k.finalize(P.es)
    P.es.close()
    return P


INPUT_ORDER = ["x", "positions", "norm_mix_g", "norm_ffn_g", "final_norm_g", "even_w_in", "hgrn_lb_logits",
               "hgrn_norm_g", "s5_a_re", "s5_a_im", "s5_log_dt", "s5_b_re", "s5_b_im", "s5_c_re", "s5_c_im",
               "s5_d", "s5_w_glu", "s5_b_glu", "even_w_out", "odd_w_in", "mla_q_norm_g", "mla_w_uq",
               "mla_kv_norm_g", "mla_w_ukv", "odd_w_out", "ffn_w_in", "ffn_conv_w", "ffn_conv_b", "ffn_w_out"]


def make_in_maps(inputs):
    shared = {}
    for nm in INPUT_ORDER:
        if nm in ("x", "positions"):
            continue
        a = np.ascontiguousarray(np.asarray(inputs[nm]))
        if nm in ("norm_mix_g", "norm_ffn_g", "hgrn_lb_logits", "ffn_w_in", "ffn_conv_w", "ffn_conv_b",
                  "ffn_w_out", "final_norm_g"):
            shared[nm] = a
        else:
            shared[nm] = np.ascontiguousarray(a[0])
    shared["rope_freq"] = (10000.0 ** (-np.arange(0, 64, 2, dtype=np.float32) / np.float32(64))).astype(np.float32)
    x = np.asarray(inputs["x"])
    pos = np.asarray(inputs["positions"])
    maps = []
    for c in range(8):
        m = dict(shared)
        m["x"] = np.ascontiguousarray(x[c])
        m["positions"] = np.ascontiguousarray(pos[c]).astype(np.int32)
        maps.append(m)
    return maps


def run(inputs, stages=("all",), debug=()):
    P = build(stages, debug)
    maps = make_in_maps(inputs)
    res = run_bass_kernel_spmd(P.nc, maps, core_ids=list(range(8)))
    return res


def kernel(**inputs):
    res = run(inputs)
    out = np.stack([np.asarray(r["out"]) for r in res.results], axis=0)
    return out.astype(np.float32)
```

```python
import math
from contextlib import ExitStack
import numpy as np
import concourse.bass as bass
import concourse.mybir as mybir
from concourse.bass_utils import run_bass_kernel_spmd

F32 = mybir.dt.float32
BF16 = mybir.dt.bfloat16
I32 = mybir.dt.int32
AF = mybir.ActivationFunctionType
ALU = mybir.AluOpType
AX = mybir.AxisListType

S = 4096
D = 1024
NB = 8
TB = 512
DFF = 2816
NFC = DFF // 128
EPS = 1e-6

ENGS = ("pe", "act", "dve", "pool", "sp")


class Buf:
    __slots__ = ("name", "w", "r")

    def __init__(self, name):
        self.name = name
        self.w = None
        self.r = []


class Op:
    __slots__ = ("eng", "fn", "deps", "key", "pos", "ndma", "waits", "flag")

    def __init__(self, eng, fn, deps, key, ndma):
        self.eng = eng
        self.fn = fn
        self.deps = deps
        self.key = key
        self.pos = -1
        self.ndma = ndma
        self.waits = []
        self.flag = False


class KB:
    def __init__(self, nc):
        self.nc = nc
        self.ops = []
        self.eng_obj = {"pe": nc.tensor, "act": nc.scalar, "dve": nc.vector,
                        "pool": nc.gpsimd, "sp": nc.sync}
        self.last = {}
        self.dma_out = []
        self.nbuf = 0

    def buf(self, name=None):
        self.nbuf += 1
        return Buf(name or f"b{self.nbuf}")

    def bufs(self, n, name="b"):
        return [self.buf(f"{name}{i}") for i in range(n)]

    def _deps(self, idx, R, W):
        deps = set()
        for b in R:
            if b.w is not None:
                deps.add(b.w)
        for b in W:
            if b.w is not None:
                deps.add(b.w)
            deps.update(b.r)
        for b in W:
            b.w = idx
            b.r = []
        for b in R:
            b.r.append(idx)
        deps.discard(idx)
        return deps

    def op(self, eng, fn, R=(), W=()):
        idx = len(self.ops)
        deps = self._deps(idx, R, W)
        self.ops.append(Op(eng, fn, deps, eng, 0))
        self.last[eng] = idx
        return idx

    def dma(self, eng, pairs, R=(), W=(), key=None, **kw):
        idx = len(self.ops)
        deps = self._deps(idx, R, W)
        if key is None:
            key = "d_" + (W[0].name if W else R[0].name)

        def fn(e, pairs=pairs, kw=kw):
            return [e.dma_start(out=o, in_=i, **kw) for (o, i) in pairs]
        self.ops.append(Op(eng, fn, deps, key, len(pairs)))
        self.dma_out.append(idx)
        return idx

    def barrier(self):
        targets = set(self.last.values()) | set(self.dma_out)
        for e in ENGS:
            idx = len(self.ops)
            self.ops.append(Op(e, None, set(targets), e, 0))
        self.dma_out = []

    def finalize(self, es):
        nc = self.nc
        ops = self.ops
        cnt = {}
        for o in ops:
            if o.ndma:
                cnt[o.key] = cnt.get(o.key, 0) + o.ndma
                o.pos = cnt[o.key]
            elif o.fn is not None:
                cnt[o.key] = cnt.get(o.key, 0) + 1
                o.pos = cnt[o.key]
            else:
                o.pos = cnt.get(o.key, 0)
        seen = {e: {} for e in ENGS}
        flagged = {}
        for o in ops:
            need = {}
            for d in o.deps:
                dop = ops[d]
                if dop.fn is None:
                    continue
                k = dop.key
                if (not dop.ndma) and k == o.eng and k in ("pe", "sp"):
                    continue
                if dop.pos > seen[o.eng].get(k, 0):
                    if dop.pos > need.get(k, (0, None))[0]:
                        need[k] = (dop.pos, dop)
            for k, (p, dop) in need.items():
                seen[o.eng][k] = p
                o.waits.append((k, dop))
                dop.flag = True
        val = {}
        rank = {}
        for i, o in enumerate(ops):
            if o.ndma:
                val[i] = 16 * o.pos
            elif o.fn is not None and o.flag:
                rank[o.key] = rank.get(o.key, 0) + 1
                val[i] = rank[o.key]
        opidx = {id(o): i for i, o in enumerate(ops)}
        keys = set()
        for o in ops:
            if o.ndma or o.flag:
                keys.add(o.key)
        sems = {}
        for kname in sorted(keys):
            sems[kname] = es.enter_context(nc.semaphore("s_" + kname))
        self.nsem = len(sems)
        for o in ops:
            e = self.eng_obj[o.eng]
            for (k, dop) in o.waits:
                e.wait_ge(sems[k], val[opidx[id(dop)]])
            if o.fn is None:
                continue
            ins = o.fn(e)
            if o.ndma:
                for i_ in ins:
                    i_.then_inc(sems[o.key], 16)
            elif o.flag:
                ins.then_inc(sems[o.key], 1)


class Prog:
    def __init__(self, debug=()):
        self.debug = set(debug)
        self.nc = bass.Bass("TRN2", target_bir_lowering=False)
        self.k = KB(self.nc)
        self.es = ExitStack()
        self.dram = {}

    def din(self, name, shape, dtype=F32):
        t = self.nc.dram_tensor(name, list(shape), dtype, kind="ExternalInput").ap()
        self.dram[name] = t
        return t

    def dout(self, name, shape, dtype=F32):
        t = self.nc.dram_tensor(name, list(shape), dtype, kind="ExternalOutput").ap()
        self.dram[name] = t
        return t

    def dint(self, name, shape, dtype=F32):
        kind = "ExternalOutput" if name in self.debug else "Internal"
        t = self.nc.dram_tensor(name, list(shape), dtype, kind=kind).ap()
        self.dram[name] = t
        return t

    def sb(self, es, name, shape, dtype):
        self.nsb = getattr(self, "nsb", 0) + 1
        return es.enter_context(self.nc.sbuf_tensor(f"{name}_{self.nsb}", list(shape), dtype))


def setup_common(P):
    nc, k = P.nc, P.k
    P.ps = [P.es.enter_context(nc.psum_tensor(f"ps{i}", [128, 512], F32)) for i in range(8)]
    P.psb = k.bufs(8, "psb")
    P.ident_f = P.sb(P.es, "ident_f", [128, 128], F32)
    P.ident = P.sb(P.es, "ident", [128, 128], BF16)
    P.epsb = P.sb(P.es, "epsb", [128, 1], F32)
    P.b_const = k.buf("const")
    ones = P.sb(P.es, "ones_f", [128, 128], F32)
    P.ones_f = ones
    k.op("pool", lambda e: e.memset(ones[:], 1.0), W=[P.b_const])
    k.op("pool", lambda e: e.affine_select(out=P.ident_f[:], in_=ones[:], pattern=[[-1, 128]],
                                           compare_op=ALU.is_equal, fill=0.0, base=0,
                                           channel_multiplier=1), R=[P.b_const], W=[P.b_const])
    k.op("pool", lambda e: e.tensor_copy(out=P.ident[:], in_=P.ident_f[:]), R=[P.b_const], W=[P.b_const])
    k.op("pool", lambda e: e.memset(P.epsb[:], EPS), W=[P.b_const])


def cast_weight(P, name, src2d, rows, cols):
    k = P.k
    dst = P.dint(name, [rows, cols], BF16)
    b = k.buf(name)
    bb = cols if cols <= 2048 else 512
    per_row = cols // bb
    rstep = max(1, 4096 // per_row)
    pairs = []
    for r0 in range(0, rows, rstep):
        r1 = min(rows, r0 + rstep)
        pairs.append((dst[r0:r1, :].rearrange("r (a b) -> r a b", b=bb),
                      src2d[r0:r1, :].rearrange("r (a b) -> r a b", b=bb)))
    k.dma("pool", pairs, W=[b], key="wc_" + name)
    return dst, b


def load_vec_fm(P, es, name, src1d, nchunk, b, eng="sp"):
    t = P.sb(es, name, [128, nchunk], F32)
    P.k.dma(eng, [(t[:], src1d.rearrange("(c p) -> p c", p=128))], W=[b],
            allow_slow_non_contiguous=True)
    return t


def norm_block(P, T, blk, src, b_src, gT):
    k, nc = P.k, P.nc
    xres, bx = T["xres"], T["b_xres"]
    for j in range(4):
        r0 = blk * TB + j * 128
        k.dma("sp", [(xres[:, j, :], src[r0:r0 + 128, :])], R=[b_src[blk * 4 + j]], W=[bx[j]], key=f"d_xres{j}")
    rms_transpose(P, T, gT)


def rms_transpose(P, T, gT):
    k = P.k
    xres, bx = T["xres"], T["b_xres"]
    hnT, bh = T["hnT"], T["b_hnT"]
    junk, bj = T["junk"], T["b_junk"]
    ss, bss = T["ss"], T["b_ss"]
    xn, bxn = T["xn"], T["b_xn"]
    for j in range(4):
        pb = T["ps_tr"][j % len(T["ps_tr"])]
        k.op("act", lambda e, j=j: e.activation(out=junk[:], in_=xres[:, j, :], func=AF.Square,
                                                accum_out=ss[:, j:j + 1]),
             R=[bx[j]], W=[bj, bss[j]])
        k.op("act", lambda e, j=j: e.activation(out=ss[:, 4 + j:5 + j], in_=ss[:, j:j + 1], func=AF.Sqrt,
                                                bias=P.epsb[:], scale=1.0 / D),
             R=[bss[j], P.b_const], W=[bss[j]])
        k.op("dve", lambda e, j=j: e.reciprocal(out=ss[:, 8 + j:9 + j], in_=ss[:, 4 + j:5 + j]),
             R=[bss[j]], W=[bss[j]])
        k.op("dve", lambda e, j=j: e.tensor_scalar(out=xn[:, j, :], in0=xres[:, j, :], scalar1=ss[:, 8 + j:9 + j],
                                                   scalar2=None, op0=ALU.mult),
             R=[bx[j], bss[j]], W=[bxn[j]])
        psT = P.ps[pb][:].bitcast(BF16)
        for c in range(8):
            k.op("pe", lambda e, j=j, c=c, psT=psT: e.transpose(out=psT[:, c * 128:(c + 1) * 128],
                                                               in_=xn[:, j, c * 128:(c + 1) * 128],
                                                               identity=P.ident[:]),
                 R=[bxn[j], P.b_const], W=[P.psb[pb]])
        k.op("dve", lambda e, j=j, psT=psT: e.tensor_tensor(
            out=hnT[:, :, j * 128:(j + 1) * 128],
            in0=psT.rearrange("p (c t) -> p c t", c=8),
            in1=gT[:, :].unsqueeze(2).broadcast_to([128, 8, 128]), op=ALU.mult),
            R=[P.psb[pb], T["b_g"]], W=[bh])


def alloc_blockbufs(P, es, pfx, ps_tr):
    k = P.k
    T = {}
    T["xres"] = P.sb(es, pfx + "xres", [128, 4, 1024], F32)
    T["b_xres"] = k.bufs(4, pfx + "xres")
    T["hnT"] = P.sb(es, pfx + "hnT", [128, 8, 512], BF16)
    T["b_hnT"] = k.buf(pfx + "hnT")
    T["junk"] = P.sb(es, pfx + "junk", [128, 1024], BF16)
    T["b_junk"] = k.buf(pfx + "junk")
    T["ss"] = P.sb(es, pfx + "ss", [128, 12], F32)
    T["b_ss"] = k.bufs(4, pfx + "ss")
    T["xn"] = P.sb(es, pfx + "xn", [128, 4, 1024], BF16)
    T["b_xn"] = k.bufs(4, pfx + "xn")
    T["ps_tr"] = ps_tr
    return T


def ffn_phase(P, layer, src, b_src, dst, b_dst, w_in_bf, b_win, w_out_bf, b_wout, final_norm=None, premix=None):
    k, nc = P.k, P.nc
    I = P.inp
    with ExitStack() as es:
        bc = k.buf("ffn_consts")
        gT = load_vec_fm(P, es, "ffn_g", I["norm_ffn_g"][layer], 8, bc)
        cb = load_vec_fm(P, es, "ffn_cb", I["ffn_conv_b"][layer], NFC, bc)
        cw = P.sb(es, "ffn_cw", [128, 3, NFC], F32)
        for t in range(3):
            k.dma("sp", [(cw[:, t, :], I["ffn_conv_w"][layer, t].rearrange("(c p) -> p c", p=128))], W=[bc],
                  allow_slow_non_contiguous=True)
        T = alloc_blockbufs(P, es, "f_", ps_tr=[6, 7])
        T["b_g"] = bc
        wout = P.sb(es, "ffn_wout", [128, NFC, 1024], BF16)
        bwo = k.buf("ffn_wout")
        k.dma("sp", [(wout[:, c0:c0 + 11, :],
                      w_out_bf[c0 * 128:(c0 + 11) * 128, :].rearrange("(c p) n -> p c n", p=128))
                     for c0 in (0, 11)], R=[b_wout], W=[bwo])
        NSL = 2
        wsl = [P.sb(es, f"ffn_wsl{i}", [128, 8, 2, 256], BF16) for i in range(NSL)]
        bws = k.bufs(NSL, "ffn_wsl")
        a_sb = [P.sb(es, f"ffn_a{i}", [128, 516], F32) for i in range(2)]
        ba = k.bufs(2, "ffn_a")
        c_sb = [P.sb(es, f"ffn_c{i}", [128, 512], F32) for i in range(2)]
        bcs = k.bufs(2, "ffn_c")
        s_sb = [P.sb(es, f"ffn_s{i}", [128, 512], F32) for i in range(2)]
        bss = k.bufs(2, "ffn_s")
        halo = P.sb(es, "ffn_halo", [128, NFC, 2], F32)
        bhalo = k.bufs(NFC, "ffn_halo")
        gTt = P.sb(es, "ffn_gT", [128, NFC, 512], BF16)
        bg = k.bufs(NFC, "ffn_gT")
        k.op("pool", lambda e: e.memset(halo[:], 0.0), W=bhalo)
        if final_norm is not None:
            fg = P.sb(es, "fin_g", [128, 1024], F32)
            bfg = k.buf("fin_g")
            k.dma("sp", [(fg[:], final_norm.partition_broadcast(128))], W=[bfg])
            ostage = [P.sb(es, f"fin_o{i}", [128, 1024], F32) for i in range(2)]
            bos = k.bufs(2, "fin_o")
        if premix is not None:
            ycatT, b_ya, b_yb, wo_bf, b_wo = premix
            wo = P.sb(es, "pm_wo", [128, 8, 1024], BF16)
            bwo2 = k.buf("pm_wo")
            k.dma("sp", [(wo[:], wo_bf.rearrange("(c p) n -> p c n", p=128))], R=[b_wo], W=[bwo2])
            yT = P.sb(es, "pm_yT", [128, 8, 512], BF16)
            byT = k.buf("pm_yT")
        nsl = 0
        cnt = 0
        for blk in range(NB):
            xres, bx = T["xres"], T["b_xres"]
            hnT, bh = T["hnT"], T["b_hnT"]
            if premix is None:
                norm_block(P, T, blk, src, b_src, gT)
            else:
                for j in range(4):
                    r0 = blk * TB + j * 128
                    k.dma("sp", [(xres[:, j, :], src[r0:r0 + 128, :])], R=[b_src[blk * 4 + j]], W=[bx[j]], key=f"d_xres{j}")
                c0 = blk * TB
                k.dma("sp", [(yT[:], ycatT[:, c0:c0 + TB].rearrange("(c p) t -> p c t", p=128))], R=[b_ya[blk], b_yb[blk]], W=[byT])
                for j in range(4):
                    for n in range(2):
                        pb = 4 + (j * 2 + n) % 2
                        for kc in range(8):
                            k.op("pe", lambda e, kc=kc, j=j, n=n, pb=pb: e.matmul(
                                P.ps[pb][:], lhsT=yT[:, kc, j * 128:(j + 1) * 128], rhs=wo[:, kc, n * 512:(n + 1) * 512],
                                start=(kc == 0), stop=(kc == 7)), R=[byT, bwo2], W=[P.psb[pb]])
                        k.op("dve", lambda e, j=j, n=n, pb=pb: e.tensor_tensor(
                            out=xres[:, j, n * 512:(n + 1) * 512], in0=xres[:, j, n * 512:(n + 1) * 512],
                            in1=P.ps[pb][:], op=ALU.add), R=[bx[j], P.psb[pb]], W=[bx[j]])
                rms_transpose(P, T, gT)
            for pr in range(NFC // 2):
                sl = nsl % NSL
                nsl += 1
                pairs = []
                for h_ in range(2):
                    c0 = h_ * DFF + pr * 256
                    pairs.append((wsl[sl][:, :, h_, :],
                                  w_in_bf[:, c0:c0 + 256].rearrange("(kc p) n -> p kc n", p=128)))
                k.dma("sp", pairs, R=[b_win], W=[bws[sl]])
                for q in range(2):
                    oc = pr * 2 + q
                    pa, pu = (0, 1) if (cnt % 2 == 0) else (2, 3)
                    i2 = cnt % 2
                    cnt += 1
                    for kc in range(8):
                        k.op("pe", lambda e, kc=kc, sl=sl, q=q, pa=pa: e.matmul(
                            P.ps[pa][:], lhsT=wsl[sl][:, kc, 0, q * 128:(q + 1) * 128], rhs=hnT[:, kc, :],
                            start=(kc == 0), stop=(kc == 7)), R=[bws[sl], bh], W=[P.psb[pa]])
                    for kc in range(8):
                        k.op("pe", lambda e, kc=kc, sl=sl, q=q, pu=pu: e.matmul(
                            P.ps[pu][:], lhsT=wsl[sl][:, kc, 1, q * 128:(q + 1) * 128], rhs=hnT[:, kc, :],
                            start=(kc == 0), stop=(kc == 7)), R=[bws[sl], bh], W=[P.psb[pu]])
                    A = a_sb[i2]
                    k.op("pool", lambda e, A=A, oc=oc: e.tensor_copy(out=A[:, 2:4], in_=halo[:, oc, :]),
                         R=[bhalo[oc]], W=[ba[i2]])
                    k.op("act", lambda e, A=A, pa=pa: e.copy(out=A[:, 4:516], in_=P.ps[pa][:]),
                         R=[P.psb[pa]], W=[ba[i2]])
                    k.op("pool", lambda e, A=A, oc=oc: e.tensor_copy(out=halo[:, oc, :], in_=A[:, 514:516]),
                         R=[ba[i2]], W=[bhalo[oc]])
                    C = c_sb[i2]
                    k.op("act", lambda e, C=C, pa=pa, oc=oc: e.activation(
                        out=C[:], in_=P.ps[pa][:], func=AF.Identity, bias=cb[:, oc:oc + 1], scale=cw[:, 2, oc:oc + 1]),
                        R=[P.psb[pa], bc], W=[bcs[i2]])
                    k.op("dve", lambda e, C=C, A=A, oc=oc: e.scalar_tensor_tensor(
                        out=C[:], in0=A[:, 3:515], scalar=cw[:, 1, oc:oc + 1], in1=C[:], op0=ALU.mult, op1=ALU.add),
                        R=[ba[i2], bc, bcs[i2]], W=[bcs[i2]])
                    k.op("dve", lambda e, C=C, A=A, oc=oc: e.scalar_tensor_tensor(
                        out=C[:], in0=A[:, 2:514], scalar=cw[:, 0, oc:oc + 1], in1=C[:], op0=ALU.mult, op1=ALU.add),
                        R=[ba[i2], bc, bcs[i2]], W=[bcs[i2]])
                    Ssb = s_sb[i2]
                    k.op("act", lambda e, C=C, Ssb=Ssb: e.activation(out=Ssb[:], in_=C[:], func=AF.Silu),
                         R=[bcs[i2]], W=[bss[i2]])
                    k.op("dve", lambda e, Ssb=Ssb, pu=pu, oc=oc: e.tensor_tensor(
                        out=gTt[:, oc, :], in0=Ssb[:], in1=P.ps[pu][:], op=ALU.mult),
                        R=[bss[i2], P.psb[pu]], W=[bg[oc]])
            for j in range(4):
                for n in range(2):
                    pb = 4 + (j * 2 + n) % 2
                    for kc in range(NFC):
                        k.op("pe", lambda e, kc=kc, j=j, n=n, pb=pb: e.matmul(
                            P.ps[pb][:], lhsT=gTt[:, kc, j * 128:(j + 1) * 128],
                            rhs=wout[:, kc, n * 512:(n + 1) * 512], start=(kc == 0), stop=(kc == NFC - 1)),
                            R=[bg[kc], bwo], W=[P.psb[pb]])
                    k.op("dve", lambda e, j=j, n=n, pb=pb: e.tensor_tensor(
                        out=xres[:, j, n * 512:(n + 1) * 512], in0=xres[:, j, n * 512:(n + 1) * 512],
                        in1=P.ps[pb][:], op=ALU.add), R=[bx[j], P.psb[pb]], W=[bx[j]])
                r0 = blk * TB + j * 128
                if final_norm is None:
                    k.dma("sp", [(dst[r0:r0 + 128, :], xres[:, j, :])], R=[bx[j]], W=[b_dst[blk * 4 + j]],
                          key=f"d_xst{j}")
                else:
                    ss, bs4 = T["ss"], T["b_ss"]
                    junk, bj = T["junk"], T["b_junk"]
                    o = ostage[j % 2]
                    k.op("act", lambda e, j=j: e.activation(out=junk[:], in_=xres[:, j, :], func=AF.Square,
                                                            accum_out=ss[:, j:j + 1]),
                         R=[bx[j]], W=[bj, bs4[j]])
                    k.op("act", lambda e, j=j: e.activation(out=ss[:, 4 + j:5 + j], in_=ss[:, j:j + 1], func=AF.Sqrt,
                                                            bias=P.epsb[:], scale=1.0 / D),
                         R=[bs4[j], P.b_const], W=[bs4[j]])
                    k.op("dve", lambda e, j=j: e.reciprocal(out=ss[:, 8 + j:9 + j], in_=ss[:, 4 + j:5 + j]),
                         R=[bs4[j]], W=[bs4[j]])
                    k.op("dve", lambda e, j=j, o=o: e.scalar_tensor_tensor(
                        out=o[:], in0=xres[:, j, :], scalar=ss[:, 8 + j:9 + j], in1=fg[:], op0=ALU.mult, op1=ALU.mult),
                        R=[bx[j], bs4[j], bfg], W=[bos[j % 2]])
                    k.dma("sp", [(dst[r0:r0 + 128, :], o[:])], R=[bos[j % 2]], W=[b_dst[blk * 4 + j]], key=f"d_ost{j % 2}")
        k.barrier()


HQ = 8
VP = 130
SCALE = 192 ** -0.5
TWO_PI = 2.0 * math.pi
CW1 = 6.28125
CW2 = TWO_PI - 6.28125


def mla_proj_phase(P, src, b_src, W):
    k, nc = P.k, P.nc
    I = P.inp
    QTn = P.dint("QTn", [HQ, 128, S], BF16)
    QTr = P.dint("QTr", [HQ, 64, S], BF16)
    KTn = P.dint("KTn", [HQ, 128, S], BF16)
    KTr = P.dint("KTr", [64, S], BF16)
    Va = P.dint("Va", [S, HQ * VP], BF16)
    P.mla_dram = dict(QTn=QTn, QTr=QTr, KTn=KTn, KTr=KTr, Va=Va)
    P.b_qkv = k.bufs(NB, "qkv")
    stats = P.sb(P.es, "mla_stats", [128, 16], F32)
    P.mla_stats = stats
    P.b_stats = k.buf("mla_stats")
    with ExitStack() as es:
        bc = k.buf("pd_consts")
        gT = load_vec_fm(P, es, "pd_g", I["norm_mix_g"][1], 8, bc)
        qg = load_vec_fm(P, es, "pd_qg", I["mla_q_norm_g"], 3, bc)
        kvg = load_vec_fm(P, es, "pd_kvg", I["mla_kv_norm_g"], 2, bc)
        g5 = P.sb(es, "pd_g5", [128, 5], F32)
        k.op("pool", lambda e: e.tensor_copy(out=g5[:, 0:3], in_=qg[:]), R=[bc], W=[bc])
        k.op("pool", lambda e: e.tensor_copy(out=g5[:, 3:5], in_=kvg[:]), R=[bc], W=[bc])
        win = P.sb(es, "pd_win", [128, 8, 704], BF16)
        k.dma("sp", [(win[:], W["odd_w_in"][0].rearrange("(c p) n -> p c n", p=128))], R=[W["odd_w_in"][1]], W=[bc])
        wuq = P.sb(es, "pd_wuq", [128, 3, 1536], BF16)
        k.dma("sp", [(wuq[:], W["mla_w_uq"][0].rearrange("(c p) n -> p c n", p=128))], R=[W["mla_w_uq"][1]], W=[bc])
        wukv = P.sb(es, "pd_wukv", [128, 2, 2048], BF16)
        k.dma("sp", [(wukv[:], W["mla_w_ukv"][0].rearrange("(c p) n -> p c n", p=128))], R=[W["mla_w_ukv"][1]], W=[bc])
        freq = P.sb(es, "pd_freq", [128, 32], F32)
        k.dma("sp", [(freq[:], I["rope_freq"].partition_broadcast(128))], W=[bc])
        posi = P.sb(es, "pd_posi", [128, 32], I32)
        k.dma("sp", [(posi[:, 8 * i:8 * i + 8], I["positions"][1024 * i:1024 * (i + 1)].rearrange("(j p) -> p j", p=128))
                     for i in range(4)], W=[bc], allow_slow_non_contiguous=True)
        posf = P.sb(es, "pd_posf", [128, 32], F32)
        ang = P.sb(es, "pd_ang", [128, 32, 32], F32)
        tmpf = P.sb(es, "pd_tmpf", [128, 32, 32], F32)
        tmpi = P.sb(es, "pd_tmpi", [128, 32, 32], I32)
        sinT = P.sb(es, "pd_sin", [128, 32, 32], F32)
        cosT = P.sb(es, "pd_cos", [128, 32, 32], F32)
        br = k.buf("pd_rope")
        k.op("dve", lambda e: e.tensor_copy(out=posf[:], in_=posi[:]), R=[bc], W=[br])
        k.op("dve", lambda e: e.tensor_tensor(out=ang[:], in0=posf[:, :].unsqueeze(2).broadcast_to([128, 32, 32]),
                                              in1=freq[:, :].unsqueeze(1).broadcast_to([128, 32, 32]), op=ALU.mult),
             R=[br, bc], W=[br])
        k.op("dve", lambda e: e.tensor_scalar(out=tmpf[:], in0=ang[:], scalar1=1.0 / TWO_PI, scalar2=None, op0=ALU.mult),
             R=[br], W=[br])
        k.op("dve", lambda e: e.tensor_copy(out=tmpi[:], in_=tmpf[:]), R=[br], W=[br])
        k.op("dve", lambda e: e.tensor_copy(out=tmpf[:], in_=tmpi[:]), R=[br], W=[br])
        k.op("dve", lambda e: e.scalar_tensor_tensor(out=ang[:], in0=tmpf[:], scalar=-CW1, in1=ang[:],
                                                     op0=ALU.mult, op1=ALU.add), R=[br], W=[br])
        k.op("dve", lambda e: e.scalar_tensor_tensor(out=ang[:], in0=tmpf[:], scalar=-CW2, in1=ang[:],
                                                     op0=ALU.mult, op1=ALU.add), R=[br], W=[br])
        k.op("dve", lambda e: e.tensor_scalar(out=ang[:], in0=ang[:], scalar1=-math.pi, scalar2=math.pi,
                                              op0=ALU.max, op1=ALU.min), R=[br], W=[br])
        k.op("act", lambda e: e.activation(out=sinT[:], in_=ang[:], func=AF.Sin), R=[br], W=[br])
        k.op("dve", lambda e: e.tensor_scalar(out=tmpf[:], in0=ang[:], scalar1=math.pi / 2, scalar2=math.pi,
                                              op0=ALU.add, op1=ALU.is_gt), R=[br], W=[br])
        k.op("dve", lambda e: e.scalar_tensor_tensor(out=tmpf[:], in0=tmpf[:], scalar=-TWO_PI, in1=ang[:],
                                                     op0=ALU.mult, op1=ALU.add), R=[br], W=[br])
        k.op("dve", lambda e: e.tensor_scalar(out=tmpf[:], in0=tmpf[:], scalar1=math.pi / 2, scalar2=math.pi,
                                              op0=ALU.add, op1=ALU.min), R=[br], W=[br])
        k.op("act", lambda e: e.activation(out=cosT[:], in_=tmpf[:], func=AF.Sin), R=[br], W=[br])
        k.op("pool", lambda e: e.memset(stats[:], 0.0), W=[P.b_stats])

        T = alloc_blockbufs(P, es, "d_", ps_tr=[7])
        T["b_g"] = bc
        junk2 = P.sb(es, "pd_junk2", [128, 1536], F32)
        bj2 = k.buf("pd_junk2")
        st = P.sb(es, "pd_st", [128, 16], F32)
        bst = k.buf("pd_st")
        cn = P.sb(es, "pd_cn", [128, 640], BF16)
        bcn = k.buf("pd_cn")
        cT = P.sb(es, "pd_cT", [128, 5, 128], BF16)
        bcT = k.buf("pd_cT")
        kr = P.sb(es, "pd_kr", [128, 64], F32)
        bkr = k.buf("pd_kr")
        krb = P.sb(es, "pd_krb", [128, 64], BF16)
        bkrb = k.buf("pd_krb")
        qsb = P.sb(es, "pd_qsb", [128, 1536], F32)
        bqsb = k.buf("pd_qsb")
        qbf = P.sb(es, "pd_qbf", [128, 8, 192], BF16)
        bqbf = k.buf("pd_qbf")
        rt = [P.sb(es, f"pd_rt{i}", [128, 8, 32], F32) for i in range(4)]
        brt = k.buf("pd_rt")
        kbf = P.sb(es, "pd_kbf", [128, 8, 128], BF16)
        bkbf = k.buf("pd_kbf")
        vst = [P.sb(es, f"pd_vst{i}", [128, 8, VP], BF16) for i in range(2)]
        bvst = k.bufs(2, "pd_vst")
        for i in range(2):
            k.op("pool", lambda e, i=i: e.memset(vst[i][:], 1.0), W=[bvst[i]])
        qTn_st = P.sb(es, "pd_qTn", [128, 8, 512], BF16)
        qTr_st = P.sb(es, "pd_qTr", [64, 8, 512], BF16)
        kTn_st = P.sb(es, "pd_kTn", [128, 8, 512], BF16)
        kTr_st = P.sb(es, "pd_kTr", [64, 512], BF16)
        bqT = k.buf("pd_qT")
        bkT = k.buf("pd_kT")
        nv = 0
        import os
        LIM = float(os.environ.get("PD_LIMIT", "99"))
        for blk in range(NB):
            if LIM < 1 or (LIM < 90 and blk > 0):
                break
            norm_block(P, T, blk, src, b_src, gT)
            hnT, bh = T["hnT"], T["b_hnT"]
            for j in range(4):
                if LIM < 2 or (LIM < 90 and j > 0):
                    break
                sub = blk * 4 + j
                tok = slice(j * 128, (j + 1) * 128)
                for kc in range(8):
                    k.op("pe", lambda e, kc=kc, tok=tok: e.matmul(P.ps[0][:, 0:384], lhsT=hnT[:, kc, tok], rhs=win[:, kc, 0:384],
                                                                  start=(kc == 0), stop=(kc == 7)), R=[bh, bc], W=[P.psb[0]])
                for kc in range(8):
                    k.op("pe", lambda e, kc=kc, tok=tok: e.matmul(P.ps[1][:, 0:320], lhsT=hnT[:, kc, tok], rhs=win[:, kc, 384:704],
                                                                  start=(kc == 0), stop=(kc == 7)), R=[bh, bc], W=[P.psb[1]])
                k.op("act", lambda e: e.activation(out=junk2[:, 0:384], in_=P.ps[0][:, 0:384], func=AF.Square,
                                                   accum_out=st[:, 0:1]), R=[P.psb[0]], W=[bj2, bst])
                k.op("act", lambda e: e.activation(out=junk2[:, 0:256], in_=P.ps[1][:, 0:256], func=AF.Square,
                                                   accum_out=st[:, 1:2]), R=[P.psb[1]], W=[bj2, bst])
                k.op("act", lambda e: e.activation(out=junk2[:, 0:64], in_=P.ps[1][:, 256:320], func=AF.Square,
                                                   accum_out=st[:, 2:3]), R=[P.psb[1]], W=[bj2, bst])
                k.op("act", lambda e: e.activation(out=st[:, 3:4], in_=st[:, 0:1], func=AF.Sqrt, bias=P.epsb[:],
                                                   scale=1.0 / 384), R=[bst, P.b_const], W=[bst])
                k.op("act", lambda e: e.activation(out=st[:, 4:5], in_=st[:, 1:2], func=AF.Sqrt, bias=P.epsb[:],
                                                   scale=1.0 / 256), R=[bst, P.b_const], W=[bst])
                k.op("dve", lambda e: e.reciprocal(out=st[:, 5:7], in_=st[:, 3:5]), R=[bst], W=[bst])
                k.op("dve", lambda e: e.tensor_scalar(out=cn[:, 0:384], in0=P.ps[0][:, 0:384], scalar1=st[:, 5:6],
                                                      scalar2=None, op0=ALU.mult), R=[P.psb[0], bst], W=[bcn])
                k.op("dve", lambda e: e.tensor_scalar(out=cn[:, 384:640], in0=P.ps[1][:, 0:256], scalar1=st[:, 6:7],
                                                      scalar2=None, op0=ALU.mult), R=[P.psb[1], bst], W=[bcn])
                k.op("act", lambda e: e.copy(out=kr[:], in_=P.ps[1][:, 256:320]), R=[P.psb[1]], W=[bkr])
                psT = P.ps[2][:].bitcast(BF16)
                for c in range(5):
                    k.op("pe", lambda e, c=c, psT=psT: e.transpose(out=psT[:, c * 128:(c + 1) * 128],
                                                                   in_=cn[:, c * 128:(c + 1) * 128], identity=P.ident[:]),
                         R=[bcn, P.b_const], W=[P.psb[2]])
                k.op("dve", lambda e, psT=psT: e.tensor_tensor(out=cT[:], in0=psT[:, 0:640].rearrange("p (c t) -> p c t", c=5),
                                                               in1=g5[:, :].unsqueeze(2).broadcast_to([128, 5, 128]), op=ALU.mult),
                     R=[P.psb[2], bc], W=[bcT])
                if LIM < 3:
                    break
                for n in range(3):
                    for kc in range(3):
                        k.op("pe", lambda e, n=n, kc=kc: e.matmul(P.ps[3 + n][:], lhsT=cT[:, kc, :],
                                                                  rhs=wuq[:, kc, n * 512:(n + 1) * 512],
                                                                  start=(kc == 0), stop=(kc == 2)), R=[bcT, bc], W=[P.psb[3 + n]])
                for n in range(3):
                    k.op("act", lambda e, n=n: e.activation(out=qsb[:, n * 512:(n + 1) * 512], in_=P.ps[3 + n][:],
                                                            func=AF.Copy, scale=SCALE), R=[P.psb[3 + n]], W=[bqsb])
                if LIM < 3.1:
                    break
                k.op("pool", lambda e: e.tensor_tensor(out=junk2[:], in0=qsb[:], in1=qsb[:], op=ALU.mult),
                     R=[bqsb], W=[bj2])
                k.op("dve", lambda e: e.tensor_reduce(out=st[:, 8:16], in_=junk2[:, :].rearrange("p (h d) -> p h d", h=8),
                                                      axis=AX.X, op=ALU.add), R=[bj2], W=[bst])
                k.op("dve", lambda e: e.tensor_tensor(out=stats[:, 0:8], in0=stats[:, 0:8], in1=st[:, 8:16], op=ALU.max),
                     R=[bst, P.b_stats], W=[P.b_stats])
                if LIM < 3.2:
                    break
                q3 = qsb[:, :].rearrange("p (h d) -> p h d", h=8)
                cosb = cosT[:, sub, :].unsqueeze(1).broadcast_to([128, 8, 32])
                sinb = sinT[:, sub, :].unsqueeze(1).broadcast_to([128, 8, 32])
                x1 = q3[:, :, 128:160]
                x2 = q3[:, :, 160:192]
                k.op("pool", lambda e, x1=x1, cosb=cosb: e.tensor_tensor(out=rt[0][:], in0=x1, in1=cosb, op=ALU.mult),
                     R=[bqsb, br], W=[brt])
                k.op("pool", lambda e, x2=x2, sinb=sinb: e.tensor_tensor(out=rt[1][:], in0=x2, in1=sinb, op=ALU.mult),
                     R=[bqsb, br], W=[brt])
                k.op("pool", lambda e, x1=x1, sinb=sinb: e.tensor_tensor(out=rt[2][:], in0=x1, in1=sinb, op=ALU.mult),
                     R=[bqsb, br], W=[brt])
                k.op("pool", lambda e, x2=x2, cosb=cosb: e.tensor_tensor(out=rt[3][:], in0=x2, in1=cosb, op=ALU.mult),
                     R=[bqsb, br], W=[brt])
                k.op("dve", lambda e: e.tensor_tensor(out=qbf[:, :, 128:160], in0=rt[0][:], in1=rt[1][:], op=ALU.subtract),
                     R=[brt], W=[bqbf])
                k.op("dve", lambda e: e.tensor_tensor(out=qbf[:, :, 160:192], in0=rt[2][:], in1=rt[3][:], op=ALU.add),
                     R=[brt], W=[bqbf])
                k.op("act", lambda e, q3=q3: e.copy(out=qbf[:, :, 0:128], in_=q3[:, :, 0:128]), R=[bqsb], W=[bqbf])
                if LIM < 3.3:
                    break
                psQn = P.ps[6][:].bitcast(BF16)
                psQr = P.ps[2][:].bitcast(BF16)
                for h in range(8):
                    k.op("pe", lambda e, h=h, psQn=psQn: e.transpose(out=psQn[:, h * 128:(h + 1) * 128], in_=qbf[:, h, 0:128],
                                                                     identity=P.ident[:]), R=[bqbf, P.b_const], W=[P.psb[6]])
                for h in range(8):
                    k.op("pe", lambda e, h=h, psQr=psQr: e.transpose(out=psQr[0:64, h * 128:(h + 1) * 128], in_=qbf[:, h, 128:192],
                                                                     identity=P.ident[:]), R=[bqbf, P.b_const], W=[P.psb[2]])
                k.op("act", lambda e, psQn=psQn, tok=tok: e.copy(out=qTn_st[:, :, tok], in_=psQn.rearrange("p (h t) -> p h t", h=8)),
                     R=[P.psb[6]], W=[bqT])
                k.op("dve", lambda e, psQr=psQr, tok=tok: e.tensor_copy(out=qTr_st[:, :, tok],
                                                                        in_=psQr[0:64, :].rearrange("p (h t) -> p h t", h=8)),
                     R=[P.psb[2]], W=[bqT])
                if LIM < 4:
                    break
                kvb = [3, 4, 5, 0]
                for n in range(4):
                    for kc in range(2):
                        k.op("pe", lambda e, n=n, kc=kc: e.matmul(P.ps[kvb[n]][:], lhsT=cT[:, 3 + kc, :],
                                                                  rhs=wukv[:, kc, n * 512:(n + 1) * 512],
                                                                  start=(kc == 0), stop=(kc == 1)), R=[bcT, bc], W=[P.psb[kvb[n]]])
                if LIM < 4.1:
                    break
                vs = vst[nv % 2]
                bvs = bvst[nv % 2]
                nv += 1
                for n in range(4):
                    pv = P.ps[kvb[n]][:, :].rearrange("p (h c) -> p h c", h=2)
                    if n % 2 == 0:
                        k.op("act", lambda e, n=n, pv=pv: e.copy(out=kbf[:, 2 * n:2 * n + 2, :], in_=pv[:, :, 0:128]),
                             R=[P.psb[kvb[n]]], W=[bkbf])
                        k.op("act", lambda e, n=n, pv=pv, vs=vs: e.copy(out=vs[:, 2 * n:2 * n + 2, 0:128], in_=pv[:, :, 128:256]),
                             R=[P.psb[kvb[n]]], W=[bvs])
                    else:
                        k.op("dve", lambda e, n=n, pv=pv: e.tensor_copy(out=kbf[:, 2 * n:2 * n + 2, :], in_=pv[:, :, 0:128]),
                             R=[P.psb[kvb[n]]], W=[bkbf])
                        k.op("dve", lambda e, n=n, pv=pv, vs=vs: e.tensor_copy(out=vs[:, 2 * n:2 * n + 2, 0:128], in_=pv[:, :, 128:256]),
                             R=[P.psb[kvb[n]]], W=[bvs])
                if LIM < 4.2:
                    break
                r0 = blk * TB + j * 128
                k.dma("sp", [(P.mla_dram["Va"][r0:r0 + 128, :], vs[:, :, :].rearrange("p h c -> p (h c)"))],
                      R=[bvs], W=[P.b_qkv[blk]], key=f"d_vst{(nv - 1) % 2}")
                if LIM < 4.3:
                    break
                k.op("pool", lambda e: e.tensor_tensor(out=junk2[:, 0:1024], in0=kbf[:, :, :].rearrange("p h c -> p (h c)"),
                                                       in1=kbf[:, :, :].rearrange("p h c -> p (h c)"), op=ALU.mult),
                     R=[bkbf], W=[bj2])
                k.op("dve", lambda e: e.tensor_reduce(out=st[:, 8:16], in_=junk2[:, 0:1024].rearrange("p (h d) -> p h d", h=8),
                                                      axis=AX.X, op=ALU.add), R=[bj2], W=[bst])
                k.op("dve", lambda e: e.tensor_scalar(out=st[:, 8:16], in0=st[:, 8:16], scalar1=st[:, 2:3], scalar2=None,
                                                      op0=ALU.add), R=[bst], W=[bst])
                k.op("dve", lambda e: e.tensor_tensor(out=stats[:, 8:16], in0=stats[:, 8:16], in1=st[:, 8:16], op=ALU.max),
                     R=[bst, P.b_stats], W=[P.b_stats])
                if LIM < 4.4:
                    break
                c1 = cosT[:, sub, :]
                s1 = sinT[:, sub, :]
                k.op("pool", lambda e, c1=c1: e.tensor_tensor(out=rt[0][:, 0, :], in0=kr[:, 0:32], in1=c1, op=ALU.mult),
                     R=[bkr, br], W=[brt])
                k.op("pool", lambda e, s1=s1: e.tensor_tensor(out=rt[1][:, 0, :], in0=kr[:, 32:64], in1=s1, op=ALU.mult),
                     R=[bkr, br], W=[brt])
                k.op("pool", lambda e, s1=s1: e.tensor_tensor(out=rt[2][:, 0, :], in0=kr[:, 0:32], in1=s1, op=ALU.mult),
                     R=[bkr, br], W=[brt])
                k.op("pool", lambda e, c1=c1: e.tensor_tensor(out=rt[3][:, 0, :], in0=kr[:, 32:64], in1=c1, op=ALU.mult),
                     R=[bkr, br], W=[brt])
                k.op("dve", lambda e: e.tensor_tensor(out=krb[:, 0:32], in0=rt[0][:, 0, :], in1=rt[1][:, 0, :], op=ALU.subtract),
                     R=[brt], W=[bkrb])
                k.op("dve", lambda e: e.tensor_tensor(out=krb[:, 32:64], in0=rt[2][:, 0, :], in1=rt[3][:, 0, :], op=ALU.add),
                     R=[brt], W=[bkrb])
                if LIM < 4.5:
                    break
                psKn = P.ps[6][:].bitcast(BF16)
                psKr = P.ps[2][:].bitcast(BF16)
                for h in range(8):
                    k.op("pe", lambda e, h=h, psKn=psKn: e.transpose(out=psKn[:, h * 128:(h + 1) * 128], in_=kbf[:, h, :],
                                                                     identity=P.ident[:]), R=[bkbf, P.b_const], W=[P.psb[6]])
                k.op("pe", lambda e, psKr=psKr: e.transpose(out=psKr[0:64, 0:128], in_=krb[:], identity=P.ident[:]),
                     R=[bkrb, P.b_const], W=[P.psb[2]])
                k.op("act", lambda e, psKn=psKn, tok=tok: e.copy(out=kTn_st[:, :, tok], in_=psKn.rearrange("p (h t) -> p h t", h=8)),
                     R=[P.psb[6]], W=[bkT])
                k.op("dve", lambda e, psKr=psKr, tok=tok: e.tensor_copy(out=kTr_st[:, tok], in_=psKr[0:64, 0:128]),
                     R=[P.psb[2]], W=[bkT])
            if LIM < 5:
                break
            c0 = blk * TB
            k.dma("sp", [(P.mla_dram["QTn"][:, :, c0:c0 + TB].rearrange("h d t -> d h t"), qTn_st[:]),
                         (P.mla_dram["QTr"][:, :, c0:c0 + TB].rearrange("h d t -> d h t"), qTr_st[:])],
                  R=[bqT], W=[P.b_qkv[blk]], key="d_qT")
            k.dma("sp", [(P.mla_dram["KTn"][:, :, c0:c0 + TB].rearrange("h d t -> d h t"), kTn_st[:]),
                         (P.mla_dram["KTr"][:, c0:c0 + TB], kTr_st[:])],
                  R=[bkT], W=[P.b_qkv[blk]], key="d_kT")
        k.barrier()


def mla_attn_phase(P, src, b_src, dst, b_dst, W):
    k, nc = P.k, P.nc
    I = P.inp
    Dm = P.mla_dram
    stats = P.mla_stats
    with ExitStack() as es:
        bc = k.buf("pe_consts")
        wo = P.sb(es, "pe_wo", [128, 8, 1024], BF16)
        k.dma("sp", [(wo[:], W["odd_w_out"][0].rearrange("(c p) n -> p c n", p=128))], R=[W["odd_w_out"][1]], W=[bc])
        tps = P.ps[7]
        sm = P.sb(es, "pe_sm", [16, 4], F32)
        dg = P.sb(es, "pe_dg", [8, 8], F32)
        negc = P.sb(es, "pe_negc", [128, 8], F32)
        k.op("pe", lambda e: e.transpose(out=tps[0:16, 0:128], in_=stats[:, 0:16], identity=P.ident_f[:]),
             R=[P.b_stats, P.b_const], W=[P.psb[7]])
        k.op("dve", lambda e: e.tensor_reduce(out=sm[:, 0:1], in_=tps[0:16, 0:128], axis=AX.X, op=ALU.max),
             R=[P.psb[7]], W=[bc])
        k.op("pe", lambda e: e.transpose(out=tps[0:1, 128:144], in_=sm[:, 0:1], identity=P.ident_f[0:16, 0:16]),
             R=[bc, P.b_const], W=[P.psb[7]])
        rowv = P.sb(es, "pe_rowv", [1, 24], F32)
        k.op("dve", lambda e: e.tensor_copy(out=rowv[:, 0:16], in_=tps[0:1, 128:144]), R=[P.psb[7]], W=[bc])
        k.op("dve", lambda e: e.tensor_tensor(out=rowv[:, 16:24], in0=rowv[:, 0:8], in1=rowv[:, 8:16], op=ALU.mult),
             R=[bc], W=[bc])
        k.op("act", lambda e: e.activation(out=rowv[:, 16:24], in_=rowv[:, 16:24], func=AF.Sqrt), R=[bc], W=[bc])
        k.op("dve", lambda e: e.tensor_scalar(out=rowv[:, 16:24], in0=rowv[:, 16:24], scalar1=-1.0, scalar2=None, op0=ALU.mult),
             R=[bc], W=[bc])
        k.op("pe", lambda e: e.matmul(tps[:, 160:168], lhsT=P.ones_f[0:1, :], rhs=rowv[:, 16:24], start=True, stop=True),
             R=[bc, P.b_const], W=[P.psb[7]])
        k.op("dve", lambda e: e.tensor_copy(out=negc[:], in_=tps[:, 160:168]), R=[P.psb[7]], W=[bc])
        tri = P.sb(es, "pe_tri", [128, 128], BF16)
        k.op("pool", lambda e: e.affine_select(out=tri[:], in_=P.ones_f[:], pattern=[[1, 128]], compare_op=ALU.is_ge,
                                               fill=0.0, base=0, channel_multiplier=-1), R=[P.b_const], W=[bc])
        KTn = P.sb(es, "pe_KTn", [128, 8, S], BF16)
        KTr = P.sb(es, "pe_KTr", [128, S], BF16)
        Vs = P.sb(es, "pe_V", [128, 32, HQ * VP], BF16)
        bK = k.bufs(8, "pe_K")
        bV = k.bufs(8, "pe_V")
        k.op("pool", lambda e: e.memset(KTr[64:128, :], 0.0), W=bK)
        for b in range(NB):
            c0 = b * TB
            k.dma("sp", [(KTn[:, :, c0:c0 + TB], Dm["KTn"][:, :, c0:c0 + TB].rearrange("h d t -> d h t")),
                         (KTr[0:64, c0:c0 + TB], Dm["KTr"][:, c0:c0 + TB])], R=[P.b_qkv[b]], W=[bK[b]])
            k.dma("sp", [(Vs[:, 4 * b:4 * b + 4, :], Dm["Va"][c0:c0 + TB, :].rearrange("(j p) c -> p j c", p=128))],
                  R=[P.b_qkv[b]], W=[bV[b]])
        NQ = 1
        Qn = [P.sb(es, f"pe_Qn{i}", [128, 8, TB], BF16) for i in range(NQ)]
        Qr = [P.sb(es, f"pe_Qr{i}", [128, 8, TB], BF16) for i in range(NQ)]
        bQ = k.bufs(NQ, "pe_Q")
        for i in range(NQ):
            k.op("pool", lambda e, i=i: e.memset(Qr[i][64:128, :, :], 0.0), W=[bQ[i]])
        NPT = 4
        PT = [P.sb(es, f"pe_PT{i}", [128, TB], BF16) for i in range(NPT)]
        bPT = k.bufs(NPT, "pe_PT")
        att = P.sb(es, "pe_att", [128, 4, 1024], BF16)
        batt = k.bufs(4, "pe_att")
        aT = P.sb(es, "pe_aT", [128, 8, TB], BF16)
        baT = k.buf("pe_aT")
        rden = P.sb(es, "pe_rden", [128, 4], F32)
        brden = k.buf("pe_rden")
        hres = [P.sb(es, f"pe_hres{i}", [128, 1024], F32) for i in range(2)]
        bhres = k.bufs(2, "pe_hres")
        npt = 0
        nsc = 0
        for qb in range(NB):
            qi = qb % NQ
            c0 = qb * TB
            k.dma("sp", [(Qn[qi][:], Dm["QTn"][:, :, c0:c0 + TB].rearrange("h d t -> d h t")),
                         (Qr[qi][0:64], Dm["QTr"][:, :, c0:c0 + TB].rearrange("h d t -> d h t"))],
                  R=[P.b_qkv[qb]], W=[bQ[qi]])
            nkt = 4 * qb + 4
            tiles = [(h, kt) for h in range(8) for kt in range(nkt)]
            SCB = (0, 1, 2, 7)
            info = {}

            def emit_qk(i):
                nonlocal nsc, npt
                h, kt = tiles[i]
                r = kt - 4 * qb
                q0 = max(r, 0) * 128
                sb_ = SCB[nsc % 4]
                nsc += 1
                pi = npt % NPT
                npt += 1
                kb = kt // 4
                info[i] = (sb_, pi, q0, r, kb)
                k.op("pe", lambda e, h=h, kt=kt, q0=q0, sb_=sb_: e.matmul(
                    P.ps[sb_][:, q0:TB], lhsT=KTn[:, h, kt * 128:(kt + 1) * 128], rhs=Qn[qi][:, h, q0:TB],
                    start=True, stop=False), R=[bK[kb], bQ[qi]], W=[P.psb[sb_]])
                k.op("pe", lambda e, h=h, kt=kt, q0=q0, sb_=sb_: e.matmul(
                    P.ps[sb_][:, q0:TB], lhsT=KTr[:, kt * 128:(kt + 1) * 128], rhs=Qr[qi][:, h, q0:TB],
                    start=False, stop=True), R=[bK[kb], bQ[qi]], W=[P.psb[sb_]])
                k.op("act", lambda e, h=h, q0=q0, sb_=sb_, pi=pi: e.activation(
                    out=PT[pi][:, q0:TB], in_=P.ps[sb_][:, q0:TB], func=AF.Exp, bias=negc[:, h:h + 1], scale=1.0),
                    R=[P.psb[sb_], bc], W=[bPT[pi]])
                if r >= 0:
                    k.op("pool", lambda e, q0=q0, pi=pi: e.tensor_tensor(out=PT[pi][:, q0:q0 + 128], in0=PT[pi][:, q0:q0 + 128],
                                                                         in1=tri[:], op=ALU.mult), R=[bPT[pi], bc], W=[bPT[pi]])

            LOOK = 2
            for i in range(min(LOOK, len(tiles))):
                emit_qk(i)
            first = [True, True]
            for i, (h, kt) in enumerate(tiles):
                if i + LOOK < len(tiles):
                    emit_qk(i + LOOK)
                ob = (3, 4) if h % 2 == 0 else (5, 6)
                if kt == 0:
                    first = [True, True]
                sb_, pi, q0, r, kb = info.pop(i)
                for js in range(max(r, 0), 4):
                    bank = ob[js // 2]
                    off = (js % 2) * 129
                    last = (kt == 4 * qb + js)
                    st_ = first[js // 2]
                    first[js // 2] = False
                    k.op("pe", lambda e, js=js, bank=bank, off=off, pi=pi, kt=kt, h=h, st_=st_, last=last: e.matmul(
                        P.ps[bank][:, off:off + 129], lhsT=PT[pi][:, js * 128:(js + 1) * 128],
                        rhs=Vs[:, kt, h * VP:h * VP + 129], start=st_, stop=last, skip_group_check=True),
                        R=[bPT[pi], bV[kb]], W=[P.psb[bank]])
                if kt == nkt - 1:
                    for js in range(4):
                        bank = ob[js // 2]
                        off = (js % 2) * 129
                        k.op("dve", lambda e, js=js, bank=bank, off=off: e.reciprocal(out=rden[:, js:js + 1],
                                                                                      in_=P.ps[bank][:, off + 128:off + 129]),
                             R=[P.psb[bank]], W=[brden])
                        k.op("dve", lambda e, js=js, bank=bank, off=off, h=h: e.tensor_scalar(
                            out=att[:, js, h * 128:(h + 1) * 128], in0=P.ps[bank][:, off:off + 128], scalar1=rden[:, js:js + 1],
                            scalar2=None, op0=ALU.mult), R=[P.psb[bank], brden], W=[batt[js]])
            for js in range(4):
                psA = P.ps[7][:].bitcast(BF16)
                for h in range(8):
                    k.op("pe", lambda e, js=js, h=h, psA=psA: e.transpose(out=psA[:, h * 128:(h + 1) * 128],
                                                                          in_=att[:, js, h * 128:(h + 1) * 128], identity=P.ident[:]),
                         R=[batt[js], P.b_const], W=[P.psb[7]])
                k.op("dve", lambda e, js=js, psA=psA: e.tensor_copy(out=aT[:, :, js * 128:(js + 1) * 128],
                                                                    in_=psA.rearrange("p (h t) -> p h t", h=8)),
                     R=[P.psb[7]], W=[baT])
            for js in range(4):
                r0 = qb * TB + js * 128
                hr = hres[js % 2]
                bhr = bhres[js % 2]
                k.dma("sp", [(hr[:], src[r0:r0 + 128, :])], R=[b_src[qb * 4 + js]], W=[bhr], key=f"d_hres{js % 2}")
                for n in range(2):
                    bank = (3, 5)[n]
                    for h in range(8):
                        k.op("pe", lambda e, js=js, n=n, h=h, bank=bank: e.matmul(
                            P.ps[bank][:], lhsT=aT[:, h, js * 128:(js + 1) * 128], rhs=wo[:, h, n * 512:(n + 1) * 512],
                            start=(h == 0), stop=(h == 7)), R=[baT, bc], W=[P.psb[bank]])
                    k.op("dve", lambda e, n=n, bank=bank, hr=hr: e.tensor_tensor(
                        out=hr[:, n * 512:(n + 1) * 512], in0=hr[:, n * 512:(n + 1) * 512], in1=P.ps[bank][:],
                        op=ALU.add), R=[bhr, P.psb[bank]], W=[bhr])
                k.dma("sp", [(dst[r0:r0 + 128, :], hr[:])], R=[bhr], W=[b_dst[qb * 4 + js]], key=f"d_hst{js % 2}")
        k.barrier()


def hgrn_phase(P, src, b_src, W):
    k, nc = P.k, P.nc
    I = P.inp
    uT_d = P.dint("uT", [512, S], BF16)
    ycatT = P.dint("ycatT", [1024, S], BF16)
    P.uT_d, P.ycatT = uT_d, ycatT
    P.b_uT = k.bufs(NB, "uT")
    P.b_ya = k.bufs(NB, "ya")
    P.b_yb = k.bufs(NB, "yb")
    with ExitStack() as es:
        bc = k.buf("pa_consts")
        gT = load_vec_fm(P, es, "pa_g", I["norm_mix_g"][0], 8, bc)
        win = P.sb(es, "pa_win", [128, 8, 2560], BF16)
        k.dma("sp", [(win[:, 4 * i:4 * i + 4, :], W["even_w_in"][0][512 * i:512 * (i + 1), :].rearrange("(c p) n -> p c n", p=128))
                     for i in range(2)], R=[W["even_w_in"][1]], W=[bc])
        l0 = load_vec_fm(P, es, "pa_l0", I["hgrn_lb_logits"][0], 4, bc)
        l1 = load_vec_fm(P, es, "pa_l1", I["hgrn_lb_logits"][1], 4, bc)
        lb = P.sb(es, "pa_lb", [128, 4], F32)
        oml = P.sb(es, "pa_oml", [128, 4], F32)
        noml = P.sb(es, "pa_noml", [128, 4], F32)
        k.op("dve", lambda e: e.tensor_tensor(out=lb[:], in0=l0[:], in1=l1[:], op=ALU.subtract), R=[bc], W=[bc])
        k.op("act", lambda e: e.activation(out=lb[:], in_=lb[:], func=AF.Sigmoid), R=[bc], W=[bc])
        k.op("dve", lambda e: e.tensor_scalar(out=oml[:], in0=lb[:], scalar1=-1.0, scalar2=1.0, op0=ALU.mult, op1=ALU.add),
             R=[bc], W=[bc])
        k.op("dve", lambda e: e.tensor_scalar(out=noml[:], in0=oml[:], scalar1=-1.0, scalar2=None, op0=ALU.mult),
             R=[bc], W=[bc])
        ng = P.sb(es, "pa_ng", [128, 512], F32)
        k.dma("sp", [(ng[:], I["hgrn_norm_g"].partition_broadcast(128))], W=[bc])
        msk = P.sb(es, "pa_msk", [128, 128], F32)
        k.op("pool", lambda e: e.affine_select(out=msk[:], in_=P.ones_f[:], pattern=[[1, 128]], compare_op=ALU.is_ge,
                                               fill=0.0, base=0, channel_multiplier=-1), R=[P.b_const], W=[bc])
        k.op("pool", lambda e: e.memset(msk[0:64, 64:128], 0.0), R=[bc], W=[bc])
        rst = P.sb(es, "pa_rst", [128, 512], F32)
        k.op("pool", lambda e: e.memset(rst[:], 1.0), W=[bc])
        k.op("pool", lambda e: e.memset(rst[:, :].rearrange("p (n c) -> p n c", c=64)[:, :, 0:1], 0.0), R=[bc], W=[bc])
        mA = P.sb(es, "pa_mA", [128, 512], BF16)
        mB = P.sb(es, "pa_mB", [128, 512], BF16)
        k.op("pool", lambda e: e.memset(mA[:], 1.0), W=[bc])
        k.op("pool", lambda e: e.memset(mA[:, :].rearrange("p (n c) -> p n c", c=128)[:, :, 64:128], 0.0), R=[bc], W=[bc])
        k.op("pool", lambda e: e.memset(mB[:], 1.0), W=[bc])
        k.op("pool", lambda e: e.memset(mB[:, :].rearrange("p (n c) -> p n c", c=128)[:, :, 0:64], 0.0), R=[bc], W=[bc])
        rmA = P.sb(es, "pa_rmA", [128, 1], F32)
        rmB = P.sb(es, "pa_rmB", [128, 1], F32)
        k.op("pool", lambda e: e.memset(rmA[0:64, :], 1.0), W=[bc])
        k.op("pool", lambda e: e.memset(rmA[64:128, :], 0.0), W=[bc])
        k.op("pool", lambda e: e.memset(rmB[0:64, :], 0.0), W=[bc])
        k.op("pool", lambda e: e.memset(rmB[64:128, :], 1.0), W=[bc])
        St = P.sb(es, "pa_S", [128, 4, 128], F32)
        Sbf = [P.sb(es, f"pa_Sbf{i}", [128, 4, 128], BF16) for i in range(2)]
        bS = k.bufs(4, "pa_S")
        bSbf = [k.bufs(4, "pa_SbfA"), k.bufs(4, "pa_SbfB")]
        k.op("pool", lambda e: e.memset(St[:], 0.0), W=bS)
        k.op("pool", lambda e: e.memset(Sbf[0][:], 0.0), W=bSbf[0])
        k.op("pool", lambda e: e.memset(Sbf[1][:], 0.0), W=bSbf[1])

        T = alloc_blockbufs(P, es, "a_", ps_tr=[7])
        T["b_g"] = bc
        sig = P.sb(es, "pa_sig", [128, 512], F32); bsig = k.buf("pa_sig")
        logf = P.sb(es, "pa_logf", [128, 512], F32); blogf = k.buf("pa_logf")
        bb = P.sb(es, "pa_b", [128, 512], F32); bbb = k.buf("pa_b")
        eb = P.sb(es, "pa_eb", [128, 4, 512], F32); beb = k.bufs(4, "pa_eb")
        enb = P.sb(es, "pa_enb", [128, 512], F32); benb = k.buf("pa_enb")
        kk = P.sb(es, "pa_kk", [128, 512], F32); bkk = k.buf("pa_kk")
        kd32 = P.sb(es, "pa_kd32", [128, 512], F32); bkd32 = k.buf("pa_kd32")
        qd = P.sb(es, "pa_qd", [128, 4, 512], BF16); bqd = k.bufs(4, "pa_qd")
        kd = P.sb(es, "pa_kd", [128, 4, 512], BF16); bkd = k.bufs(4, "pa_kd")
        qdA = P.sb(es, "pa_qdA", [128, 4, 512], BF16); qdB = P.sb(es, "pa_qdB", [128, 4, 512], BF16)
        kbT = P.sb(es, "pa_kbT", [128, 4, 512], BF16); bkbT = k.bufs(4, "pa_kbT")
        uTs = P.sb(es, "pa_uT", [128, 4, 512], BF16); buTs = k.buf("pa_uTs")
        kbtm = P.sb(es, "pa_kbtm", [128, 4, 128], BF16); bkbtm = k.buf("pa_kbtm")
        kbtmB = P.sb(es, "pa_kbtmB", [128, 4, 128], BF16)
        vtm = P.sb(es, "pa_vtm", [128, 512], BF16); bvtm = k.buf("pa_vtm")
        ngsg = P.sb(es, "pa_ngsg", [128, 512], F32); bngsg = k.buf("pa_ngsg")
        attm = P.sb(es, "pa_attm", [128, 4, 128], BF16); battm = k.bufs(4, "pa_attm")
        yatm = P.sb(es, "pa_yatm", [128, 512], BF16); byatm = k.buf("pa_yatm")
        yaT = P.sb(es, "pa_yaT", [128, 4, 512], BF16); byaT = k.buf("pa_yaT")
        sst = P.sb(es, "pa_sst", [128, 16], F32); bsst = k.bufs(4, "pa_sst")
        junk = P.sb(es, "pa_junk", [128, 128], F32); bjunk = k.buf("pa_junk")
        import os
        HL = float(os.environ.get("HG_LIMIT", "99"))
        for blk in range(NB):
            if HL < 1 or (HL < 90 and blk > 0):
                break
            norm_block(P, T, blk, src, b_src, gT)
            hnT, bh = T["hnT"], T["b_hnT"]
            c0 = blk * TB
            for c in range(4):
                pb = 4 + c % 2
                for kc in range(8):
                    k.op("pe", lambda e, c=c, kc=kc, pb=pb: e.matmul(P.ps[pb][:], lhsT=win[:, kc, 2048 + c * 128:2048 + (c + 1) * 128],
                                                                     rhs=hnT[:, kc, :], start=(kc == 0), stop=(kc == 7)),
                         R=[bc, bh], W=[P.psb[pb]])
                k.op("act", lambda e, c=c, pb=pb: e.copy(out=uTs[:, c, :], in_=P.ps[pb][:]), R=[P.psb[pb]], W=[buTs])
            k.dma("sp", [(uT_d[:, c0:c0 + TB].rearrange("(c p) t -> p c t", p=128), uTs[:])], R=[buTs], W=[P.b_uT[blk]])
            if HL < 1.1:
                break
            for h in range(4):
                if HL < 1.2 and h > 0:
                    break
                pq, pf = 4, 5
                for kc in range(8):
                    k.op("pe", lambda e, h=h, kc=kc: e.matmul(P.ps[pf][:], lhsT=win[:, kc, 512 + h * 128:512 + (h + 1) * 128],
                                                              rhs=hnT[:, kc, :], start=(kc == 0), stop=(kc == 7)),
                         R=[bc, bh], W=[P.psb[pf]])
                for kc in range(8):
                    k.op("pe", lambda e, h=h, kc=kc: e.matmul(P.ps[pq][:], lhsT=win[:, kc, h * 128:(h + 1) * 128],
                                                              rhs=hnT[:, kc, :], start=(kc == 0), stop=(kc == 7)),
                         R=[bc, bh], W=[P.psb[pq]])
                k.op("act", lambda e: e.activation(out=sig[:], in_=P.ps[pf][:], func=AF.Sigmoid), R=[P.psb[pf]], W=[bsig])
                k.op("act", lambda e, h=h: e.activation(out=logf[:], in_=sig[:], func=AF.Ln, bias=lb[:, h:h + 1],
                                                        scale=oml[:, h:h + 1]), R=[bsig, bc], W=[blogf])
                if HL < 1.15:
                    break
                k.op("dve", lambda e: e.tensor_tensor_scan(out=bb[:], data0=rst[:], data1=logf[:], initial=0.0,
                                                           op0=ALU.mult, op1=ALU.add), R=[blogf, bc], W=[bbb])
                if HL < 1.17:
                    break
                k.op("act", lambda e, h=h: e.activation(out=eb[:, h, :], in_=bb[:], func=AF.Exp), R=[bbb], W=[beb[h]])
                k.op("act", lambda e: e.activation(out=enb[:], in_=bb[:], func=AF.Exp, scale=-1.0), R=[bbb], W=[benb])
                k.op("dve", lambda e, h=h: e.tensor_scalar(out=kk[:], in0=sig[:], scalar1=noml[:, h:h + 1], scalar2=oml[:, h:h + 1],
                                                           op0=ALU.mult, op1=ALU.add), R=[bsig, bc], W=[bkk])
                k.op("dve", lambda e, h=h: e.tensor_tensor(out=qd[:, h, :], in0=eb[:, h, :], in1=P.ps[pq][:], op=ALU.mult),
                     R=[beb[h], P.psb[pq]], W=[bqd[h]])
                k.op("pool", lambda e, h=h: e.tensor_tensor(out=qdA[:, h, :], in0=qd[:, h, :], in1=mA[:], op=ALU.mult),
                     R=[bqd[h], bc], W=[bqd[h]])
                k.op("pool", lambda e, h=h: e.tensor_tensor(out=qdB[:, h, :], in0=qd[:, h, :], in1=mB[:], op=ALU.mult),
                     R=[bqd[h], bc], W=[bqd[h]])
                k.op("dve", lambda e: e.tensor_tensor(out=kd32[:], in0=kk[:], in1=enb[:], op=ALU.mult), R=[bkk, benb], W=[bkd32])
                if HL < 1.18:
                    break
                k.op("pool", lambda e, h=h: e.tensor_copy(out=kd[:, h, :], in_=kd32[:]), R=[bkd32], W=[bkd[h]])
                k.op("pool", lambda e, h=h: e.tensor_tensor(
                    out=kbT[:, h, :].rearrange("p (n c) -> p n c", c=64), in0=kd32[:, :].rearrange("p (n c) -> p n c", c=64),
                    in1=eb[:, h, :].rearrange("p (n c) -> p n c", c=64)[:, :, 63:64].broadcast_to([128, 8, 64]), op=ALU.mult),
                    R=[bkd32, beb[h]], W=[bkbT[h]])
            for j in range(4):
                if HL < 2 or (HL < 90 and j > 0):
                    break
                tok = slice(j * 128, (j + 1) * 128)
                for kc in range(8):
                    k.op("pe", lambda e, kc=kc, tok=tok: e.matmul(P.ps[4][:], lhsT=hnT[:, kc, tok], rhs=win[:, kc, 1024:1536],
                                                                  start=(kc == 0), stop=(kc == 7)), R=[bc, bh], W=[P.psb[4]])
                for kc in range(8):
                    k.op("pe", lambda e, kc=kc, tok=tok: e.matmul(P.ps[5][:], lhsT=hnT[:, kc, tok], rhs=win[:, kc, 1536:2048],
                                                                  start=(kc == 0), stop=(kc == 7)), R=[bc, bh], W=[P.psb[5]])
                k.op("act", lambda e: e.copy(out=vtm[:], in_=P.ps[4][:]), R=[P.psb[4]], W=[bvtm])
                k.op("act", lambda e: e.activation(out=ngsg[:], in_=P.ps[5][:], func=AF.Silu), R=[P.psb[5]], W=[bngsg])
                k.op("pool", lambda e: e.tensor_tensor(out=ngsg[:], in0=ngsg[:], in1=ng[:], op=ALU.mult), R=[bngsg, bc], W=[bngsg])
                psK = P.ps[6][:].bitcast(BF16)
                for h in range(4):
                    k.op("pe", lambda e, h=h, tok=tok, psK=psK: e.transpose(out=psK[:, h * 128:(h + 1) * 128], in_=kbT[:, h, tok],
                                                                            identity=P.ident[:]), R=[bkbT[h], P.b_const], W=[P.psb[6]])
                k.op("act", lambda e, psK=psK: e.activation(out=kbtm[:], in_=psK[:, 0:512].rearrange("p (h d) -> p h d", h=4),
                                                            func=AF.Copy, scale=rmA[:, 0:1]), R=[P.psb[6], bc], W=[bkbtm])
                k.op("act", lambda e, psK=psK: e.activation(out=kbtmB[:], in_=psK[:, 0:512].rearrange("p (h d) -> p h d", h=4),
                                                            func=AF.Copy, scale=rmB[:, 0:1]), R=[P.psb[6], bc], W=[bkbtm])
                tA = j * 128 + 63
                tB = j * 128 + 127
                if HL < 3:
                    break
                for h in range(4):
                    pb = h
                    vh = vtm[:, h * 128:(h + 1) * 128]
                    k.op("pe", lambda e, h=h, tok=tok, pb=pb: e.matmul(P.ps[pb][:, 0:128], lhsT=kd[:, h, tok], rhs=qd[:, h, tok],
                                                                       start=True, stop=True), R=[bkd[h], bqd[h]], W=[P.psb[pb]])
                    k.op("pe", lambda e, h=h, pb=pb, vh=vh: e.matmul(P.ps[pb][:, 128:256], lhsT=kbtm[:, h, :], rhs=vh,
                                                                     start=True, stop=True), R=[bkbtm, bvtm], W=[P.psb[pb]])
                    k.op("pe", lambda e, h=h, pb=pb, vh=vh: e.matmul(P.ps[pb][:, 256:384], lhsT=kbtmB[:, h, :], rhs=vh,
                                                                     start=True, stop=True), R=[bkbtm, bvtm], W=[P.psb[pb]])
                    k.op("dve", lambda e, h=h, pb=pb: e.tensor_tensor(out=attm[:, h, :], in0=P.ps[pb][:, 0:128], in1=msk[:], op=ALU.mult),
                         R=[P.psb[pb], bc], W=[battm[h]])
                    k.op("dve", lambda e, h=h, pb=pb, tA=tA: e.scalar_tensor_tensor(
                        out=St[:, h, :], in0=St[:, h, :], scalar=eb[:, h, tA:tA + 1], in1=P.ps[pb][:, 128:256], op0=ALU.mult, op1=ALU.add),
                        R=[bS[h], beb[h], P.psb[pb]], W=[bS[h]])
                    k.op("act", lambda e, h=h: e.copy(out=Sbf[1][:, h, :], in_=St[:, h, :]), R=[bS[h]], W=[bSbf[1][h]])
                if HL < 4:
                    break
                for h in range(4):
                    pb = h
                    po = 4 + h % 2
                    vh = vtm[:, h * 128:(h + 1) * 128]
                    tokA = slice(j * 128, j * 128 + 64)
                    tokB = slice(j * 128 + 64, j * 128 + 128)
                    k.op("pe", lambda e, h=h, po=po, vh=vh: e.matmul(P.ps[po][:, 0:128], lhsT=attm[:, h, :], rhs=vh,
                                                                     start=True, stop=False), R=[battm[h], bvtm], W=[P.psb[po]])
                    k.op("pe", lambda e, h=h, po=po, tok=tok: e.matmul(P.ps[po][:, 0:128], lhsT=qdA[:, h, tok], rhs=Sbf[0][:, h, :],
                                                                       start=False, stop=False),
                         R=[bqd[h], bSbf[0][h]], W=[P.psb[po]])
                    k.op("pe", lambda e, h=h, po=po, tok=tok: e.matmul(P.ps[po][:, 0:128], lhsT=qdB[:, h, tok], rhs=Sbf[1][:, h, :],
                                                                       start=False, stop=True),
                         R=[bqd[h], bSbf[1][h]], W=[P.psb[po]])
                    k.op("dve", lambda e, h=h, pb=pb, tB=tB: e.scalar_tensor_tensor(
                        out=St[:, h, :], in0=St[:, h, :], scalar=eb[:, h, tB:tB + 1], in1=P.ps[pb][:, 256:384], op0=ALU.mult, op1=ALU.add),
                        R=[bS[h], beb[h], P.psb[pb]], W=[bS[h]])
                    k.op("act", lambda e, h=h: e.copy(out=Sbf[0][:, h, :], in_=St[:, h, :]), R=[bS[h]], W=[bSbf[0][h]])
                    k.op("act", lambda e, h=h, po=po: e.activation(out=junk[:], in_=P.ps[po][:, 0:128], func=AF.Square,
                                                                   accum_out=sst[:, h:h + 1]), R=[P.psb[po]], W=[bjunk, bsst[h]])
                    k.op("act", lambda e, h=h: e.activation(out=sst[:, 4 + h:5 + h], in_=sst[:, h:h + 1], func=AF.Sqrt, bias=P.epsb[:],
                                                            scale=1.0 / 128), R=[bsst[h], P.b_const], W=[bsst[h]])
                    k.op("dve", lambda e, h=h: e.reciprocal(out=sst[:, 8 + h:9 + h], in_=sst[:, 4 + h:5 + h]), R=[bsst[h]], W=[bsst[h]])
                    k.op("dve", lambda e, h=h, po=po: e.scalar_tensor_tensor(
                        out=yatm[:, h * 128:(h + 1) * 128], in0=P.ps[po][:, 0:128], scalar=sst[:, 8 + h:9 + h],
                        in1=ngsg[:, h * 128:(h + 1) * 128], op0=ALU.mult, op1=ALU.mult),
                        R=[P.psb[po], bsst[h], bngsg], W=[byatm])
                if HL < 5:
                    break
                psY = P.ps[6][:].bitcast(BF16)
                for h in range(4):
                    k.op("pe", lambda e, h=h, psY=psY: e.transpose(out=psY[:, h * 128:(h + 1) * 128], in_=yatm[:, h * 128:(h + 1) * 128],
                                                                   identity=P.ident[:]), R=[byatm, P.b_const], W=[P.psb[6]])
                k.op("dve", lambda e, psY=psY, tok=tok: e.tensor_copy(out=yaT[:, :, tok], in_=psY[:, 0:512].rearrange("p (h t) -> p h t", h=4)),
                     R=[P.psb[6]], W=[byaT])
            if HL < 6:
                break
            k.dma("sp", [(ycatT[0:512, c0:c0 + TB].rearrange("(c p) t -> p c t", p=128), yaT[:])], R=[byaT], W=[P.b_ya[blk]])
        k.barrier()


def s5_phase(P, W):
    k, nc = P.k, P.nc
    I = P.inp
    L = 16
    NCH = S // L
    G = 32
    with ExitStack() as es:
        bs = k.buf("s5_tab")

        def dv(fn, R=(), Wb=()):
            k.op("dve", fn, R=[bs] + list(R), W=[bs] + list(Wb))

        def ac(fn, R=(), Wb=()):
            k.op("act", fn, R=[bs] + list(R), W=[bs] + list(Wb))

        def t32(name, shape=(128, 32)):
            return P.sb(es, "s5_" + name, list(shape), F32)

        AR, AI, LDT = t32("AR"), t32("AI"), t32("LDT")
        k.dma("sp", [(AR[0:64, :], I["s5_a_re"].rearrange("g p -> p g")), (AR[64:128, :], I["s5_a_re"].rearrange("g p -> p g")),
                     (AI[0:64, :], I["s5_a_im"].rearrange("g p -> p g")), (AI[64:128, :], I["s5_a_im"].rearrange("g p -> p g")),
                     (LDT[:], I["s5_log_dt"].partition_broadcast(128))], W=[bs], allow_slow_non_contiguous=True)
        B1 = t32("B1", (128, 32, 16))
        B2 = t32("B2", (128, 32, 16))
        k.dma("sp", [(B1[0:64], I["s5_b_re"].rearrange("g p h -> p g h")), (B1[64:128], I["s5_b_im"].rearrange("g p h -> p g h")),
                     (B2[0:64], I["s5_b_im"].rearrange("g p h -> p g h")), (B2[64:128], I["s5_b_re"].rearrange("g p h -> p g h"))],
              W=[bs], key="d_s5_b")
        CN1 = t32("CN1", (128, 4, 128))
        CN2 = t32("CN2", (128, 4, 128))
        cre = I["s5_c_re"].rearrange("(f a) h p -> (a h) f p", f=4)
        cim = I["s5_c_im"].rearrange("(f a) h p -> (a h) f p", f=4)
        k.dma("sp", [(CN1[:, :, 0:64], cre), (CN1[:, :, 64:128], cim), (CN2[:, :, 0:64], cim), (CN2[:, :, 64:128], cre)],
              W=[bs], key="d_s5_c")
        dT = load_vec_fm(P, es, "s5_dT", I["s5_d"], 4, bs)
        bgl = load_vec_fm(P, es, "s5_bglu", I["s5_b_glu"], 4, bs)
        sgn = t32("sgn", (128, 1))
        dv(lambda e: e.memset(sgn[0:64, :], -1.0))
        dv(lambda e: e.memset(sgn[64:128, :], 1.0))
        DT, XR, TH = t32("DT"), t32("XR"), t32("TH")
        ac(lambda e: e.activation(out=DT[:], in_=LDT[:], func=AF.Exp))
        dv(lambda e: e.tensor_tensor(out=XR[:], in0=AR[:], in1=DT[:], op=ALU.mult))
        dv(lambda e: e.tensor_tensor(out=TH[:], in0=AI[:], in1=DT[:], op=ALU.mult))
        MAG = t32("MAG")
        ac(lambda e: e.activation(out=MAG[:], in_=XR[:], func=AF.Exp))
        tf, r1, r2 = t32("tf"), t32("r1"), t32("r2")
        ti_ = P.sb(es, "s5_ti", [128, 32], I32)
        dv(lambda e: e.tensor_scalar(out=tf[:], in0=TH[:], scalar1=1.0 / TWO_PI, scalar2=None, op0=ALU.mult))
        dv(lambda e: e.tensor_copy(out=ti_[:], in_=tf[:]))
        dv(lambda e: e.tensor_copy(out=tf[:], in_=ti_[:]))
        dv(lambda e: e.scalar_tensor_tensor(out=r1[:], in0=tf[:], scalar=-CW1, in1=TH[:], op0=ALU.mult, op1=ALU.add))
        dv(lambda e: e.scalar_tensor_tensor(out=r1[:], in0=tf[:], scalar=-CW2, in1=r1[:], op0=ALU.mult, op1=ALU.add))
        dv(lambda e: e.tensor_scalar(out=r1[:], in0=r1[:], scalar1=-math.pi, scalar2=math.pi, op0=ALU.max, op1=ALU.min))
        SN, CS = t32("SN"), t32("CS")
        ac(lambda e: e.activation(out=SN[:], in_=r1[:], func=AF.Sin))
        dv(lambda e: e.tensor_scalar(out=tf[:], in0=r1[:], scalar1=math.pi / 2, scalar2=math.pi, op0=ALU.add, op1=ALU.is_gt))
        dv(lambda e: e.scalar_tensor_tensor(out=tf[:], in0=tf[:], scalar=-TWO_PI, in1=r1[:], op0=ALU.mult, op1=ALU.add))
        dv(lambda e: e.tensor_scalar(out=tf[:], in0=tf[:], scalar1=math.pi / 2, scalar2=math.pi, op0=ALU.add, op1=ALU.min))
        ac(lambda e: e.activation(out=CS[:], in_=tf[:], func=AF.Sin))
        pows = list(range(0, 17)) + [16 << i for i in range(1, 8)]
        TR = {n: t32(f"TR{n}") for n in pows}
        TI = {n: t32(f"TI{n}") for n in pows}
        dv(lambda e: e.memset(TR[0][:], 1.0))
        dv(lambda e: e.memset(TI[0][:], 0.0))
        dv(lambda e: e.tensor_tensor(out=TR[1][:], in0=MAG[:], in1=CS[:], op=ALU.mult))
        dv(lambda e: e.tensor_tensor(out=TI[1][:], in0=MAG[:], in1=SN[:], op=ALU.mult))
        ta, tb = t32("ta"), t32("tb")

        def cmul(oR, oI, aR, aI, bR, bI):
            dv(lambda e: e.tensor_tensor(out=ta[:], in0=aI[:], in1=bI[:], op=ALU.mult))
            dv(lambda e: e.tensor_tensor(out=tb[:], in0=aR[:], in1=bI[:], op=ALU.mult))
            dv(lambda e: e.tensor_tensor(out=oR[:], in0=aR[:], in1=bR[:], op=ALU.mult))
            dv(lambda e: e.tensor_tensor(out=oR[:], in0=oR[:], in1=ta[:], op=ALU.subtract))
            dv(lambda e: e.tensor_tensor(out=oI[:], in0=aI[:], in1=bR[:], op=ALU.mult))
            dv(lambda e: e.tensor_tensor(out=oI[:], in0=oI[:], in1=tb[:], op=ALU.add))
        for n in range(2, 17):
            cmul(TR[n], TI[n], TR[n - 1], TI[n - 1], TR[1], TI[1])
        for i in range(1, 8):
            n = 16 << i
            cmul(TR[n], TI[n], TR[n // 2], TI[n // 2], TR[n // 2], TI[n // 2])
        den, xr_, cr_, ci_ = t32("den"), t32("xr"), t32("cr"), t32("ci")
        dv(lambda e: e.tensor_tensor(out=den[:], in0=AR[:], in1=AR[:], op=ALU.mult))
        dv(lambda e: e.tensor_tensor(out=ta[:], in0=AI[:], in1=AI[:], op=ALU.mult))
        dv(lambda e: e.tensor_tensor(out=den[:], in0=den[:], in1=ta[:], op=ALU.add))
        dv(lambda e: e.reciprocal(out=den[:], in_=den[:]))
        dv(lambda e: e.tensor_scalar(out=xr_[:], in0=TR[1][:], scalar1=-1.0, scalar2=None, op0=ALU.add))
        dv(lambda e: e.tensor_tensor(out=cr_[:], in0=xr_[:], in1=AR[:], op=ALU.mult))
        dv(lambda e: e.tensor_tensor(out=ta[:], in0=TI[1][:], in1=AI[:], op=ALU.mult))
        dv(lambda e: e.tensor_tensor(out=cr_[:], in0=cr_[:], in1=ta[:], op=ALU.add))
        dv(lambda e: e.tensor_tensor(out=cr_[:], in0=cr_[:], in1=den[:], op=ALU.mult))
        dv(lambda e: e.tensor_tensor(out=ci_[:], in0=TI[1][:], in1=AR[:], op=ALU.mult))
        dv(lambda e: e.tensor_tensor(out=ta[:], in0=xr_[:], in1=AI[:], op=ALU.mult))
        dv(lambda e: e.tensor_tensor(out=ci_[:], in0=ci_[:], in1=ta[:], op=ALU.subtract))
        dv(lambda e: e.tensor_tensor(out=ci_[:], in0=ci_[:], in1=den[:], op=ALU.mult))
        cis = t32("cis")
        dv(lambda e: e.tensor_scalar(out=cis[:], in0=ci_[:], scalar1=sgn[:, 0:1], scalar2=None, op0=ALU.mult))
        ncis = t32("ncis")
        dv(lambda e: e.tensor_scalar(out=ncis[:], in0=cis[:], scalar1=-1.0, scalar2=None, op0=ALU.mult))

        def bc16(t):
            return t[:, :].unsqueeze(2).broadcast_to([128, 32, 16])
        BB = t32("BB", (128, 32, 16))
        BBs = t32("BBs", (128, 32, 16))
        tmp3 = t32("tmp3", (128, 32, 16))
        dv(lambda e: e.tensor_tensor(out=BB[:], in0=B1[:], in1=bc16(cr_), op=ALU.mult))
        dv(lambda e: e.tensor_tensor(out=tmp3[:], in0=B2[:], in1=bc16(cis), op=ALU.mult))
        dv(lambda e: e.tensor_tensor(out=BB[:], in0=BB[:], in1=tmp3[:], op=ALU.add))
        dv(lambda e: e.tensor_tensor(out=BBs[:], in0=B2[:], in1=bc16(cr_), op=ALU.mult))
        dv(lambda e: e.tensor_tensor(out=tmp3[:], in0=B1[:], in1=bc16(ncis), op=ALU.mult))
        dv(lambda e: e.tensor_tensor(out=BBs[:], in0=BBs[:], in1=tmp3[:], op=ALU.add))
        CT1 = t32("CT1", (128, 32, 16))
        CT2 = t32("CT2", (128, 32, 16))
        for f in range(4):
            for (CN, CT) in ((CN1, CT1), (CN2, CT2)):
                k.op("pe", lambda e, f=f, CN=CN: e.transpose(out=P.ps[0][:, 0:128], in_=CN[:, f, :], identity=P.ident_f[:]),
                     R=[bs, P.b_const], W=[P.psb[0]])
                dv(lambda e, f=f, CT=CT: e.tensor_copy(out=CT[:, 8 * f:8 * f + 8, :].rearrange("p g h -> p (g h)"), in_=P.ps[0][:, 0:128]),
                   R=[P.psb[0]])
        CTn = t32("CTn", (128, 32, 16))
        dv(lambda e: e.tensor_scalar(out=CTn[:], in0=CT1[:], scalar1=sgn[:, 0:1], scalar2=-1.0, op0=ALU.mult, op1=ALU.mult))
        pi_ = P.sb(es, "s5_pi", [128, 1], I32)
        k.op("pool", lambda e: e.iota(out=pi_[:], pattern=[[0, 1]], base=0, channel_multiplier=1), R=[bs], W=[bs])
        pg_i = P.sb(es, "s5_pgi", [128, 1], I32)
        om_i = P.sb(es, "s5_omi", [128, 1], I32)
        dv(lambda e: e.tensor_scalar(out=pg_i[:], in0=pi_[:], scalar1=4, scalar2=None, op0=ALU.arith_shift_right))
        dv(lambda e: e.tensor_scalar(out=om_i[:], in0=pg_i[:], scalar1=1, scalar2=None, op0=ALU.bitwise_and))
        pgf, om, em = t32("pgf", (128, 1)), t32("om", (128, 1)), t32("em", (128, 1))
        dv(lambda e: e.tensor_copy(out=pgf[:], in_=pg_i[:]))
        dv(lambda e: e.tensor_copy(out=om[:], in_=om_i[:]))
        dv(lambda e: e.tensor_scalar(out=em[:], in0=om[:], scalar1=-1.0, scalar2=1.0, op0=ALU.mult, op1=ALU.add))
        ji = P.sb(es, "s5_ji", [128, 128], I32)
        k.op("pool", lambda e: e.iota(out=ji[:], pattern=[[1, 128]], base=0, channel_multiplier=0), R=[bs], W=[bs])
        dv(lambda e: e.tensor_scalar(out=ji[:], in0=ji[:], scalar1=4, scalar2=None, op0=ALU.arith_shift_right))
        bmask = t32("bmask", (128, 128))
        dv(lambda e: e.tensor_copy(out=bmask[:], in_=ji[:]))
        dv(lambda e: e.tensor_scalar(out=bmask[:], in0=bmask[:], scalar1=pgf[:, 0:1], scalar2=None, op0=ALU.is_equal))
        esw = t32("esw", (128, 128))
        dv(lambda e: e.memset(esw[:], 0.0))
        dv(lambda e: e.tensor_copy(out=esw[0:64, 64:128], in_=P.ident_f[0:64, 0:64]), R=[P.b_const])
        dv(lambda e: e.tensor_copy(out=esw[64:128, 0:64], in_=P.ident_f[64:128, 64:128]), R=[P.b_const])

        uT = P.sb(es, "s5_uT", [128, 4, S], BF16)
        buT = k.bufs(NB, "s5_uT")
        for b in range(NB):
            c0 = b * TB
            k.dma("sp", [(uT[:, :, c0:c0 + TB], P.uT_d[:, c0:c0 + TB].rearrange("(c p) t -> p c t", p=128))],
                  R=[P.b_uT[b]], W=[buT[b]])
        Xb = P.sb(es, "s5_Xb", [128, G, NCH], BF16)
        bXb = k.bufs(G, "s5_Xb")
        Kb = P.sb(es, "s5_Kb", [128, 4, L, 128], BF16)
        bKb = k.buf("s5_Kb")

        def mb_table(n, MB):
            tis = t32(f"tis_{n}") if False else ta
            dv(lambda e: e.tensor_scalar(out=ta[:], in0=TI[n][:], scalar1=sgn[:, 0:1], scalar2=None, op0=ALU.mult))
            dv(lambda e: e.tensor_tensor(out=MB[:], in0=BB[:], in1=bc16(TR[n]), op=ALU.mult))
            dv(lambda e: e.tensor_tensor(out=tmp3[:], in0=BBs[:], in1=bc16(ta), op=ALU.mult))
            dv(lambda e: e.tensor_tensor(out=MB[:], in0=MB[:], in1=tmp3[:], op=ALU.add))

        with ExitStack() as es2:
            Pm = P.sb(es2, "s5_Pm", [128, 4, L, 2, 128], BF16)
            bPm = k.buf("s5_Pm")
            MB = P.sb(es2, "s5_MB", [128, 32, 16], F32)
            for n in range(L):
                mb_table(n, MB)
                j = L - 1 - n
                for f in range(4):
                    mbf = MB[:, 8 * f:8 * f + 8, :].rearrange("p g h -> p (g h)")
                    k.op("pe", lambda e, mbf=mbf: e.transpose(out=P.ps[0][:, 0:128], in_=mbf, identity=P.ident_f[:]),
                         R=[bs, P.b_const], W=[P.psb[0]])
                    k.op("dve", lambda e, f=f, j=j: e.tensor_scalar(out=Pm[:, f, j, 0, :], in0=P.ps[0][:, 0:128], scalar1=em[:, 0:1],
                                                                    scalar2=None, op0=ALU.mult), R=[P.psb[0], bs], W=[bPm])
                    k.op("dve", lambda e, f=f, j=j: e.tensor_scalar(out=Pm[:, f, j, 1, :], in0=P.ps[0][:, 0:128], scalar1=om[:, 0:1],
                                                                    scalar2=None, op0=ALU.mult), R=[P.psb[0], bs], W=[bPm])
                    ctf = CTn[:, 8 * f:8 * f + 8, :].rearrange("p g h -> p (g h)")
                    k.op("pe", lambda e, mbf=mbf, ctf=ctf: e.matmul(P.ps[1][:, 0:128], lhsT=mbf, rhs=ctf, start=True, stop=True),
                         R=[bs], W=[P.psb[1]])
                    if n == 0:
                        k.op("dve", lambda e: e.tensor_tensor(out=tmpK[:], in0=P.ps[1][:, 0:128], in1=bmask[:], op=ALU.mult),
                             R=[P.psb[1], bs], W=[bs]) if False else None
                    k.op("dve", lambda e, f=f, n=n: e.tensor_tensor(out=Kb[:, f, n, :], in0=P.ps[1][:, 0:128], in1=bmask[:], op=ALU.mult),
                         R=[P.psb[1], bs], W=[bKb])
                    if n == 0:
                        k.op("dve", lambda e, f=f: e.scalar_tensor_tensor(out=Kb[:, f, 0, :], in0=P.ident_f[:], scalar=dT[:, f:f + 1],
                                                                          in1=Kb[:, f, 0, :], op0=ALU.mult, op1=ALU.add),
                             R=[bKb, bs, P.b_const], W=[bKb])
            NBG = 4
            Xs = [P.sb(es2, f"s5_Xs{i}", [128, NCH], F32) for i in range(NBG)]
            bXs = k.bufs(NBG, "s5_Xs")
            NMT = 8
            MT = [P.sb(es2, f"s5_MT{i}", [128, 128], F32) for i in range(NMT)]
            bMT = k.bufs(NMT, "s5_MT")
            tM = [P.sb(es2, f"s5_tM{i}", [128, 128], F32) for i in range(4)]
            btM = k.bufs(4, "s5_tM")
            w2 = P.sb(es2, "s5_w2", [128, 32, 8], F32)
            for i in range(8):
                n = 16 << i
                dv(lambda e, i=i, n=n: e.tensor_scalar(out=w2[:, :, i], in0=TI[n][:], scalar1=sgn[:, 0:1], scalar2=-1.0,
                                                       op0=ALU.mult, op1=ALU.mult))
            nmt = 0
            for g0 in range(0, G, NBG):
                gs = list(range(g0, g0 + NBG))
                for g in gs:
                    f, gq = g // 8, (g % 8) // 2
                    m = g % 2
                    vb = g % NBG
                    for j in range(L):
                        k.op("pe", lambda e, f=f, gq=gq, m=m, j=j, vb=vb: e.matmul(
                            P.ps[vb][:, 0:NCH], lhsT=Pm[32 * gq:32 * gq + 32, f, j, m, :],
                            rhs=uT[32 * gq:32 * gq + 32, f, :].rearrange("p (c l) -> p l c", l=L)[:, j, :],
                            start=(j == 0), stop=(j == L - 1), tile_position=(32 * gq, 0)),
                            R=[bPm] + buT, W=[P.psb[vb]])
                for g in gs:
                    vb = g % NBG
                    X = Xs[g % NBG]
                    k.op("act", lambda e, X=X, vb=vb: e.copy(out=X[:], in_=P.ps[vb][:, 0:NCH]), R=[P.psb[vb]], W=[bXs[g % NBG]])
                for i in range(8):
                    sh = 1 << i
                    n = 16 << i
                    for g in gs:
                        X = Xs[g % NBG]
                        mt = MT[nmt % NMT]
                        bmt = bMT[nmt % NMT]
                        tm_ = tM[nmt % 4]
                        btm = btM[nmt % 4]
                        nmt += 1
                        k.op("act", lambda e, mt=mt, g=g, n=n: e.activation(out=mt[:], in_=P.ident_f[:], func=AF.Copy, scale=TR[n][:, g:g + 1]),
                             R=[bs, P.b_const], W=[bmt])
                        k.op("dve", lambda e, mt=mt, g=g, i=i: e.scalar_tensor_tensor(out=mt[:], in0=esw[:], scalar=w2[:, g, i:i + 1], in1=mt[:],
                                                                                    op0=ALU.mult, op1=ALU.add), R=[bs, bmt], W=[bmt])
                        pb = 4 + g % NBG
                        k.op("pe", lambda e, mt=mt, X=X, sh=sh, pb=pb: e.matmul(P.ps[pb][:, sh:NCH], lhsT=mt[:], rhs=X[:, 0:NCH - sh],
                                                                                start=True, stop=True), R=[bmt, bXs[g % NBG]], W=[P.psb[pb]])
                        k.op("dve", lambda e, X=X, sh=sh, pb=pb: e.tensor_tensor(out=X[:, sh:NCH], in0=X[:, sh:NCH], in1=P.ps[pb][:, sh:NCH],
                                                                                 op=ALU.add), R=[P.psb[pb], bXs[g % NBG]], W=[bXs[g % NBG]])
                for g in gs:
                    X = Xs[g % NBG]
                    k.op("pool", lambda e, g=g: e.memset(Xb[:, g, 0:1], 0.0), W=[bXb[g]])
                    k.op("act", lambda e, g=g, X=X: e.copy(out=Xb[:, g, 1:NCH], in_=X[:, 0:NCH - 1]), R=[bXs[g % NBG]], W=[bXb[g]])
            k.barrier()
        with ExitStack() as es3:
            Qm = P.sb(es3, "s5_Qm", [128, L, 16, 2, 32], BF16)
            bQm = k.buf("s5_Qm")
            Qt = P.sb(es3, "s5_Qt", [128, 32, 16], F32)
            trs = P.sb(es3, "s5_trs", [128, 32], F32)
            nti = P.sb(es3, "s5_nti", [128, 32], F32)
            mpat = P.sb(es3, "s5_mpat", [128, 2, 2, 16], F32)
            dv(lambda e: e.memset(mpat[:], 0.0))
            dv(lambda e: e.memset(mpat[:, 0, 0, :], 1.0))
            dv(lambda e: e.memset(mpat[:, 1, 1, :], 1.0))
            for tau in range(L):
                n = tau + 1
                dv(lambda e, n=n: e.tensor_scalar(out=trs[:], in0=TR[n][:], scalar1=sgn[:, 0:1], scalar2=-1.0, op0=ALU.mult, op1=ALU.mult))
                dv(lambda e, n=n: e.tensor_scalar(out=nti[:], in0=TI[n][:], scalar1=-1.0, scalar2=None, op0=ALU.mult))
                dv(lambda e: e.tensor_tensor(out=Qt[:], in0=CT1[:], in1=bc16(trs), op=ALU.mult))
                dv(lambda e: e.tensor_tensor(out=tmp3[:], in0=CT2[:], in1=bc16(nti), op=ALU.mult))
                dv(lambda e: e.tensor_tensor(out=Qt[:], in0=Qt[:], in1=tmp3[:], op=ALU.add))
                for mem in range(2):
                    k.op("dve", lambda e, tau=tau, mem=mem: e.tensor_tensor(
                        out=Qm[:, tau, :, mem, :], in0=Qt[:, :, :].rearrange("p (q m) h -> p q (m h)", m=2),
                        in1=mpat[:, mem, :, :].rearrange("p m h -> p (m h)").unsqueeze(1).broadcast_to([128, 16, 32]), op=ALU.mult),
                        R=[bs], W=[bQm])
            zT = P.sb(es3, "s5_zT", [128, 4, S], BF16)
            bz = k.bufs(4, "s5_zT")
            ys = [P.sb(es3, f"s5_ys{i}", [128, NCH], F32) for i in range(2)]
            bys = k.bufs(2, "s5_ys")
            yv = [P.sb(es3, f"s5_yv{i}", [128, NCH], F32) for i in range(2)]
            byv = k.bufs(2, "s5_yv")
            y2 = [P.sb(es3, f"s5_y2{i}", [128, NCH], F32) for i in range(2)]
            by2 = k.bufs(2, "s5_y2")
            cnt = 0
            for f in range(4):
                ul = uT[:, f, :].rearrange("p (c l) -> p l c", l=L)
                for tau in range(L):
                    i2 = cnt % 2
                    pb = cnt % 4
                    cnt += 1
                    for q in range(4):
                        for mem in range(2):
                            g = f * 8 + q * 2 + mem
                            k.op("pe", lambda e, q=q, mem=mem, g=g, tau=tau, pb=pb, f=f: e.matmul(
                                P.ps[pb][32 * q:32 * q + 32, 256:512], lhsT=Qm[:, tau, f * 4 + q, mem, :], rhs=Xb[:, g, :],
                                start=(mem == 0), stop=(mem == 1), tile_position=(0, 32 * q)),
                                R=[bQm, bXb[g]], W=[P.psb[pb]])
                    k.op("act", lambda e, pb=pb, i2=i2: e.copy(out=ys[i2][:], in_=P.ps[pb][:, 256:512]), R=[P.psb[pb]], W=[bys[i2]])
                    for j in range(tau + 1):
                        k.op("pe", lambda e, f=f, tau=tau, j=j, pb=pb, ul=ul: e.matmul(
                            P.ps[pb][:, 0:256], lhsT=Kb[:, f, tau - j, :], rhs=ul[:, j, :], start=(j == 0), stop=(j == tau)),
                            R=[bKb] + buT, W=[P.psb[pb]])
                    Y = yv[i2]
                    k.op("dve", lambda e, pb=pb, i2=i2, Y=Y: e.tensor_tensor(out=Y[:], in0=ys[i2][:], in1=P.ps[pb][:, 0:256], op=ALU.add),
                         R=[bys[i2], P.psb[pb]], W=[byv[i2]])
                    Y2 = y2[i2]
                    k.op("pool", lambda e, Y=Y, Y2=Y2: e.tensor_tensor(out=Y2[:], in0=Y[:], in1=Y[:], op=ALU.mult), R=[byv[i2]], W=[by2[i2]])
                    k.op("pool", lambda e, Y2=Y2: e.tensor_scalar(out=Y2[:], in0=Y2[:], scalar1=0.044715, scalar2=1.0, op0=ALU.mult, op1=ALU.add),
                         R=[by2[i2]], W=[by2[i2]])
                    k.op("dve", lambda e, Y=Y, Y2=Y2: e.tensor_tensor(out=Y2[:], in0=Y2[:], in1=Y[:], op=ALU.mult), R=[by2[i2], byv[i2]], W=[by2[i2]])
                    k.op("act", lambda e, Y2=Y2: e.activation(out=Y2[:], in_=Y2[:], func=AF.Sigmoid, scale=1.5957691216057308),
                         R=[by2[i2]], W=[by2[i2]])
                    k.op("dve", lambda e, Y=Y, Y2=Y2, f=f, tau=tau: e.tensor_tensor(
                        out=zT[:, f, :].rearrange("p (c l) -> p l c", l=L)[:, tau, :], in0=Y[:], in1=Y2[:], op=ALU.mult),
                        R=[by2[i2], byv[i2]], W=[bz[f]])
            wgl = P.sb(es3, "s5_wgl", [128, 4, 512], BF16)
            bwgl = k.buf("s5_wgl")
            k.dma("sp", [(wgl[:], W["s5_w_glu"][0].rearrange("(c p) n -> p c n", p=128))], R=[W["s5_w_glu"][1]], W=[bwgl])
            sg = [P.sb(es3, f"s5_sg{i}", [128, TB], F32) for i in range(2)]
            bsg = k.bufs(2, "s5_sg")
            ybT = [P.sb(es3, f"s5_ybT{i}", [128, 4, TB], BF16) for i in range(2)]
            bybT = k.bufs(2, "s5_ybT")
            cnt = 0
            for blk in range(NB):
                c0 = blk * TB
                for fo in range(4):
                    pb = 4 + cnt % 2
                    i2 = cnt % 2
                    cnt += 1
                    for fi in range(4):
                        k.op("pe", lambda e, fo=fo, fi=fi, pb=pb, c0=c0: e.matmul(
                            P.ps[pb][:], lhsT=wgl[:, fi, fo * 128:(fo + 1) * 128], rhs=zT[:, fi, c0:c0 + TB],
                            start=(fi == 0), stop=(fi == 3)), R=[bwgl] + bz, W=[P.psb[pb]])
                    k.op("act", lambda e, fo=fo, pb=pb, i2=i2: e.activation(out=sg[i2][:], in_=P.ps[pb][:], func=AF.Sigmoid,
                                                                           bias=bgl[:, fo:fo + 1], scale=1.0), R=[P.psb[pb], bs], W=[bsg[i2]])
                    k.op("dve", lambda e, fo=fo, i2=i2, blk=blk, c0=c0: e.tensor_tensor(
                        out=ybT[blk % 2][:, fo, :], in0=sg[i2][:], in1=zT[:, fo, c0:c0 + TB], op=ALU.mult),
                        R=[bsg[i2]] + bz, W=[bybT[blk % 2]])
                k.dma("sp", [(P.ycatT[512:1024, c0:c0 + TB].rearrange("(c p) t -> p c t", p=128), ybT[blk % 2][:])],
                      R=[bybT[blk % 2]], W=[P.b_yb[blk]], key=f"d_ybT{blk % 2}")
            k.barrier()


def build(stages=("all",), debug=()):
    P = Prog(debug)
    k = P.k
    I = {}
    P.inp = I
    I["x"] = P.din("x", [S, D])
    I["positions"] = P.din("positions", [S], I32)
    for nm, shp in [("norm_mix_g", [2, D]), ("norm_ffn_g", [2, D]), ("final_norm_g", [D]),
                    ("even_w_in", [D, 2560]), ("hgrn_lb_logits", [2, 512]), ("hgrn_norm_g", [512]),
                    ("s5_a_re", [32, 64]), ("s5_a_im", [32, 64]), ("s5_log_dt", [32]),
                    ("s5_b_re", [32, 64, 16]), ("s5_b_im", [32, 64, 16]),
                    ("s5_c_re", [32, 16, 64]), ("s5_c_im", [32, 16, 64]),
                    ("s5_d", [512]), ("s5_w_glu", [512, 512]), ("s5_b_glu", [512]),
                    ("even_w_out", [D, D]), ("odd_w_in", [D, 704]), ("mla_q_norm_g", [384]),
                    ("mla_w_uq", [384, 1536]), ("mla_kv_norm_g", [256]), ("mla_w_ukv", [256, 2048]),
                    ("odd_w_out", [D, D]), ("ffn_w_in", [2, D, 2 * DFF]), ("ffn_conv_w", [2, 3, DFF]),
                    ("ffn_conv_b", [2, DFF]), ("ffn_w_out", [2, DFF, D])]:
        I[nm] = P.din(nm, shp)
    I["rope_freq"] = P.din("rope_freq", [32])
    out = P.dout("out", [S, D])
    P.b_out = k.bufs(32, "out")
    P.b_x = k.bufs(32, "x")
    setup_common(P)
    if "ffn1" in stages:
        w_in_bf, b1 = cast_weight(P, "w_ffn_in1", I["ffn_w_in"][1], D, 2 * DFF)
        w_out_bf, b2 = cast_weight(P, "w_ffn_out1", I["ffn_w_out"][1], DFF, D)
        ffn_phase(P, 1, I["x"], P.b_x, out, P.b_out, w_in_bf, b1, w_out_bf, b2, final_norm=I["final_norm_g"])
    W = {}
    if "hgrn" in stages:
        W["even_w_in"] = cast_weight(P, "bf_even_w_in", I["even_w_in"], D, 2560)
        hgrn_phase(P, I["x"], P.b_x, W)
    if "s5" in stages:
        W["s5_w_glu"] = cast_weight(P, "bf_s5_w_glu", I["s5_w_glu"], 512, 512)
        s5_phase(P, W)
    if "mla" in stages:
        for nm, r, c in [("odd_w_in", D, 704), ("mla_w_uq", 384, 1536), ("mla_w_ukv", 256, 2048), ("odd_w_out", D, D)]:
            W[nm] = cast_weight(P, "bf_" + nm, I[nm], r, c)
        mla_proj_phase(P, I["x"], P.b_x, W)
        if "mla_pd_only" not in stages:
            mla_attn_phase(P, I["x"], P.b_x, out, P.b_out, W)
    if "all" in stages:
        W["even_w_in"] = cast_weight(P, "bf_even_w_in", I["even_w_in"], D, 2560)
        W["s5_w_glu"] = cast_weight(P, "bf_s5_w_glu", I["s5_w_glu"], 512, 512)
        W["even_w_out"] = cast_weight(P, "bf_even_w_out", I["even_w_out"], D, D)
        W["ffn_in0"] = cast_weight(P, "bf_ffn_in0", I["ffn_w_in"][0], D, 2 * DFF)
        W["ffn_out0"] = cast_weight(P, "bf_ffn_out0", I["ffn_w_out"][0], DFF, D)
        for nm, r, c in [("odd_w_in", D, 704), ("mla_w_uq", 384, 1536), ("mla_w_ukv", 256, 2048), ("odd_w_out", D, D)]:
            W[nm] = cast_weight(P, "bf_" + nm, I[nm], r, c)
        W["ffn_in1"] = cast_weight(P, "bf_ffn_in1", I["ffn_w_in"][1], D, 2 * DFF)
        W["ffn_out1"] = cast_weight(P, "bf_ffn_out1", I["ffn_w_out"][1], DFF, D)
        h2 = P.dint("h2", [S, D], F32)
        h3 = P.dint("h3", [S, D], F32)
        b_h2 = k.bufs(32, "h2")
        b_h3 = k.bufs(32, "h3")
        hgrn_phase(P, I["x"], P.b_x, W)
        s5_phase(P, W)
        ffn_phase(P, 0, I["x"], P.b_x, h2, b_h2, W["ffn_in0"][0], W["ffn_in0"][1], W["ffn_out0"][0], W["ffn_out0"][1],
                  premix=(P.ycatT, P.b_ya, P.b_yb, W["even_w_out"][0], W["even_w_out"][1]))
        mla_proj_phase(P, h2, b_h2, W)
        mla_attn_phase(P, h2, b_h2, h3, b_h3, W)
        ffn_phase(P, 1, h3, b_h3, out, P.b_out, W["ffn_in1"][0], W["ffn_in1"][1], W["ffn_out1"][0], W["ffn_out1"][1],
                  final_norm=I["final_norm_g"])
    k.barrier()
    k.finalize(P.es)
    P.es.close()
    return P


INPUT_ORDER = ["x", "positions", "norm_mix_g", "norm_ffn_g", "final_norm_g", "even_w_in", "hgrn_lb_logits",
               "hgrn_norm_g", "s5_a_re", "s5_a_im", "s5_log_dt", "s5_b_re", "s5_b_im", "s5_c_re", "s5_c_im",
               "s5_d", "s5_w_glu", "s5_b_glu", "even_w_out", "odd_w_in", "mla_q_norm_g", "mla_w_uq",
               "mla_kv_norm_g", "mla_w_ukv", "odd_w_out", "ffn_w_in", "ffn_conv_w", "ffn_conv_b", "ffn_w_out"]


def make_in_maps(inputs):
    shared = {}
    for nm in INPUT_ORDER:
        if nm in ("x", "positions"):
            continue
        a = np.ascontiguousarray(np.asarray(inputs[nm]))
        if nm in ("norm_mix_g", "norm_ffn_g", "hgrn_lb_logits", "ffn_w_in", "ffn_conv_w", "ffn_conv_b",
                  "ffn_w_out", "final_norm_g"):
            shared[nm] = a
        else:
            shared[nm] = np.ascontiguousarray(a[0])
    shared["rope_freq"] = (10000.0 ** (-np.arange(0, 64, 2, dtype=np.float32) / np.float32(64))).astype(np.float32)
    x = np.asarray(inputs["x"])
    pos = np.asarray(inputs["positions"])
    maps = []
    for c in range(8):
        m = dict(shared)
        m["x"] = np.ascontiguousarray(x[c])
        m["positions"] = np.ascontiguousarray(pos[c]).astype(np.int32)
        maps.append(m)
    return maps


def run(inputs, stages=("all",), debug=()):
    P = build(stages, debug)
    maps = make_in_maps(inputs)
    res = run_bass_kernel_spmd(P.nc, maps, core_ids=list(range(8)))
    return res


def kernel(**inputs):
    res = run(inputs)
    out = np.stack([np.asarray(r["out"]) for r in res.results], axis=0)
    return out.astype(np.float32)
```

```python
import math
from contextlib import ExitStack
import numpy as np
import concourse.bass as bass
import concourse.mybir as mybir
from concourse.bass_utils import run_bass_kernel_spmd

F32 = mybir.dt.float32
BF16 = mybir.dt.bfloat16
I32 = mybir.dt.int32
AF = mybir.ActivationFunctionType
ALU = mybir.AluOpType
AX = mybir.AxisListType

S = 4096
D = 1024
NB = 8
TB = 512
DFF = 2816
NFC = DFF // 128
EPS = 1e-6

ENGS = ("pe", "act", "dve", "pool", "sp")


class Buf:
    __slots__ = ("name", "w", "r")

    def __init__(self, name):
        self.name = name
        self.w = None
        self.r = []


class Op:
    __slots__ = ("eng", "fn", "deps", "key", "pos", "ndma", "waits", "flag")

    def __init__(self, eng, fn, deps, key, ndma):
        self.eng = eng
        self.fn = fn
        self.deps = deps
        self.key = key
        self.pos = -1
        self.ndma = ndma
        self.waits = []
        self.flag = False


class KB:
    def __init__(self, nc):
        self.nc = nc
        self.ops = []
        self.eng_obj = {"pe": nc.tensor, "act": nc.scalar, "dve": nc.vector,
                        "pool": nc.gpsimd, "sp": nc.sync}
        self.last = {}
        self.dma_out = []
        self.nbuf = 0

    def buf(self, name=None):
        self.nbuf += 1
        return Buf(name or f"b{self.nbuf}")

    def bufs(self, n, name="b"):
        return [self.buf(f"{name}{i}") for i in range(n)]

    def _deps(self, idx, R, W):
        deps = set()
        for b in R:
            if b.w is not None:
                deps.add(b.w)
        for b in W:
            if b.w is not None:
                deps.add(b.w)
            deps.update(b.r)
        for b in W:
            b.w = idx
            b.r = []
        for b in R:
            b.r.append(idx)
        deps.discard(idx)
        return deps

    def op(self, eng, fn, R=(), W=()):
        idx = len(self.ops)
        deps = self._deps(idx, R, W)
        self.ops.append(Op(eng, fn, deps, eng, 0))
        self.last[eng] = idx
        return idx

    def dma(self, eng, pairs, R=(), W=(), key=None, **kw):
        idx = len(self.ops)
        deps = self._deps(idx, R, W)
        if key is None:
            key = "d_" + (W[0].name if W else R[0].name)

        def fn(e, pairs=pairs, kw=kw):
            return [e.dma_start(out=o, in_=i, **kw) for (o, i) in pairs]
        self.ops.append(Op(eng, fn, deps, key, len(pairs)))
        self.dma_out.append(idx)
        return idx

    def barrier(self):
        targets = set(self.last.values()) | set(self.dma_out)
        for e in ENGS:
            idx = len(self.ops)
            self.ops.append(Op(e, None, set(targets), e, 0))
        self.dma_out = []

    def finalize(self, es):
        nc = self.nc
        ops = self.ops
        cnt = {}
        for o in ops:
            if o.ndma:
                cnt[o.key] = cnt.get(o.key, 0) + o.ndma
                o.pos = cnt[o.key]
            elif o.fn is not None:
                cnt[o.key] = cnt.get(o.key, 0) + 1
                o.pos = cnt[o.key]
            else:
                o.pos = cnt.get(o.key, 0)
        seen = {e: {} for e in ENGS}
        flagged = {}
        for o in ops:
            need = {}
            for d in o.deps:
                dop = ops[d]
                if dop.fn is None:
                    continue
                k = dop.key
                if (not dop.ndma) and k == o.eng and k in ("pe", "sp"):
                    continue
                if dop.pos > seen[o.eng].get(k, 0):
                    if dop.pos > need.get(k, (0, None))[0]:
                        need[k] = (dop.pos, dop)
            for k, (p, dop) in need.items():
                seen[o.eng][k] = p
                o.waits.append((k, dop))
                dop.flag = True
        val = {}
        rank = {}
        for i, o in enumerate(ops):
            if o.ndma:
                val[i] = 16 * o.pos
            elif o.fn is not None and o.flag:
                rank[o.key] = rank.get(o.key, 0) + 1
                val[i] = rank[o.key]
        opidx = {id(o): i for i, o in enumerate(ops)}
        keys = set()
        for o in ops:
            if o.ndma or o.flag:
                keys.add(o.key)
        sems = {}
        for kname in sorted(keys):
            sems[kname] = es.enter_context(nc.semaphore("s_" + kname))
        self.nsem = len(sems)
        for o in ops:
            e = self.eng_obj[o.eng]
            for (k, dop) in o.waits:
                e.wait_ge(sems[k], val[opidx[id(dop)]])
            if o.fn is None:
                continue
            ins = o.fn(e)
            if o.ndma:
                for i_ in ins:
                    i_.then_inc(sems[o.key], 16)
            elif o.flag:
                ins.then_inc(sems[o.key], 1)


class Prog:
    def __init__(self, debug=()):
        self.debug = set(debug)
        self.nc = bass.Bass("TRN2", target_bir_lowering=False)
        self.k = KB(self.nc)
        self.es = ExitStack()
        self.dram = {}

    def din(self, name, shape, dtype=F32):
        t = self.nc.dram_tensor(name, list(shape), dtype, kind="ExternalInput").ap()
        self.dram[name] = t
        return t

    def dout(self, name, shape, dtype=F32):
        t = self.nc.dram_tensor(name, list(shape), dtype, kind="ExternalOutput").ap()
        self.dram[name] = t
        return t

    def dint(self, name, shape, dtype=F32):
        kind = "ExternalOutput" if name in self.debug else "Internal"
        t = self.nc.dram_tensor(name, list(shape), dtype, kind=kind).ap()
        self.dram[name] = t
        return t

    def sb(self, es, name, shape, dtype):
        self.nsb = getattr(self, "nsb", 0) + 1
        return es.enter_context(self.nc.sbuf_tensor(f"{name}_{self.nsb}", list(shape), dtype))


def setup_common(P):
    nc, k = P.nc, P.k
    P.ps = [P.es.enter_context(nc.psum_tensor(f"ps{i}", [128, 512], F32)) for i in range(8)]
    P.psb = k.bufs(8, "psb")
    P.ident_f = P.sb(P.es, "ident_f", [128, 128], F32)
    P.ident = P.sb(P.es, "ident", [128, 128], BF16)
    P.epsb = P.sb(P.es, "epsb", [128, 1], F32)
    P.b_const = k.buf("const")
    ones = P.sb(P.es, "ones_f", [128, 128], F32)
    P.ones_f = ones
    k.op("pool", lambda e: e.memset(ones[:], 1.0), W=[P.b_const])
    k.op("pool", lambda e: e.affine_select(out=P.ident_f[:], in_=ones[:], pattern=[[-1, 128]],
                                           compare_op=ALU.is_equal, fill=0.0, base=0,
                                           channel_multiplier=1), R=[P.b_const], W=[P.b_const])
    k.op("pool", lambda e: e.tensor_copy(out=P.ident[:], in_=P.ident_f[:]), R=[P.b_const], W=[P.b_const])
    k.op("pool", lambda e: e.memset(P.epsb[:], EPS), W=[P.b_const])


def cast_weight(P, name, src2d, rows, cols):
    k = P.k
    dst = P.dint(name, [rows, cols], BF16)
    b = k.buf(name)
    bb = cols if cols <= 2048 else 512
    per_row = cols // bb
    rstep = max(1, 4096 // per_row)
    pairs = []
    for r0 in range(0, rows, rstep):
        r1 = min(rows, r0 + rstep)
        pairs.append((dst[r0:r1, :].rearrange("r (a b) -> r a b", b=bb),
                      src2d[r0:r1, :].rearrange("r (a b) -> r a b", b=bb)))
    k.dma("pool", pairs, W=[b], key="wc_" + name)
    return dst, b


def load_vec_fm(P, es, name, src1d, nchunk, b, eng="sp"):
    t = P.sb(es, name, [128, nchunk], F32)
    P.k.dma(eng, [(t[:], src1d.rearrange("(c p) -> p c", p=128))], W=[b],
            allow_slow_non_contiguous=True)
    return t


def norm_block(P, T, blk, src, b_src, gT):
    k, nc = P.k, P.nc
    xres, bx = T["xres"], T["b_xres"]
    for j in range(4):
        r0 = blk * TB + j * 128
        k.dma("sp", [(xres[:, j, :], src[r0:r0 + 128, :])], R=[b_src[blk * 4 + j]], W=[bx[j]], key=f"d_xres{j}")
    rms_transpose(P, T, gT)


def rms_stats(P, T):
    k = P.k
    xres, bx = T["xres"], T["b_xres"]
    junk, bj = T["junk"], T["b_junk"]
    ss, bss = T["ss"], T["b_ss"]
    xn, bxn = T["xn"], T["b_xn"]
    for j in range(4):
        k.op("act", lambda e, j=j: e.activation(out=junk[:], in_=xres[:, j, :], func=AF.Square,
                                                accum_out=ss[:, j:j + 1]),
             R=[bx[j]], W=[bj, bss[j]])
        k.op("act", lambda e, j=j: e.activation(out=ss[:, 4 + j:5 + j], in_=ss[:, j:j + 1], func=AF.Sqrt,
                                                bias=P.epsb[:], scale=1.0 / D),
             R=[bss[j], P.b_const], W=[bss[j]])
        k.op("dve", lambda e, j=j: e.reciprocal(out=ss[:, 8 + j:9 + j], in_=ss[:, 4 + j:5 + j]),
             R=[bss[j]], W=[bss[j]])
        k.op("dve", lambda e, j=j: e.tensor_scalar(out=xn[:, j, :], in0=xres[:, j, :], scalar1=ss[:, 8 + j:9 + j],
                                                   scalar2=None, op0=ALU.mult),
             R=[bx[j], bss[j]], W=[bxn[j]])


def rms_tr(P, T, gT):
    k = P.k
    hnT, bh = T["hnT"], T["b_hnT"]
    xn, bxn = T["xn"], T["b_xn"]
    for j in range(4):
        pb = T["ps_tr"][j % len(T["ps_tr"])]
        psT = P.ps[pb][:].bitcast(BF16)
        for c in range(8):
            k.op("pe", lambda e, j=j, c=c, psT=psT: e.transpose(out=psT[:, c * 128:(c + 1) * 128],
                                                               in_=xn[:, j, c * 128:(c + 1) * 128],
                                                               identity=P.ident[:]),
                 R=[bxn[j], P.b_const], W=[P.psb[pb]])
        k.op("dve", lambda e, j=j, psT=psT: e.tensor_tensor(
            out=hnT[:, :, j * 128:(j + 1) * 128],
            in0=psT.rearrange("p (c t) -> p c t", c=8),
            in1=gT[:, :].unsqueeze(2).broadcast_to([128, 8, 128]), op=ALU.mult),
            R=[P.psb[pb], T["b_g"]], W=[bh])


def rms_transpose(P, T, gT):
    rms_stats(P, T)
    rms_tr(P, T, gT)


def alloc_blockbufs(P, es, pfx, ps_tr):
    k = P.k
    T = {}
    T["xres"] = P.sb(es, pfx + "xres", [128, 4, 1024], F32)
    T["b_xres"] = k.bufs(4, pfx + "xres")
    T["hnT"] = P.sb(es, pfx + "hnT", [128, 8, 512], BF16)
    T["b_hnT"] = k.buf(pfx + "hnT")
    T["junk"] = P.sb(es, pfx + "junk", [128, 1024], BF16)
    T["b_junk"] = k.buf(pfx + "junk")
    T["ss"] = P.sb(es, pfx + "ss", [128, 12], F32)
    T["b_ss"] = k.bufs(4, pfx + "ss")
    T["xn"] = P.sb(es, pfx + "xn", [128, 4, 1024], BF16)
    T["b_xn"] = k.bufs(4, pfx + "xn")
    T["ps_tr"] = ps_tr
    return T


def ffn_phase(P, layer, src, b_src, dst, b_dst, w_in_bf, b_win, w_out_bf, b_wout, final_norm=None, premix=None):
    k, nc = P.k, P.nc
    I = P.inp
    with ExitStack() as es:
        bc = k.buf("ffn_consts")
        gT = load_vec_fm(P, es, "ffn_g", I["norm_ffn_g"][layer], 8, bc)
        cb = load_vec_fm(P, es, "ffn_cb", I["ffn_conv_b"][layer], NFC, bc)
        cw = P.sb(es, "ffn_cw", [128, 3, NFC], F32)
        for t in range(3):
            k.dma("sp", [(cw[:, t, :], I["ffn_conv_w"][layer, t].rearrange("(c p) -> p c", p=128))], W=[bc],
                  allow_slow_non_contiguous=True)
        TT = [alloc_blockbufs(P, es, f"f{i}_", ps_tr=[6, 7]) for i in range(2)]
        for T_ in TT:
            T_["b_g"] = bc
        wout = P.sb(es, "ffn_wout", [128, NFC, 1024], BF16)
        bwo = k.buf("ffn_wout")
        k.dma("sp", [(wout[:, c0:c0 + 11, :],
                      w_out_bf[c0 * 128:(c0 + 11) * 128, :].rearrange("(c p) n -> p c n", p=128))
                     for c0 in (0, 11)], R=[b_wout], W=[bwo])
        NSL = 2
        wsl = [P.sb(es, f"ffn_wsl{i}", [128, 8, 2, 256], BF16) for i in range(NSL)]
        bws = k.bufs(NSL, "ffn_wsl")
        a_sb = [P.sb(es, f"ffn_a{i}", [128, 516], F32) for i in range(2)]
        ba = k.bufs(2, "ffn_a")
        c_sb = [P.sb(es, f"ffn_c{i}", [128, 512], F32) for i in range(2)]
        bcs = k.bufs(2, "ffn_c")
        s_sb = [P.sb(es, f"ffn_s{i}", [128, 512], F32) for i in range(2)]
        bss = k.bufs(2, "ffn_s")
        halo = P.sb(es, "ffn_halo", [128, NFC, 2], F32)
        bhalo = k.bufs(NFC, "ffn_halo")
        gTt = P.sb(es, "ffn_gT", [128, NFC, 512], BF16)
        bg = k.bufs(NFC, "ffn_gT")
        k.op("pool", lambda e: e.memset(halo[:], 0.0), W=bhalo)
        if final_norm is not None:
            fg = P.sb(es, "fin_g", [128, 1024], F32)
            bfg = k.buf("fin_g")
            k.dma("sp", [(fg[:], final_norm.partition_broadcast(128))], W=[bfg])
            ostage = [P.sb(es, f"fin_o{i}", [128, 1024], F32) for i in range(2)]
            bos = k.bufs(2, "fin_o")
        if premix is not None:
            ycatT, b_ya, b_yb, wo_bf, b_wo = premix
            wo = P.sb(es, "pm_wo", [128, 8, 1024], BF16)
            bwo2 = k.buf("pm_wo")
            k.dma("sp", [(wo[:], wo_bf.rearrange("(c p) n -> p c n", p=128))], R=[b_wo], W=[bwo2])
            yTs = [P.sb(es, f"pm_yT{i}", [128, 8, 512], BF16) for i in range(2)]
            byTs = k.bufs(2, "pm_yT")
        nsl = 0
        cnt = 0
        def stage_a(blk):
            T = TT[blk % 2]
            xres, bx = T["xres"], T["b_xres"]
            for j in range(4):
                r0 = blk * TB + j * 128
                k.dma("sp", [(xres[:, j, :], src[r0:r0 + 128, :])], R=[b_src[blk * 4 + j]], W=[bx[j]], key=f"d_xres{blk % 2}_{j}")
            if premix is not None:
                yT, byT = yTs[blk % 2], byTs[blk % 2]
                c0 = blk * TB
                k.dma("sp", [(yT[:], ycatT[:, c0:c0 + TB].rearrange("(c p) t -> p c t", p=128))], R=[b_ya[blk], b_yb[blk]], W=[byT])
                for j in range(4):
                    for n in range(2):
                        pb = 4 + (j * 2 + n) % 2
                        for kc in range(8):
                            k.op("pe", lambda e, kc=kc, j=j, n=n, pb=pb, yT=yT: e.matmul(
                                P.ps[pb][:], lhsT=yT[:, kc, j * 128:(j + 1) * 128], rhs=wo[:, kc, n * 512:(n + 1) * 512],
                                start=(kc == 0), stop=(kc == 7)), R=[byT, bwo2], W=[P.psb[pb]])
                        k.op("dve", lambda e, j=j, n=n, pb=pb, xres=xres: e.tensor_tensor(
                            out=xres[:, j, n * 512:(n + 1) * 512], in0=xres[:, j, n * 512:(n + 1) * 512],
                            in1=P.ps[pb][:], op=ALU.add), R=[bx[j], P.psb[pb]], W=[bx[j]])
            rms_stats(P, T)

        stage_a(0)
        rms_tr(P, TT[0], gT)
        def do_block(blk, T):
            nonlocal nsl, cnt
            xres, bx = T["xres"], T["b_xres"]
            hnT, bh = T["hnT"], T["b_hnT"]
            for pr in range(NFC // 2):
                if pr == 4 and blk + 1 < NB:
                    stage_a(blk + 1)
                sl = nsl % NSL
                nsl += 1
                pairs = []
                for h_ in range(2):
                    c0 = h_ * DFF + pr * 256
                    pairs.append((wsl[sl][:, :, h_, :],
                                  w_in_bf[:, c0:c0 + 256].rearrange("(kc p) n -> p kc n", p=128)))
                k.dma("sp", pairs, R=[b_win], W=[bws[sl]])
                for q in range(2):
                    oc = pr * 2 + q
                    pa, pu = (0, 1) if (cnt % 2 == 0) else (2, 3)
                    i2 = cnt % 2
                    cnt += 1
                    for kc in range(8):
                        k.op("pe", lambda e, kc=kc, sl=sl, q=q, pa=pa: e.matmul(
                            P.ps[pa][:], lhsT=wsl[sl][:, kc, 0, q * 128:(q + 1) * 128], rhs=hnT[:, kc, :],
                            start=(kc == 0), stop=(kc == 7)), R=[bws[sl], bh], W=[P.psb[pa]])
                    for kc in range(8):
                        k.op("pe", lambda e, kc=kc, sl=sl, q=q, pu=pu: e.matmul(
                            P.ps[pu][:], lhsT=wsl[sl][:, kc, 1, q * 128:(q + 1) * 128], rhs=hnT[:, kc, :],
                            start=(kc == 0), stop=(kc == 7)), R=[bws[sl], bh], W=[P.psb[pu]])
                    A = a_sb[i2]
                    k.op("pool", lambda e, A=A, oc=oc: e.tensor_copy(out=A[:, 2:4], in_=halo[:, oc, :]),
                         R=[bhalo[oc]], W=[ba[i2]])
                    k.op("act", lambda e, A=A, pa=pa: e.copy(out=A[:, 4:516], in_=P.ps[pa][:]),
                         R=[P.psb[pa]], W=[ba[i2]])
                    k.op("pool", lambda e, A=A, oc=oc: e.tensor_copy(out=halo[:, oc, :], in_=A[:, 514:516]),
                         R=[ba[i2]], W=[bhalo[oc]])
                    C = c_sb[i2]
                    k.op("act", lambda e, C=C, pa=pa, oc=oc: e.activation(
                        out=C[:], in_=P.ps[pa][:], func=AF.Identity, bias=cb[:, oc:oc + 1], scale=cw[:, 2, oc:oc + 1]),
                        R=[P.psb[pa], bc], W=[bcs[i2]])
                    k.op("dve", lambda e, C=C, A=A, oc=oc: e.scalar_tensor_tensor(
                        out=C[:], in0=A[:, 3:515], scalar=cw[:, 1, oc:oc + 1], in1=C[:], op0=ALU.mult, op1=ALU.add),
                        R=[ba[i2], bc, bcs[i2]], W=[bcs[i2]])
                    k.op("dve", lambda e, C=C, A=A, oc=oc: e.scalar_tensor_tensor(
                        out=C[:], in0=A[:, 2:514], scalar=cw[:, 0, oc:oc + 1], in1=C[:], op0=ALU.mult, op1=ALU.add),
                        R=[ba[i2], bc, bcs[i2]], W=[bcs[i2]])
                    Ssb = s_sb[i2]
                    k.op("act", lambda e, C=C, Ssb=Ssb: e.activation(out=Ssb[:], in_=C[:], func=AF.Silu),
                         R=[bcs[i2]], W=[bss[i2]])
                    k.op("dve", lambda e, Ssb=Ssb, pu=pu, oc=oc: e.tensor_tensor(
                        out=gTt[:, oc, :], in0=Ssb[:], in1=P.ps[pu][:], op=ALU.mult),
                        R=[bss[i2], P.psb[pu]], W=[bg[oc]])
            if blk + 1 < NB:
                rms_tr(P, TT[(blk + 1) % 2], gT)
            for j in range(4):
                for n in range(2):
                    pb = 4 + (j * 2 + n) % 2
                    for kc in range(NFC):
                        k.op("pe", lambda e, kc=kc, j=j, n=n, pb=pb: e.matmul(
                            P.ps[pb][:], lhsT=gTt[:, kc, j * 128:(j + 1) * 128],
                            rhs=wout[:, kc, n * 512:(n + 1) * 512], start=(kc == 0), stop=(kc == NFC - 1)),
                            R=[bg[kc], bwo], W=[P.psb[pb]])
                    k.op("dve", lambda e, j=j, n=n, pb=pb: e.tensor_tensor(
                        out=xres[:, j, n * 512:(n + 1) * 512], in0=xres[:, j, n * 512:(n + 1) * 512],
                        in1=P.ps[pb][:], op=ALU.add), R=[bx[j], P.psb[pb]], W=[bx[j]])
                r0 = blk * TB + j * 128
                if final_norm is None:
                    k.dma("sp", [(dst[r0:r0 + 128, :], xres[:, j, :])], R=[bx[j]], W=[b_dst[blk * 4 + j]],
                          key=f"d_xst{blk % 2}_{j}")
                else:
                    ss, bs4 = T["ss"], T["b_ss"]
                    junk, bj = T["junk"], T["b_junk"]
                    o = ostage[j % 2]
                    k.op("act", lambda e, j=j: e.activation(out=junk[:], in_=xres[:, j, :], func=AF.Square,
                                                            accum_out=ss[:, j:j + 1]),
                         R=[bx[j]], W=[bj, bs4[j]])
                    k.op("act", lambda e, j=j: e.activation(out=ss[:, 4 + j:5 + j], in_=ss[:, j:j + 1], func=AF.Sqrt,
                                                            bias=P.epsb[:], scale=1.0 / D),
                         R=[bs4[j], P.b_const], W=[bs4[j]])
                    k.op("dve", lambda e, j=j: e.reciprocal(out=ss[:, 8 + j:9 + j], in_=ss[:, 4 + j:5 + j]),
                         R=[bs4[j]], W=[bs4[j]])
                    k.op("dve", lambda e, j=j, o=o: e.scalar_tensor_tensor(
                        out=o[:], in0=xres[:, j, :], scalar=ss[:, 8 + j:9 + j], in1=fg[:], op0=ALU.mult, op1=ALU.mult),
                        R=[bx[j], bs4[j], bfg], W=[bos[j % 2]])
                    k.dma("sp", [(dst[r0:r0 + 128, :], o[:])], R=[bos[j % 2]], W=[b_dst[blk * 4 + j]], key=f"d_ost{j % 2}")
        for blk in range(NB):
            do_block(blk, TT[blk % 2])
        k.barrier()


HQ = 8
VP = 130
SCALE = 192 ** -0.5
TWO_PI = 2.0 * math.pi
CW1 = 6.28125
CW2 = TWO_PI - 6.28125


def mla_proj_phase(P, src, b_src, W):
    k, nc = P.k, P.nc
    I = P.inp
    QTn = P.dint("QTn", [HQ, 128, S], BF16)
    QTr = P.dint("QTr", [HQ, 64, S], BF16)
    KTn = P.dint("KTn", [HQ, 128, S], BF16)
    KTr = P.dint("KTr", [64, S], BF16)
    Va = P.dint("Va", [S, HQ * VP], BF16)
    P.mla_dram = dict(QTn=QTn, QTr=QTr, KTn=KTn, KTr=KTr, Va=Va)
    P.b_qkv = k.bufs(NB, "qkv")
    stats = P.sb(P.es, "mla_stats", [128, 16], F32)
    P.mla_stats = stats
    P.b_stats = k.buf("mla_stats")
    with ExitStack() as es:
        bc = k.buf("pd_consts")
        gT = load_vec_fm(P, es, "pd_g", I["norm_mix_g"][1], 8, bc)
        qg = load_vec_fm(P, es, "pd_qg", I["mla_q_norm_g"], 3, bc)
        kvg = load_vec_fm(P, es, "pd_kvg", I["mla_kv_norm_g"], 2, bc)
        g5 = P.sb(es, "pd_g5", [128, 5], F32)
        k.op("pool", lambda e: e.tensor_copy(out=g5[:, 0:3], in_=qg[:]), R=[bc], W=[bc])
        k.op("pool", lambda e: e.tensor_copy(out=g5[:, 3:5], in_=kvg[:]), R=[bc], W=[bc])
        win = P.sb(es, "pd_win", [128, 8, 704], BF16)
        k.dma("sp", [(win[:], W["odd_w_in"][0].rearrange("(c p) n -> p c n", p=128))], R=[W["odd_w_in"][1]], W=[bc])
        wuq = P.sb(es, "pd_wuq", [128, 3, 1536], BF16)
        k.dma("sp", [(wuq[:], W["mla_w_uq"][0].rearrange("(c p) n -> p c n", p=128))], R=[W["mla_w_uq"][1]], W=[bc])
        wukv = P.sb(es, "pd_wukv", [128, 2, 2048], BF16)
        k.dma("sp", [(wukv[:], W["mla_w_ukv"][0].rearrange("(c p) n -> p c n", p=128))], R=[W["mla_w_ukv"][1]], W=[bc])
        freq = P.sb(es, "pd_freq", [128, 32], F32)
        k.dma("sp", [(freq[:], I["rope_freq"].partition_broadcast(128))], W=[bc])
        posi = P.sb(es, "pd_posi", [128, 32], I32)
        k.dma("sp", [(posi[:, 8 * i:8 * i + 8], I["positions"][1024 * i:1024 * (i + 1)].rearrange("(j p) -> p j", p=128))
                     for i in range(4)], W=[bc], allow_slow_non_contiguous=True)
        posf = P.sb(es, "pd_posf", [128, 32], F32)
        ang = P.sb(es, "pd_ang", [128, 32, 32], F32)
        tmpf = P.sb(es, "pd_tmpf", [128, 32, 32], F32)
        tmpi = P.sb(es, "pd_tmpi", [128, 32, 32], I32)
        sinT = P.sb(es, "pd_sin", [128, 32, 32], F32)
        cosT = P.sb(es, "pd_cos", [128, 32, 32], F32)
        br = k.buf("pd_rope")
        k.op("dve", lambda e: e.tensor_copy(out=posf[:], in_=posi[:]), R=[bc], W=[br])
        k.op("dve", lambda e: e.tensor_tensor(out=ang[:], in0=posf[:, :].unsqueeze(2).broadcast_to([128, 32, 32]),
                                              in1=freq[:, :].unsqueeze(1).broadcast_to([128, 32, 32]), op=ALU.mult),
             R=[br, bc], W=[br])
        k.op("dve", lambda e: e.tensor_scalar(out=tmpf[:], in0=ang[:], scalar1=1.0 / TWO_PI, scalar2=None, op0=ALU.mult),
             R=[br], W=[br])
        k.op("dve", lambda e: e.tensor_copy(out=tmpi[:], in_=tmpf[:]), R=[br], W=[br])
        k.op("dve", lambda e: e.tensor_copy(out=tmpf[:], in_=tmpi[:]), R=[br], W=[br])
        k.op("dve", lambda e: e.scalar_tensor_tensor(out=ang[:], in0=tmpf[:], scalar=-CW1, in1=ang[:],
                                                     op0=ALU.mult, op1=ALU.add), R=[br], W=[br])
        k.op("dve", lambda e: e.scalar_tensor_tensor(out=ang[:], in0=tmpf[:], scalar=-CW2, in1=ang[:],
                                                     op0=ALU.mult, op1=ALU.add), R=[br], W=[br])
        k.op("dve", lambda e: e.tensor_scalar(out=ang[:], in0=ang[:], scalar1=-math.pi, scalar2=math.pi,
                                              op0=ALU.max, op1=ALU.min), R=[br], W=[br])
        k.op("act", lambda e: e.activation(out=sinT[:], in_=ang[:], func=AF.Sin), R=[br], W=[br])
        k.op("dve", lambda e: e.tensor_scalar(out=tmpf[:], in0=ang[:], scalar1=math.pi / 2, scalar2=math.pi,
                                              op0=ALU.add, op1=ALU.is_gt), R=[br], W=[br])
        k.op("dve", lambda e: e.scalar_tensor_tensor(out=tmpf[:], in0=tmpf[:], scalar=-TWO_PI, in1=ang[:],
                                                     op0=ALU.mult, op1=ALU.add), R=[br], W=[br])
        k.op("dve", lambda e: e.tensor_scalar(out=tmpf[:], in0=tmpf[:], scalar1=math.pi / 2, scalar2=math.pi,
                                              op0=ALU.add, op1=ALU.min), R=[br], W=[br])
        k.op("act", lambda e: e.activation(out=cosT[:], in_=tmpf[:], func=AF.Sin), R=[br], W=[br])
        k.op("pool", lambda e: e.memset(stats[:], 0.0), W=[P.b_stats])

        T = alloc_blockbufs(P, es, "d_", ps_tr=[7])
        T["b_g"] = bc
        SETS = []
        for si in range(2):
            Bd = {}
            def mk(name, shape, dt, Bd=Bd, si=si):
                Bd[name] = P.sb(es, f"pd_{name}{si}", shape, dt)
                Bd["b_" + name] = k.buf(f"pd_{name}{si}")
            mk("junk2", [128, 1536], F32)
            mk("st", [128, 16], F32)
            mk("cn", [128, 640], BF16)
            mk("cT", [128, 5, 128], BF16)
            mk("kr", [128, 64], F32)
            mk("krb", [128, 64], BF16)
            mk("qsb", [128, 1536], F32)
            mk("qbf", [128, 8, 192], BF16)
            mk("kbf", [128, 8, 128], BF16)
            Bd["rt"] = [P.sb(es, f"pd_rt{si}_{i}", [128, 8, 32], F32) for i in range(4)]
            Bd["b_rt"] = k.buf(f"pd_rt{si}")
            SETS.append(Bd)
        vst = [P.sb(es, f"pd_vst{i}", [128, 8, VP], BF16) for i in range(2)]
        bvst = k.bufs(2, "pd_vst")
        for i in range(2):
            k.op("pool", lambda e, i=i: e.memset(vst[i][:], 1.0), W=[bvst[i]])
        qTn_st = P.sb(es, "pd_qTn", [128, 8, 512], BF16)
        qTr_st = P.sb(es, "pd_qTr", [64, 8, 512], BF16)
        kTn_st = P.sb(es, "pd_kTn", [128, 8, 512], BF16)
        kTr_st = P.sb(es, "pd_kTr", [64, 512], BF16)
        bqT = k.buf("pd_qT")
        bkT = k.buf("pd_kT")
        nv = 0
        import os
        LIM = float(os.environ.get("PD_LIMIT", "99"))
        for blk in range(NB):
            if LIM < 1 or (LIM < 90 and blk > 0):
                break
            norm_block(P, T, blk, src, b_src, gT)
            hnT, bh = T["hnT"], T["b_hnT"]
            def do_sub(blk, j, Bd):
                junk2, bj2, st, bst, cn, bcn, cT, bcT = Bd["junk2"], Bd["b_junk2"], Bd["st"], Bd["b_st"], Bd["cn"], Bd["b_cn"], Bd["cT"], Bd["b_cT"]
                kr, bkr, krb, bkrb, qsb, bqsb, qbf, bqbf = Bd["kr"], Bd["b_kr"], Bd["krb"], Bd["b_krb"], Bd["qsb"], Bd["b_qsb"], Bd["qbf"], Bd["b_qbf"]
                kbf, bkbf, rt, brt = Bd["kbf"], Bd["b_kbf"], Bd["rt"], Bd["b_rt"]
                nonlocal nv
                sub = blk * 4 + j
                tok = slice(j * 128, (j + 1) * 128)
                for kc in range(8):
                    k.op("pe", lambda e, kc=kc, tok=tok: e.matmul(P.ps[0][:, 0:384], lhsT=hnT[:, kc, tok], rhs=win[:, kc, 0:384],
                                                                  start=(kc == 0), stop=(kc == 7)), R=[bh, bc], W=[P.psb[0]])
                for kc in range(8):
                    k.op("pe", lambda e, kc=kc, tok=tok: e.matmul(P.ps[1][:, 0:320], lhsT=hnT[:, kc, tok], rhs=win[:, kc, 384:704],
                                                                  start=(kc == 0), stop=(kc == 7)), R=[bh, bc], W=[P.psb[1]])
                k.op("act", lambda e: e.activation(out=junk2[:, 0:384], in_=P.ps[0][:, 0:384], func=AF.Square,
                                                   accum_out=st[:, 0:1]), R=[P.psb[0]], W=[bj2, bst])
                k.op("act", lambda e: e.activation(out=junk2[:, 0:256], in_=P.ps[1][:, 0:256], func=AF.Square,
                                                   accum_out=st[:, 1:2]), R=[P.psb[1]], W=[bj2, bst])
                k.op("act", lambda e: e.activation(out=junk2[:, 0:64], in_=P.ps[1][:, 256:320], func=AF.Square,
                                                   accum_out=st[:, 2:3]), R=[P.psb[1]], W=[bj2, bst])
                k.op("act", lambda e: e.activation(out=st[:, 3:4], in_=st[:, 0:1], func=AF.Sqrt, bias=P.epsb[:],
                                                   scale=1.0 / 384), R=[bst, P.b_const], W=[bst])
                k.op("act", lambda e: e.activation(out=st[:, 4:5], in_=st[:, 1:2], func=AF.Sqrt, bias=P.epsb[:],
                                                   scale=1.0 / 256), R=[bst, P.b_const], W=[bst])
                k.op("dve", lambda e: e.reciprocal(out=st[:, 5:7], in_=st[:, 3:5]), R=[bst], W=[bst])
                k.op("dve", lambda e: e.tensor_scalar(out=cn[:, 0:384], in0=P.ps[0][:, 0:384], scalar1=st[:, 5:6],
                                                      scalar2=None, op0=ALU.mult), R=[P.psb[0], bst], W=[bcn])
                k.op("dve", lambda e: e.tensor_scalar(out=cn[:, 384:640], in0=P.ps[1][:, 0:256], scalar1=st[:, 6:7],
                                                      scalar2=None, op0=ALU.mult), R=[P.psb[1], bst], W=[bcn])
                k.op("act", lambda e: e.copy(out=kr[:], in_=P.ps[1][:, 256:320]), R=[P.psb[1]], W=[bkr])
                psT = P.ps[2][:].bitcast(BF16)
                for c in range(5):
                    k.op("pe", lambda e, c=c, psT=psT: e.transpose(out=psT[:, c * 128:(c + 1) * 128],
                                                                   in_=cn[:, c * 128:(c + 1) * 128], identity=P.ident[:]),
                         R=[bcn, P.b_const], W=[P.psb[2]])
                k.op("dve", lambda e, psT=psT: e.tensor_tensor(out=cT[:], in0=psT[:, 0:640].rearrange("p (c t) -> p c t", c=5),
                                                               in1=g5[:, :].unsqueeze(2).broadcast_to([128, 5, 128]), op=ALU.mult),
                     R=[P.psb[2], bc], W=[bcT])
                if LIM < 3:
                    return
                for n in range(3):
                    for kc in range(3):
                        k.op("pe", lambda e, n=n, kc=kc: e.matmul(P.ps[3 + n][:], lhsT=cT[:, kc, :],
                                                                  rhs=wuq[:, kc, n * 512:(n + 1) * 512],
                                                                  start=(kc == 0), stop=(kc == 2)), R=[bcT, bc], W=[P.psb[3 + n]])
                kvb = [0, 1, 2, 6]
                for n in range(4):
                    for kc in range(2):
                        k.op("pe", lambda e, n=n, kc=kc: e.matmul(P.ps[kvb[n]][:], lhsT=cT[:, 3 + kc, :],
                                                                  rhs=wukv[:, kc, n * 512:(n + 1) * 512],
                                                                  start=(kc == 0), stop=(kc == 1)), R=[bcT, bc], W=[P.psb[kvb[n]]])
                for n in range(3):
                    k.op("act", lambda e, n=n: e.activation(out=qsb[:, n * 512:(n + 1) * 512], in_=P.ps[3 + n][:],
                                                            func=AF.Copy, scale=SCALE), R=[P.psb[3 + n]], W=[bqsb])
                if LIM < 3.1:
                    return
                k.op("pool", lambda e: e.tensor_tensor(out=junk2[:], in0=qsb[:], in1=qsb[:], op=ALU.mult),
                     R=[bqsb], W=[bj2])
                k.op("dve", lambda e: e.tensor_reduce(out=st[:, 8:16], in_=junk2[:, :].rearrange("p (h d) -> p h d", h=8),
                                                      axis=AX.X, op=ALU.add), R=[bj2], W=[bst])
                k.op("dve", lambda e: e.tensor_tensor(out=stats[:, 0:8], in0=stats[:, 0:8], in1=st[:, 8:16], op=ALU.max),
                     R=[bst, P.b_stats], W=[P.b_stats])
                if LIM < 3.2:
                    return
                q3 = qsb[:, :].rearrange("p (h d) -> p h d", h=8)
                cosb = cosT[:, sub, :].unsqueeze(1).broadcast_to([128, 8, 32])
                sinb = sinT[:, sub, :].unsqueeze(1).broadcast_to([128, 8, 32])
                x1 = q3[:, :, 128:160]
                x2 = q3[:, :, 160:192]
                k.op("pool", lambda e, x1=x1, cosb=cosb: e.tensor_tensor(out=rt[0][:], in0=x1, in1=cosb, op=ALU.mult),
                     R=[bqsb, br], W=[brt])
                k.op("pool", lambda e, x2=x2, sinb=sinb: e.tensor_tensor(out=rt[1][:], in0=x2, in1=sinb, op=ALU.mult),
                     R=[bqsb, br], W=[brt])
                k.op("pool", lambda e, x1=x1, sinb=sinb: e.tensor_tensor(out=rt[2][:], in0=x1, in1=sinb, op=ALU.mult),
                     R=[bqsb, br], W=[brt])
                k.op("pool", lambda e, x2=x2, cosb=cosb: e.tensor_tensor(out=rt[3][:], in0=x2, in1=cosb, op=ALU.mult),
                     R=[bqsb, br], W=[brt])
                k.op("dve", lambda e: e.tensor_tensor(out=qbf[:, :, 128:160], in0=rt[0][:], in1=rt[1][:], op=ALU.subtract),
                     R=[brt], W=[bqbf])
                k.op("dve", lambda e: e.tensor_tensor(out=qbf[:, :, 160:192], in0=rt[2][:], in1=rt[3][:], op=ALU.add),
                     R=[brt], W=[bqbf])
                k.op("act", lambda e, q3=q3: e.copy(out=qbf[:, :, 0:128], in_=q3[:, :, 0:128]), R=[bqsb], W=[bqbf])
                if LIM < 3.3:
                    return
                psQn = P.ps[7][:].bitcast(BF16)
                psQr = P.ps[3][:].bitcast(BF16)
                for h in range(8):
                    k.op("pe", lambda e, h=h, psQn=psQn: e.transpose(out=psQn[:, h * 128:(h + 1) * 128], in_=qbf[:, h, 0:128],
                                                                     identity=P.ident[:]), R=[bqbf, P.b_const], W=[P.psb[7]])
                for h in range(8):
                    k.op("pe", lambda e, h=h, psQr=psQr: e.transpose(out=psQr[0:64, h * 128:(h + 1) * 128], in_=qbf[:, h, 128:192],
                                                                     identity=P.ident[:]), R=[bqbf, P.b_const], W=[P.psb[3]])
                k.op("act", lambda e, psQn=psQn, tok=tok: e.copy(out=qTn_st[:, :, tok], in_=psQn.rearrange("p (h t) -> p h t", h=8)),
                     R=[P.psb[7]], W=[bqT])
                k.op("dve", lambda e, psQr=psQr, tok=tok: e.tensor_copy(out=qTr_st[:, :, tok],
                                                                        in_=psQr[0:64, :].rearrange("p (h t) -> p h t", h=8)),
                     R=[P.psb[3]], W=[bqT])
                if LIM < 4:
                    return
                if LIM < 4.1:
                    return
                vs = vst[nv % 2]
                bvs = bvst[nv % 2]
                nv += 1
                for n in range(4):
                    pv = P.ps[kvb[n]][:, :].rearrange("p (h c) -> p h c", h=2)
                    if n % 2 == 0:
                        k.op("act", lambda e, n=n, pv=pv: e.copy(out=kbf[:, 2 * n:2 * n + 2, :], in_=pv[:, :, 0:128]),
                             R=[P.psb[kvb[n]]], W=[bkbf])
                        k.op("act", lambda e, n=n, pv=pv, vs=vs: e.copy(out=vs[:, 2 * n:2 * n + 2, 0:128], in_=pv[:, :, 128:256]),
                             R=[P.psb[kvb[n]]], W=[bvs])
                    else:
                        k.op("dve", lambda e, n=n, pv=pv: e.tensor_copy(out=kbf[:, 2 * n:2 * n + 2, :], in_=pv[:, :, 0:128]),
                             R=[P.psb[kvb[n]]], W=[bkbf])
                        k.op("dve", lambda e, n=n, pv=pv, vs=vs: e.tensor_copy(out=vs[:, 2 * n:2 * n + 2, 0:128], in_=pv[:, :, 128:256]),
                             R=[P.psb[kvb[n]]], W=[bvs])
                if LIM < 4.2:
                    return
                r0 = blk * TB + j * 128
                k.dma("sp", [(P.mla_dram["Va"][r0:r0 + 128, :], vs[:, :, :].rearrange("p h c -> p (h c)"))],
                      R=[bvs], W=[P.b_qkv[blk]], key=f"d_vst{(nv - 1) % 2}")
                if LIM < 4.3:
                    return
                k.op("pool", lambda e: e.tensor_tensor(out=junk2[:, 0:1024], in0=kbf[:, :, :].rearrange("p h c -> p (h c)"),
                                                       in1=kbf[:, :, :].rearrange("p h c -> p (h c)"), op=ALU.mult),
                     R=[bkbf], W=[bj2])
                k.op("dve", lambda e: e.tensor_reduce(out=st[:, 8:16], in_=junk2[:, 0:1024].rearrange("p (h d) -> p h d", h=8),
                                                      axis=AX.X, op=ALU.add), R=[bj2], W=[bst])
                k.op("dve", lambda e: e.tensor_scalar(out=st[:, 8:16], in0=st[:, 8:16], scalar1=st[:, 2:3], scalar2=None,
                                                      op0=ALU.add), R=[bst], W=[bst])
                k.op("dve", lambda e: e.tensor_tensor(out=stats[:, 8:16], in0=stats[:, 8:16], in1=st[:, 8:16], op=ALU.max),
                     R=[bst, P.b_stats], W=[P.b_stats])
                if LIM < 4.4:
                    return
                c1 = cosT[:, sub, :]
                s1 = sinT[:, sub, :]
                k.op("pool", lambda e, c1=c1: e.tensor_tensor(out=rt[0][:, 0, :], in0=kr[:, 0:32], in1=c1, op=ALU.mult),
                     R=[bkr, br], W=[brt])
                k.op("pool", lambda e, s1=s1: e.tensor_tensor(out=rt[1][:, 0, :], in0=kr[:, 32:64], in1=s1, op=ALU.mult),
                     R=[bkr, br], W=[brt])
                k.op("pool", lambda e, s1=s1: e.tensor_tensor(out=rt[2][:, 0, :], in0=kr[:, 0:32], in1=s1, op=ALU.mult),
                     R=[bkr, br], W=[brt])
                k.op("pool", lambda e, c1=c1: e.tensor_tensor(out=rt[3][:, 0, :], in0=kr[:, 32:64], in1=c1, op=ALU.mult),
                     R=[bkr, br], W=[brt])
                k.op("dve", lambda e: e.tensor_tensor(out=krb[:, 0:32], in0=rt[0][:, 0, :], in1=rt[1][:, 0, :], op=ALU.subtract),
                     R=[brt], W=[bkrb])
                k.op("dve", lambda e: e.tensor_tensor(out=krb[:, 32:64], in0=rt[2][:, 0, :], in1=rt[3][:, 0, :], op=ALU.add),
                     R=[brt], W=[bkrb])
                if LIM < 4.5:
                    return
                psKn = P.ps[7][:].bitcast(BF16)
                psKr = P.ps[4][:].bitcast(BF16)
                for h in range(8):
                    k.op("pe", lambda e, h=h, psKn=psKn: e.transpose(out=psKn[:, h * 128:(h + 1) * 128], in_=kbf[:, h, :],
                                                                     identity=P.ident[:]), R=[bkbf, P.b_const], W=[P.psb[7]])
                k.op("pe", lambda e, psKr=psKr: e.transpose(out=psKr[0:64, 0:128], in_=krb[:], identity=P.ident[:]),
                     R=[bkrb, P.b_const], W=[P.psb[4]])
                k.op("act", lambda e, psKn=psKn, tok=tok: e.copy(out=kTn_st[:, :, tok], in_=psKn.rearrange("p (h t) -> p h t", h=8)),
                     R=[P.psb[7]], W=[bkT])
                k.op("dve", lambda e, psKr=psKr, tok=tok: e.tensor_copy(out=kTr_st[:, tok], in_=psKr[0:64, 0:128]),
                     R=[P.psb[4]], W=[bkT])

            for j in range(4):
                if LIM < 2 or (LIM < 90 and j > 0):
                    break
                do_sub(blk, j, SETS[(blk * 4 + j) % 2])
            if LIM < 5:
                break
            c0 = blk * TB
            k.dma("sp", [(P.mla_dram["QTn"][:, :, c0:c0 + TB].rearrange("h d t -> d h t"), qTn_st[:]),
                         (P.mla_dram["QTr"][:, :, c0:c0 + TB].rearrange("h d t -> d h t"), qTr_st[:])],
                  R=[bqT], W=[P.b_qkv[blk]], key="d_qT")
            k.dma("sp", [(P.mla_dram["KTn"][:, :, c0:c0 + TB].rearrange("h d t -> d h t"), kTn_st[:]),
                         (P.mla_dram["KTr"][:, c0:c0 + TB], kTr_st[:])],
                  R=[bkT], W=[P.b_qkv[blk]], key="d_kT")
        k.barrier()


def mla_attn_phase(P, src, b_src, dst, b_dst, W):
    k, nc = P.k, P.nc
    I = P.inp
    Dm = P.mla_dram
    stats = P.mla_stats
    with ExitStack() as es:
        bc = k.buf("pe_consts")
        wo = P.sb(es, "pe_wo", [128, 8, 1024], BF16)
        k.dma("sp", [(wo[:], W["odd_w_out"][0].rearrange("(c p) n -> p c n", p=128))], R=[W["odd_w_out"][1]], W=[bc])
        tps = P.ps[7]
        sm = P.sb(es, "pe_sm", [16, 4], F32)
        dg = P.sb(es, "pe_dg", [8, 8], F32)
        negc = P.sb(es, "pe_negc", [128, 8], F32)
        k.op("pe", lambda e: e.transpose(out=tps[0:16, 0:128], in_=stats[:, 0:16], identity=P.ident_f[:]),
             R=[P.b_stats, P.b_const], W=[P.psb[7]])
        k.op("dve", lambda e: e.tensor_reduce(out=sm[:, 0:1], in_=tps[0:16, 0:128], axis=AX.X, op=ALU.max),
             R=[P.psb[7]], W=[bc])
        k.op("pe", lambda e: e.transpose(out=tps[0:1, 128:144], in_=sm[:, 0:1], identity=P.ident_f[0:16, 0:16]),
             R=[bc, P.b_const], W=[P.psb[7]])
        rowv = P.sb(es, "pe_rowv", [1, 24], F32)
        k.op("dve", lambda e: e.tensor_copy(out=rowv[:, 0:16], in_=tps[0:1, 128:144]), R=[P.psb[7]], W=[bc])
        k.op("dve", lambda e: e.tensor_tensor(out=rowv[:, 16:24], in0=rowv[:, 0:8], in1=rowv[:, 8:16], op=ALU.mult),
             R=[bc], W=[bc])
        k.op("act", lambda e: e.activation(out=rowv[:, 16:24], in_=rowv[:, 16:24], func=AF.Sqrt), R=[bc], W=[bc])
        k.op("dve", lambda e: e.tensor_scalar(out=rowv[:, 16:24], in0=rowv[:, 16:24], scalar1=-1.0, scalar2=None, op0=ALU.mult),
             R=[bc], W=[bc])
        k.op("pe", lambda e: e.matmul(tps[:, 160:168], lhsT=P.ones_f[0:1, :], rhs=rowv[:, 16:24], start=True, stop=True),
             R=[bc, P.b_const], W=[P.psb[7]])
        k.op("dve", lambda e: e.tensor_copy(out=negc[:], in_=tps[:, 160:168]), R=[P.psb[7]], W=[bc])
        tri = P.sb(es, "pe_tri", [128, 128], BF16)
        k.op("pool", lambda e: e.affine_select(out=tri[:], in_=P.ones_f[:], pattern=[[1, 128]], compare_op=ALU.is_ge,
                                               fill=0.0, base=0, channel_multiplier=-1), R=[P.b_const], W=[bc])
        KTn = P.sb(es, "pe_KTn", [128, 8, S], BF16)
        KTr = P.sb(es, "pe_KTr", [128, S], BF16)
        Vs = P.sb(es, "pe_V", [128, 32, HQ * VP], BF16)
        bK = k.bufs(8, "pe_K")
        bV = k.bufs(8, "pe_V")
        k.op("pool", lambda e: e.memset(KTr[64:128, :], 0.0), W=bK)
        for b in range(NB):
            c0 = b * TB
            k.dma("sp", [(KTn[:, :, c0:c0 + TB], Dm["KTn"][:, :, c0:c0 + TB].rearrange("h d t -> d h t")),
                         (KTr[0:64, c0:c0 + TB], Dm["KTr"][:, c0:c0 + TB]),
                         (Vs[:, 4 * b:4 * b + 4, :], Dm["Va"][c0:c0 + TB, :].rearrange("(j p) c -> p j c", p=128))],
                  R=[P.b_qkv[b]], W=[bK[b], bV[b]], key=f"d_peKV{b}")
        NQ = 1
        Qn = [P.sb(es, f"pe_Qn{i}", [128, 8, TB], BF16) for i in range(NQ)]
        Qr = [P.sb(es, f"pe_Qr{i}", [128, 8, TB], BF16) for i in range(NQ)]
        bQ = k.bufs(NQ, "pe_Q")
        for i in range(NQ):
            k.op("pool", lambda e, i=i: e.memset(Qr[i][64:128, :, :], 0.0), W=[bQ[i]])
        NPT = 4
        PT = [P.sb(es, f"pe_PT{i}", [128, TB], BF16) for i in range(NPT)]
        bPT = k.bufs(NPT, "pe_PT")
        att = P.sb(es, "pe_att", [128, 4, 1024], BF16)
        batt = k.bufs(4, "pe_att")
        aT = P.sb(es, "pe_aT", [128, 8, TB], BF16)
        baT = k.buf("pe_aT")
        rden = P.sb(es, "pe_rden", [128, 4], F32)
        brden = k.buf("pe_rden")
        hres = [P.sb(es, f"pe_hres{i}", [128, 1024], F32) for i in range(2)]
        bhres = k.bufs(2, "pe_hres")
        npt = 0
        nsc = 0
        for qb in range(NB):
            qi = qb % NQ
            c0 = qb * TB
            k.dma("sp", [(Qn[qi][:], Dm["QTn"][:, :, c0:c0 + TB].rearrange("h d t -> d h t")),
                         (Qr[qi][0:64], Dm["QTr"][:, :, c0:c0 + TB].rearrange("h d t -> d h t"))],
                  R=[P.b_qkv[qb]], W=[bQ[qi]])
            nkt = 4 * qb + 4
            tiles = [(h, kt) for h in range(8) for kt in range(nkt)]
            SCB = (0, 1, 2, 7)
            info = {}

            def emit_qk(i):
                nonlocal nsc, npt
                h, kt = tiles[i]
                r = kt - 4 * qb
                q0 = max(r, 0) * 128
                sb_ = SCB[nsc % 4]
                nsc += 1
                pi = npt % NPT
                npt += 1
                kb = kt // 4
                info[i] = (sb_, pi, q0, r, kb)
                k.op("pe", lambda e, h=h, kt=kt, q0=q0, sb_=sb_: e.matmul(
                    P.ps[sb_][:, q0:TB], lhsT=KTn[:, h, kt * 128:(kt + 1) * 128], rhs=Qn[qi][:, h, q0:TB],
                    start=True, stop=False), R=[bK[kb], bQ[qi]], W=[P.psb[sb_]])
                k.op("pe", lambda e, h=h, kt=kt, q0=q0, sb_=sb_: e.matmul(
                    P.ps[sb_][:, q0:TB], lhsT=KTr[:, kt * 128:(kt + 1) * 128], rhs=Qr[qi][:, h, q0:TB],
                    start=False, stop=True), R=[bK[kb], bQ[qi]], W=[P.psb[sb_]])
                k.op("act", lambda e, h=h, q0=q0, sb_=sb_, pi=pi: e.activation(
                    out=PT[pi][:, q0:TB], in_=P.ps[sb_][:, q0:TB], func=AF.Exp, bias=negc[:, h:h + 1], scale=1.0),
                    R=[P.psb[sb_], bc], W=[bPT[pi]])
                if r >= 0:
                    k.op("pool", lambda e, q0=q0, pi=pi: e.tensor_tensor(out=PT[pi][:, q0:q0 + 128], in0=PT[pi][:, q0:q0 + 128],
                                                                         in1=tri[:], op=ALU.mult), R=[bPT[pi], bc], W=[bPT[pi]])

            LOOK = 2
            for i in range(min(LOOK, len(tiles))):
                emit_qk(i)
            first = [True, True]
            for i, (h, kt) in enumerate(tiles):
                if i + LOOK < len(tiles):
                    emit_qk(i + LOOK)
                ob = (3, 4) if h % 2 == 0 else (5, 6)
                if kt == 0:
                    first = [True, True]
                sb_, pi, q0, r, kb = info.pop(i)
                for js in range(max(r, 0), 4):
                    bank = ob[js // 2]
                    off = (js % 2) * 129
                    last = (kt == 4 * qb + js)
                    st_ = first[js // 2]
                    first[js // 2] = False
                    k.op("pe", lambda e, js=js, bank=bank, off=off, pi=pi, kt=kt, h=h, st_=st_, last=last: e.matmul(
                        P.ps[bank][:, off:off + 129], lhsT=PT[pi][:, js * 128:(js + 1) * 128],
                        rhs=Vs[:, kt, h * VP:h * VP + 129], start=st_, stop=last, skip_group_check=True),
                        R=[bPT[pi], bV[kb]], W=[P.psb[bank]])
                if kt == nkt - 1:
                    for js in range(4):
                        bank = ob[js // 2]
                        off = (js % 2) * 129
                        k.op("dve", lambda e, js=js, bank=bank, off=off: e.reciprocal(out=rden[:, js:js + 1],
                                                                                      in_=P.ps[bank][:, off + 128:off + 129]),
                             R=[P.psb[bank]], W=[brden])
                        k.op("dve", lambda e, js=js, bank=bank, off=off, h=h: e.tensor_scalar(
                            out=att[:, js, h * 128:(h + 1) * 128], in0=P.ps[bank][:, off:off + 128], scalar1=rden[:, js:js + 1],
                            scalar2=None, op0=ALU.mult), R=[P.psb[bank], brden], W=[batt[js]])
            for js in range(4):
                psA = P.ps[7][:].bitcast(BF16)
                for h in range(8):
                    k.op("pe", lambda e, js=js, h=h, psA=psA: e.transpose(out=psA[:, h * 128:(h + 1) * 128],
                                                                          in_=att[:, js, h * 128:(h + 1) * 128], identity=P.ident[:]),
                         R=[batt[js], P.b_const], W=[P.psb[7]])
                k.op("dve", lambda e, js=js, psA=psA: e.tensor_copy(out=aT[:, :, js * 128:(js + 1) * 128],
                                                                    in_=psA.rearrange("p (h t) -> p h t", h=8)),
                     R=[P.psb[7]], W=[baT])
            for js in range(4):
                r0 = qb * TB + js * 128
                hr = hres[js % 2]
                bhr = bhres[js % 2]
                k.dma("sp", [(hr[:], src[r0:r0 + 128, :])], R=[b_src[qb * 4 + js]], W=[bhr], key=f"d_hres{js % 2}")
                for n in range(2):
                    bank = (3, 5)[n]
                    for h in range(8):
                        k.op("pe", lambda e, js=js, n=n, h=h, bank=bank: e.matmul(
                            P.ps[bank][:], lhsT=aT[:, h, js * 128:(js + 1) * 128], rhs=wo[:, h, n * 512:(n + 1) * 512],
                            start=(h == 0), stop=(h == 7)), R=[baT, bc], W=[P.psb[bank]])
                    k.op("dve", lambda e, n=n, bank=bank, hr=hr: e.tensor_tensor(
                        out=hr[:, n * 512:(n + 1) * 512], in0=hr[:, n * 512:(n + 1) * 512], in1=P.ps[bank][:],
                        op=ALU.add), R=[bhr, P.psb[bank]], W=[bhr])
                k.dma("sp", [(dst[r0:r0 + 128, :], hr[:])], R=[bhr], W=[b_dst[qb * 4 + js]], key=f"d_hst{js % 2}")
        k.barrier()


def hgrn_phase(P, src, b_src, W):
    k, nc = P.k, P.nc
    I = P.inp
    uT_d = P.dint("uT", [512, S], BF16)
    ycatT = P.dint("ycatT", [1024, S], BF16)
    P.uT_d, P.ycatT = uT_d, ycatT
    P.b_uT = k.bufs(NB, "uT")
    P.b_ya = k.bufs(NB, "ya")
    P.b_yb = k.bufs(NB, "yb")
    with ExitStack() as es:
        bc = k.buf("pa_consts")
        gT = load_vec_fm(P, es, "pa_g", I["norm_mix_g"][0], 8, bc)
        win = P.sb(es, "pa_win", [128, 8, 2560], BF16)
        k.dma("sp", [(win[:, 4 * i:4 * i + 4, :], W["even_w_in"][0][512 * i:512 * (i + 1), :].rearrange("(c p) n -> p c n", p=128))
                     for i in range(2)], R=[W["even_w_in"][1]], W=[bc])
        l0 = load_vec_fm(P, es, "pa_l0", I["hgrn_lb_logits"][0], 4, bc)
        l1 = load_vec_fm(P, es, "pa_l1", I["hgrn_lb_logits"][1], 4, bc)
        lb = P.sb(es, "pa_lb", [128, 4], F32)
        oml = P.sb(es, "pa_oml", [128, 4], F32)
        noml = P.sb(es, "pa_noml", [128, 4], F32)
        k.op("dve", lambda e: e.tensor_tensor(out=lb[:], in0=l0[:], in1=l1[:], op=ALU.subtract), R=[bc], W=[bc])
        k.op("act", lambda e: e.activation(out=lb[:], in_=lb[:], func=AF.Sigmoid), R=[bc], W=[bc])
        k.op("dve", lambda e: e.tensor_scalar(out=oml[:], in0=lb[:], scalar1=-1.0, scalar2=1.0, op0=ALU.mult, op1=ALU.add),
             R=[bc], W=[bc])
        k.op("dve", lambda e: e.tensor_scalar(out=noml[:], in0=oml[:], scalar1=-1.0, scalar2=None, op0=ALU.mult),
             R=[bc], W=[bc])
        ng = P.sb(es, "pa_ng", [128, 512], F32)
        k.dma("sp", [(ng[:], I["hgrn_norm_g"].partition_broadcast(128))], W=[bc])
        msk = P.sb(es, "pa_msk", [128, 128], F32)
        k.op("pool", lambda e: e.affine_select(out=msk[:], in_=P.ones_f[:], pattern=[[1, 128]], compare_op=ALU.is_ge,
                                               fill=0.0, base=0, channel_multiplier=-1), R=[P.b_const], W=[bc])
        k.op("pool", lambda e: e.memset(msk[0:64, 64:128], 0.0), R=[bc], W=[bc])
        rst = P.sb(es, "pa_rst", [128, 512], F32)
        k.op("pool", lambda e: e.memset(rst[:], 1.0), W=[bc])
        k.op("pool", lambda e: e.memset(rst[:, :].rearrange("p (n c) -> p n c", c=64)[:, :, 0:1], 0.0), R=[bc], W=[bc])
        mA = P.sb(es, "pa_mA", [128, 512], BF16)
        mB = P.sb(es, "pa_mB", [128, 512], BF16)
        k.op("pool", lambda e: e.memset(mA[:], 1.0), W=[bc])
        k.op("pool", lambda e: e.memset(mA[:, :].rearrange("p (n c) -> p n c", c=128)[:, :, 64:128], 0.0), R=[bc], W=[bc])
        k.op("pool", lambda e: e.memset(mB[:], 1.0), W=[bc])
        k.op("pool", lambda e: e.memset(mB[:, :].rearrange("p (n c) -> p n c", c=128)[:, :, 0:64], 0.0), R=[bc], W=[bc])
        rmA = P.sb(es, "pa_rmA", [128, 1], F32)
        rmB = P.sb(es, "pa_rmB", [128, 1], F32)
        k.op("pool", lambda e: e.memset(rmA[0:64, :], 1.0), W=[bc])
        k.op("pool", lambda e: e.memset(rmA[64:128, :], 0.0), W=[bc])
        k.op("pool", lambda e: e.memset(rmB[0:64, :], 0.0), W=[bc])
        k.op("pool", lambda e: e.memset(rmB[64:128, :], 1.0), W=[bc])
        St = P.sb(es, "pa_S", [128, 4, 128], F32)
        Sbf = [P.sb(es, f"pa_Sbf{i}", [128, 4, 128], BF16) for i in range(2)]
        bS = k.bufs(4, "pa_S")
        bSbf = [k.bufs(4, "pa_SbfA"), k.bufs(4, "pa_SbfB")]
        k.op("pool", lambda e: e.memset(St[:], 0.0), W=bS)
        k.op("pool", lambda e: e.memset(Sbf[0][:], 0.0), W=bSbf[0])
        k.op("pool", lambda e: e.memset(Sbf[1][:], 0.0), W=bSbf[1])

        T = alloc_blockbufs(P, es, "a_", ps_tr=[7])
        T["b_g"] = bc
        sig = P.sb(es, "pa_sig", [128, 512], F32); bsig = k.buf("pa_sig")
        logf = P.sb(es, "pa_logf", [128, 512], F32); blogf = k.buf("pa_logf")
        bb = P.sb(es, "pa_b", [128, 512], F32); bbb = k.buf("pa_b")
        eb = P.sb(es, "pa_eb", [128, 4, 512], F32); beb = k.bufs(4, "pa_eb")
        enb = P.sb(es, "pa_enb", [128, 512], F32); benb = k.buf("pa_enb")
        kk = P.sb(es, "pa_kk", [128, 512], F32); bkk = k.buf("pa_kk")
        kd32 = P.sb(es, "pa_kd32", [128, 512], F32); bkd32 = k.buf("pa_kd32")
        qd = P.sb(es, "pa_qd", [128, 4, 512], BF16); bqd = k.bufs(4, "pa_qd")
        kd = P.sb(es, "pa_kd", [128, 4, 512], BF16); bkd = k.bufs(4, "pa_kd")
        qdA = P.sb(es, "pa_qdA", [128, 4, 512], BF16); qdB = P.sb(es, "pa_qdB", [128, 4, 512], BF16)
        kbT = P.sb(es, "pa_kbT", [128, 4, 512], BF16); bkbT = k.bufs(4, "pa_kbT")
        uTs = P.sb(es, "pa_uT", [128, 4, 512], BF16); buTs = k.buf("pa_uTs")
        kbtm = P.sb(es, "pa_kbtm", [128, 4, 128], BF16); bkbtm = k.buf("pa_kbtm")
        kbtmB = P.sb(es, "pa_kbtmB", [128, 4, 128], BF16)
        vtm = P.sb(es, "pa_vtm", [128, 512], BF16); bvtm = k.buf("pa_vtm")
        ngsg = P.sb(es, "pa_ngsg", [128, 512], F32); bngsg = k.buf("pa_ngsg")
        attm = P.sb(es, "pa_attm", [128, 4, 128], BF16); battm = k.bufs(4, "pa_attm")
        yatm = P.sb(es, "pa_yatm", [128, 512], BF16); byatm = k.buf("pa_yatm")
        yaT = P.sb(es, "pa_yaT", [128, 4, 512], BF16); byaT = k.buf("pa_yaT")
        sst = P.sb(es, "pa_sst", [128, 16], F32); bsst = k.bufs(4, "pa_sst")
        junk = P.sb(es, "pa_junk", [128, 128], F32); bjunk = k.buf("pa_junk")
        import os
        HL = float(os.environ.get("HG_LIMIT", "99"))
        for blk in range(NB):
            if HL < 1 or (HL < 90 and blk > 0):
                break
            norm_block(P, T, blk, src, b_src, gT)
            hnT, bh = T["hnT"], T["b_hnT"]
            c0 = blk * TB
            for c in range(4):
                pb = 4 + c % 2
                for kc in range(8):
                    k.op("pe", lambda e, c=c, kc=kc, pb=pb: e.matmul(P.ps[pb][:], lhsT=win[:, kc, 2048 + c * 128:2048 + (c + 1) * 128],
                                                                     rhs=hnT[:, kc, :], start=(kc == 0), stop=(kc == 7)),
                         R=[bc, bh], W=[P.psb[pb]])
                k.op("act", lambda e, c=c, pb=pb: e.copy(out=uTs[:, c, :], in_=P.ps[pb][:]), R=[P.psb[pb]], W=[buTs])
            k.dma("sp", [(uT_d[:, c0:c0 + TB].rearrange("(c p) t -> p c t", p=128), uTs[:])], R=[buTs], W=[P.b_uT[blk]], key="d_uTst")
            if HL < 1.1:
                break
            for h in range(4):
                if HL < 1.2 and h > 0:
                    break
                pq, pf = 4, 5
                for kc in range(8):
                    k.op("pe", lambda e, h=h, kc=kc: e.matmul(P.ps[pf][:], lhsT=win[:, kc, 512 + h * 128:512 + (h + 1) * 128],
                                                              rhs=hnT[:, kc, :], start=(kc == 0), stop=(kc == 7)),
                         R=[bc, bh], W=[P.psb[pf]])
                for kc in range(8):
                    k.op("pe", lambda e, h=h, kc=kc: e.matmul(P.ps[pq][:], lhsT=win[:, kc, h * 128:(h + 1) * 128],
                                                              rhs=hnT[:, kc, :], start=(kc == 0), stop=(kc == 7)),
                         R=[bc, bh], W=[P.psb[pq]])
                k.op("act", lambda e: e.activation(out=sig[:], in_=P.ps[pf][:], func=AF.Sigmoid), R=[P.psb[pf]], W=[bsig])
                k.op("act", lambda e, h=h: e.activation(out=logf[:], in_=sig[:], func=AF.Ln, bias=lb[:, h:h + 1],
                                                        scale=oml[:, h:h + 1]), R=[bsig, bc], W=[blogf])
                if HL < 1.15:
                    break
                k.op("dve", lambda e: e.tensor_tensor_scan(out=bb[:], data0=rst[:], data1=logf[:], initial=0.0,
                                                           op0=ALU.mult, op1=ALU.add), R=[blogf, bc], W=[bbb])
                if HL < 1.17:
                    break
                k.op("act", lambda e, h=h: e.activation(out=eb[:, h, :], in_=bb[:], func=AF.Exp), R=[bbb], W=[beb[h]])
                k.op("act", lambda e: e.activation(out=enb[:], in_=bb[:], func=AF.Exp, scale=-1.0), R=[bbb], W=[benb])
                k.op("dve", lambda e, h=h: e.tensor_scalar(out=kk[:], in0=sig[:], scalar1=noml[:, h:h + 1], scalar2=oml[:, h:h + 1],
                                                           op0=ALU.mult, op1=ALU.add), R=[bsig, bc], W=[bkk])
                k.op("dve", lambda e, h=h: e.tensor_tensor(out=qd[:, h, :], in0=eb[:, h, :], in1=P.ps[pq][:], op=ALU.mult),
                     R=[beb[h], P.psb[pq]], W=[bqd[h]])
                k.op("pool", lambda e, h=h: e.tensor_tensor(out=qdA[:, h, :], in0=qd[:, h, :], in1=mA[:], op=ALU.mult),
                     R=[bqd[h], bc], W=[bqd[h]])
                k.op("pool", lambda e, h=h: e.tensor_tensor(out=qdB[:, h, :], in0=qd[:, h, :], in1=mB[:], op=ALU.mult),
                     R=[bqd[h], bc], W=[bqd[h]])
                k.op("dve", lambda e: e.tensor_tensor(out=kd32[:], in0=kk[:], in1=enb[:], op=ALU.mult), R=[bkk, benb], W=[bkd32])
                if HL < 1.18:
                    break
                k.op("pool", lambda e, h=h: e.tensor_copy(out=kd[:, h, :], in_=kd32[:]), R=[bkd32], W=[bkd[h]])
                k.op("pool", lambda e, h=h: e.tensor_tensor(
                    out=kbT[:, h, :].rearrange("p (n c) -> p n c", c=64), in0=kd32[:, :].rearrange("p (n c) -> p n c", c=64),
                    in1=eb[:, h, :].rearrange("p (n c) -> p n c", c=64)[:, :, 63:64].broadcast_to([128, 8, 64]), op=ALU.mult),
                    R=[bkd32, beb[h]], W=[bkbT[h]])
            for j in range(4):
                if HL < 2 or (HL < 90 and j > 0):
                    break
                tok = slice(j * 128, (j + 1) * 128)
                for kc in range(8):
                    k.op("pe", lambda e, kc=kc, tok=tok: e.matmul(P.ps[4][:], lhsT=hnT[:, kc, tok], rhs=win[:, kc, 1024:1536],
                                                                  start=(kc == 0), stop=(kc == 7)), R=[bc, bh], W=[P.psb[4]])
                for kc in range(8):
                    k.op("pe", lambda e, kc=kc, tok=tok: e.matmul(P.ps[5][:], lhsT=hnT[:, kc, tok], rhs=win[:, kc, 1536:2048],
                                                                  start=(kc == 0), stop=(kc == 7)), R=[bc, bh], W=[P.psb[5]])
                k.op("act", lambda e: e.copy(out=vtm[:], in_=P.ps[4][:]), R=[P.psb[4]], W=[bvtm])
                k.op("act", lambda e: e.activation(out=ngsg[:], in_=P.ps[5][:], func=AF.Silu), R=[P.psb[5]], W=[bngsg])
                k.op("pool", lambda e: e.tensor_tensor(out=ngsg[:], in0=ngsg[:], in1=ng[:], op=ALU.mult), R=[bngsg, bc], W=[bngsg])
                psK = P.ps[6][:].bitcast(BF16)
                for h in range(4):
                    k.op("pe", lambda e, h=h, tok=tok, psK=psK: e.transpose(out=psK[:, h * 128:(h + 1) * 128], in_=kbT[:, h, tok],
                                                                            identity=P.ident[:]), R=[bkbT[h], P.b_const], W=[P.psb[6]])
                k.op("act", lambda e, psK=psK: e.activation(out=kbtm[:], in_=psK[:, 0:512].rearrange("p (h d) -> p h d", h=4),
                                                            func=AF.Copy, scale=rmA[:, 0:1]), R=[P.psb[6], bc], W=[bkbtm])
                k.op("act", lambda e, psK=psK: e.activation(out=kbtmB[:], in_=psK[:, 0:512].rearrange("p (h d) -> p h d", h=4),
                                                            func=AF.Copy, scale=rmB[:, 0:1]), R=[P.psb[6], bc], W=[bkbtm])
                tA = j * 128 + 63
                tB = j * 128 + 127
                if HL < 3:
                    break
                for h in range(4):
                    pb = h
                    vh = vtm[:, h * 128:(h + 1) * 128]
                    k.op("pe", lambda e, h=h, tok=tok, pb=pb: e.matmul(P.ps[pb][:, 0:128], lhsT=kd[:, h, tok], rhs=qd[:, h, tok],
                                                                       start=True, stop=True), R=[bkd[h], bqd[h]], W=[P.psb[pb]])
                    k.op("pe", lambda e, h=h, pb=pb, vh=vh: e.matmul(P.ps[pb][:, 128:256], lhsT=kbtm[:, h, :], rhs=vh,
                                                                     start=True, stop=True), R=[bkbtm, bvtm], W=[P.psb[pb]])
                    k.op("pe", lambda e, h=h, pb=pb, vh=vh: e.matmul(P.ps[pb][:, 256:384], lhsT=kbtmB[:, h, :], rhs=vh,
                                                                     start=True, stop=True), R=[bkbtm, bvtm], W=[P.psb[pb]])
                    k.op("dve", lambda e, h=h, pb=pb: e.tensor_tensor(out=attm[:, h, :], in0=P.ps[pb][:, 0:128], in1=msk[:], op=ALU.mult),
                         R=[P.psb[pb], bc], W=[battm[h]])
                    k.op("dve", lambda e, h=h, pb=pb, tA=tA: e.scalar_tensor_tensor(
                        out=St[:, h, :], in0=St[:, h, :], scalar=eb[:, h, tA:tA + 1], in1=P.ps[pb][:, 128:256], op0=ALU.mult, op1=ALU.add),
                        R=[bS[h], beb[h], P.psb[pb]], W=[bS[h]])
                    k.op("act", lambda e, h=h: e.copy(out=Sbf[1][:, h, :], in_=St[:, h, :]), R=[bS[h]], W=[bSbf[1][h]])
                if HL < 4:
                    break
                for h in range(4):
                    pb = h
                    po = 4 + h % 2
                    vh = vtm[:, h * 128:(h + 1) * 128]
                    tokA = slice(j * 128, j * 128 + 64)
                    tokB = slice(j * 128 + 64, j * 128 + 128)
                    k.op("pe", lambda e, h=h, po=po, vh=vh: e.matmul(P.ps[po][:, 0:128], lhsT=attm[:, h, :], rhs=vh,
                                                                     start=True, stop=False), R=[battm[h], bvtm], W=[P.psb[po]])
                    k.op("pe", lambda e, h=h, po=po, tok=tok: e.matmul(P.ps[po][:, 0:128], lhsT=qdA[:, h, tok], rhs=Sbf[0][:, h, :],
                                                                       start=False, stop=False),
                         R=[bqd[h], bSbf[0][h]], W=[P.psb[po]])
                    k.op("pe", lambda e, h=h, po=po, tok=tok: e.matmul(P.ps[po][:, 0:128], lhsT=qdB[:, h, tok], rhs=Sbf[1][:, h, :],
                                                                       start=False, stop=True),
                         R=[bqd[h], bSbf[1][h]], W=[P.psb[po]])
                    k.op("dve", lambda e, h=h, pb=pb, tB=tB: e.scalar_tensor_tensor(
                        out=St[:, h, :], in0=St[:, h, :], scalar=eb[:, h, tB:tB + 1], in1=P.ps[pb][:, 256:384], op0=ALU.mult, op1=ALU.add),
                        R=[bS[h], beb[h], P.psb[pb]], W=[bS[h]])
                    k.op("act", lambda e, h=h: e.copy(out=Sbf[0][:, h, :], in_=St[:, h, :]), R=[bS[h]], W=[bSbf[0][h]])
                    k.op("act", lambda e, h=h, po=po: e.activation(out=junk[:], in_=P.ps[po][:, 0:128], func=AF.Square,
                                                                   accum_out=sst[:, h:h + 1]), R=[P.psb[po]], W=[bjunk, bsst[h]])
                    k.op("act", lambda e, h=h: e.activation(out=sst[:, 4 + h:5 + h], in_=sst[:, h:h + 1], func=AF.Sqrt, bias=P.epsb[:],
                                                            scale=1.0 / 128), R=[bsst[h], P.b_const], W=[bsst[h]])
                    k.op("dve", lambda e, h=h: e.reciprocal(out=sst[:, 8 + h:9 + h], in_=sst[:, 4 + h:5 + h]), R=[bsst[h]], W=[bsst[h]])
                    k.op("dve", lambda e, h=h, po=po: e.scalar_tensor_tensor(
                        out=yatm[:, h * 128:(h + 1) * 128], in0=P.ps[po][:, 0:128], scalar=sst[:, 8 + h:9 + h],
                        in1=ngsg[:, h * 128:(h + 1) * 128], op0=ALU.mult, op1=ALU.mult),
                        R=[P.psb[po], bsst[h], bngsg], W=[byatm])
                if HL < 5:
                    break
                psY = P.ps[6][:].bitcast(BF16)
                for h in range(4):
                    k.op("pe", lambda e, h=h, psY=psY: e.transpose(out=psY[:, h * 128:(h + 1) * 128], in_=yatm[:, h * 128:(h + 1) * 128],
                                                                   identity=P.ident[:]), R=[byatm, P.b_const], W=[P.psb[6]])
                k.op("dve", lambda e, psY=psY, tok=tok: e.tensor_copy(out=yaT[:, :, tok], in_=psY[:, 0:512].rearrange("p (h t) -> p h t", h=4)),
                     R=[P.psb[6]], W=[byaT])
            if HL < 6:
                break
            k.dma("sp", [(ycatT[0:512, c0:c0 + TB].rearrange("(c p) t -> p c t", p=128), yaT[:])], R=[byaT], W=[P.b_ya[blk]], key="d_yast")
        k.barrier()


def s5_phase(P, W):
    k, nc = P.k, P.nc
    I = P.inp
    L = 16
    NCH = S // L
    G = 32
    with ExitStack() as es:
        bs = k.buf("s5_tab")

        def dv(fn, R=(), Wb=()):
            k.op("dve", fn, R=[bs] + list(R), W=[bs] + list(Wb))

        def ac(fn, R=(), Wb=()):
            k.op("act", fn, R=[bs] + list(R), W=[bs] + list(Wb))

        def t32(name, shape=(128, 32)):
            return P.sb(es, "s5_" + name, list(shape), F32)

        AR, AI, LDT = t32("AR"), t32("AI"), t32("LDT")
        k.dma("sp", [(AR[0:64, :], I["s5_a_re"].rearrange("g p -> p g")), (AR[64:128, :], I["s5_a_re"].rearrange("g p -> p g")),
                     (AI[0:64, :], I["s5_a_im"].rearrange("g p -> p g")), (AI[64:128, :], I["s5_a_im"].rearrange("g p -> p g")),
                     (LDT[:], I["s5_log_dt"].partition_broadcast(128))], W=[bs], allow_slow_non_contiguous=True)
        B1 = t32("B1", (128, 32, 16))
        B2 = t32("B2", (128, 32, 16))
        k.dma("sp", [(B1[0:64], I["s5_b_re"].rearrange("g p h -> p g h")), (B1[64:128], I["s5_b_im"].rearrange("g p h -> p g h")),
                     (B2[0:64], I["s5_b_im"].rearrange("g p h -> p g h")), (B2[64:128], I["s5_b_re"].rearrange("g p h -> p g h"))],
              W=[bs], key="d_s5_b")
        CN1 = t32("CN1", (128, 4, 128))
        CN2 = t32("CN2", (128, 4, 128))
        cre = I["s5_c_re"].rearrange("(f a) h p -> (a h) f p", f=4)
        cim = I["s5_c_im"].rearrange("(f a) h p -> (a h) f p", f=4)
        k.dma("sp", [(CN1[:, :, 0:64], cre), (CN1[:, :, 64:128], cim), (CN2[:, :, 0:64], cim), (CN2[:, :, 64:128], cre)],
              W=[bs], key="d_s5_c")
        dT = load_vec_fm(P, es, "s5_dT", I["s5_d"], 4, bs)
        bgl = load_vec_fm(P, es, "s5_bglu", I["s5_b_glu"], 4, bs)
        sgn = t32("sgn", (128, 1))
        dv(lambda e: e.memset(sgn[0:64, :], -1.0))
        dv(lambda e: e.memset(sgn[64:128, :], 1.0))
        DT, XR, TH = t32("DT"), t32("XR"), t32("TH")
        ac(lambda e: e.activation(out=DT[:], in_=LDT[:], func=AF.Exp))
        dv(lambda e: e.tensor_tensor(out=XR[:], in0=AR[:], in1=DT[:], op=ALU.mult))
        dv(lambda e: e.tensor_tensor(out=TH[:], in0=AI[:], in1=DT[:], op=ALU.mult))
        MAG = t32("MAG")
        ac(lambda e: e.activation(out=MAG[:], in_=XR[:], func=AF.Exp))
        tf, r1, r2 = t32("tf"), t32("r1"), t32("r2")
        ti_ = P.sb(es, "s5_ti", [128, 32], I32)
        dv(lambda e: e.tensor_scalar(out=tf[:], in0=TH[:], scalar1=1.0 / TWO_PI, scalar2=None, op0=ALU.mult))
        dv(lambda e: e.tensor_copy(out=ti_[:], in_=tf[:]))
        dv(lambda e: e.tensor_copy(out=tf[:], in_=ti_[:]))
        dv(lambda e: e.scalar_tensor_tensor(out=r1[:], in0=tf[:], scalar=-CW1, in1=TH[:], op0=ALU.mult, op1=ALU.add))
        dv(lambda e: e.scalar_tensor_tensor(out=r1[:], in0=tf[:], scalar=-CW2, in1=r1[:], op0=ALU.mult, op1=ALU.add))
        dv(lambda e: e.tensor_scalar(out=r1[:], in0=r1[:], scalar1=-math.pi, scalar2=math.pi, op0=ALU.max, op1=ALU.min))
        SN, CS = t32("SN"), t32("CS")
        ac(lambda e: e.activation(out=SN[:], in_=r1[:], func=AF.Sin))
        dv(lambda e: e.tensor_scalar(out=tf[:], in0=r1[:], scalar1=math.pi / 2, scalar2=math.pi, op0=ALU.add, op1=ALU.is_gt))
        dv(lambda e: e.scalar_tensor_tensor(out=tf[:], in0=tf[:], scalar=-TWO_PI, in1=r1[:], op0=ALU.mult, op1=ALU.add))
        dv(lambda e: e.tensor_scalar(out=tf[:], in0=tf[:], scalar1=math.pi / 2, scalar2=math.pi, op0=ALU.add, op1=ALU.min))
        ac(lambda e: e.activation(out=CS[:], in_=tf[:], func=AF.Sin))
        pows = list(range(0, 17)) + [16 << i for i in range(1, 8)]
        TR = {n: t32(f"TR{n}") for n in pows}
        TI = {n: t32(f"TI{n}") for n in pows}
        dv(lambda e: e.memset(TR[0][:], 1.0))
        dv(lambda e: e.memset(TI[0][:], 0.0))
        dv(lambda e: e.tensor_tensor(out=TR[1][:], in0=MAG[:], in1=CS[:], op=ALU.mult))
        dv(lambda e: e.tensor_tensor(out=TI[1][:], in0=MAG[:], in1=SN[:], op=ALU.mult))
        ta, tb = t32("ta"), t32("tb")

        def cmul(oR, oI, aR, aI, bR, bI):
            dv(lambda e: e.tensor_tensor(out=ta[:], in0=aI[:], in1=bI[:], op=ALU.mult))
            dv(lambda e: e.tensor_tensor(out=tb[:], in0=aR[:], in1=bI[:], op=ALU.mult))
            dv(lambda e: e.tensor_tensor(out=oR[:], in0=aR[:], in1=bR[:], op=ALU.mult))
            dv(lambda e: e.tensor_tensor(out=oR[:], in0=oR[:], in1=ta[:], op=ALU.subtract))
            dv(lambda e: e.tensor_tensor(out=oI[:], in0=aI[:], in1=bR[:], op=ALU.mult))
            dv(lambda e: e.tensor_tensor(out=oI[:], in0=oI[:], in1=tb[:], op=ALU.add))
        for n in range(2, 17):
            cmul(TR[n], TI[n], TR[n - 1], TI[n - 1], TR[1], TI[1])
        for i in range(1, 8):
            n = 16 << i
            cmul(TR[n], TI[n], TR[n // 2], TI[n // 2], TR[n // 2], TI[n // 2])
        den, xr_, cr_, ci_ = t32("den"), t32("xr"), t32("cr"), t32("ci")
        dv(lambda e: e.tensor_tensor(out=den[:], in0=AR[:], in1=AR[:], op=ALU.mult))
        dv(lambda e: e.tensor_tensor(out=ta[:], in0=AI[:], in1=AI[:], op=ALU.mult))
        dv(lambda e: e.tensor_tensor(out=den[:], in0=den[:], in1=ta[:], op=ALU.add))
        dv(lambda e: e.reciprocal(out=den[:], in_=den[:]))
        dv(lambda e: e.tensor_scalar(out=xr_[:], in0=TR[1][:], scalar1=-1.0, scalar2=None, op0=ALU.add))
        dv(lambda e: e.tensor_tensor(out=cr_[:], in0=xr_[:], in1=AR[:], op=ALU.mult))
        dv(lambda e: e.tensor_tensor(out=ta[:], in0=TI[1][:], in1=AI[:], op=ALU.mult))
        dv(lambda e: e.tensor_tensor(out=cr_[:], in0=cr_[:], in1=ta[:], op=ALU.add))
        dv(lambda e: e.tensor_tensor(out=cr_[:], in0=cr_[:], in1=den[:], op=ALU.mult))
        dv(lambda e: e.tensor_tensor(out=ci_[:], in0=TI[1][:], in1=AR[:], op=ALU.mult))
        dv(lambda e: e.tensor_tensor(out=ta[:], in0=xr_[:], in1=AI[:], op=ALU.mult))
        dv(lambda e: e.tensor_tensor(out=ci_[:], in0=ci_[:], in1=ta[:], op=ALU.subtract))
        dv(lambda e: e.tensor_tensor(out=ci_[:], in0=ci_[:], in1=den[:], op=ALU.mult))
        cis = t32("cis")
        dv(lambda e: e.tensor_scalar(out=cis[:], in0=ci_[:], scalar1=sgn[:, 0:1], scalar2=None, op0=ALU.mult))
        ncis = t32("ncis")
        dv(lambda e: e.tensor_scalar(out=ncis[:], in0=cis[:], scalar1=-1.0, scalar2=None, op0=ALU.mult))

        def bc16(t):
            return t[:, :].unsqueeze(2).broadcast_to([128, 32, 16])
        BB = t32("BB", (128, 32, 16))
        BBs = t32("BBs", (128, 32, 16))
        tmp3 = t32("tmp3", (128, 32, 16))
        dv(lambda e: e.tensor_tensor(out=BB[:], in0=B1[:], in1=bc16(cr_), op=ALU.mult))
        dv(lambda e: e.tensor_tensor(out=tmp3[:], in0=B2[:], in1=bc16(cis), op=ALU.mult))
        dv(lambda e: e.tensor_tensor(out=BB[:], in0=BB[:], in1=tmp3[:], op=ALU.add))
        dv(lambda e: e.tensor_tensor(out=BBs[:], in0=B2[:], in1=bc16(cr_), op=ALU.mult))
        dv(lambda e: e.tensor_tensor(out=tmp3[:], in0=B1[:], in1=bc16(ncis), op=ALU.mult))
        dv(lambda e: e.tensor_tensor(out=BBs[:], in0=BBs[:], in1=tmp3[:], op=ALU.add))
        CT1 = t32("CT1", (128, 32, 16))
        CT2 = t32("CT2", (128, 32, 16))
        for f in range(4):
            for (CN, CT) in ((CN1, CT1), (CN2, CT2)):
                k.op("pe", lambda e, f=f, CN=CN: e.transpose(out=P.ps[0][:, 0:128], in_=CN[:, f, :], identity=P.ident_f[:]),
                     R=[bs, P.b_const], W=[P.psb[0]])
                dv(lambda e, f=f, CT=CT: e.tensor_copy(out=CT[:, 8 * f:8 * f + 8, :].rearrange("p g h -> p (g h)"), in_=P.ps[0][:, 0:128]),
                   R=[P.psb[0]])
        CTn = t32("CTn", (128, 32, 16))
        dv(lambda e: e.tensor_scalar(out=CTn[:], in0=CT1[:], scalar1=sgn[:, 0:1], scalar2=-1.0, op0=ALU.mult, op1=ALU.mult))
        pi_ = P.sb(es, "s5_pi", [128, 1], I32)
        k.op("pool", lambda e: e.iota(out=pi_[:], pattern=[[0, 1]], base=0, channel_multiplier=1), R=[bs], W=[bs])
        pg_i = P.sb(es, "s5_pgi", [128, 1], I32)
        om_i = P.sb(es, "s5_omi", [128, 1], I32)
        dv(lambda e: e.tensor_scalar(out=pg_i[:], in0=pi_[:], scalar1=4, scalar2=None, op0=ALU.arith_shift_right))
        dv(lambda e: e.tensor_scalar(out=om_i[:], in0=pg_i[:], scalar1=1, scalar2=None, op0=ALU.bitwise_and))
        pgf, om, em = t32("pgf", (128, 1)), t32("om", (128, 1)), t32("em", (128, 1))
        dv(lambda e: e.tensor_copy(out=pgf[:], in_=pg_i[:]))
        dv(lambda e: e.tensor_copy(out=om[:], in_=om_i[:]))
        dv(lambda e: e.tensor_scalar(out=em[:], in0=om[:], scalar1=-1.0, scalar2=1.0, op0=ALU.mult, op1=ALU.add))
        ji = P.sb(es, "s5_ji", [128, 128], I32)
        k.op("pool", lambda e: e.iota(out=ji[:], pattern=[[1, 128]], base=0, channel_multiplier=0), R=[bs], W=[bs])
        dv(lambda e: e.tensor_scalar(out=ji[:], in0=ji[:], scalar1=4, scalar2=None, op0=ALU.arith_shift_right))
        bmask = t32("bmask", (128, 128))
        dv(lambda e: e.tensor_copy(out=bmask[:], in_=ji[:]))
        dv(lambda e: e.tensor_scalar(out=bmask[:], in0=bmask[:], scalar1=pgf[:, 0:1], scalar2=None, op0=ALU.is_equal))
        esw = t32("esw", (128, 128))
        dv(lambda e: e.memset(esw[:], 0.0))
        dv(lambda e: e.tensor_copy(out=esw[0:64, 64:128], in_=P.ident_f[0:64, 0:64]), R=[P.b_const])
        dv(lambda e: e.tensor_copy(out=esw[64:128, 0:64], in_=P.ident_f[64:128, 64:128]), R=[P.b_const])

        uT = P.sb(es, "s5_uT", [128, 4, S], BF16)
        buT = k.bufs(NB, "s5_uT")
        for b in range(NB):
            c0 = b * TB
            k.dma("sp", [(uT[:, :, c0:c0 + TB], P.uT_d[:, c0:c0 + TB].rearrange("(c p) t -> p c t", p=128))],
                  R=[P.b_uT[b]], W=[buT[b]], key="d_s5uT")
        Xb = P.sb(es, "s5_Xb", [128, G, NCH], BF16)
        bXb = k.bufs(G, "s5_Xb")
        Kb = P.sb(es, "s5_Kb", [128, 4, L, 128], BF16)
        bKb = k.buf("s5_Kb")

        def mb_table(n, MB):
            tis = t32(f"tis_{n}") if False else ta
            dv(lambda e: e.tensor_scalar(out=ta[:], in0=TI[n][:], scalar1=sgn[:, 0:1], scalar2=None, op0=ALU.mult))
            dv(lambda e: e.tensor_tensor(out=MB[:], in0=BB[:], in1=bc16(TR[n]), op=ALU.mult))
            dv(lambda e: e.tensor_tensor(out=tmp3[:], in0=BBs[:], in1=bc16(ta), op=ALU.mult))
            dv(lambda e: e.tensor_tensor(out=MB[:], in0=MB[:], in1=tmp3[:], op=ALU.add))

        with ExitStack() as es2:
            Pm = P.sb(es2, "s5_Pm", [128, 4, L, 2, 128], BF16)
            bPm = k.buf("s5_Pm")
            MB = P.sb(es2, "s5_MB", [128, 32, 16], F32)
            for n in range(L):
                mb_table(n, MB)
                j = L - 1 - n
                for f in range(4):
                    mbf = MB[:, 8 * f:8 * f + 8, :].rearrange("p g h -> p (g h)")
                    k.op("pe", lambda e, mbf=mbf: e.transpose(out=P.ps[0][:, 0:128], in_=mbf, identity=P.ident_f[:]),
                         R=[bs, P.b_const], W=[P.psb[0]])
                    k.op("dve", lambda e, f=f, j=j: e.tensor_scalar(out=Pm[:, f, j, 0, :], in0=P.ps[0][:, 0:128], scalar1=em[:, 0:1],
                                                                    scalar2=None, op0=ALU.mult), R=[P.psb[0], bs], W=[bPm])
                    k.op("dve", lambda e, f=f, j=j: e.tensor_scalar(out=Pm[:, f, j, 1, :], in0=P.ps[0][:, 0:128], scalar1=om[:, 0:1],
                                                                    scalar2=None, op0=ALU.mult), R=[P.psb[0], bs], W=[bPm])
                    ctf = CTn[:, 8 * f:8 * f + 8, :].rearrange("p g h -> p (g h)")
                    k.op("pe", lambda e, mbf=mbf, ctf=ctf: e.matmul(P.ps[1][:, 0:128], lhsT=mbf, rhs=ctf, start=True, stop=True),
                         R=[bs], W=[P.psb[1]])
                    if n == 0:
                        k.op("dve", lambda e: e.tensor_tensor(out=tmpK[:], in0=P.ps[1][:, 0:128], in1=bmask[:], op=ALU.mult),
                             R=[P.psb[1], bs], W=[bs]) if False else None
                    k.op("dve", lambda e, f=f, n=n: e.tensor_tensor(out=Kb[:, f, n, :], in0=P.ps[1][:, 0:128], in1=bmask[:], op=ALU.mult),
                         R=[P.psb[1], bs], W=[bKb])
                    if n == 0:
                        k.op("dve", lambda e, f=f: e.scalar_tensor_tensor(out=Kb[:, f, 0, :], in0=P.ident_f[:], scalar=dT[:, f:f + 1],
                                                                          in1=Kb[:, f, 0, :], op0=ALU.mult, op1=ALU.add),
                             R=[bKb, bs, P.b_const], W=[bKb])
            NBG = 4
            Xs = [P.sb(es2, f"s5_Xs{i}", [128, NCH], F32) for i in range(NBG)]
            bXs = k.bufs(NBG, "s5_Xs")
            NMT = 8
            MT = [P.sb(es2, f"s5_MT{i}", [128, 128], F32) for i in range(NMT)]
            bMT = k.bufs(NMT, "s5_MT")
            tM = [P.sb(es2, f"s5_tM{i}", [128, 128], F32) for i in range(4)]
            btM = k.bufs(4, "s5_tM")
            w2 = P.sb(es2, "s5_w2", [128, 32, 8], F32)
            for i in range(8):
                n = 16 << i
                dv(lambda e, i=i, n=n: e.tensor_scalar(out=w2[:, :, i], in0=TI[n][:], scalar1=sgn[:, 0:1], scalar2=-1.0,
                                                       op0=ALU.mult, op1=ALU.mult))
            nmt = 0
            for g0 in range(0, G, NBG):
                gs = list(range(g0, g0 + NBG))
                for g in gs:
                    f, gq = g // 8, (g % 8) // 2
                    m = g % 2
                    vb = g % NBG
                    for j in range(L):
                        k.op("pe", lambda e, f=f, gq=gq, m=m, j=j, vb=vb: e.matmul(
                            P.ps[vb][:, 0:NCH], lhsT=Pm[32 * gq:32 * gq + 32, f, j, m, :],
                            rhs=uT[32 * gq:32 * gq + 32, f, :].rearrange("p (c l) -> p l c", l=L)[:, j, :],
                            start=(j == 0), stop=(j == L - 1), tile_position=(32 * gq, 0)),
                            R=[bPm] + buT, W=[P.psb[vb]])
                for g in gs:
                    vb = g % NBG
                    X = Xs[g % NBG]
                    k.op("act", lambda e, X=X, vb=vb: e.copy(out=X[:], in_=P.ps[vb][:, 0:NCH]), R=[P.psb[vb]], W=[bXs[g % NBG]])
                for i in range(8):
                    sh = 1 << i
                    n = 16 << i
                    for g in gs:
                        X = Xs[g % NBG]
                        mt = MT[nmt % NMT]
                        bmt = bMT[nmt % NMT]
                        tm_ = tM[nmt % 4]
                        btm = btM[nmt % 4]
                        nmt += 1
                        k.op("act", lambda e, mt=mt, g=g, n=n: e.activation(out=mt[:], in_=P.ident_f[:], func=AF.Copy, scale=TR[n][:, g:g + 1]),
                             R=[bs, P.b_const], W=[bmt])
                        k.op("dve", lambda e, mt=mt, g=g, i=i: e.scalar_tensor_tensor(out=mt[:], in0=esw[:], scalar=w2[:, g, i:i + 1], in1=mt[:],
                                                                                    op0=ALU.mult, op1=ALU.add), R=[bs, bmt], W=[bmt])
                        pb = 4 + g % NBG
                        k.op("pe", lambda e, mt=mt, X=X, sh=sh, pb=pb: e.matmul(P.ps[pb][:, sh:NCH], lhsT=mt[:], rhs=X[:, 0:NCH - sh],
                                                                                start=True, stop=True), R=[bmt, bXs[g % NBG]], W=[P.psb[pb]])
                        k.op("dve", lambda e, X=X, sh=sh, pb=pb: e.tensor_tensor(out=X[:, sh:NCH], in0=X[:, sh:NCH], in1=P.ps[pb][:, sh:NCH],
                                                                                 op=ALU.add), R=[P.psb[pb], bXs[g % NBG]], W=[bXs[g % NBG]])
                for g in gs:
                    X = Xs[g % NBG]
                    k.op("pool", lambda e, g=g: e.memset(Xb[:, g, 0:1], 0.0), W=[bXb[g]])
                    k.op("act", lambda e, g=g, X=X: e.copy(out=Xb[:, g, 1:NCH], in_=X[:, 0:NCH - 1]), R=[bXs[g % NBG]], W=[bXb[g]])
            k.barrier()
        with ExitStack() as es3:
            Qm = P.sb(es3, "s5_Qm", [128, L, 16, 2, 32], BF16)
            bQm = k.buf("s5_Qm")
            Qt = P.sb(es3, "s5_Qt", [128, 32, 16], F32)
            trs = P.sb(es3, "s5_trs", [128, 32], F32)
            nti = P.sb(es3, "s5_nti", [128, 32], F32)
            mpat = P.sb(es3, "s5_mpat", [128, 2, 2, 16], F32)
            dv(lambda e: e.memset(mpat[:], 0.0))
            dv(lambda e: e.memset(mpat[:, 0, 0, :], 1.0))
            dv(lambda e: e.memset(mpat[:, 1, 1, :], 1.0))
            for tau in range(L):
                n = tau + 1
                dv(lambda e, n=n: e.tensor_scalar(out=trs[:], in0=TR[n][:], scalar1=sgn[:, 0:1], scalar2=-1.0, op0=ALU.mult, op1=ALU.mult))
                dv(lambda e, n=n: e.tensor_scalar(out=nti[:], in0=TI[n][:], scalar1=-1.0, scalar2=None, op0=ALU.mult))
                dv(lambda e: e.tensor_tensor(out=Qt[:], in0=CT1[:], in1=bc16(trs), op=ALU.mult))
                dv(lambda e: e.tensor_tensor(out=tmp3[:], in0=CT2[:], in1=bc16(nti), op=ALU.mult))
                dv(lambda e: e.tensor_tensor(out=Qt[:], in0=Qt[:], in1=tmp3[:], op=ALU.add))
                for mem in range(2):
                    k.op("dve", lambda e, tau=tau, mem=mem: e.tensor_tensor(
                        out=Qm[:, tau, :, mem, :], in0=Qt[:, :, :].rearrange("p (q m) h -> p q (m h)", m=2),
                        in1=mpat[:, mem, :, :].rearrange("p m h -> p (m h)").unsqueeze(1).broadcast_to([128, 16, 32]), op=ALU.mult),
                        R=[bs], W=[bQm])
            zT = P.sb(es3, "s5_zT", [128, 4, S], BF16)
            bz = k.bufs(4, "s5_zT")
            ys = [P.sb(es3, f"s5_ys{i}", [128, NCH], F32) for i in range(2)]
            bys = k.bufs(2, "s5_ys")
            yv = [P.sb(es3, f"s5_yv{i}", [128, NCH], F32) for i in range(2)]
            byv = k.bufs(2, "s5_yv")
            y2 = [P.sb(es3, f"s5_y2{i}", [128, NCH], F32) for i in range(2)]
            by2 = k.bufs(2, "s5_y2")
            cnt = 0
            for f in range(4):
                ul = uT[:, f, :].rearrange("p (c l) -> p l c", l=L)
                for tau in range(L):
                    i2 = cnt % 2
                    pb = cnt % 4
                    cnt += 1
                    for q in range(4):
                        for mem in range(2):
                            g = f * 8 + q * 2 + mem
                            k.op("pe", lambda e, q=q, mem=mem, g=g, tau=tau, pb=pb, f=f: e.matmul(
                                P.ps[pb][32 * q:32 * q + 32, 256:512], lhsT=Qm[:, tau, f * 4 + q, mem, :], rhs=Xb[:, g, :],
                                start=(mem == 0), stop=(mem == 1), tile_position=(0, 32 * q)),
                                R=[bQm, bXb[g]], W=[P.psb[pb]])
                    k.op("act", lambda e, pb=pb, i2=i2: e.copy(out=ys[i2][:], in_=P.ps[pb][:, 256:512]), R=[P.psb[pb]], W=[bys[i2]])
                    for j in range(tau + 1):
                        k.op("pe", lambda e, f=f, tau=tau, j=j, pb=pb, ul=ul: e.matmul(
                            P.ps[pb][:, 0:256], lhsT=Kb[:, f, tau - j, :], rhs=ul[:, j, :], start=(j == 0), stop=(j == tau)),
                            R=[bKb] + buT, W=[P.psb[pb]])
                    Y = yv[i2]
                    k.op("dve", lambda e, pb=pb, i2=i2, Y=Y: e.tensor_tensor(out=Y[:], in0=ys[i2][:], in1=P.ps[pb][:, 0:256], op=ALU.add),
                         R=[bys[i2], P.psb[pb]], W=[byv[i2]])
                    Y2 = y2[i2]
                    k.op("pool", lambda e, Y=Y, Y2=Y2: e.tensor_tensor(out=Y2[:], in0=Y[:], in1=Y[:], op=ALU.mult), R=[byv[i2]], W=[by2[i2]])
                    k.op("pool", lambda e, Y2=Y2: e.tensor_scalar(out=Y2[:], in0=Y2[:], scalar1=0.044715, scalar2=1.0, op0=ALU.mult, op1=ALU.add),
                         R=[by2[i2]], W=[by2[i2]])
                    k.op("dve", lambda e, Y=Y, Y2=Y2: e.tensor_tensor(out=Y2[:], in0=Y2[:], in1=Y[:], op=ALU.mult), R=[by2[i2], byv[i2]], W=[by2[i2]])
                    k.op("act", lambda e, Y2=Y2: e.activation(out=Y2[:], in_=Y2[:], func=AF.Sigmoid, scale=1.5957691216057308),
                         R=[by2[i2]], W=[by2[i2]])
                    k.op("dve", lambda e, Y=Y, Y2=Y2, f=f, tau=tau: e.tensor_tensor(
                        out=zT[:, f, :].rearrange("p (c l) -> p l c", l=L)[:, tau, :], in0=Y[:], in1=Y2[:], op=ALU.mult),
                        R=[by2[i2], byv[i2]], W=[bz[f]])
            wgl = P.sb(es3, "s5_wgl", [128, 4, 512], BF16)
            bwgl = k.buf("s5_wgl")
            k.dma("sp", [(wgl[:], W["s5_w_glu"][0].rearrange("(c p) n -> p c n", p=128))], R=[W["s5_w_glu"][1]], W=[bwgl])
            sg = [P.sb(es3, f"s5_sg{i}", [128, TB], F32) for i in range(2)]
            bsg = k.bufs(2, "s5_sg")
            ybT = [P.sb(es3, f"s5_ybT{i}", [128, 4, TB], BF16) for i in range(2)]
            bybT = k.bufs(2, "s5_ybT")
            cnt = 0
            for blk in range(NB):
                c0 = blk * TB
                for fo in range(4):
                    pb = 4 + cnt % 2
                    i2 = cnt % 2
                    cnt += 1
                    for fi in range(4):
                        k.op("pe", lambda e, fo=fo, fi=fi, pb=pb, c0=c0: e.matmul(
                            P.ps[pb][:], lhsT=wgl[:, fi, fo * 128:(fo + 1) * 128], rhs=zT[:, fi, c0:c0 + TB],
                            start=(fi == 0), stop=(fi == 3)), R=[bwgl] + bz, W=[P.psb[pb]])
                    k.op("act", lambda e, fo=fo, pb=pb, i2=i2: e.activation(out=sg[i2][:], in_=P.ps[pb][:], func=AF.Sigmoid,
                                                                           bias=bgl[:, fo:fo + 1], scale=1.0), R=[P.psb[pb], bs], W=[bsg[i2]])
                    k.op("dve", lambda e, fo=fo, i2=i2, blk=blk, c0=c0: e.tensor_tensor(
                        out=ybT[blk % 2][:, fo, :], in0=sg[i2][:], in1=zT[:, fo, c0:c0 + TB], op=ALU.mult),
                        R=[bsg[i2]] + bz, W=[bybT[blk % 2]])
                k.dma("sp", [(P.ycatT[512:1024, c0:c0 + TB].rearrange("(c p) t -> p c t", p=128), ybT[blk % 2][:])],
                      R=[bybT[blk % 2]], W=[P.b_yb[blk]], key=f"d_ybT{blk % 2}")
            k.barrier()


def build(stages=("all",), debug=()):
    P = Prog(debug)
    k = P.k
    I = {}
    P.inp = I
    I["x"] = P.din("x", [S, D])
    I["positions"] = P.din("positions", [S], I32)
    for nm, shp in [("norm_mix_g", [2, D]), ("norm_ffn_g", [2, D]), ("final_norm_g", [D]),
                    ("even_w_in", [D, 2560]), ("hgrn_lb_logits", [2, 512]), ("hgrn_norm_g", [512]),
                    ("s5_a_re", [32, 64]), ("s5_a_im", [32, 64]), ("s5_log_dt", [32]),
                    ("s5_b_re", [32, 64, 16]), ("s5_b_im", [32, 64, 16]),
                    ("s5_c_re", [32, 16, 64]), ("s5_c_im", [32, 16, 64]),
                    ("s5_d", [512]), ("s5_w_glu", [512, 512]), ("s5_b_glu", [512]),
                    ("even_w_out", [D, D]), ("odd_w_in", [D, 704]), ("mla_q_norm_g", [384]),
                    ("mla_w_uq", [384, 1536]), ("mla_kv_norm_g", [256]), ("mla_w_ukv", [256, 2048]),
                    ("odd_w_out", [D, D]), ("ffn_w_in", [2, D, 2 * DFF]), ("ffn_conv_w", [2, 3, DFF]),
                    ("ffn_conv_b", [2, DFF]), ("ffn_w_out", [2, DFF, D])]:
        I[nm] = P.din(nm, shp)
    I["rope_freq"] = P.din("rope_freq", [32])
    out = P.dout("out", [S, D])
    P.b_out = k.bufs(32, "out")
    P.b_x = k.bufs(32, "x")
    setup_common(P)
    if "ffn1" in stages:
        w_in_bf, b1 = cast_weight(P, "w_ffn_in1", I["ffn_w_in"][1], D, 2 * DFF)
        w_out_bf, b2 = cast_weight(P, "w_ffn_out1", I["ffn_w_out"][1], DFF, D)
        ffn_phase(P, 1, I["x"], P.b_x, out, P.b_out, w_in_bf, b1, w_out_bf, b2, final_norm=I["final_norm_g"])
    W = {}
    if "hgrn" in stages:
        W["even_w_in"] = cast_weight(P, "bf_even_w_in", I["even_w_in"], D, 2560)
        hgrn_phase(P, I["x"], P.b_x, W)
    if "s5" in stages:
        W["s5_w_glu"] = cast_weight(P, "bf_s5_w_glu", I["s5_w_glu"], 512, 512)
        s5_phase(P, W)
    if "mla" in stages:
        for nm, r, c in [("odd_w_in", D, 704), ("mla_w_uq", 384, 1536), ("mla_w_ukv", 256, 2048), ("odd_w_out", D, D)]:
            W[nm] = cast_weight(P, "bf_" + nm, I[nm], r, c)
        mla_proj_phase(P, I["x"], P.b_x, W)
        if "mla_pd_only" not in stages:
            mla_attn_phase(P, I["x"], P.b_x, out, P.b_out, W)
    if "all" in stages:
        W["even_w_in"] = cast_weight(P, "bf_even_w_in", I["even_w_in"], D, 2560)
        W["s5_w_glu"] = cast_weight(P, "bf_s5_w_glu", I["s5_w_glu"], 512, 512)
        W["even_w_out"] = cast_weight(P, "bf_even_w_out", I["even_w_out"], D, D)
        W["ffn_in0"] = cast_weight(P, "bf_ffn_in0", I["ffn_w_in"][0], D, 2 * DFF)
        W["ffn_out0"] = cast_weight(P, "bf_ffn_out0", I["ffn_w_out"][0], DFF, D)
        for nm, r, c in [("odd_w_in", D, 704), ("mla_w_uq", 384, 1536), ("mla_w_ukv", 256, 2048), ("odd_w_out", D, D)]:
            W[nm] = cast_weight(P, "bf_" + nm, I[nm], r, c)
        W["ffn_in1"] = cast_weight(P, "bf_ffn_in1", I["ffn_w_in"][1], D, 2 * DFF)
        W["ffn_out1"] = cast_weight(P, "bf_ffn_out1", I["ffn_w_out"][1], DFF, D)
        h2 = P.dint("h2", [S, D], F32)
        h3 = P.dint("h3", [S, D], F32)
        b_h2 = k.bufs(32, "h2")
        b_h3 = k.bufs(32, "h3")
        hgrn_phase(P, I["x"], P.b_x, W)
        s5_phase(P, W)
        ffn_phase(P, 0, I["x"], P.b_x, h2, b_h2, W["ffn_in0"][0], W["ffn_in0"][1], W["ffn_out0"][0], W["ffn_out0"][1],
                  premix=(P.ycatT, P.b_ya, P.b_yb, W["even_w_out"][0], W["even_w_out"][1]))
        mla_proj_phase(P, h2, b_h2, W)
        mla_attn_phase(P, h2, b_h2, h3, b_h3, W)
        ffn_phase(P, 1, h3, b_h3, out, P.b_out, W["ffn_in1"][0], W["ffn_in1"][1], W["ffn_out1"][0], W["ffn_out1"][1],
                  final_norm=I["final_norm_g"])
    k.barrier()
    k.finalize(P.es)
    P.es.close()
    return P


INPUT_ORDER = ["x", "positions", "norm_mix_g", "norm_ffn_g", "final_norm_g", "even_w_in", "hgrn_lb_logits",
               "hgrn_norm_g", "s5_a_re", "s5_a_im", "s5_log_dt", "s5_b_re", "s5_b_im", "s5_c_re", "s5_c_im",
               "s5_d", "s5_w_glu", "s5_b_glu", "even_w_out", "odd_w_in", "mla_q_norm_g", "mla_w_uq",
               "mla_kv_norm_g", "mla_w_ukv", "odd_w_out", "ffn_w_in", "ffn_conv_w", "ffn_conv_b", "ffn_w_out"]


def make_in_maps(inputs):
    shared = {}
    for nm in INPUT_ORDER:
        if nm in ("x", "positions"):
            continue
        a = np.ascontiguousarray(np.asarray(inputs[nm]))
        if nm in ("norm_mix_g", "norm_ffn_g", "hgrn_lb_logits", "ffn_w_in", "ffn_conv_w", "ffn_conv_b",
                  "ffn_w_out", "final_norm_g"):
            shared[nm] = a
        else:
            shared[nm] = np.ascontiguousarray(a[0])
    shared["rope_freq"] = (10000.0 ** (-np.arange(0, 64, 2, dtype=np.float32) / np.float32(64))).astype(np.float32)
    x = np.asarray(inputs["x"])
    pos = np.asarray(inputs["positions"])
    maps = []
    for c in range(8):
        m = dict(shared)
        m["x"] = np.ascontiguousarray(x[c])
        m["positions"] = np.ascontiguousarray(pos[c]).astype(np.int32)
        maps.append(m)
    return maps


def run(inputs, stages=("all",), debug=()):
    P = build(stages, debug)
    maps = make_in_maps(inputs)
    res = run_bass_kernel_spmd(P.nc, maps, core_ids=list(range(8)))
    return res


def kernel(**inputs):
    res = run(inputs)
    out = np.stack([np.asarray(r["out"]) for r in res.results], axis=0)
    return out.astype(np.float32)
```

```python
import math
from contextlib import ExitStack
import numpy as np
import concourse.bass as bass
import concourse.mybir as mybir
from concourse.bass_utils import run_bass_kernel_spmd

F32 = mybir.dt.float32
BF16 = mybir.dt.bfloat16
I32 = mybir.dt.int32
AF = mybir.ActivationFunctionType
ALU = mybir.AluOpType
AX = mybir.AxisListType

S = 4096
D = 1024
NB = 8
TB = 512
DFF = 2816
NFC = DFF // 128
EPS = 1e-6

ENGS = ("pe", "act", "dve", "pool", "sp")


class Buf:
    __slots__ = ("name", "w", "r")

    def __init__(self, name):
        self.name = name
        self.w = None
        self.r = []


class Op:
    __slots__ = ("eng", "fn", "deps", "key", "pos", "ndma", "waits", "flag")

    def __init__(self, eng, fn, deps, key, ndma):
        self.eng = eng
        self.fn = fn
        self.deps = deps
        self.key = key
        self.pos = -1
        self.ndma = ndma
        self.waits = []
        self.flag = False


class KB:
    def __init__(self, nc):
        self.nc = nc
        self.ops = []
        self.eng_obj = {"pe": nc.tensor, "act": nc.scalar, "dve": nc.vector,
                        "pool": nc.gpsimd, "sp": nc.sync}
        self.last = {}
        self.dma_out = []
        self.nbuf = 0

    def buf(self, name=None):
        self.nbuf += 1
        return Buf(name or f"b{self.nbuf}")

    def bufs(self, n, name="b"):
        return [self.buf(f"{name}{i}") for i in range(n)]

    def _deps(self, idx, R, W):
        deps = set()
        for b in R:
            if b.w is not None:
                deps.add(b.w)
        for b in W:
            if b.w is not None:
                deps.add(b.w)
            deps.update(b.r)
        for b in W:
            b.w = idx
            b.r = []
        for b in R:
            b.r.append(idx)
        deps.discard(idx)
        return deps

    def op(self, eng, fn, R=(), W=()):
        idx = len(self.ops)
        deps = self._deps(idx, R, W)
        self.ops.append(Op(eng, fn, deps, eng, 0))
        self.last[eng] = idx
        return idx

    def dma(self, eng, pairs, R=(), W=(), key=None, **kw):
        idx = len(self.ops)
        deps = self._deps(idx, R, W)
        if key is None:
            key = "d_" + (W[0].name if W else R[0].name)

        def fn(e, pairs=pairs, kw=kw):
            return [e.dma_start(out=o, in_=i, **kw) for (o, i) in pairs]
        self.ops.append(Op(eng, fn, deps, key, len(pairs)))
        self.dma_out.append(idx)
        return idx

    def barrier(self):
        targets = set(self.last.values()) | set(self.dma_out)
        for e in ENGS:
            idx = len(self.ops)
            self.ops.append(Op(e, None, set(targets), e, 0))
        self.dma_out = []

    def reorder(self):
        import heapq
        ops = self.ops
        n = len(ops)
        DUR = {"pe": 0.22, "act": 0.45, "dve": 0.40, "pool": 0.9, "sp": 0.1}
        LAT = 0.8
        new_order = []
        seg_start = 0
        i = 0
        bounds = []
        while i < n:
            if ops[i].fn is None:
                j = i
                while j < n and ops[j].fn is None:
                    j += 1
                bounds.append((seg_start, i, j))
                seg_start = j
                i = j
            else:
                i += 1
        if seg_start < n:
            bounds.append((seg_start, n, n))
        for (lo, hi, nxt) in bounds:
            idxs = range(lo, hi)
            users = {}
            ndeps = {}
            est = {}
            for x in idxs:
                d_in = [d for d in ops[x].deps if lo <= d < hi]
                ndeps[x] = len(d_in)
                est[x] = 0.0
                for d in d_in:
                    users.setdefault(d, []).append(x)
            heaps = {e: [] for e in ENGS}
            for x in idxs:
                if ndeps[x] == 0:
                    heapq.heappush(heaps[ops[x].eng], (0.0, x))
            free = {e: 0.0 for e in ENGS}
            fin = {}
            left = hi - lo
            while left:
                best = None
                for e in ENGS:
                    if heaps[e]:
                        t_, x = heaps[e][0]
                        st_ = max(t_, free[e])
                        if best is None or (st_, x) < (best[0], best[2]):
                            best = (st_, e, x)
                st_, e, x = best
                heapq.heappop(heaps[e])
                o = ops[x]
                if o.ndma:
                    free[e] = st_ + (2.0 if e == "pool" else 0.15)
                    fin[x] = st_ + 4.0
                else:
                    free[e] = st_ + DUR[e]
                    fin[x] = free[e]
                new_order.append(x)
                left -= 1
                for u in users.get(x, ()):
                    lat = 0.05 if (ops[u].eng == e and not o.ndma) else LAT
                    est[u] = max(est[u], fin[x] + lat)
                    ndeps[u] -= 1
                    if ndeps[u] == 0:
                        heapq.heappush(heaps[ops[u].eng], (est[u], u))
            seg_new = new_order[len(new_order) - (hi - lo):]
            last_eng = {}
            for x in seg_new:
                if not ops[x].ndma:
                    last_eng[ops[x].eng] = x
            for bidx in range(hi, nxt):
                ops[bidx].deps = set(ops[bidx].deps) | set(last_eng.values())
            new_order.extend(range(hi, nxt))
        assert len(new_order) == n and len(set(new_order)) == n
        remap = {old: new for new, old in enumerate(new_order)}
        newops = [ops[i_] for i_ in new_order]
        for o in newops:
            o.deps = {remap[d] for d in o.deps}
        self.ops = newops

    def finalize(self, es):
        nc = self.nc
        if getattr(self, "do_reorder", False):
            self.reorder()
        ops = self.ops
        cnt = {}
        for o in ops:
            if o.ndma:
                cnt[o.key] = cnt.get(o.key, 0) + o.ndma
                o.pos = cnt[o.key]
            elif o.fn is not None:
                cnt[o.key] = cnt.get(o.key, 0) + 1
                o.pos = cnt[o.key]
            else:
                o.pos = cnt.get(o.key, 0)
        seen = {e: {} for e in ENGS}
        flagged = {}
        for o in ops:
            need = {}
            for d in o.deps:
                dop = ops[d]
                if dop.fn is None:
                    continue
                k = dop.key
                if (not dop.ndma) and k == o.eng and k in ("pe", "sp"):
                    continue
                if dop.pos > seen[o.eng].get(k, 0):
                    if dop.pos > need.get(k, (0, None))[0]:
                        need[k] = (dop.pos, dop)
            for k, (p, dop) in need.items():
                seen[o.eng][k] = p
                o.waits.append((k, dop))
                dop.flag = True
        val = {}
        rank = {}
        for i, o in enumerate(ops):
            if o.ndma:
                val[i] = 16 * o.pos
            elif o.fn is not None and o.flag:
                rank[o.key] = rank.get(o.key, 0) + 1
                val[i] = rank[o.key]
        opidx = {id(o): i for i, o in enumerate(ops)}
        keys = set()
        for o in ops:
            if o.ndma or o.flag:
                keys.add(o.key)
        sems = {}
        for kname in sorted(keys):
            sems[kname] = es.enter_context(nc.semaphore("s_" + kname))
        self.nsem = len(sems)
        for o in ops:
            e = self.eng_obj[o.eng]
            for (k, dop) in o.waits:
                e.wait_ge(sems[k], val[opidx[id(dop)]])
            if o.fn is None:
                continue
            ins = o.fn(e)
            if o.ndma:
                for i_ in ins:
                    i_.then_inc(sems[o.key], 16)
            elif o.flag:
                ins.then_inc(sems[o.key], 1)


class Prog:
    def __init__(self, debug=()):
        self.debug = set(debug)
        self.nc = bass.Bass("TRN2", target_bir_lowering=False)
        self.k = KB(self.nc)
        self.es = ExitStack()
        self.dram = {}

    def din(self, name, shape, dtype=F32):
        t = self.nc.dram_tensor(name, list(shape), dtype, kind="ExternalInput").ap()
        self.dram[name] = t
        return t

    def dout(self, name, shape, dtype=F32):
        t = self.nc.dram_tensor(name, list(shape), dtype, kind="ExternalOutput").ap()
        self.dram[name] = t
        return t

    def dint(self, name, shape, dtype=F32):
        kind = "ExternalOutput" if name in self.debug else "Internal"
        t = self.nc.dram_tensor(name, list(shape), dtype, kind=kind).ap()
        self.dram[name] = t
        return t

    def sb(self, es, name, shape, dtype):
        self.nsb = getattr(self, "nsb", 0) + 1
        return es.enter_context(self.nc.sbuf_tensor(f"{name}_{self.nsb}", list(shape), dtype))


def setup_common(P):
    nc, k = P.nc, P.k
    P.ps = [P.es.enter_context(nc.psum_tensor(f"ps{i}", [128, 512], F32)) for i in range(8)]
    P.psb = k.bufs(8, "psb")
    P.ident_f = P.sb(P.es, "ident_f", [128, 128], F32)
    P.ident = P.sb(P.es, "ident", [128, 128], BF16)
    P.epsb = P.sb(P.es, "epsb", [128, 1], F32)
    P.b_const = k.buf("const")
    ones = P.sb(P.es, "ones_f", [128, 128], F32)
    P.ones_f = ones
    k.op("pool", lambda e: e.memset(ones[:], 1.0), W=[P.b_const])
    k.op("pool", lambda e: e.affine_select(out=P.ident_f[:], in_=ones[:], pattern=[[-1, 128]],
                                           compare_op=ALU.is_equal, fill=0.0, base=0,
                                           channel_multiplier=1), R=[P.b_const], W=[P.b_const])
    k.op("pool", lambda e: e.tensor_copy(out=P.ident[:], in_=P.ident_f[:]), R=[P.b_const], W=[P.b_const])
    k.op("pool", lambda e: e.memset(P.epsb[:], EPS), W=[P.b_const])


def cast_weight(P, name, src2d, rows, cols):
    k = P.k
    dst = P.dint(name, [rows, cols], BF16)
    b = k.buf(name)
    bb = cols if cols <= 2048 else 512
    per_row = cols // bb
    rstep = max(1, 4096 // per_row)
    pairs = []
    for r0 in range(0, rows, rstep):
        r1 = min(rows, r0 + rstep)
        pairs.append((dst[r0:r1, :].rearrange("r (a b) -> r a b", b=bb),
                      src2d[r0:r1, :].rearrange("r (a b) -> r a b", b=bb)))
    k.dma("pool", pairs, W=[b], key="wc_" + name)
    return dst, b


def load_vec_fm(P, es, name, src1d, nchunk, b, eng="sp"):
    t = P.sb(es, name, [128, nchunk], F32)
    P.k.dma(eng, [(t[:], src1d.rearrange("(c p) -> p c", p=128))], W=[b],
            allow_slow_non_contiguous=True)
    return t


def norm_block(P, T, blk, src, b_src, gT):
    k, nc = P.k, P.nc
    xres, bx = T["xres"], T["b_xres"]
    for j in range(4):
        r0 = blk * TB + j * 128
        k.dma("sp", [(xres[:, j, :], src[r0:r0 + 128, :])], R=[b_src[blk * 4 + j]], W=[bx[j]], key=f"d_xres{j}")
    rms_transpose(P, T, gT)


def rms_stats(P, T):
    k = P.k
    xres, bx = T["xres"], T["b_xres"]
    junk, bj = T["junk"], T["b_junk"]
    ss, bss = T["ss"], T["b_ss"]
    xn, bxn = T["xn"], T["b_xn"]
    for j in range(4):
        k.op("act", lambda e, j=j: e.activation(out=junk[:], in_=xres[:, j, :], func=AF.Square,
                                                accum_out=ss[:, j:j + 1]),
             R=[bx[j]], W=[bj, bss[j]])
        k.op("act", lambda e, j=j: e.activation(out=ss[:, 4 + j:5 + j], in_=ss[:, j:j + 1], func=AF.Sqrt,
                                                bias=P.epsb[:], scale=1.0 / D),
             R=[bss[j], P.b_const], W=[bss[j]])
        k.op("dve", lambda e, j=j: e.reciprocal(out=ss[:, 8 + j:9 + j], in_=ss[:, 4 + j:5 + j]),
             R=[bss[j]], W=[bss[j]])
        k.op("dve", lambda e, j=j: e.tensor_scalar(out=xn[:, j, :], in0=xres[:, j, :], scalar1=ss[:, 8 + j:9 + j],
                                                   scalar2=None, op0=ALU.mult),
             R=[bx[j], bss[j]], W=[bxn[j]])


def rms_tr(P, T, gT):
    k = P.k
    hnT, bh = T["hnT"], T["b_hnT"]
    xn, bxn = T["xn"], T["b_xn"]
    for j in range(4):
        pb = T["ps_tr"][j % len(T["ps_tr"])]
        psT = P.ps[pb][:].bitcast(BF16)
        for c in range(8):
            k.op("pe", lambda e, j=j, c=c, psT=psT: e.transpose(out=psT[:, c * 128:(c + 1) * 128],
                                                               in_=xn[:, j, c * 128:(c + 1) * 128],
                                                               identity=P.ident[:]),
                 R=[bxn[j], P.b_const], W=[P.psb[pb]])
        k.op("dve", lambda e, j=j, psT=psT: e.tensor_tensor(
            out=hnT[:, :, j * 128:(j + 1) * 128],
            in0=psT.rearrange("p (c t) -> p c t", c=8),
            in1=gT[:, :].unsqueeze(2).broadcast_to([128, 8, 128]), op=ALU.mult),
            R=[P.psb[pb], T["b_g"]], W=[bh])


def rms_transpose(P, T, gT):
    rms_stats(P, T)
    rms_tr(P, T, gT)


def alloc_blockbufs(P, es, pfx, ps_tr):
    k = P.k
    T = {}
    T["xres"] = P.sb(es, pfx + "xres", [128, 4, 1024], F32)
    T["b_xres"] = k.bufs(4, pfx + "xres")
    T["hnT"] = P.sb(es, pfx + "hnT", [128, 8, 512], BF16)
    T["b_hnT"] = k.buf(pfx + "hnT")
    T["junk"] = P.sb(es, pfx + "junk", [128, 1024], BF16)
    T["b_junk"] = k.buf(pfx + "junk")
    T["ss"] = P.sb(es, pfx + "ss", [128, 12], F32)
    T["b_ss"] = k.bufs(4, pfx + "ss")
    T["xn"] = P.sb(es, pfx + "xn", [128, 4, 1024], BF16)
    T["b_xn"] = k.bufs(4, pfx + "xn")
    T["ps_tr"] = ps_tr
    return T


def ffn_phase(P, layer, src, b_src, dst, b_dst, w_in_bf, b_win, w_out_bf, b_wout, final_norm=None, premix=None):
    k, nc = P.k, P.nc
    I = P.inp
    with ExitStack() as es:
        bc = k.buf("ffn_consts")
        gT = load_vec_fm(P, es, "ffn_g", I["norm_ffn_g"][layer], 8, bc)
        cb = load_vec_fm(P, es, "ffn_cb", I["ffn_conv_b"][layer], NFC, bc)
        cw = P.sb(es, "ffn_cw", [128, 3, NFC], F32)
        for t in range(3):
            k.dma("sp", [(cw[:, t, :], I["ffn_conv_w"][layer, t].rearrange("(c p) -> p c", p=128))], W=[bc],
                  allow_slow_non_contiguous=True)
        TT = [alloc_blockbufs(P, es, f"f{i}_", ps_tr=[6, 7]) for i in range(2)]
        for T_ in TT:
            T_["b_g"] = bc
        wout = P.sb(es, "ffn_wout", [128, NFC, 1024], BF16)
        bwo = k.buf("ffn_wout")
        k.dma("sp", [(wout[:, c0:c0 + 11, :],
                      w_out_bf[c0 * 128:(c0 + 11) * 128, :].rearrange("(c p) n -> p c n", p=128))
                     for c0 in (0, 11)], R=[b_wout], W=[bwo])
        NSL = 2
        wsl = [P.sb(es, f"ffn_wsl{i}", [128, 8, 2, 256], BF16) for i in range(NSL)]
        bws = k.bufs(NSL, "ffn_wsl")
        a_sb = [P.sb(es, f"ffn_a{i}", [128, 516], F32) for i in range(2)]
        ba = k.bufs(2, "ffn_a")
        c_sb = [P.sb(es, f"ffn_c{i}", [128, 512], F32) for i in range(2)]
        bcs = k.bufs(2, "ffn_c")
        s_sb = [P.sb(es, f"ffn_s{i}", [128, 512], F32) for i in range(2)]
        bss = k.bufs(2, "ffn_s")
        halo = P.sb(es, "ffn_halo", [128, NFC, 2], F32)
        bhalo = k.bufs(NFC, "ffn_halo")
        gTt = P.sb(es, "ffn_gT", [128, NFC, 512], BF16)
        bg = k.bufs(NFC, "ffn_gT")
        k.op("pool", lambda e: e.memset(halo[:], 0.0), W=bhalo)
        if final_norm is not None:
            fg = P.sb(es, "fin_g", [128, 1024], F32)
            bfg = k.buf("fin_g")
            k.dma("sp", [(fg[:], final_norm.partition_broadcast(128))], W=[bfg])
            ostage = [P.sb(es, f"fin_o{i}", [128, 1024], F32) for i in range(2)]
            bos = k.bufs(2, "fin_o")
        if premix is not None:
            ycatT, b_ya, b_yb, wo_bf, b_wo = premix
            wo = P.sb(es, "pm_wo", [128, 8, 1024], BF16)
            bwo2 = k.buf("pm_wo")
            k.dma("sp", [(wo[:], wo_bf.rearrange("(c p) n -> p c n", p=128))], R=[b_wo], W=[bwo2])
            yTs = [P.sb(es, f"pm_yT{i}", [128, 8, 512], BF16) for i in range(2)]
            byTs = k.bufs(2, "pm_yT")
        nsl = 0
        cnt = 0
        def stage_a(blk):
            T = TT[blk % 2]
            xres, bx = T["xres"], T["b_xres"]
            for j in range(4):
                r0 = blk * TB + j * 128
                k.dma("sp", [(xres[:, j, :], src[r0:r0 + 128, :])], R=[b_src[blk * 4 + j]], W=[bx[j]], key=f"d_xres{blk % 2}_{j}")
            if premix is not None:
                yT, byT = yTs[blk % 2], byTs[blk % 2]
                c0 = blk * TB
                k.dma("sp", [(yT[:], ycatT[:, c0:c0 + TB].rearrange("(c p) t -> p c t", p=128))], R=[b_ya[blk], b_yb[blk]], W=[byT])
                for j in range(4):
                    for n in range(2):
                        pb = 4 + (j * 2 + n) % 2
                        for kc in range(8):
                            k.op("pe", lambda e, kc=kc, j=j, n=n, pb=pb, yT=yT: e.matmul(
                                P.ps[pb][:], lhsT=yT[:, kc, j * 128:(j + 1) * 128], rhs=wo[:, kc, n * 512:(n + 1) * 512],
                                start=(kc == 0), stop=(kc == 7)), R=[byT, bwo2], W=[P.psb[pb]])
                        k.op("dve", lambda e, j=j, n=n, pb=pb, xres=xres: e.tensor_tensor(
                            out=xres[:, j, n * 512:(n + 1) * 512], in0=xres[:, j, n * 512:(n + 1) * 512],
                            in1=P.ps[pb][:], op=ALU.add), R=[bx[j], P.psb[pb]], W=[bx[j]])
            rms_stats(P, T)

        stage_a(0)
        rms_tr(P, TT[0], gT)
        def do_block(blk, T):
            nonlocal nsl, cnt
            xres, bx = T["xres"], T["b_xres"]
            hnT, bh = T["hnT"], T["b_hnT"]
            for pr in range(NFC // 2):
                if pr == 4 and blk + 1 < NB:
                    stage_a(blk + 1)
                sl = nsl % NSL
                nsl += 1
                pairs = []
                for h_ in range(2):
                    c0 = h_ * DFF + pr * 256
                    pairs.append((wsl[sl][:, :, h_, :],
                                  w_in_bf[:, c0:c0 + 256].rearrange("(kc p) n -> p kc n", p=128)))
                k.dma("sp", pairs, R=[b_win], W=[bws[sl]])
                for q in range(2):
                    oc = pr * 2 + q
                    pa, pu = (0, 1) if (cnt % 2 == 0) else (2, 3)
                    i2 = cnt % 2
                    cnt += 1
                    for kc in range(8):
                        k.op("pe", lambda e, kc=kc, sl=sl, q=q, pa=pa: e.matmul(
                            P.ps[pa][:], lhsT=wsl[sl][:, kc, 0, q * 128:(q + 1) * 128], rhs=hnT[:, kc, :],
                            start=(kc == 0), stop=(kc == 7)), R=[bws[sl], bh], W=[P.psb[pa]])
                    for kc in range(8):
                        k.op("pe", lambda e, kc=kc, sl=sl, q=q, pu=pu: e.matmul(
                            P.ps[pu][:], lhsT=wsl[sl][:, kc, 1, q * 128:(q + 1) * 128], rhs=hnT[:, kc, :],
                            start=(kc == 0), stop=(kc == 7)), R=[bws[sl], bh], W=[P.psb[pu]])
                    A = a_sb[i2]
                    k.op("pool", lambda e, A=A, oc=oc: e.tensor_copy(out=A[:, 2:4], in_=halo[:, oc, :]),
                         R=[bhalo[oc]], W=[ba[i2]])
                    k.op("act", lambda e, A=A, pa=pa: e.copy(out=A[:, 4:516], in_=P.ps[pa][:]),
                         R=[P.psb[pa]], W=[ba[i2]])
                    k.op("pool", lambda e, A=A, oc=oc: e.tensor_copy(out=halo[:, oc, :], in_=A[:, 514:516]),
                         R=[ba[i2]], W=[bhalo[oc]])
                    C = c_sb[i2]
                    k.op("act", lambda e, C=C, pa=pa, oc=oc: e.activation(
                        out=C[:], in_=P.ps[pa][:], func=AF.Identity, bias=cb[:, oc:oc + 1], scale=cw[:, 2, oc:oc + 1]),
                        R=[P.psb[pa], bc], W=[bcs[i2]])
                    k.op("dve", lambda e, C=C, A=A, oc=oc: e.scalar_tensor_tensor(
                        out=C[:], in0=A[:, 3:515], scalar=cw[:, 1, oc:oc + 1], in1=C[:], op0=ALU.mult, op1=ALU.add),
                        R=[ba[i2], bc, bcs[i2]], W=[bcs[i2]])
                    k.op("dve", lambda e, C=C, A=A, oc=oc: e.scalar_tensor_tensor(
                        out=C[:], in0=A[:, 2:514], scalar=cw[:, 0, oc:oc + 1], in1=C[:], op0=ALU.mult, op1=ALU.add),
                        R=[ba[i2], bc, bcs[i2]], W=[bcs[i2]])
                    Ssb = s_sb[i2]
                    k.op("act", lambda e, C=C, Ssb=Ssb: e.activation(out=Ssb[:], in_=C[:], func=AF.Silu),
                         R=[bcs[i2]], W=[bss[i2]])
                    k.op("dve", lambda e, Ssb=Ssb, pu=pu, oc=oc: e.tensor_tensor(
                        out=gTt[:, oc, :], in0=Ssb[:], in1=P.ps[pu][:], op=ALU.mult),
                        R=[bss[i2], P.psb[pu]], W=[bg[oc]])
            if blk + 1 < NB:
                rms_tr(P, TT[(blk + 1) % 2], gT)
            for j in range(4):
                for n in range(2):
                    pb = 4 + (j * 2 + n) % 2
                    for kc in range(NFC):
                        k.op("pe", lambda e, kc=kc, j=j, n=n, pb=pb: e.matmul(
                            P.ps[pb][:], lhsT=gTt[:, kc, j * 128:(j + 1) * 128],
                            rhs=wout[:, kc, n * 512:(n + 1) * 512], start=(kc == 0), stop=(kc == NFC - 1)),
                            R=[bg[kc], bwo], W=[P.psb[pb]])
                    k.op("dve", lambda e, j=j, n=n, pb=pb: e.tensor_tensor(
                        out=xres[:, j, n * 512:(n + 1) * 512], in0=xres[:, j, n * 512:(n + 1) * 512],
                        in1=P.ps[pb][:], op=ALU.add), R=[bx[j], P.psb[pb]], W=[bx[j]])
                r0 = blk * TB + j * 128
                if final_norm is None:
                    k.dma("sp", [(dst[r0:r0 + 128, :], xres[:, j, :])], R=[bx[j]], W=[b_dst[blk * 4 + j]],
                          key=f"d_xst{blk % 2}_{j}")
                else:
                    ss, bs4 = T["ss"], T["b_ss"]
                    junk, bj = T["junk"], T["b_junk"]
                    o = ostage[j % 2]
                    k.op("act", lambda e, j=j: e.activation(out=junk[:], in_=xres[:, j, :], func=AF.Square,
                                                            accum_out=ss[:, j:j + 1]),
                         R=[bx[j]], W=[bj, bs4[j]])
                    k.op("act", lambda e, j=j: e.activation(out=ss[:, 4 + j:5 + j], in_=ss[:, j:j + 1], func=AF.Sqrt,
                                                            bias=P.epsb[:], scale=1.0 / D),
                         R=[bs4[j], P.b_const], W=[bs4[j]])
                    k.op("dve", lambda e, j=j: e.reciprocal(out=ss[:, 8 + j:9 + j], in_=ss[:, 4 + j:5 + j]),
                         R=[bs4[j]], W=[bs4[j]])
                    k.op("dve", lambda e, j=j, o=o: e.scalar_tensor_tensor(
                        out=o[:], in0=xres[:, j, :], scalar=ss[:, 8 + j:9 + j], in1=fg[:], op0=ALU.mult, op1=ALU.mult),
                        R=[bx[j], bs4[j], bfg], W=[bos[j % 2]])
                    k.dma("sp", [(dst[r0:r0 + 128, :], o[:])], R=[bos[j % 2]], W=[b_dst[blk * 4 + j]], key=f"d_ost{j % 2}")
        for blk in range(NB):
            do_block(blk, TT[blk % 2])
        k.barrier()


HQ = 8
VP = 130
SCALE = 192 ** -0.5
TWO_PI = 2.0 * math.pi
CW1 = 6.28125
CW2 = TWO_PI - 6.28125


def mla_proj_phase(P, src, b_src, W):
    k, nc = P.k, P.nc
    I = P.inp
    QTn = P.dint("QTn", [HQ, 128, S], BF16)
    QTr = P.dint("QTr", [HQ, 64, S], BF16)
    KTn = P.dint("KTn", [HQ, 128, S], BF16)
    KTr = P.dint("KTr", [64, S], BF16)
    Va = P.dint("Va", [S, HQ * VP], BF16)
    P.mla_dram = dict(QTn=QTn, QTr=QTr, KTn=KTn, KTr=KTr, Va=Va)
    P.b_qkv = k.bufs(NB, "qkv")
    stats = P.sb(P.es, "mla_stats", [128, 16], F32)
    P.mla_stats = stats
    P.b_stats = k.buf("mla_stats")
    with ExitStack() as es:
        bc = k.buf("pd_consts")
        gT = load_vec_fm(P, es, "pd_g", I["norm_mix_g"][1], 8, bc)
        qg = load_vec_fm(P, es, "pd_qg", I["mla_q_norm_g"], 3, bc)
        kvg = load_vec_fm(P, es, "pd_kvg", I["mla_kv_norm_g"], 2, bc)
        g5 = P.sb(es, "pd_g5", [128, 5], F32)
        k.op("pool", lambda e: e.tensor_copy(out=g5[:, 0:3], in_=qg[:]), R=[bc], W=[bc])
        k.op("pool", lambda e: e.tensor_copy(out=g5[:, 3:5], in_=kvg[:]), R=[bc], W=[bc])
        win = P.sb(es, "pd_win", [128, 8, 704], BF16)
        k.dma("sp", [(win[:], W["odd_w_in"][0].rearrange("(c p) n -> p c n", p=128))], R=[W["odd_w_in"][1]], W=[bc])
        wuq = P.sb(es, "pd_wuq", [128, 3, 1536], BF16)
        k.dma("sp", [(wuq[:], W["mla_w_uq"][0].rearrange("(c p) n -> p c n", p=128))], R=[W["mla_w_uq"][1]], W=[bc])
        wukv = P.sb(es, "pd_wukv", [128, 2, 2048], BF16)
        k.dma("sp", [(wukv[:], W["mla_w_ukv"][0].rearrange("(c p) n -> p c n", p=128))], R=[W["mla_w_ukv"][1]], W=[bc])
        freq = P.sb(es, "pd_freq", [128, 32], F32)
        k.dma("sp", [(freq[:], I["rope_freq"].partition_broadcast(128))], W=[bc])
        posi = P.sb(es, "pd_posi", [128, 32], I32)
        k.dma("sp", [(posi[:, 8 * i:8 * i + 8], I["positions"][1024 * i:1024 * (i + 1)].rearrange("(j p) -> p j", p=128))
                     for i in range(4)], W=[bc], allow_slow_non_contiguous=True)
        posf = P.sb(es, "pd_posf", [128, 32], F32)
        ang = P.sb(es, "pd_ang", [128, 32, 32], F32)
        tmpf = P.sb(es, "pd_tmpf", [128, 32, 32], F32)
        tmpi = P.sb(es, "pd_tmpi", [128, 32, 32], I32)
        sinT = P.sb(es, "pd_sin", [128, 32, 32], F32)
        cosT = P.sb(es, "pd_cos", [128, 32, 32], F32)
        br = k.buf("pd_rope")
        k.op("dve", lambda e: e.tensor_copy(out=posf[:], in_=posi[:]), R=[bc], W=[br])
        k.op("dve", lambda e: e.tensor_tensor(out=ang[:], in0=posf[:, :].unsqueeze(2).broadcast_to([128, 32, 32]),
                                              in1=freq[:, :].unsqueeze(1).broadcast_to([128, 32, 32]), op=ALU.mult),
             R=[br, bc], W=[br])
        k.op("dve", lambda e: e.tensor_scalar(out=tmpf[:], in0=ang[:], scalar1=1.0 / TWO_PI, scalar2=None, op0=ALU.mult),
             R=[br], W=[br])
        k.op("dve", lambda e: e.tensor_copy(out=tmpi[:], in_=tmpf[:]), R=[br], W=[br])
        k.op("dve", lambda e: e.tensor_copy(out=tmpf[:], in_=tmpi[:]), R=[br], W=[br])
        k.op("dve", lambda e: e.scalar_tensor_tensor(out=ang[:], in0=tmpf[:], scalar=-CW1, in1=ang[:],
                                                     op0=ALU.mult, op1=ALU.add), R=[br], W=[br])
        k.op("dve", lambda e: e.scalar_tensor_tensor(out=ang[:], in0=tmpf[:], scalar=-CW2, in1=ang[:],
                                                     op0=ALU.mult, op1=ALU.add), R=[br], W=[br])
        k.op("dve", lambda e: e.tensor_scalar(out=ang[:], in0=ang[:], scalar1=-math.pi, scalar2=math.pi,
                                              op0=ALU.max, op1=ALU.min), R=[br], W=[br])
        k.op("act", lambda e: e.activation(out=sinT[:], in_=ang[:], func=AF.Sin), R=[br], W=[br])
        k.op("dve", lambda e: e.tensor_scalar(out=tmpf[:], in0=ang[:], scalar1=math.pi / 2, scalar2=math.pi,
                                              op0=ALU.add, op1=ALU.is_gt), R=[br], W=[br])
        k.op("dve", lambda e: e.scalar_tensor_tensor(out=tmpf[:], in0=tmpf[:], scalar=-TWO_PI, in1=ang[:],
                                                     op0=ALU.mult, op1=ALU.add), R=[br], W=[br])
        k.op("dve", lambda e: e.tensor_scalar(out=tmpf[:], in0=tmpf[:], scalar1=math.pi / 2, scalar2=math.pi,
                                              op0=ALU.add, op1=ALU.min), R=[br], W=[br])
        k.op("act", lambda e: e.activation(out=cosT[:], in_=tmpf[:], func=AF.Sin), R=[br], W=[br])
        k.op("pool", lambda e: e.memset(stats[:], 0.0), W=[P.b_stats])

        T = alloc_blockbufs(P, es, "d_", ps_tr=[7])
        T["b_g"] = bc
        SETS = []
        for si in range(2):
            Bd = {}
            def mk(name, shape, dt, Bd=Bd, si=si):
                Bd[name] = P.sb(es, f"pd_{name}{si}", shape, dt)
                Bd["b_" + name] = k.buf(f"pd_{name}{si}")
            mk("junk2", [128, 1536], F32)
            mk("st", [128, 16], F32)
            mk("cn", [128, 640], BF16)
            mk("cT", [128, 5, 128], BF16)
            mk("kr", [128, 64], F32)
            mk("krb", [128, 64], BF16)
            mk("qsb", [128, 1536], F32)
            mk("qbf", [128, 8, 192], BF16)
            mk("kbf", [128, 8, 128], BF16)
            Bd["rt"] = [P.sb(es, f"pd_rt{si}_{i}", [128, 8, 32], F32) for i in range(4)]
            Bd["b_rt"] = k.buf(f"pd_rt{si}")
            SETS.append(Bd)
        vst = [P.sb(es, f"pd_vst{i}", [128, 8, VP], BF16) for i in range(2)]
        bvst = k.bufs(2, "pd_vst")
        for i in range(2):
            k.op("pool", lambda e, i=i: e.memset(vst[i][:], 1.0), W=[bvst[i]])
        qTn_st = P.sb(es, "pd_qTn", [128, 8, 512], BF16)
        qTr_st = P.sb(es, "pd_qTr", [64, 8, 512], BF16)
        kTn_st = P.sb(es, "pd_kTn", [128, 8, 512], BF16)
        kTr_st = P.sb(es, "pd_kTr", [64, 512], BF16)
        bqT = k.buf("pd_qT")
        bkT = k.buf("pd_kT")
        nv = 0
        import os
        LIM = float(os.environ.get("PD_LIMIT", "99"))
        for blk in range(NB):
            if LIM < 1 or (LIM < 90 and blk > 0):
                break
            norm_block(P, T, blk, src, b_src, gT)
            hnT, bh = T["hnT"], T["b_hnT"]
            def do_sub(blk, j, Bd):
                junk2, bj2, st, bst, cn, bcn, cT, bcT = Bd["junk2"], Bd["b_junk2"], Bd["st"], Bd["b_st"], Bd["cn"], Bd["b_cn"], Bd["cT"], Bd["b_cT"]
                kr, bkr, krb, bkrb, qsb, bqsb, qbf, bqbf = Bd["kr"], Bd["b_kr"], Bd["krb"], Bd["b_krb"], Bd["qsb"], Bd["b_qsb"], Bd["qbf"], Bd["b_qbf"]
                kbf, bkbf, rt, brt = Bd["kbf"], Bd["b_kbf"], Bd["rt"], Bd["b_rt"]
                nonlocal nv
                sub = blk * 4 + j
                tok = slice(j * 128, (j + 1) * 128)
                for kc in range(8):
                    k.op("pe", lambda e, kc=kc, tok=tok: e.matmul(P.ps[0][:, 0:384], lhsT=hnT[:, kc, tok], rhs=win[:, kc, 0:384],
                                                                  start=(kc == 0), stop=(kc == 7)), R=[bh, bc], W=[P.psb[0]])
                for kc in range(8):
                    k.op("pe", lambda e, kc=kc, tok=tok: e.matmul(P.ps[1][:, 0:320], lhsT=hnT[:, kc, tok], rhs=win[:, kc, 384:704],
                                                                  start=(kc == 0), stop=(kc == 7)), R=[bh, bc], W=[P.psb[1]])
                k.op("act", lambda e: e.activation(out=junk2[:, 0:384], in_=P.ps[0][:, 0:384], func=AF.Square,
                                                   accum_out=st[:, 0:1]), R=[P.psb[0]], W=[bj2, bst])
                k.op("act", lambda e: e.activation(out=junk2[:, 0:256], in_=P.ps[1][:, 0:256], func=AF.Square,
                                                   accum_out=st[:, 1:2]), R=[P.psb[1]], W=[bj2, bst])
                k.op("act", lambda e: e.activation(out=junk2[:, 0:64], in_=P.ps[1][:, 256:320], func=AF.Square,
                                                   accum_out=st[:, 2:3]), R=[P.psb[1]], W=[bj2, bst])
                k.op("act", lambda e: e.activation(out=st[:, 3:4], in_=st[:, 0:1], func=AF.Sqrt, bias=P.epsb[:],
                                                   scale=1.0 / 384), R=[bst, P.b_const], W=[bst])
                k.op("act", lambda e: e.activation(out=st[:, 4:5], in_=st[:, 1:2], func=AF.Sqrt, bias=P.epsb[:],
                                                   scale=1.0 / 256), R=[bst, P.b_const], W=[bst])
                k.op("dve", lambda e: e.reciprocal(out=st[:, 5:7], in_=st[:, 3:5]), R=[bst], W=[bst])
                k.op("dve", lambda e: e.tensor_scalar(out=cn[:, 0:384], in0=P.ps[0][:, 0:384], scalar1=st[:, 5:6],
                                                      scalar2=None, op0=ALU.mult), R=[P.psb[0], bst], W=[bcn])
                k.op("dve", lambda e: e.tensor_scalar(out=cn[:, 384:640], in0=P.ps[1][:, 0:256], scalar1=st[:, 6:7],
                                                      scalar2=None, op0=ALU.mult), R=[P.psb[1], bst], W=[bcn])
                k.op("act", lambda e: e.copy(out=kr[:], in_=P.ps[1][:, 256:320]), R=[P.psb[1]], W=[bkr])
                psT = P.ps[2][:].bitcast(BF16)
                for c in range(5):
                    k.op("pe", lambda e, c=c, psT=psT: e.transpose(out=psT[:, c * 128:(c + 1) * 128],
                                                                   in_=cn[:, c * 128:(c + 1) * 128], identity=P.ident[:]),
                         R=[bcn, P.b_const], W=[P.psb[2]])
                k.op("dve", lambda e, psT=psT: e.tensor_tensor(out=cT[:], in0=psT[:, 0:640].rearrange("p (c t) -> p c t", c=5),
                                                               in1=g5[:, :].unsqueeze(2).broadcast_to([128, 5, 128]), op=ALU.mult),
                     R=[P.psb[2], bc], W=[bcT])
                if LIM < 3:
                    return
                for n in range(3):
                    for kc in range(3):
                        k.op("pe", lambda e, n=n, kc=kc: e.matmul(P.ps[3 + n][:], lhsT=cT[:, kc, :],
                                                                  rhs=wuq[:, kc, n * 512:(n + 1) * 512],
                                                                  start=(kc == 0), stop=(kc == 2)), R=[bcT, bc], W=[P.psb[3 + n]])
                kvb = [0, 1, 2, 6]
                for n in range(4):
                    for kc in range(2):
                        k.op("pe", lambda e, n=n, kc=kc: e.matmul(P.ps[kvb[n]][:], lhsT=cT[:, 3 + kc, :],
                                                                  rhs=wukv[:, kc, n * 512:(n + 1) * 512],
                                                                  start=(kc == 0), stop=(kc == 1)), R=[bcT, bc], W=[P.psb[kvb[n]]])
                for n in range(3):
                    k.op("act", lambda e, n=n: e.activation(out=qsb[:, n * 512:(n + 1) * 512], in_=P.ps[3 + n][:],
                                                            func=AF.Copy, scale=SCALE), R=[P.psb[3 + n]], W=[bqsb])
                if LIM < 3.1:
                    return
                k.op("pool", lambda e: e.tensor_tensor(out=junk2[:], in0=qsb[:], in1=qsb[:], op=ALU.mult),
                     R=[bqsb], W=[bj2])
                k.op("dve", lambda e: e.tensor_reduce(out=st[:, 8:16], in_=junk2[:, :].rearrange("p (h d) -> p h d", h=8),
                                                      axis=AX.X, op=ALU.add), R=[bj2], W=[bst])
                k.op("dve", lambda e: e.tensor_tensor(out=stats[:, 0:8], in0=stats[:, 0:8], in1=st[:, 8:16], op=ALU.max),
                     R=[bst, P.b_stats], W=[P.b_stats])
                if LIM < 3.2:
                    return
                q3 = qsb[:, :].rearrange("p (h d) -> p h d", h=8)
                cosb = cosT[:, sub, :].unsqueeze(1).broadcast_to([128, 8, 32])
                sinb = sinT[:, sub, :].unsqueeze(1).broadcast_to([128, 8, 32])
                x1 = q3[:, :, 128:160]
                x2 = q3[:, :, 160:192]
                k.op("pool", lambda e, x1=x1, cosb=cosb: e.tensor_tensor(out=rt[0][:], in0=x1, in1=cosb, op=ALU.mult),
                     R=[bqsb, br], W=[brt])
                k.op("pool", lambda e, x2=x2, sinb=sinb: e.tensor_tensor(out=rt[1][:], in0=x2, in1=sinb, op=ALU.mult),
                     R=[bqsb, br], W=[brt])
                k.op("pool", lambda e, x1=x1, sinb=sinb: e.tensor_tensor(out=rt[2][:], in0=x1, in1=sinb, op=ALU.mult),
                     R=[bqsb, br], W=[brt])
                k.op("pool", lambda e, x2=x2, cosb=cosb: e.tensor_tensor(out=rt[3][:], in0=x2, in1=cosb, op=ALU.mult),
                     R=[bqsb, br], W=[brt])
                k.op("dve", lambda e: e.tensor_tensor(out=qbf[:, :, 128:160], in0=rt[0][:], in1=rt[1][:], op=ALU.subtract),
                     R=[brt], W=[bqbf])
                k.op("dve", lambda e: e.tensor_tensor(out=qbf[:, :, 160:192], in0=rt[2][:], in1=rt[3][:], op=ALU.add),
                     R=[brt], W=[bqbf])
                k.op("act", lambda e, q3=q3: e.copy(out=qbf[:, :, 0:128], in_=q3[:, :, 0:128]), R=[bqsb], W=[bqbf])
                if LIM < 3.3:
                    return
                psQn = P.ps[7][:].bitcast(BF16)
                psQr = P.ps[3][:].bitcast(BF16)
                for h in range(8):
                    k.op("pe", lambda e, h=h, psQn=psQn: e.transpose(out=psQn[:, h * 128:(h + 1) * 128], in_=qbf[:, h, 0:128],
                                                                     identity=P.ident[:]), R=[bqbf, P.b_const], W=[P.psb[7]])
                for h in range(8):
                    k.op("pe", lambda e, h=h, psQr=psQr: e.transpose(out=psQr[0:64, h * 128:(h + 1) * 128], in_=qbf[:, h, 128:192],
                                                                     identity=P.ident[:]), R=[bqbf, P.b_const], W=[P.psb[3]])
                k.op("act", lambda e, psQn=psQn, tok=tok: e.copy(out=qTn_st[:, :, tok], in_=psQn.rearrange("p (h t) -> p h t", h=8)),
                     R=[P.psb[7]], W=[bqT])
                k.op("dve", lambda e, psQr=psQr, tok=tok: e.tensor_copy(out=qTr_st[:, :, tok],
                                                                        in_=psQr[0:64, :].rearrange("p (h t) -> p h t", h=8)),
                     R=[P.psb[3]], W=[bqT])
                if LIM < 4:
                    return
                if LIM < 4.1:
                    return
                vs = vst[nv % 2]
                bvs = bvst[nv % 2]
                nv += 1
                for n in range(4):
                    pv = P.ps[kvb[n]][:, :].rearrange("p (h c) -> p h c", h=2)
                    if n % 2 == 0:
                        k.op("act", lambda e, n=n, pv=pv: e.copy(out=kbf[:, 2 * n:2 * n + 2, :], in_=pv[:, :, 0:128]),
                             R=[P.psb[kvb[n]]], W=[bkbf])
                        k.op("act", lambda e, n=n, pv=pv, vs=vs: e.copy(out=vs[:, 2 * n:2 * n + 2, 0:128], in_=pv[:, :, 128:256]),
                             R=[P.psb[kvb[n]]], W=[bvs])
                    else:
                        k.op("dve", lambda e, n=n, pv=pv: e.tensor_copy(out=kbf[:, 2 * n:2 * n + 2, :], in_=pv[:, :, 0:128]),
                             R=[P.psb[kvb[n]]], W=[bkbf])
                        k.op("dve", lambda e, n=n, pv=pv, vs=vs: e.tensor_copy(out=vs[:, 2 * n:2 * n + 2, 0:128], in_=pv[:, :, 128:256]),
                             R=[P.psb[kvb[n]]], W=[bvs])
                if LIM < 4.2:
                    return
                r0 = blk * TB + j * 128
                k.dma("sp", [(P.mla_dram["Va"][r0:r0 + 128, :], vs[:, :, :].rearrange("p h c -> p (h c)"))],
                      R=[bvs], W=[P.b_qkv[blk]], key=f"d_vst{(nv - 1) % 2}")
                if LIM < 4.3:
                    return
                k.op("pool", lambda e: e.tensor_tensor(out=junk2[:, 0:1024], in0=kbf[:, :, :].rearrange("p h c -> p (h c)"),
                                                       in1=kbf[:, :, :].rearrange("p h c -> p (h c)"), op=ALU.mult),
                     R=[bkbf], W=[bj2])
                k.op("dve", lambda e: e.tensor_reduce(out=st[:, 8:16], in_=junk2[:, 0:1024].rearrange("p (h d) -> p h d", h=8),
                                                      axis=AX.X, op=ALU.add), R=[bj2], W=[bst])
                k.op("dve", lambda e: e.tensor_scalar(out=st[:, 8:16], in0=st[:, 8:16], scalar1=st[:, 2:3], scalar2=None,
                                                      op0=ALU.add), R=[bst], W=[bst])
                k.op("dve", lambda e: e.tensor_tensor(out=stats[:, 8:16], in0=stats[:, 8:16], in1=st[:, 8:16], op=ALU.max),
                     R=[bst, P.b_stats], W=[P.b_stats])
                if LIM < 4.4:
                    return
                c1 = cosT[:, sub, :]
                s1 = sinT[:, sub, :]
                k.op("pool", lambda e, c1=c1: e.tensor_tensor(out=rt[0][:, 0, :], in0=kr[:, 0:32], in1=c1, op=ALU.mult),
                     R=[bkr, br], W=[brt])
                k.op("pool", lambda e, s1=s1: e.tensor_tensor(out=rt[1][:, 0, :], in0=kr[:, 32:64], in1=s1, op=ALU.mult),
                     R=[bkr, br], W=[brt])
                k.op("pool", lambda e, s1=s1: e.tensor_tensor(out=rt[2][:, 0, :], in0=kr[:, 0:32], in1=s1, op=ALU.mult),
                     R=[bkr, br], W=[brt])
                k.op("pool", lambda e, c1=c1: e.tensor_tensor(out=rt[3][:, 0, :], in0=kr[:, 32:64], in1=c1, op=ALU.mult),
                     R=[bkr, br], W=[brt])
                k.op("dve", lambda e: e.tensor_tensor(out=krb[:, 0:32], in0=rt[0][:, 0, :], in1=rt[1][:, 0, :], op=ALU.subtract),
                     R=[brt], W=[bkrb])
                k.op("dve", lambda e: e.tensor_tensor(out=krb[:, 32:64], in0=rt[2][:, 0, :], in1=rt[3][:, 0, :], op=ALU.add),
                     R=[brt], W=[bkrb])
                if LIM < 4.5:
                    return
                psKn = P.ps[7][:].bitcast(BF16)
                psKr = P.ps[4][:].bitcast(BF16)
                for h in range(8):
                    k.op("pe", lambda e, h=h, psKn=psKn: e.transpose(out=psKn[:, h * 128:(h + 1) * 128], in_=kbf[:, h, :],
                                                                     identity=P.ident[:]), R=[bkbf, P.b_const], W=[P.psb[7]])
                k.op("pe", lambda e, psKr=psKr: e.transpose(out=psKr[0:64, 0:128], in_=krb[:], identity=P.ident[:]),
                     R=[bkrb, P.b_const], W=[P.psb[4]])
                k.op("act", lambda e, psKn=psKn, tok=tok: e.copy(out=kTn_st[:, :, tok], in_=psKn.rearrange("p (h t) -> p h t", h=8)),
                     R=[P.psb[7]], W=[bkT])
                k.op("dve", lambda e, psKr=psKr, tok=tok: e.tensor_copy(out=kTr_st[:, tok], in_=psKr[0:64, 0:128]),
                     R=[P.psb[4]], W=[bkT])

            for j in range(4):
                if LIM < 2 or (LIM < 90 and j > 0):
                    break
                do_sub(blk, j, SETS[(blk * 4 + j) % 2])
            if LIM < 5:
                break
            c0 = blk * TB
            k.dma("sp", [(P.mla_dram["QTn"][:, :, c0:c0 + TB].rearrange("h d t -> d h t"), qTn_st[:]),
                         (P.mla_dram["QTr"][:, :, c0:c0 + TB].rearrange("h d t -> d h t"), qTr_st[:])],
                  R=[bqT], W=[P.b_qkv[blk]], key="d_qT")
            k.dma("sp", [(P.mla_dram["KTn"][:, :, c0:c0 + TB].rearrange("h d t -> d h t"), kTn_st[:]),
                         (P.mla_dram["KTr"][:, c0:c0 + TB], kTr_st[:])],
                  R=[bkT], W=[P.b_qkv[blk]], key="d_kT")
        k.barrier()


def mla_attn_phase(P, src, b_src, dst, b_dst, W):
    k, nc = P.k, P.nc
    I = P.inp
    Dm = P.mla_dram
    stats = P.mla_stats
    with ExitStack() as es:
        bc = k.buf("pe_consts")
        wo = P.sb(es, "pe_wo", [128, 8, 1024], BF16)
        k.dma("sp", [(wo[:], W["odd_w_out"][0].rearrange("(c p) n -> p c n", p=128))], R=[W["odd_w_out"][1]], W=[bc])
        tps = P.ps[7]
        sm = P.sb(es, "pe_sm", [16, 4], F32)
        dg = P.sb(es, "pe_dg", [8, 8], F32)
        negc = P.sb(es, "pe_negc", [128, 8], F32)
        k.op("pe", lambda e: e.transpose(out=tps[0:16, 0:128], in_=stats[:, 0:16], identity=P.ident_f[:]),
             R=[P.b_stats, P.b_const], W=[P.psb[7]])
        k.op("dve", lambda e: e.tensor_reduce(out=sm[:, 0:1], in_=tps[0:16, 0:128], axis=AX.X, op=ALU.max),
             R=[P.psb[7]], W=[bc])
        k.op("pe", lambda e: e.transpose(out=tps[0:1, 128:144], in_=sm[:, 0:1], identity=P.ident_f[0:16, 0:16]),
             R=[bc, P.b_const], W=[P.psb[7]])
        rowv = P.sb(es, "pe_rowv", [1, 24], F32)
        k.op("dve", lambda e: e.tensor_copy(out=rowv[:, 0:16], in_=tps[0:1, 128:144]), R=[P.psb[7]], W=[bc])
        k.op("dve", lambda e: e.tensor_tensor(out=rowv[:, 16:24], in0=rowv[:, 0:8], in1=rowv[:, 8:16], op=ALU.mult),
             R=[bc], W=[bc])
        k.op("act", lambda e: e.activation(out=rowv[:, 16:24], in_=rowv[:, 16:24], func=AF.Sqrt), R=[bc], W=[bc])
        k.op("dve", lambda e: e.tensor_scalar(out=rowv[:, 16:24], in0=rowv[:, 16:24], scalar1=-1.0, scalar2=None, op0=ALU.mult),
             R=[bc], W=[bc])
        k.op("pe", lambda e: e.matmul(tps[:, 160:168], lhsT=P.ones_f[0:1, :], rhs=rowv[:, 16:24], start=True, stop=True),
             R=[bc, P.b_const], W=[P.psb[7]])
        k.op("dve", lambda e: e.tensor_copy(out=negc[:], in_=tps[:, 160:168]), R=[P.psb[7]], W=[bc])
        tri = P.sb(es, "pe_tri", [128, 128], BF16)
        k.op("pool", lambda e: e.affine_select(out=tri[:], in_=P.ones_f[:], pattern=[[1, 128]], compare_op=ALU.is_ge,
                                               fill=0.0, base=0, channel_multiplier=-1), R=[P.b_const], W=[bc])
        KTn = P.sb(es, "pe_KTn", [128, 8, S], BF16)
        KTr = P.sb(es, "pe_KTr", [128, S], BF16)
        Vs = P.sb(es, "pe_V", [128, 32, HQ * VP], BF16)
        bK = k.bufs(8, "pe_K")
        bV = k.bufs(8, "pe_V")
        k.op("pool", lambda e: e.memset(KTr[64:128, :], 0.0), W=bK)
        for b in range(NB):
            c0 = b * TB
            k.dma("sp", [(KTn[:, :, c0:c0 + TB], Dm["KTn"][:, :, c0:c0 + TB].rearrange("h d t -> d h t")),
                         (KTr[0:64, c0:c0 + TB], Dm["KTr"][:, c0:c0 + TB]),
                         (Vs[:, 4 * b:4 * b + 4, :], Dm["Va"][c0:c0 + TB, :].rearrange("(j p) c -> p j c", p=128))],
                  R=[P.b_qkv[b]], W=[bK[b], bV[b]], key=f"d_peKV{b}")
        NQ = 1
        Qn = [P.sb(es, f"pe_Qn{i}", [128, 8, TB], BF16) for i in range(NQ)]
        Qr = [P.sb(es, f"pe_Qr{i}", [128, 8, TB], BF16) for i in range(NQ)]
        bQ = k.bufs(NQ, "pe_Q")
        for i in range(NQ):
            k.op("pool", lambda e, i=i: e.memset(Qr[i][64:128, :, :], 0.0), W=[bQ[i]])
        NPT = 4
        PT = [P.sb(es, f"pe_PT{i}", [128, TB], BF16) for i in range(NPT)]
        bPT = k.bufs(NPT, "pe_PT")
        att = P.sb(es, "pe_att", [128, 4, 1024], BF16)
        batt = k.bufs(4, "pe_att")
        aT = P.sb(es, "pe_aT", [128, 8, TB], BF16)
        baT = k.buf("pe_aT")
        rden = P.sb(es, "pe_rden", [128, 4], F32)
        brden = k.buf("pe_rden")
        hres = [P.sb(es, f"pe_hres{i}", [128, 1024], F32) for i in range(2)]
        bhres = k.bufs(2, "pe_hres")
        npt = 0
        nsc = 0
        for qb in range(NB):
            qi = qb % NQ
            c0 = qb * TB
            k.dma("sp", [(Qn[qi][:], Dm["QTn"][:, :, c0:c0 + TB].rearrange("h d t -> d h t")),
                         (Qr[qi][0:64], Dm["QTr"][:, :, c0:c0 + TB].rearrange("h d t -> d h t"))],
                  R=[P.b_qkv[qb]], W=[bQ[qi]])
            nkt = 4 * qb + 4
            tiles = [(h, kt) for h in range(8) for kt in range(nkt)]
            SCB = (0, 1, 2, 7)
            info = {}

            def emit_qk(i):
                nonlocal nsc, npt
                h, kt = tiles[i]
                r = kt - 4 * qb
                q0 = max(r, 0) * 128
                sb_ = SCB[nsc % 4]
                nsc += 1
                pi = npt % NPT
                npt += 1
                kb = kt // 4
                info[i] = (sb_, pi, q0, r, kb)
                k.op("pe", lambda e, h=h, kt=kt, q0=q0, sb_=sb_: e.matmul(
                    P.ps[sb_][:, q0:TB], lhsT=KTn[:, h, kt * 128:(kt + 1) * 128], rhs=Qn[qi][:, h, q0:TB],
                    start=True, stop=False), R=[bK[kb], bQ[qi]], W=[P.psb[sb_]])
                k.op("pe", lambda e, h=h, kt=kt, q0=q0, sb_=sb_: e.matmul(
                    P.ps[sb_][:, q0:TB], lhsT=KTr[:, kt * 128:(kt + 1) * 128], rhs=Qr[qi][:, h, q0:TB],
                    start=False, stop=True), R=[bK[kb], bQ[qi]], W=[P.psb[sb_]])
                k.op("act", lambda e, h=h, q0=q0, sb_=sb_, pi=pi: e.activation(
                    out=PT[pi][:, q0:TB], in_=P.ps[sb_][:, q0:TB], func=AF.Exp, bias=negc[:, h:h + 1], scale=1.0),
                    R=[P.psb[sb_], bc], W=[bPT[pi]])
                if r >= 0:
                    k.op("pool", lambda e, q0=q0, pi=pi: e.tensor_tensor(out=PT[pi][:, q0:q0 + 128], in0=PT[pi][:, q0:q0 + 128],
                                                                         in1=tri[:], op=ALU.mult), R=[bPT[pi], bc], W=[bPT[pi]])

            LOOK = 2
            for i in range(min(LOOK, len(tiles))):
                emit_qk(i)
            first = [True, True]
            for i, (h, kt) in enumerate(tiles):
                if i + LOOK < len(tiles):
                    emit_qk(i + LOOK)
                ob = (3, 4) if h % 2 == 0 else (5, 6)
                if kt == 0:
                    first = [True, True]
                sb_, pi, q0, r, kb = info.pop(i)
                for js in range(max(r, 0), 4):
                    bank = ob[js // 2]
                    off = (js % 2) * 129
                    last = (kt == 4 * qb + js)
                    st_ = first[js // 2]
                    first[js // 2] = False
                    k.op("pe", lambda e, js=js, bank=bank, off=off, pi=pi, kt=kt, h=h, st_=st_, last=last: e.matmul(
                        P.ps[bank][:, off:off + 129], lhsT=PT[pi][:, js * 128:(js + 1) * 128],
                        rhs=Vs[:, kt, h * VP:h * VP + 129], start=st_, stop=last, skip_group_check=True),
                        R=[bPT[pi], bV[kb]], W=[P.psb[bank]])
                if kt == nkt - 1:
                    for js in range(4):
                        bank = ob[js // 2]
                        off = (js % 2) * 129
                        k.op("dve", lambda e, js=js, bank=bank, off=off: e.reciprocal(out=rden[:, js:js + 1],
                                                                                      in_=P.ps[bank][:, off + 128:off + 129]),
                             R=[P.psb[bank]], W=[brden])
                        k.op("dve", lambda e, js=js, bank=bank, off=off, h=h: e.tensor_scalar(
                            out=att[:, js, h * 128:(h + 1) * 128], in0=P.ps[bank][:, off:off + 128], scalar1=rden[:, js:js + 1],
                            scalar2=None, op0=ALU.mult), R=[P.psb[bank], brden], W=[batt[js]])
            for js in range(4):
                psA = P.ps[7][:].bitcast(BF16)
                for h in range(8):
                    k.op("pe", lambda e, js=js, h=h, psA=psA: e.transpose(out=psA[:, h * 128:(h + 1) * 128],
                                                                          in_=att[:, js, h * 128:(h + 1) * 128], identity=P.ident[:]),
                         R=[batt[js], P.b_const], W=[P.psb[7]])
                k.op("dve", lambda e, js=js, psA=psA: e.tensor_copy(out=aT[:, :, js * 128:(js + 1) * 128],
                                                                    in_=psA.rearrange("p (h t) -> p h t", h=8)),
                     R=[P.psb[7]], W=[baT])
            for js in range(4):
                r0 = qb * TB + js * 128
                hr = hres[js % 2]
                bhr = bhres[js % 2]
                k.dma("sp", [(hr[:], src[r0:r0 + 128, :])], R=[b_src[qb * 4 + js]], W=[bhr], key=f"d_hres{js % 2}")
                for n in range(2):
                    bank = (3, 5)[n]
                    for h in range(8):
                        k.op("pe", lambda e, js=js, n=n, h=h, bank=bank: e.matmul(
                            P.ps[bank][:], lhsT=aT[:, h, js * 128:(js + 1) * 128], rhs=wo[:, h, n * 512:(n + 1) * 512],
                            start=(h == 0), stop=(h == 7)), R=[baT, bc], W=[P.psb[bank]])
                    k.op("dve", lambda e, n=n, bank=bank, hr=hr: e.tensor_tensor(
                        out=hr[:, n * 512:(n + 1) * 512], in0=hr[:, n * 512:(n + 1) * 512], in1=P.ps[bank][:],
                        op=ALU.add), R=[bhr, P.psb[bank]], W=[bhr])
                k.dma("sp", [(dst[r0:r0 + 128, :], hr[:])], R=[bhr], W=[b_dst[qb * 4 + js]], key=f"d_hst{js % 2}")
        k.barrier()


def hgrn_phase(P, src, b_src, W):
    k, nc = P.k, P.nc
    I = P.inp
    uT_d = P.dint("uT", [512, S], BF16)
    ycatT = P.dint("ycatT", [1024, S], BF16)
    P.uT_d, P.ycatT = uT_d, ycatT
    P.b_uT = k.bufs(NB, "uT")
    P.b_ya = k.bufs(NB, "ya")
    P.b_yb = k.bufs(NB, "yb")
    with ExitStack() as es:
        bc = k.buf("pa_consts")
        gT = load_vec_fm(P, es, "pa_g", I["norm_mix_g"][0], 8, bc)
        win = P.sb(es, "pa_win", [128, 8, 2560], BF16)
        k.dma("sp", [(win[:, 4 * i:4 * i + 4, :], W["even_w_in"][0][512 * i:512 * (i + 1), :].rearrange("(c p) n -> p c n", p=128))
                     for i in range(2)], R=[W["even_w_in"][1]], W=[bc])
        l0 = load_vec_fm(P, es, "pa_l0", I["hgrn_lb_logits"][0], 4, bc)
        l1 = load_vec_fm(P, es, "pa_l1", I["hgrn_lb_logits"][1], 4, bc)
        lb = P.sb(es, "pa_lb", [128, 4], F32)
        oml = P.sb(es, "pa_oml", [128, 4], F32)
        noml = P.sb(es, "pa_noml", [128, 4], F32)
        k.op("dve", lambda e: e.tensor_tensor(out=lb[:], in0=l0[:], in1=l1[:], op=ALU.subtract), R=[bc], W=[bc])
        k.op("act", lambda e: e.activation(out=lb[:], in_=lb[:], func=AF.Sigmoid), R=[bc], W=[bc])
        k.op("dve", lambda e: e.tensor_scalar(out=oml[:], in0=lb[:], scalar1=-1.0, scalar2=1.0, op0=ALU.mult, op1=ALU.add),
             R=[bc], W=[bc])
        k.op("dve", lambda e: e.tensor_scalar(out=noml[:], in0=oml[:], scalar1=-1.0, scalar2=None, op0=ALU.mult),
             R=[bc], W=[bc])
        ng = P.sb(es, "pa_ng", [128, 512], F32)
        k.dma("sp", [(ng[:], I["hgrn_norm_g"].partition_broadcast(128))], W=[bc])
        msk = P.sb(es, "pa_msk", [128, 128], F32)
        k.op("pool", lambda e: e.affine_select(out=msk[:], in_=P.ones_f[:], pattern=[[1, 128]], compare_op=ALU.is_ge,
                                               fill=0.0, base=0, channel_multiplier=-1), R=[P.b_const], W=[bc])
        k.op("pool", lambda e: e.memset(msk[0:64, 64:128], 0.0), R=[bc], W=[bc])
        rst = P.sb(es, "pa_rst", [128, 512], F32)
        k.op("pool", lambda e: e.memset(rst[:], 1.0), W=[bc])
        k.op("pool", lambda e: e.memset(rst[:, :].rearrange("p (n c) -> p n c", c=64)[:, :, 0:1], 0.0), R=[bc], W=[bc])
        mA = P.sb(es, "pa_mA", [128, 512], BF16)
        mB = P.sb(es, "pa_mB", [128, 512], BF16)
        k.op("pool", lambda e: e.memset(mA[:], 1.0), W=[bc])
        k.op("pool", lambda e: e.memset(mA[:, :].rearrange("p (n c) -> p n c", c=128)[:, :, 64:128], 0.0), R=[bc], W=[bc])
        k.op("pool", lambda e: e.memset(mB[:], 1.0), W=[bc])
        k.op("pool", lambda e: e.memset(mB[:, :].rearrange("p (n c) -> p n c", c=128)[:, :, 0:64], 0.0), R=[bc], W=[bc])
        rmA = P.sb(es, "pa_rmA", [128, 1], F32)
        rmB = P.sb(es, "pa_rmB", [128, 1], F32)
        k.op("pool", lambda e: e.memset(rmA[0:64, :], 1.0), W=[bc])
        k.op("pool", lambda e: e.memset(rmA[64:128, :], 0.0), W=[bc])
        k.op("pool", lambda e: e.memset(rmB[0:64, :], 0.0), W=[bc])
        k.op("pool", lambda e: e.memset(rmB[64:128, :], 1.0), W=[bc])
        St = P.sb(es, "pa_S", [128, 4, 128], F32)
        Sbf = [P.sb(es, f"pa_Sbf{i}", [128, 4, 128], BF16) for i in range(2)]
        bS = k.bufs(4, "pa_S")
        bSbf = [k.bufs(4, "pa_SbfA"), k.bufs(4, "pa_SbfB")]
        k.op("pool", lambda e: e.memset(St[:], 0.0), W=bS)
        k.op("pool", lambda e: e.memset(Sbf[0][:], 0.0), W=bSbf[0])
        k.op("pool", lambda e: e.memset(Sbf[1][:], 0.0), W=bSbf[1])

        T = alloc_blockbufs(P, es, "a_", ps_tr=[7])
        T["b_g"] = bc
        sig = P.sb(es, "pa_sig", [128, 512], F32); bsig = k.buf("pa_sig")
        logf = P.sb(es, "pa_logf", [128, 512], F32); blogf = k.buf("pa_logf")
        bb = P.sb(es, "pa_b", [128, 512], F32); bbb = k.buf("pa_b")
        eb = P.sb(es, "pa_eb", [128, 4, 512], F32); beb = k.bufs(4, "pa_eb")
        enb = P.sb(es, "pa_enb", [128, 512], F32); benb = k.buf("pa_enb")
        kk = P.sb(es, "pa_kk", [128, 512], F32); bkk = k.buf("pa_kk")
        kd32 = P.sb(es, "pa_kd32", [128, 512], F32); bkd32 = k.buf("pa_kd32")
        qd = P.sb(es, "pa_qd", [128, 4, 512], BF16); bqd = k.bufs(4, "pa_qd")
        kd = P.sb(es, "pa_kd", [128, 4, 512], BF16); bkd = k.bufs(4, "pa_kd")
        qdA = P.sb(es, "pa_qdA", [128, 4, 512], BF16); qdB = P.sb(es, "pa_qdB", [128, 4, 512], BF16)
        kbT = P.sb(es, "pa_kbT", [128, 4, 512], BF16); bkbT = k.bufs(4, "pa_kbT")
        uTs = P.sb(es, "pa_uT", [128, 4, 512], BF16); buTs = k.buf("pa_uTs")
        kbtm = P.sb(es, "pa_kbtm", [128, 4, 128], BF16); bkbtm = k.buf("pa_kbtm")
        kbtmB = P.sb(es, "pa_kbtmB", [128, 4, 128], BF16)
        vtm = P.sb(es, "pa_vtm", [128, 512], BF16); bvtm = k.buf("pa_vtm")
        ngsg = P.sb(es, "pa_ngsg", [128, 512], F32); bngsg = k.buf("pa_ngsg")
        attm = P.sb(es, "pa_attm", [128, 4, 128], BF16); battm = k.bufs(4, "pa_attm")
        yatm = P.sb(es, "pa_yatm", [128, 512], BF16); byatm = k.buf("pa_yatm")
        yaT = P.sb(es, "pa_yaT", [128, 4, 512], BF16); byaT = k.buf("pa_yaT")
        sst = P.sb(es, "pa_sst", [128, 16], F32); bsst = k.bufs(4, "pa_sst")
        junk = P.sb(es, "pa_junk", [128, 128], F32); bjunk = k.buf("pa_junk")
        import os
        HL = float(os.environ.get("HG_LIMIT", "99"))
        for blk in range(NB):
            if HL < 1 or (HL < 90 and blk > 0):
                break
            norm_block(P, T, blk, src, b_src, gT)
            hnT, bh = T["hnT"], T["b_hnT"]
            c0 = blk * TB
            for c in range(4):
                pb = 4 + c % 2
                for kc in range(8):
                    k.op("pe", lambda e, c=c, kc=kc, pb=pb: e.matmul(P.ps[pb][:], lhsT=win[:, kc, 2048 + c * 128:2048 + (c + 1) * 128],
                                                                     rhs=hnT[:, kc, :], start=(kc == 0), stop=(kc == 7)),
                         R=[bc, bh], W=[P.psb[pb]])
                k.op("act", lambda e, c=c, pb=pb: e.copy(out=uTs[:, c, :], in_=P.ps[pb][:]), R=[P.psb[pb]], W=[buTs])
            k.dma("sp", [(uT_d[:, c0:c0 + TB].rearrange("(c p) t -> p c t", p=128), uTs[:])], R=[buTs], W=[P.b_uT[blk]], key="d_uTst")
            if HL < 1.1:
                break
            for h in range(4):
                if HL < 1.2 and h > 0:
                    break
                pq, pf = 4, 5
                for kc in range(8):
                    k.op("pe", lambda e, h=h, kc=kc: e.matmul(P.ps[pf][:], lhsT=win[:, kc, 512 + h * 128:512 + (h + 1) * 128],
                                                              rhs=hnT[:, kc, :], start=(kc == 0), stop=(kc == 7)),
                         R=[bc, bh], W=[P.psb[pf]])
                for kc in range(8):
                    k.op("pe", lambda e, h=h, kc=kc: e.matmul(P.ps[pq][:], lhsT=win[:, kc, h * 128:(h + 1) * 128],
                                                              rhs=hnT[:, kc, :], start=(kc == 0), stop=(kc == 7)),
                         R=[bc, bh], W=[P.psb[pq]])
                k.op("act", lambda e: e.activation(out=sig[:], in_=P.ps[pf][:], func=AF.Sigmoid), R=[P.psb[pf]], W=[bsig])
                k.op("act", lambda e, h=h: e.activation(out=logf[:], in_=sig[:], func=AF.Ln, bias=lb[:, h:h + 1],
                                                        scale=oml[:, h:h + 1]), R=[bsig, bc], W=[blogf])
                if HL < 1.15:
                    break
                k.op("dve", lambda e: e.tensor_tensor_scan(out=bb[:], data0=rst[:], data1=logf[:], initial=0.0,
                                                           op0=ALU.mult, op1=ALU.add), R=[blogf, bc], W=[bbb])
                if HL < 1.17:
                    break
                k.op("act", lambda e, h=h: e.activation(out=eb[:, h, :], in_=bb[:], func=AF.Exp), R=[bbb], W=[beb[h]])
                k.op("act", lambda e: e.activation(out=enb[:], in_=bb[:], func=AF.Exp, scale=-1.0), R=[bbb], W=[benb])
                k.op("dve", lambda e, h=h: e.tensor_scalar(out=kk[:], in0=sig[:], scalar1=noml[:, h:h + 1], scalar2=oml[:, h:h + 1],
                                                           op0=ALU.mult, op1=ALU.add), R=[bsig, bc], W=[bkk])
                k.op("dve", lambda e, h=h: e.tensor_tensor(out=qd[:, h, :], in0=eb[:, h, :], in1=P.ps[pq][:], op=ALU.mult),
                     R=[beb[h], P.psb[pq]], W=[bqd[h]])
                k.op("pool", lambda e, h=h: e.tensor_tensor(out=qdA[:, h, :], in0=qd[:, h, :], in1=mA[:], op=ALU.mult),
                     R=[bqd[h], bc], W=[bqd[h]])
                k.op("pool", lambda e, h=h: e.tensor_tensor(out=qdB[:, h, :], in0=qd[:, h, :], in1=mB[:], op=ALU.mult),
                     R=[bqd[h], bc], W=[bqd[h]])
                k.op("dve", lambda e: e.tensor_tensor(out=kd32[:], in0=kk[:], in1=enb[:], op=ALU.mult), R=[bkk, benb], W=[bkd32])
                if HL < 1.18:
                    break
                k.op("pool", lambda e, h=h: e.tensor_copy(out=kd[:, h, :], in_=kd32[:]), R=[bkd32], W=[bkd[h]])
                k.op("pool", lambda e, h=h: e.tensor_tensor(
                    out=kbT[:, h, :].rearrange("p (n c) -> p n c", c=64), in0=kd32[:, :].rearrange("p (n c) -> p n c", c=64),
                    in1=eb[:, h, :].rearrange("p (n c) -> p n c", c=64)[:, :, 63:64].broadcast_to([128, 8, 64]), op=ALU.mult),
                    R=[bkd32, beb[h]], W=[bkbT[h]])
            for j in range(4):
                if HL < 2 or (HL < 90 and j > 0):
                    break
                tok = slice(j * 128, (j + 1) * 128)
                for kc in range(8):
                    k.op("pe", lambda e, kc=kc, tok=tok: e.matmul(P.ps[4][:], lhsT=hnT[:, kc, tok], rhs=win[:, kc, 1024:1536],
                                                                  start=(kc == 0), stop=(kc == 7)), R=[bc, bh], W=[P.psb[4]])
                for kc in range(8):
                    k.op("pe", lambda e, kc=kc, tok=tok: e.matmul(P.ps[5][:], lhsT=hnT[:, kc, tok], rhs=win[:, kc, 1536:2048],
                                                                  start=(kc == 0), stop=(kc == 7)), R=[bc, bh], W=[P.psb[5]])
                k.op("act", lambda e: e.copy(out=vtm[:], in_=P.ps[4][:]), R=[P.psb[4]], W=[bvtm])
                k.op("act", lambda e: e.activation(out=ngsg[:], in_=P.ps[5][:], func=AF.Silu), R=[P.psb[5]], W=[bngsg])
                k.op("pool", lambda e: e.tensor_tensor(out=ngsg[:], in0=ngsg[:], in1=ng[:], op=ALU.mult), R=[bngsg, bc], W=[bngsg])
                psK = P.ps[6][:].bitcast(BF16)
                for h in range(4):
                    k.op("pe", lambda e, h=h, tok=tok, psK=psK: e.transpose(out=psK[:, h * 128:(h + 1) * 128], in_=kbT[:, h, tok],
                                                                            identity=P.ident[:]), R=[bkbT[h], P.b_const], W=[P.psb[6]])
                k.op("act", lambda e, psK=psK: e.activation(out=kbtm[:], in_=psK[:, 0:512].rearrange("p (h d) -> p h d", h=4),
                                                            func=AF.Copy, scale=rmA[:, 0:1]), R=[P.psb[6], bc], W=[bkbtm])
                k.op("act", lambda e, psK=psK: e.activation(out=kbtmB[:], in_=psK[:, 0:512].rearrange("p (h d) -> p h d", h=4),
                                                            func=AF.Copy, scale=rmB[:, 0:1]), R=[P.psb[6], bc], W=[bkbtm])
                tA = j * 128 + 63
                tB = j * 128 + 127
                if HL < 3:
                    break
                for h in range(4):
                    pb = h
                    vh = vtm[:, h * 128:(h + 1) * 128]
                    k.op("pe", lambda e, h=h, tok=tok, pb=pb: e.matmul(P.ps[pb][:, 0:128], lhsT=kd[:, h, tok], rhs=qd[:, h, tok],
                                                                       start=True, stop=True), R=[bkd[h], bqd[h]], W=[P.psb[pb]])
                    k.op("pe", lambda e, h=h, pb=pb, vh=vh: e.matmul(P.ps[pb][:, 128:256], lhsT=kbtm[:, h, :], rhs=vh,
                                                                     start=True, stop=True), R=[bkbtm, bvtm], W=[P.psb[pb]])
                    k.op("pe", lambda e, h=h, pb=pb, vh=vh: e.matmul(P.ps[pb][:, 256:384], lhsT=kbtmB[:, h, :], rhs=vh,
                                                                     start=True, stop=True), R=[bkbtm, bvtm], W=[P.psb[pb]])
                    k.op("dve", lambda e, h=h, pb=pb: e.tensor_tensor(out=attm[:, h, :], in0=P.ps[pb][:, 0:128], in1=msk[:], op=ALU.mult),
                         R=[P.psb[pb], bc], W=[battm[h]])
                    k.op("dve", lambda e, h=h, pb=pb, tA=tA: e.scalar_tensor_tensor(
                        out=St[:, h, :], in0=St[:, h, :], scalar=eb[:, h, tA:tA + 1], in1=P.ps[pb][:, 128:256], op0=ALU.mult, op1=ALU.add),
                        R=[bS[h], beb[h], P.psb[pb]], W=[bS[h]])
                    k.op("act", lambda e, h=h: e.copy(out=Sbf[1][:, h, :], in_=St[:, h, :]), R=[bS[h]], W=[bSbf[1][h]])
                if HL < 4:
                    break
                for h in range(4):
                    pb = h
                    po = 4 + h % 2
                    vh = vtm[:, h * 128:(h + 1) * 128]
                    tokA = slice(j * 128, j * 128 + 64)
                    tokB = slice(j * 128 + 64, j * 128 + 128)
                    k.op("pe", lambda e, h=h, po=po, vh=vh: e.matmul(P.ps[po][:, 0:128], lhsT=attm[:, h, :], rhs=vh,
                                                                     start=True, stop=False), R=[battm[h], bvtm], W=[P.psb[po]])
                    k.op("pe", lambda e, h=h, po=po, tok=tok: e.matmul(P.ps[po][:, 0:128], lhsT=qdA[:, h, tok], rhs=Sbf[0][:, h, :],
                                                                       start=False, stop=False),
                         R=[bqd[h], bSbf[0][h]], W=[P.psb[po]])
                    k.op("pe", lambda e, h=h, po=po, tok=tok: e.matmul(P.ps[po][:, 0:128], lhsT=qdB[:, h, tok], rhs=Sbf[1][:, h, :],
                                                                       start=False, stop=True),
                         R=[bqd[h], bSbf[1][h]], W=[P.psb[po]])
                    k.op("dve", lambda e, h=h, pb=pb, tB=tB: e.scalar_tensor_tensor(
                        out=St[:, h, :], in0=St[:, h, :], scalar=eb[:, h, tB:tB + 1], in1=P.ps[pb][:, 256:384], op0=ALU.mult, op1=ALU.add),
                        R=[bS[h], beb[h], P.psb[pb]], W=[bS[h]])
                    k.op("act", lambda e, h=h: e.copy(out=Sbf[0][:, h, :], in_=St[:, h, :]), R=[bS[h]], W=[bSbf[0][h]])
                    k.op("act", lambda e, h=h, po=po: e.activation(out=junk[:], in_=P.ps[po][:, 0:128], func=AF.Square,
                                                                   accum_out=sst[:, h:h + 1]), R=[P.psb[po]], W=[bjunk, bsst[h]])
                    k.op("act", lambda e, h=h: e.activation(out=sst[:, 4 + h:5 + h], in_=sst[:, h:h + 1], func=AF.Sqrt, bias=P.epsb[:],
                                                            scale=1.0 / 128), R=[bsst[h], P.b_const], W=[bsst[h]])
                    k.op("dve", lambda e, h=h: e.reciprocal(out=sst[:, 8 + h:9 + h], in_=sst[:, 4 + h:5 + h]), R=[bsst[h]], W=[bsst[h]])
                    k.op("dve", lambda e, h=h, po=po: e.scalar_tensor_tensor(
                        out=yatm[:, h * 128:(h + 1) * 128], in0=P.ps[po][:, 0:128], scalar=sst[:, 8 + h:9 + h],
                        in1=ngsg[:, h * 128:(h + 1) * 128], op0=ALU.mult, op1=ALU.mult),
                        R=[P.psb[po], bsst[h], bngsg], W=[byatm])
                if HL < 5:
                    break
                psY = P.ps[6][:].bitcast(BF16)
                for h in range(4):
                    k.op("pe", lambda e, h=h, psY=psY: e.transpose(out=psY[:, h * 128:(h + 1) * 128], in_=yatm[:, h * 128:(h + 1) * 128],
                                                                   identity=P.ident[:]), R=[byatm, P.b_const], W=[P.psb[6]])
                k.op("dve", lambda e, psY=psY, tok=tok: e.tensor_copy(out=yaT[:, :, tok], in_=psY[:, 0:512].rearrange("p (h t) -> p h t", h=4)),
                     R=[P.psb[6]], W=[byaT])
            if HL < 6:
                break
            k.dma("sp", [(ycatT[0:512, c0:c0 + TB].rearrange("(c p) t -> p c t", p=128), yaT[:])], R=[byaT], W=[P.b_ya[blk]], key="d_yast")
        k.barrier()


def s5_phase(P, W):
    k, nc = P.k, P.nc
    I = P.inp
    L = 16
    NCH = S // L
    G = 32
    with ExitStack() as es:
        bs = k.buf("s5_tab")

        def dv(fn, R=(), Wb=()):
            k.op("dve", fn, R=[bs] + list(R), W=[bs] + list(Wb))

        def ac(fn, R=(), Wb=()):
            k.op("act", fn, R=[bs] + list(R), W=[bs] + list(Wb))

        def t32(name, shape=(128, 32)):
            return P.sb(es, "s5_" + name, list(shape), F32)

        AR, AI, LDT = t32("AR"), t32("AI"), t32("LDT")
        k.dma("sp", [(AR[0:64, :], I["s5_a_re"].rearrange("g p -> p g")), (AR[64:128, :], I["s5_a_re"].rearrange("g p -> p g")),
                     (AI[0:64, :], I["s5_a_im"].rearrange("g p -> p g")), (AI[64:128, :], I["s5_a_im"].rearrange("g p -> p g")),
                     (LDT[:], I["s5_log_dt"].partition_broadcast(128))], W=[bs], allow_slow_non_contiguous=True)
        B1 = t32("B1", (128, 32, 16))
        B2 = t32("B2", (128, 32, 16))
        k.dma("sp", [(B1[0:64], I["s5_b_re"].rearrange("g p h -> p g h")), (B1[64:128], I["s5_b_im"].rearrange("g p h -> p g h")),
                     (B2[0:64], I["s5_b_im"].rearrange("g p h -> p g h")), (B2[64:128], I["s5_b_re"].rearrange("g p h -> p g h"))],
              W=[bs], key="d_s5_b")
        CN1 = t32("CN1", (128, 4, 128))
        CN2 = t32("CN2", (128, 4, 128))
        cre = I["s5_c_re"].rearrange("(f a) h p -> (a h) f p", f=4)
        cim = I["s5_c_im"].rearrange("(f a) h p -> (a h) f p", f=4)
        k.dma("sp", [(CN1[:, :, 0:64], cre), (CN1[:, :, 64:128], cim), (CN2[:, :, 0:64], cim), (CN2[:, :, 64:128], cre)],
              W=[bs], key="d_s5_c")
        dT = load_vec_fm(P, es, "s5_dT", I["s5_d"], 4, bs)
        bgl = load_vec_fm(P, es, "s5_bglu", I["s5_b_glu"], 4, bs)
        sgn = t32("sgn", (128, 1))
        dv(lambda e: e.memset(sgn[0:64, :], -1.0))
        dv(lambda e: e.memset(sgn[64:128, :], 1.0))
        DT, XR, TH = t32("DT"), t32("XR"), t32("TH")
        ac(lambda e: e.activation(out=DT[:], in_=LDT[:], func=AF.Exp))
        dv(lambda e: e.tensor_tensor(out=XR[:], in0=AR[:], in1=DT[:], op=ALU.mult))
        dv(lambda e: e.tensor_tensor(out=TH[:], in0=AI[:], in1=DT[:], op=ALU.mult))
        MAG = t32("MAG")
        ac(lambda e: e.activation(out=MAG[:], in_=XR[:], func=AF.Exp))
        tf, r1, r2 = t32("tf"), t32("r1"), t32("r2")
        ti_ = P.sb(es, "s5_ti", [128, 32], I32)
        dv(lambda e: e.tensor_scalar(out=tf[:], in0=TH[:], scalar1=1.0 / TWO_PI, scalar2=None, op0=ALU.mult))
        dv(lambda e: e.tensor_copy(out=ti_[:], in_=tf[:]))
        dv(lambda e: e.tensor_copy(out=tf[:], in_=ti_[:]))
        dv(lambda e: e.scalar_tensor_tensor(out=r1[:], in0=tf[:], scalar=-CW1, in1=TH[:], op0=ALU.mult, op1=ALU.add))
        dv(lambda e: e.scalar_tensor_tensor(out=r1[:], in0=tf[:], scalar=-CW2, in1=r1[:], op0=ALU.mult, op1=ALU.add))
        dv(lambda e: e.tensor_scalar(out=r1[:], in0=r1[:], scalar1=-math.pi, scalar2=math.pi, op0=ALU.max, op1=ALU.min))
        SN, CS = t32("SN"), t32("CS")
        ac(lambda e: e.activation(out=SN[:], in_=r1[:], func=AF.Sin))
        dv(lambda e: e.tensor_scalar(out=tf[:], in0=r1[:], scalar1=math.pi / 2, scalar2=math.pi, op0=ALU.add, op1=ALU.is_gt))
        dv(lambda e: e.scalar_tensor_tensor(out=tf[:], in0=tf[:], scalar=-TWO_PI, in1=r1[:], op0=ALU.mult, op1=ALU.add))
        dv(lambda e: e.tensor_scalar(out=tf[:], in0=tf[:], scalar1=math.pi / 2, scalar2=math.pi, op0=ALU.add, op1=ALU.min))
        ac(lambda e: e.activation(out=CS[:], in_=tf[:], func=AF.Sin))
        pows = list(range(0, 17)) + [16 << i for i in range(1, 8)]
        TR = {n: t32(f"TR{n}") for n in pows}
        TI = {n: t32(f"TI{n}") for n in pows}
        dv(lambda e: e.memset(TR[0][:], 1.0))
        dv(lambda e: e.memset(TI[0][:], 0.0))
        dv(lambda e: e.tensor_tensor(out=TR[1][:], in0=MAG[:], in1=CS[:], op=ALU.mult))
        dv(lambda e: e.tensor_tensor(out=TI[1][:], in0=MAG[:], in1=SN[:], op=ALU.mult))
        ta, tb = t32("ta"), t32("tb")

        def cmul(oR, oI, aR, aI, bR, bI):
            dv(lambda e: e.tensor_tensor(out=ta[:], in0=aI[:], in1=bI[:], op=ALU.mult))
            dv(lambda e: e.tensor_tensor(out=tb[:], in0=aR[:], in1=bI[:], op=ALU.mult))
            dv(lambda e: e.tensor_tensor(out=oR[:], in0=aR[:], in1=bR[:], op=ALU.mult))
            dv(lambda e: e.tensor_tensor(out=oR[:], in0=oR[:], in1=ta[:], op=ALU.subtract))
            dv(lambda e: e.tensor_tensor(out=oI[:], in0=aI[:], in1=bR[:], op=ALU.mult))
            dv(lambda e: e.tensor_tensor(out=oI[:], in0=oI[:], in1=tb[:], op=ALU.add))
        for n in range(2, 17):
            cmul(TR[n], TI[n], TR[n - 1], TI[n - 1], TR[1], TI[1])
        for i in range(1, 8):
            n = 16 << i
            cmul(TR[n], TI[n], TR[n // 2], TI[n // 2], TR[n // 2], TI[n // 2])
        den, xr_, cr_, ci_ = t32("den"), t32("xr"), t32("cr"), t32("ci")
        dv(lambda e: e.tensor_tensor(out=den[:], in0=AR[:], in1=AR[:], op=ALU.mult))
        dv(lambda e: e.tensor_tensor(out=ta[:], in0=AI[:], in1=AI[:], op=ALU.mult))
        dv(lambda e: e.tensor_tensor(out=den[:], in0=den[:], in1=ta[:], op=ALU.add))
        dv(lambda e: e.reciprocal(out=den[:], in_=den[:]))
        dv(lambda e: e.tensor_scalar(out=xr_[:], in0=TR[1][:], scalar1=-1.0, scalar2=None, op0=ALU.add))
        dv(lambda e: e.tensor_tensor(out=cr_[:], in0=xr_[:], in1=AR[:], op=ALU.mult))
        dv(lambda e: e.tensor_tensor(out=ta[:], in0=TI[1][:], in1=AI[:], op=ALU.mult))
        dv(lambda e: e.tensor_tensor(out=cr_[:], in0=cr_[:], in1=ta[:], op=ALU.add))
        dv(lambda e: e.tensor_tensor(out=cr_[:], in0=cr_[:], in1=den[:], op=ALU.mult))
        dv(lambda e: e.tensor_tensor(out=ci_[:], in0=TI[1][:], in1=AR[:], op=ALU.mult))
        dv(lambda e: e.tensor_tensor(out=ta[:], in0=xr_[:], in1=AI[:], op=ALU.mult))
        dv(lambda e: e.tensor_tensor(out=ci_[:], in0=ci_[:], in1=ta[:], op=ALU.subtract))
        dv(lambda e: e.tensor_tensor(out=ci_[:], in0=ci_[:], in1=den[:], op=ALU.mult))
        cis = t32("cis")
        dv(lambda e: e.tensor_scalar(out=cis[:], in0=ci_[:], scalar1=sgn[:, 0:1], scalar2=None, op0=ALU.mult))
        ncis = t32("ncis")
        dv(lambda e: e.tensor_scalar(out=ncis[:], in0=cis[:], scalar1=-1.0, scalar2=None, op0=ALU.mult))

        def bc16(t):
            return t[:, :].unsqueeze(2).broadcast_to([128, 32, 16])
        BB = t32("BB", (128, 32, 16))
        BBs = t32("BBs", (128, 32, 16))
        tmp3 = t32("tmp3", (128, 32, 16))
        dv(lambda e: e.tensor_tensor(out=BB[:], in0=B1[:], in1=bc16(cr_), op=ALU.mult))
        dv(lambda e: e.tensor_tensor(out=tmp3[:], in0=B2[:], in1=bc16(cis), op=ALU.mult))
        dv(lambda e: e.tensor_tensor(out=BB[:], in0=BB[:], in1=tmp3[:], op=ALU.add))
        dv(lambda e: e.tensor_tensor(out=BBs[:], in0=B2[:], in1=bc16(cr_), op=ALU.mult))
        dv(lambda e: e.tensor_tensor(out=tmp3[:], in0=B1[:], in1=bc16(ncis), op=ALU.mult))
        dv(lambda e: e.tensor_tensor(out=BBs[:], in0=BBs[:], in1=tmp3[:], op=ALU.add))
        CT1 = t32("CT1", (128, 32, 16))
        CT2 = t32("CT2", (128, 32, 16))
        for f in range(4):
            for (CN, CT) in ((CN1, CT1), (CN2, CT2)):
                k.op("pe", lambda e, f=f, CN=CN: e.transpose(out=P.ps[0][:, 0:128], in_=CN[:, f, :], identity=P.ident_f[:]),
                     R=[bs, P.b_const], W=[P.psb[0]])
                dv(lambda e, f=f, CT=CT: e.tensor_copy(out=CT[:, 8 * f:8 * f + 8, :].rearrange("p g h -> p (g h)"), in_=P.ps[0][:, 0:128]),
                   R=[P.psb[0]])
        CTn = t32("CTn", (128, 32, 16))
        dv(lambda e: e.tensor_scalar(out=CTn[:], in0=CT1[:], scalar1=sgn[:, 0:1], scalar2=-1.0, op0=ALU.mult, op1=ALU.mult))
        pi_ = P.sb(es, "s5_pi", [128, 1], I32)
        k.op("pool", lambda e: e.iota(out=pi_[:], pattern=[[0, 1]], base=0, channel_multiplier=1), R=[bs], W=[bs])
        pg_i = P.sb(es, "s5_pgi", [128, 1], I32)
        om_i = P.sb(es, "s5_omi", [128, 1], I32)
        dv(lambda e: e.tensor_scalar(out=pg_i[:], in0=pi_[:], scalar1=4, scalar2=None, op0=ALU.arith_shift_right))
        dv(lambda e: e.tensor_scalar(out=om_i[:], in0=pg_i[:], scalar1=1, scalar2=None, op0=ALU.bitwise_and))
        pgf, om, em = t32("pgf", (128, 1)), t32("om", (128, 1)), t32("em", (128, 1))
        dv(lambda e: e.tensor_copy(out=pgf[:], in_=pg_i[:]))
        dv(lambda e: e.tensor_copy(out=om[:], in_=om_i[:]))
        dv(lambda e: e.tensor_scalar(out=em[:], in0=om[:], scalar1=-1.0, scalar2=1.0, op0=ALU.mult, op1=ALU.add))
        ji = P.sb(es, "s5_ji", [128, 128], I32)
        k.op("pool", lambda e: e.iota(out=ji[:], pattern=[[1, 128]], base=0, channel_multiplier=0), R=[bs], W=[bs])
        dv(lambda e: e.tensor_scalar(out=ji[:], in0=ji[:], scalar1=4, scalar2=None, op0=ALU.arith_shift_right))
        bmask = t32("bmask", (128, 128))
        dv(lambda e: e.tensor_copy(out=bmask[:], in_=ji[:]))
        dv(lambda e: e.tensor_scalar(out=bmask[:], in0=bmask[:], scalar1=pgf[:, 0:1], scalar2=None, op0=ALU.is_equal))
        esw = t32("esw", (128, 128))
        dv(lambda e: e.memset(esw[:], 0.0))
        dv(lambda e: e.tensor_copy(out=esw[0:64, 64:128], in_=P.ident_f[0:64, 0:64]), R=[P.b_const])
        dv(lambda e: e.tensor_copy(out=esw[64:128, 0:64], in_=P.ident_f[64:128, 64:128]), R=[P.b_const])

        uT = P.sb(es, "s5_uT", [128, 4, S], BF16)
        buT = k.bufs(NB, "s5_uT")
        for b in range(NB):
            c0 = b * TB
            k.dma("sp", [(uT[:, :, c0:c0 + TB], P.uT_d[:, c0:c0 + TB].rearrange("(c p) t -> p c t", p=128))],
                  R=[P.b_uT[b]], W=[buT[b]], key="d_s5uT")
        Xb = P.sb(es, "s5_Xb", [128, G, NCH], BF16)
        bXb = k.bufs(G, "s5_Xb")
        Kb = P.sb(es, "s5_Kb", [128, 4, L, 128], BF16)
        bKb = k.buf("s5_Kb")

        def mb_table(n, MB):
            tis = t32(f"tis_{n}") if False else ta
            dv(lambda e: e.tensor_scalar(out=ta[:], in0=TI[n][:], scalar1=sgn[:, 0:1], scalar2=None, op0=ALU.mult))
            dv(lambda e: e.tensor_tensor(out=MB[:], in0=BB[:], in1=bc16(TR[n]), op=ALU.mult))
            dv(lambda e: e.tensor_tensor(out=tmp3[:], in0=BBs[:], in1=bc16(ta), op=ALU.mult))
            dv(lambda e: e.tensor_tensor(out=MB[:], in0=MB[:], in1=tmp3[:], op=ALU.add))

        with ExitStack() as es2:
            Pm = P.sb(es2, "s5_Pm", [128, 4, L, 2, 128], BF16)
            bPm = k.buf("s5_Pm")
            MB = P.sb(es2, "s5_MB", [128, 32, 16], F32)
            for n in range(L):
                mb_table(n, MB)
                j = L - 1 - n
                for f in range(4):
                    mbf = MB[:, 8 * f:8 * f + 8, :].rearrange("p g h -> p (g h)")
                    k.op("pe", lambda e, mbf=mbf: e.transpose(out=P.ps[0][:, 0:128], in_=mbf, identity=P.ident_f[:]),
                         R=[bs, P.b_const], W=[P.psb[0]])
                    k.op("dve", lambda e, f=f, j=j: e.tensor_scalar(out=Pm[:, f, j, 0, :], in0=P.ps[0][:, 0:128], scalar1=em[:, 0:1],
                                                                    scalar2=None, op0=ALU.mult), R=[P.psb[0], bs], W=[bPm])
                    k.op("dve", lambda e, f=f, j=j: e.tensor_scalar(out=Pm[:, f, j, 1, :], in0=P.ps[0][:, 0:128], scalar1=om[:, 0:1],
                                                                    scalar2=None, op0=ALU.mult), R=[P.psb[0], bs], W=[bPm])
                    ctf = CTn[:, 8 * f:8 * f + 8, :].rearrange("p g h -> p (g h)")
                    k.op("pe", lambda e, mbf=mbf, ctf=ctf: e.matmul(P.ps[1][:, 0:128], lhsT=mbf, rhs=ctf, start=True, stop=True),
                         R=[bs], W=[P.psb[1]])
                    if n == 0:
                        k.op("dve", lambda e: e.tensor_tensor(out=tmpK[:], in0=P.ps[1][:, 0:128], in1=bmask[:], op=ALU.mult),
                             R=[P.psb[1], bs], W=[bs]) if False else None
                    k.op("dve", lambda e, f=f, n=n: e.tensor_tensor(out=Kb[:, f, n, :], in0=P.ps[1][:, 0:128], in1=bmask[:], op=ALU.mult),
                         R=[P.psb[1], bs], W=[bKb])
                    if n == 0:
                        k.op("dve", lambda e, f=f: e.scalar_tensor_tensor(out=Kb[:, f, 0, :], in0=P.ident_f[:], scalar=dT[:, f:f + 1],
                                                                          in1=Kb[:, f, 0, :], op0=ALU.mult, op1=ALU.add),
                             R=[bKb, bs, P.b_const], W=[bKb])
            NBG = 4
            Xs = [P.sb(es2, f"s5_Xs{i}", [128, NCH], F32) for i in range(NBG)]
            bXs = k.bufs(NBG, "s5_Xs")
            NMT = 8
            MT = [P.sb(es2, f"s5_MT{i}", [128, 128], F32) for i in range(NMT)]
            bMT = k.bufs(NMT, "s5_MT")
            tM = [P.sb(es2, f"s5_tM{i}", [128, 128], F32) for i in range(4)]
            btM = k.bufs(4, "s5_tM")
            w2 = P.sb(es2, "s5_w2", [128, 32, 8], F32)
            for i in range(8):
                n = 16 << i
                dv(lambda e, i=i, n=n: e.tensor_scalar(out=w2[:, :, i], in0=TI[n][:], scalar1=sgn[:, 0:1], scalar2=-1.0,
                                                       op0=ALU.mult, op1=ALU.mult))
            nmt = 0
            for g0 in range(0, G, NBG):
                gs = list(range(g0, g0 + NBG))
                for g in gs:
                    f, gq = g // 8, (g % 8) // 2
                    m = g % 2
                    vb = g % NBG
                    for j in range(L):
                        k.op("pe", lambda e, f=f, gq=gq, m=m, j=j, vb=vb: e.matmul(
                            P.ps[vb][:, 0:NCH], lhsT=Pm[32 * gq:32 * gq + 32, f, j, m, :],
                            rhs=uT[32 * gq:32 * gq + 32, f, :].rearrange("p (c l) -> p l c", l=L)[:, j, :],
                            start=(j == 0), stop=(j == L - 1), tile_position=(32 * gq, 0)),
                            R=[bPm] + buT, W=[P.psb[vb]])
                for g in gs:
                    vb = g % NBG
                    X = Xs[g % NBG]
                    k.op("act", lambda e, X=X, vb=vb: e.copy(out=X[:], in_=P.ps[vb][:, 0:NCH]), R=[P.psb[vb]], W=[bXs[g % NBG]])
                for i in range(8):
                    sh = 1 << i
                    n = 16 << i
                    for g in gs:
                        X = Xs[g % NBG]
                        mt = MT[nmt % NMT]
                        bmt = bMT[nmt % NMT]
                        tm_ = tM[nmt % 4]
                        btm = btM[nmt % 4]
                        nmt += 1
                        k.op("act", lambda e, mt=mt, g=g, n=n: e.activation(out=mt[:], in_=P.ident_f[:], func=AF.Copy, scale=TR[n][:, g:g + 1]),
                             R=[bs, P.b_const], W=[bmt])
                        k.op("dve", lambda e, mt=mt, g=g, i=i: e.scalar_tensor_tensor(out=mt[:], in0=esw[:], scalar=w2[:, g, i:i + 1], in1=mt[:],
                                                                                    op0=ALU.mult, op1=ALU.add), R=[bs, bmt], W=[bmt])
                        pb = 4 + g % NBG
                        k.op("pe", lambda e, mt=mt, X=X, sh=sh, pb=pb: e.matmul(P.ps[pb][:, sh:NCH], lhsT=mt[:], rhs=X[:, 0:NCH - sh],
                                                                                start=True, stop=True), R=[bmt, bXs[g % NBG]], W=[P.psb[pb]])
                        k.op("dve", lambda e, X=X, sh=sh, pb=pb: e.tensor_tensor(out=X[:, sh:NCH], in0=X[:, sh:NCH], in1=P.ps[pb][:, sh:NCH],
                                                                                 op=ALU.add), R=[P.psb[pb], bXs[g % NBG]], W=[bXs[g % NBG]])
                for g in gs:
                    X = Xs[g % NBG]
                    k.op("pool", lambda e, g=g: e.memset(Xb[:, g, 0:1], 0.0), W=[bXb[g]])
                    k.op("act", lambda e, g=g, X=X: e.copy(out=Xb[:, g, 1:NCH], in_=X[:, 0:NCH - 1]), R=[bXs[g % NBG]], W=[bXb[g]])
            k.barrier()
        with ExitStack() as es3:
            Qm = P.sb(es3, "s5_Qm", [128, L, 16, 2, 32], BF16)
            bQm = k.buf("s5_Qm")
            Qt = P.sb(es3, "s5_Qt", [128, 32, 16], F32)
            trs = P.sb(es3, "s5_trs", [128, 32], F32)
            nti = P.sb(es3, "s5_nti", [128, 32], F32)
            mpat = P.sb(es3, "s5_mpat", [128, 2, 2, 16], F32)
            dv(lambda e: e.memset(mpat[:], 0.0))
            dv(lambda e: e.memset(mpat[:, 0, 0, :], 1.0))
            dv(lambda e: e.memset(mpat[:, 1, 1, :], 1.0))
            for tau in range(L):
                n = tau + 1
                dv(lambda e, n=n: e.tensor_scalar(out=trs[:], in0=TR[n][:], scalar1=sgn[:, 0:1], scalar2=-1.0, op0=ALU.mult, op1=ALU.mult))
                dv(lambda e, n=n: e.tensor_scalar(out=nti[:], in0=TI[n][:], scalar1=-1.0, scalar2=None, op0=ALU.mult))
                dv(lambda e: e.tensor_tensor(out=Qt[:], in0=CT1[:], in1=bc16(trs), op=ALU.mult))
                dv(lambda e: e.tensor_tensor(out=tmp3[:], in0=CT2[:], in1=bc16(nti), op=ALU.mult))
                dv(lambda e: e.tensor_tensor(out=Qt[:], in0=Qt[:], in1=tmp3[:], op=ALU.add))
                for mem in range(2):
                    k.op("dve", lambda e, tau=tau, mem=mem: e.tensor_tensor(
                        out=Qm[:, tau, :, mem, :], in0=Qt[:, :, :].rearrange("p (q m) h -> p q (m h)", m=2),
                        in1=mpat[:, mem, :, :].rearrange("p m h -> p (m h)").unsqueeze(1).broadcast_to([128, 16, 32]), op=ALU.mult),
                        R=[bs], W=[bQm])
            zT = P.sb(es3, "s5_zT", [128, 4, S], BF16)
            bz = k.bufs(4, "s5_zT")
            ys = [P.sb(es3, f"s5_ys{i}", [128, NCH], F32) for i in range(2)]
            bys = k.bufs(2, "s5_ys")
            yv = [P.sb(es3, f"s5_yv{i}", [128, NCH], F32) for i in range(2)]
            byv = k.bufs(2, "s5_yv")
            y2 = [P.sb(es3, f"s5_y2{i}", [128, NCH], F32) for i in range(2)]
            by2 = k.bufs(2, "s5_y2")
            cnt = 0
            for f in range(4):
                ul = uT[:, f, :].rearrange("p (c l) -> p l c", l=L)
                for tau in range(L):
                    i2 = cnt % 2
                    pb = cnt % 4
                    cnt += 1
                    for q in range(4):
                        for mem in range(2):
                            g = f * 8 + q * 2 + mem
                            k.op("pe", lambda e, q=q, mem=mem, g=g, tau=tau, pb=pb, f=f: e.matmul(
                                P.ps[pb][32 * q:32 * q + 32, 256:512], lhsT=Qm[:, tau, f * 4 + q, mem, :], rhs=Xb[:, g, :],
                                start=(mem == 0), stop=(mem == 1), tile_position=(0, 32 * q)),
                                R=[bQm, bXb[g]], W=[P.psb[pb]])
                    k.op("act", lambda e, pb=pb, i2=i2: e.copy(out=ys[i2][:], in_=P.ps[pb][:, 256:512]), R=[P.psb[pb]], W=[bys[i2]])
                    for j in range(tau + 1):
                        k.op("pe", lambda e, f=f, tau=tau, j=j, pb=pb, ul=ul: e.matmul(
                            P.ps[pb][:, 0:256], lhsT=Kb[:, f, tau - j, :], rhs=ul[:, j, :], start=(j == 0), stop=(j == tau)),
                            R=[bKb] + buT, W=[P.psb[pb]])
                    Y = yv[i2]
                    k.op("dve", lambda e, pb=pb, i2=i2, Y=Y: e.tensor_tensor(out=Y[:], in0=ys[i2][:], in1=P.ps[pb][:, 0:256], op=ALU.add),
                         R=[bys[i2], P.psb[pb]], W=[byv[i2]])
                    Y2 = y2[i2]
                    k.op("pool", lambda e, Y=Y, Y2=Y2: e.tensor_tensor(out=Y2[:], in0=Y[:], in1=Y[:], op=ALU.mult), R=[byv[i2]], W=[by2[i2]])
                    k.op("pool", lambda e, Y2=Y2: e.tensor_scalar(out=Y2[:], in0=Y2[:], scalar1=0.044715, scalar2=1.0, op0=ALU.mult, op1=ALU.add),
                         R=[by2[i2]], W=[by2[i2]])
                    k.op("dve", lambda e, Y=Y, Y2=Y2: e.tensor_tensor(out=Y2[:], in0=Y2[:], in1=Y[:], op=ALU.mult), R=[by2[i2], byv[i2]], W=[by2[i2]])
                    k.op("act", lambda e, Y2=Y2: e.activation(out=Y2[:], in_=Y2[:], func=AF.Sigmoid, scale=1.5957691216057308),
                         R=[by2[i2]], W=[by2[i2]])
                    k.op("dve", lambda e, Y=Y, Y2=Y2, f=f, tau=tau: e.tensor_tensor(
                        out=zT[:, f, :].rearrange("p (c l) -> p l c", l=L)[:, tau, :], in0=Y[:], in1=Y2[:], op=ALU.mult),
                        R=[by2[i2], byv[i2]], W=[bz[f]])
            wgl = P.sb(es3, "s5_wgl", [128, 4, 512], BF16)
            bwgl = k.buf("s5_wgl")
            k.dma("sp", [(wgl[:], W["s5_w_glu"][0].rearrange("(c p) n -> p c n", p=128))], R=[W["s5_w_glu"][1]], W=[bwgl])
            sg = [P.sb(es3, f"s5_sg{i}", [128, TB], F32) for i in range(2)]
            bsg = k.bufs(2, "s5_sg")
            ybT = [P.sb(es3, f"s5_ybT{i}", [128, 4, TB], BF16) for i in range(2)]
            bybT = k.bufs(2, "s5_ybT")
            cnt = 0
            for blk in range(NB):
                c0 = blk * TB
                for fo in range(4):
                    pb = 4 + cnt % 2
                    i2 = cnt % 2
                    cnt += 1
                    for fi in range(4):
                        k.op("pe", lambda e, fo=fo, fi=fi, pb=pb, c0=c0: e.matmul(
                            P.ps[pb][:], lhsT=wgl[:, fi, fo * 128:(fo + 1) * 128], rhs=zT[:, fi, c0:c0 + TB],
                            start=(fi == 0), stop=(fi == 3)), R=[bwgl] + bz, W=[P.psb[pb]])
                    k.op("act", lambda e, fo=fo, pb=pb, i2=i2: e.activation(out=sg[i2][:], in_=P.ps[pb][:], func=AF.Sigmoid,
                                                                           bias=bgl[:, fo:fo + 1], scale=1.0), R=[P.psb[pb], bs], W=[bsg[i2]])
                    k.op("dve", lambda e, fo=fo, i2=i2, blk=blk, c0=c0: e.tensor_tensor(
                        out=ybT[blk % 2][:, fo, :], in0=sg[i2][:], in1=zT[:, fo, c0:c0 + TB], op=ALU.mult),
                        R=[bsg[i2]] + bz, W=[bybT[blk % 2]])
                k.dma("sp", [(P.ycatT[512:1024, c0:c0 + TB].rearrange("(c p) t -> p c t", p=128), ybT[blk % 2][:])],
                      R=[bybT[blk % 2]], W=[P.b_yb[blk]], key=f"d_ybT{blk % 2}")
            k.barrier()


def build(stages=("all",), debug=()):
    P = Prog(debug)
    k = P.k
    I = {}
    P.inp = I
    I["x"] = P.din("x", [S, D])
    I["positions"] = P.din("positions", [S], I32)
    for nm, shp in [("norm_mix_g", [2, D]), ("norm_ffn_g", [2, D]), ("final_norm_g", [D]),
                    ("even_w_in", [D, 2560]), ("hgrn_lb_logits", [2, 512]), ("hgrn_norm_g", [512]),
                    ("s5_a_re", [32, 64]), ("s5_a_im", [32, 64]), ("s5_log_dt", [32]),
                    ("s5_b_re", [32, 64, 16]), ("s5_b_im", [32, 64, 16]),
                    ("s5_c_re", [32, 16, 64]), ("s5_c_im", [32, 16, 64]),
                    ("s5_d", [512]), ("s5_w_glu", [512, 512]), ("s5_b_glu", [512]),
                    ("even_w_out", [D, D]), ("odd_w_in", [D, 704]), ("mla_q_norm_g", [384]),
                    ("mla_w_uq", [384, 1536]), ("mla_kv_norm_g", [256]), ("mla_w_ukv", [256, 2048]),
                    ("odd_w_out", [D, D]), ("ffn_w_in", [2, D, 2 * DFF]), ("ffn_conv_w", [2, 3, DFF]),
                    ("ffn_conv_b", [2, DFF]), ("ffn_w_out", [2, DFF, D])]:
        I[nm] = P.din(nm, shp)
    I["rope_freq"] = P.din("rope_freq", [32])
    out = P.dout("out", [S, D])
    P.b_out = k.bufs(32, "out")
    P.b_x = k.bufs(32, "x")
    setup_common(P)
    if "ffn1" in stages:
        w_in_bf, b1 = cast_weight(P, "w_ffn_in1", I["ffn_w_in"][1], D, 2 * DFF)
        w_out_bf, b2 = cast_weight(P, "w_ffn_out1", I["ffn_w_out"][1], DFF, D)
        ffn_phase(P, 1, I["x"], P.b_x, out, P.b_out, w_in_bf, b1, w_out_bf, b2, final_norm=I["final_norm_g"])
    W = {}
    if "hgrn" in stages:
        W["even_w_in"] = cast_weight(P, "bf_even_w_in", I["even_w_in"], D, 2560)
        hgrn_phase(P, I["x"], P.b_x, W)
    if "s5" in stages:
        W["s5_w_glu"] = cast_weight(P, "bf_s5_w_glu", I["s5_w_glu"], 512, 512)
        s5_phase(P, W)
    if "mla" in stages:
        for nm, r, c in [("odd_w_in", D, 704), ("mla_w_uq", 384, 1536), ("mla_w_ukv", 256, 2048), ("odd_w_out", D, D)]:
            W[nm] = cast_weight(P, "bf_" + nm, I[nm], r, c)
        mla_proj_phase(P, I["x"], P.b_x, W)
        if "mla_pd_only" not in stages:
            mla_attn_phase(P, I["x"], P.b_x, out, P.b_out, W)
    if "all" in stages:
        W["even_w_in"] = cast_weight(P, "bf_even_w_in", I["even_w_in"], D, 2560)
        W["s5_w_glu"] = cast_weight(P, "bf_s5_w_glu", I["s5_w_glu"], 512, 512)
        W["even_w_out"] = cast_weight(P, "bf_even_w_out", I["even_w_out"], D, D)
        W["ffn_in0"] = cast_weight(P, "bf_ffn_in0", I["ffn_w_in"][0], D, 2 * DFF)
        W["ffn_out0"] = cast_weight(P, "bf_ffn_out0", I["ffn_w_out"][0], DFF, D)
        for nm, r, c in [("odd_w_in", D, 704), ("mla_w_uq", 384, 1536), ("mla_w_ukv", 256, 2048), ("odd_w_out", D, D)]:
            W[nm] = cast_weight(P, "bf_" + nm, I[nm], r, c)
        W["ffn_in1"] = cast_weight(P, "bf_ffn_in1", I["ffn_w_in"][1], D, 2 * DFF)
        W["ffn_out1"] = cast_weight(P, "bf_ffn_out1", I["ffn_w_out"][1], DFF, D)
        h2 = P.dint("h2", [S, D], F32)
        h3 = P.dint("h3", [S, D], F32)
        b_h2 = k.bufs(32, "h2")
        b_h3 = k.bufs(32, "h3")
        hgrn_phase(P, I["x"], P.b_x, W)
        s5_phase(P, W)
        ffn_phase(P, 0, I["x"], P.b_x, h2, b_h2, W["ffn_in0"][0], W["ffn_in0"][1], W["ffn_out0"][0], W["ffn_out0"][1],
                  premix=(P.ycatT, P.b_ya, P.b_yb, W["even_w_out"][0], W["even_w_out"][1]))
        mla_proj_phase(P, h2, b_h2, W)
        mla_attn_phase(P, h2, b_h2, h3, b_h3, W)
        ffn_phase(P, 1, h3, b_h3, out, P.b_out, W["ffn_in1"][0], W["ffn_in1"][1], W["ffn_out1"][0], W["ffn_out1"][1],
                  final_norm=I["final_norm_g"])
    k.barrier()
    import os as _os
    k.do_reorder = _os.environ.get("KREORDER", "1") == "1"
    k.finalize(P.es)
    P.es.close()
    return P


INPUT_ORDER = ["x", "positions", "norm_mix_g", "norm_ffn_g", "final_norm_g", "even_w_in", "hgrn_lb_logits",
               "hgrn_norm_g", "s5_a_re", "s5_a_im", "s5_log_dt", "s5_b_re", "s5_b_im", "s5_c_re", "s5_c_im",
               "s5_d", "s5_w_glu", "s5_b_glu", "even_w_out", "odd_w_in", "mla_q_norm_g", "mla_w_uq",
               "mla_kv_norm_g", "mla_w_ukv", "odd_w_out", "ffn_w_in", "ffn_conv_w", "ffn_conv_b", "ffn_w_out"]


def make_in_maps(inputs):
    shared = {}
    for nm in INPUT_ORDER:
        if nm in ("x", "positions"):
            continue
        a = np.ascontiguousarray(np.asarray(inputs[nm]))
        if nm in ("norm_mix_g", "norm_ffn_g", "hgrn_lb_logits", "ffn_w_in", "ffn_conv_w", "ffn_conv_b",
                  "ffn_w_out", "final_norm_g"):
            shared[nm] = a
        else:
            shared[nm] = np.ascontiguousarray(a[0])
    shared["rope_freq"] = (10000.0 ** (-np.arange(0, 64, 2, dtype=np.float32) / np.float32(64))).astype(np.float32)
    x = np.asarray(inputs["x"])
    pos = np.asarray(inputs["positions"])
    maps = []
    for c in range(8):
        m = dict(shared)
        m["x"] = np.ascontiguousarray(x[c])
        m["positions"] = np.ascontiguousarray(pos[c]).astype(np.int32)
        maps.append(m)
    return maps


def run(inputs, stages=("all",), debug=()):
    P = build(stages, debug)
    maps = make_in_maps(inputs)
    res = run_bass_kernel_spmd(P.nc, maps, core_ids=list(range(8)))
    return res


def kernel(**inputs):
    res = run(inputs)
    out = np.stack([np.asarray(r["out"]) for r in res.results], axis=0)
    return out.astype(np.float32)
```

```python
import math
from contextlib import ExitStack
import numpy as np
import concourse.bass as bass
import concourse.mybir as mybir
from concourse.bass_utils import run_bass_kernel_spmd

F32 = mybir.dt.float32
BF16 = mybir.dt.bfloat16
I32 = mybir.dt.int32
AF = mybir.ActivationFunctionType
ALU = mybir.AluOpType
AX = mybir.AxisListType

S = 4096
D = 1024
NB = 8
TB = 512
DFF = 2816
NFC = DFF // 128
EPS = 1e-6

ENGS = ("pe", "act", "dve", "pool", "sp")


class Buf:
    __slots__ = ("name", "w", "r")

    def __init__(self, name):
        self.name = name
        self.w = None
        self.r = []


class Op:
    __slots__ = ("eng", "fn", "deps", "key", "pos", "ndma", "waits", "flag")

    def __init__(self, eng, fn, deps, key, ndma):
        self.eng = eng
        self.fn = fn
        self.deps = deps
        self.key = key
        self.pos = -1
        self.ndma = ndma
        self.waits = []
        self.flag = False


class KB:
    def __init__(self, nc):
        self.nc = nc
        self.ops = []
        self.eng_obj = {"pe": nc.tensor, "act": nc.scalar, "dve": nc.vector,
                        "pool": nc.gpsimd, "sp": nc.sync}
        self.last = {}
        self.dma_out = []
        self.nbuf = 0

    def buf(self, name=None):
        self.nbuf += 1
        return Buf(name or f"b{self.nbuf}")

    def bufs(self, n, name="b"):
        return [self.buf(f"{name}{i}") for i in range(n)]

    def _deps(self, idx, R, W):
        deps = set()
        for b in R:
            if b.w is not None:
                deps.add(b.w)
        for b in W:
            if b.w is not None:
                deps.add(b.w)
            deps.update(b.r)
        for b in W:
            b.w = idx
            b.r = []
        for b in R:
            b.r.append(idx)
        deps.discard(idx)
        return deps

    def op(self, eng, fn, R=(), W=()):
        idx = len(self.ops)
        deps = self._deps(idx, R, W)
        self.ops.append(Op(eng, fn, deps, eng, 0))
        self.last[eng] = idx
        return idx

    def dma(self, eng, pairs, R=(), W=(), key=None, **kw):
        idx = len(self.ops)
        deps = self._deps(idx, R, W)
        if key is None:
            key = "d_" + (W[0].name if W else R[0].name)

        def fn(e, pairs=pairs, kw=kw):
            return [e.dma_start(out=o, in_=i, **kw) for (o, i) in pairs]
        self.ops.append(Op(eng, fn, deps, key, len(pairs)))
        self.dma_out.append(idx)
        return idx

    def barrier(self):
        targets = set(self.last.values()) | set(self.dma_out)
        for e in ENGS:
            idx = len(self.ops)
            self.ops.append(Op(e, None, set(targets), e, 0))
        self.dma_out = []

    def reorder(self):
        import heapq
        ops = self.ops
        n = len(ops)
        DUR = {"pe": 0.22, "act": 0.45, "dve": 0.40, "pool": 0.9, "sp": 0.1}
        LAT = 0.8
        new_order = []
        seg_start = 0
        i = 0
        bounds = []
        while i < n:
            if ops[i].fn is None:
                j = i
                while j < n and ops[j].fn is None:
                    j += 1
                bounds.append((seg_start, i, j))
                seg_start = j
                i = j
            else:
                i += 1
        if seg_start < n:
            bounds.append((seg_start, n, n))
        for (lo, hi, nxt) in bounds:
            idxs = range(lo, hi)
            users = {}
            ndeps = {}
            est = {}
            for x in idxs:
                d_in = [d for d in ops[x].deps if lo <= d < hi]
                ndeps[x] = len(d_in)
                est[x] = 0.0
                for d in d_in:
                    users.setdefault(d, []).append(x)
            heaps = {e: [] for e in ENGS}
            for x in idxs:
                if ndeps[x] == 0:
                    heapq.heappush(heaps[ops[x].eng], (0.0, x))
            free = {e: 0.0 for e in ENGS}
            fin = {}
            left = hi - lo
            while left:
                best = None
                for e in ENGS:
                    if heaps[e]:
                        t_, x = heaps[e][0]
                        st_ = max(t_, free[e])
                        if best is None or (st_, x) < (best[0], best[2]):
                            best = (st_, e, x)
                st_, e, x = best
                heapq.heappop(heaps[e])
                o = ops[x]
                if o.ndma:
                    free[e] = st_ + (2.0 if e == "pool" else 0.15)
                    fin[x] = st_ + 4.0
                else:
                    free[e] = st_ + DUR[e]
                    fin[x] = free[e]
                new_order.append(x)
                left -= 1
                for u in users.get(x, ()):
                    lat = 0.05 if (ops[u].eng == e and not o.ndma) else LAT
                    est[u] = max(est[u], fin[x] + lat)
                    ndeps[u] -= 1
                    if ndeps[u] == 0:
                        heapq.heappush(heaps[ops[u].eng], (est[u], u))
            seg_new = new_order[len(new_order) - (hi - lo):]
            last_eng = {}
            for x in seg_new:
                if not ops[x].ndma:
                    last_eng[ops[x].eng] = x
            for bidx in range(hi, nxt):
                ops[bidx].deps = set(ops[bidx].deps) | set(last_eng.values())
            new_order.extend(range(hi, nxt))
        assert len(new_order) == n and len(set(new_order)) == n
        remap = {old: new for new, old in enumerate(new_order)}
        newops = [ops[i_] for i_ in new_order]
        for o in newops:
            o.deps = {remap[d] for d in o.deps}
        self.ops = newops

    def finalize(self, es):
        nc = self.nc
        if getattr(self, "do_reorder", False):
            self.reorder()
        ops = self.ops
        cnt = {}
        for o in ops:
            if o.ndma:
                cnt[o.key] = cnt.get(o.key, 0) + o.ndma
                o.pos = cnt[o.key]
            elif o.fn is not None:
                cnt[o.key] = cnt.get(o.key, 0) + 1
                o.pos = cnt[o.key]
            else:
                o.pos = cnt.get(o.key, 0)
        seen = {e: {} for e in ENGS}
        flagged = {}
        for o in ops:
            need = {}
            for d in o.deps:
                dop = ops[d]
                if dop.fn is None:
                    continue
                k = dop.key
                if (not dop.ndma) and k == o.eng and k in ("pe", "sp"):
                    continue
                if dop.pos > seen[o.eng].get(k, 0):
                    if dop.pos > need.get(k, (0, None))[0]:
                        need[k] = (dop.pos, dop)
            for k, (p, dop) in need.items():
                seen[o.eng][k] = p
                o.waits.append((k, dop))
                dop.flag = True
        val = {}
        rank = {}
        for i, o in enumerate(ops):
            if o.ndma:
                val[i] = 16 * o.pos
            elif o.fn is not None and o.flag:
                rank[o.key] = rank.get(o.key, 0) + 1
                val[i] = rank[o.key]
        opidx = {id(o): i for i, o in enumerate(ops)}
        keys = set()
        for o in ops:
            if o.ndma or o.flag:
                keys.add(o.key)
        sems = {}
        for kname in sorted(keys):
            sems[kname] = es.enter_context(nc.semaphore("s_" + kname))
        self.nsem = len(sems)
        for o in ops:
            e = self.eng_obj[o.eng]
            for (k, dop) in o.waits:
                e.wait_ge(sems[k], val[opidx[id(dop)]])
            if o.fn is None:
                continue
            ins = o.fn(e)
            if o.ndma:
                for i_ in ins:
                    i_.then_inc(sems[o.key], 16)
            elif o.flag:
                ins.then_inc(sems[o.key], 1)


class Prog:
    def __init__(self, debug=()):
        self.debug = set(debug)
        self.nc = bass.Bass("TRN2", target_bir_lowering=False)
        self.k = KB(self.nc)
        self.es = ExitStack()
        self.dram = {}

    def din(self, name, shape, dtype=F32):
        t = self.nc.dram_tensor(name, list(shape), dtype, kind="ExternalInput").ap()
        self.dram[name] = t
        return t

    def dout(self, name, shape, dtype=F32):
        t = self.nc.dram_tensor(name, list(shape), dtype, kind="ExternalOutput").ap()
        self.dram[name] = t
        return t

    def dint(self, name, shape, dtype=F32):
        kind = "ExternalOutput" if name in self.debug else "Internal"
        t = self.nc.dram_tensor(name, list(shape), dtype, kind=kind).ap()
        self.dram[name] = t
        return t

    def sb(self, es, name, shape, dtype):
        self.nsb = getattr(self, "nsb", 0) + 1
        return es.enter_context(self.nc.sbuf_tensor(f"{name}_{self.nsb}", list(shape), dtype))


def setup_common(P):
    nc, k = P.nc, P.k
    P.ps = [P.es.enter_context(nc.psum_tensor(f"ps{i}", [128, 512], F32)) for i in range(8)]
    P.psb = k.bufs(8, "psb")
    P.ident_f = P.sb(P.es, "ident_f", [128, 128], F32)
    P.ident = P.sb(P.es, "ident", [128, 128], BF16)
    P.epsb = P.sb(P.es, "epsb", [128, 1], F32)
    P.b_const = k.buf("const")
    ones = P.sb(P.es, "ones_f", [128, 128], F32)
    P.ones_f = ones
    k.op("pool", lambda e: e.memset(ones[:], 1.0), W=[P.b_const])
    k.op("pool", lambda e: e.affine_select(out=P.ident_f[:], in_=ones[:], pattern=[[-1, 128]],
                                           compare_op=ALU.is_equal, fill=0.0, base=0,
                                           channel_multiplier=1), R=[P.b_const], W=[P.b_const])
    k.op("pool", lambda e: e.tensor_copy(out=P.ident[:], in_=P.ident_f[:]), R=[P.b_const], W=[P.b_const])
    k.op("pool", lambda e: e.memset(P.epsb[:], EPS), W=[P.b_const])


def cast_weight(P, name, src2d, rows, cols):
    k = P.k
    dst = P.dint(name, [rows, cols], BF16)
    b = k.buf(name)
    bb = cols if cols <= 2048 else 512
    per_row = cols // bb
    rstep = max(1, 4096 // per_row)
    pairs = []
    for r0 in range(0, rows, rstep):
        r1 = min(rows, r0 + rstep)
        pairs.append((dst[r0:r1, :].rearrange("r (a b) -> r a b", b=bb),
                      src2d[r0:r1, :].rearrange("r (a b) -> r a b", b=bb)))
    k.dma("pool", pairs, W=[b], key="wc_" + name)
    return dst, b


def load_vec_fm(P, es, name, src1d, nchunk, b, eng="sp"):
    t = P.sb(es, name, [128, nchunk], F32)
    P.k.dma(eng, [(t[:], src1d.rearrange("(c p) -> p c", p=128))], W=[b],
            allow_slow_non_contiguous=True)
    return t


def norm_block(P, T, blk, src, b_src, gT):
    k, nc = P.k, P.nc
    xres, bx = T["xres"], T["b_xres"]
    for j in range(4):
        r0 = blk * TB + j * 128
        k.dma("sp", [(xres[:, j, :], src[r0:r0 + 128, :])], R=[b_src[blk * 4 + j]], W=[bx[j]], key=f"d_xres{j}")
    rms_transpose(P, T, gT)


def rms_stats(P, T):
    k = P.k
    xres, bx = T["xres"], T["b_xres"]
    junk, bj = T["junk"], T["b_junk"]
    ss, bss = T["ss"], T["b_ss"]
    xn, bxn = T["xn"], T["b_xn"]
    for j in range(4):
        k.op("act", lambda e, j=j: e.activation(out=junk[:], in_=xres[:, j, :], func=AF.Square,
                                                accum_out=ss[:, j:j + 1]),
             R=[bx[j]], W=[bj, bss[j]])
        k.op("act", lambda e, j=j: e.activation(out=ss[:, 4 + j:5 + j], in_=ss[:, j:j + 1], func=AF.Sqrt,
                                                bias=P.epsb[:], scale=1.0 / D),
             R=[bss[j], P.b_const], W=[bss[j]])
        k.op("dve", lambda e, j=j: e.reciprocal(out=ss[:, 8 + j:9 + j], in_=ss[:, 4 + j:5 + j]),
             R=[bss[j]], W=[bss[j]])
        k.op("dve", lambda e, j=j: e.tensor_scalar(out=xn[:, j, :], in0=xres[:, j, :], scalar1=ss[:, 8 + j:9 + j],
                                                   scalar2=None, op0=ALU.mult),
             R=[bx[j], bss[j]], W=[bxn[j]])


def rms_tr(P, T, gT):
    k = P.k
    hnT, bh = T["hnT"], T["b_hnT"]
    xn, bxn = T["xn"], T["b_xn"]
    for j in range(4):
        pb = T["ps_tr"][j % len(T["ps_tr"])]
        psT = P.ps[pb][:].bitcast(BF16)
        for c in range(8):
            k.op("pe", lambda e, j=j, c=c, psT=psT: e.transpose(out=psT[:, c * 128:(c + 1) * 128],
                                                               in_=xn[:, j, c * 128:(c + 1) * 128],
                                                               identity=P.ident[:]),
                 R=[bxn[j], P.b_const], W=[P.psb[pb]])
        k.op("dve", lambda e, j=j, psT=psT: e.tensor_tensor(
            out=hnT[:, :, j * 128:(j + 1) * 128],
            in0=psT.rearrange("p (c t) -> p c t", c=8),
            in1=gT[:, :].unsqueeze(2).broadcast_to([128, 8, 128]), op=ALU.mult),
            R=[P.psb[pb], T["b_g"]], W=[bh])


def rms_transpose(P, T, gT):
    rms_stats(P, T)
    rms_tr(P, T, gT)


def alloc_blockbufs(P, es, pfx, ps_tr):
    k = P.k
    T = {}
    T["xres"] = P.sb(es, pfx + "xres", [128, 4, 1024], F32)
    T["b_xres"] = k.bufs(4, pfx + "xres")
    T["hnT"] = P.sb(es, pfx + "hnT", [128, 8, 512], BF16)
    T["b_hnT"] = k.buf(pfx + "hnT")
    T["junk"] = P.sb(es, pfx + "junk", [128, 1024], BF16)
    T["b_junk"] = k.buf(pfx + "junk")
    T["ss"] = P.sb(es, pfx + "ss", [128, 12], F32)
    T["b_ss"] = k.bufs(4, pfx + "ss")
    T["xn"] = P.sb(es, pfx + "xn", [128, 4, 1024], BF16)
    T["b_xn"] = k.bufs(4, pfx + "xn")
    T["ps_tr"] = ps_tr
    return T


def ffn_phase(P, layer, src, b_src, dst, b_dst, w_in_bf, b_win, w_out_bf, b_wout, final_norm=None, premix=None):
    k, nc = P.k, P.nc
    I = P.inp
    with ExitStack() as es:
        bc = k.buf("ffn_consts")
        gT = load_vec_fm(P, es, "ffn_g", I["norm_ffn_g"][layer], 8, bc)
        cb = load_vec_fm(P, es, "ffn_cb", I["ffn_conv_b"][layer], NFC, bc)
        cw = P.sb(es, "ffn_cw", [128, 3, NFC], F32)
        for t in range(3):
            k.dma("sp", [(cw[:, t, :], I["ffn_conv_w"][layer, t].rearrange("(c p) -> p c", p=128))], W=[bc],
                  allow_slow_non_contiguous=True)
        TT = [alloc_blockbufs(P, es, f"f{i}_", ps_tr=[6, 7]) for i in range(2)]
        for T_ in TT:
            T_["b_g"] = bc
        wout = P.sb(es, "ffn_wout", [128, NFC, 1024], BF16)
        bwo = k.buf("ffn_wout")
        k.dma("sp", [(wout[:, c0:c0 + 11, :],
                      w_out_bf[c0 * 128:(c0 + 11) * 128, :].rearrange("(c p) n -> p c n", p=128))
                     for c0 in (0, 11)], R=[b_wout], W=[bwo])
        NSL = 2
        wsl = [P.sb(es, f"ffn_wsl{i}", [128, 8, 2, 256], BF16) for i in range(NSL)]
        bws = k.bufs(NSL, "ffn_wsl")
        a_sb = [P.sb(es, f"ffn_a{i}", [128, 516], F32) for i in range(2)]
        ba = k.bufs(2, "ffn_a")
        c_sb = [P.sb(es, f"ffn_c{i}", [128, 512], F32) for i in range(2)]
        bcs = k.bufs(2, "ffn_c")
        s_sb = [P.sb(es, f"ffn_s{i}", [128, 512], F32) for i in range(2)]
        bss = k.bufs(2, "ffn_s")
        halo = P.sb(es, "ffn_halo", [128, NFC, 2], F32)
        bhalo = k.bufs(NFC, "ffn_halo")
        gTt = P.sb(es, "ffn_gT", [128, NFC, 512], BF16)
        bg = k.bufs(NFC, "ffn_gT")
        k.op("pool", lambda e: e.memset(halo[:], 0.0), W=bhalo)
        if final_norm is not None:
            fg = P.sb(es, "fin_g", [128, 1024], F32)
            bfg = k.buf("fin_g")
            k.dma("sp", [(fg[:], final_norm.partition_broadcast(128))], W=[bfg])
            ostage = [P.sb(es, f"fin_o{i}", [128, 1024], F32) for i in range(2)]
            bos = k.bufs(2, "fin_o")
        if premix is not None:
            ycatT, b_ya, b_yb, wo_bf, b_wo = premix
            wo = P.sb(es, "pm_wo", [128, 8, 1024], BF16)
            bwo2 = k.buf("pm_wo")
            k.dma("sp", [(wo[:], wo_bf.rearrange("(c p) n -> p c n", p=128))], R=[b_wo], W=[bwo2])
            yTs = [P.sb(es, f"pm_yT{i}", [128, 8, 512], BF16) for i in range(2)]
            byTs = k.bufs(2, "pm_yT")
        nsl = 0
        cnt = 0
        def stage_a(blk):
            T = TT[blk % 2]
            xres, bx = T["xres"], T["b_xres"]
            for j in range(4):
                r0 = blk * TB + j * 128
                k.dma("sp", [(xres[:, j, :], src[r0:r0 + 128, :])], R=[b_src[blk * 4 + j]], W=[bx[j]], key=f"d_xres{blk % 2}_{j}")
            if premix is not None:
                yT, byT = yTs[blk % 2], byTs[blk % 2]
                c0 = blk * TB
                k.dma("sp", [(yT[:], ycatT[:, c0:c0 + TB].rearrange("(c p) t -> p c t", p=128))], R=[b_ya[blk], b_yb[blk]], W=[byT])
                for j in range(4):
                    for n in range(2):
                        pb = 4 + (j * 2 + n) % 2
                        for kc in range(8):
                            k.op("pe", lambda e, kc=kc, j=j, n=n, pb=pb, yT=yT: e.matmul(
                                P.ps[pb][:], lhsT=yT[:, kc, j * 128:(j + 1) * 128], rhs=wo[:, kc, n * 512:(n + 1) * 512],
                                start=(kc == 0), stop=(kc == 7)), R=[byT, bwo2], W=[P.psb[pb]])
                        k.op("dve", lambda e, j=j, n=n, pb=pb, xres=xres: e.tensor_tensor(
                            out=xres[:, j, n * 512:(n + 1) * 512], in0=xres[:, j, n * 512:(n + 1) * 512],
                            in1=P.ps[pb][:], op=ALU.add), R=[bx[j], P.psb[pb]], W=[bx[j]])
            rms_stats(P, T)

        stage_a(0)
        rms_tr(P, TT[0], gT)
        def do_block(blk, T):
            nonlocal nsl, cnt
            xres, bx = T["xres"], T["b_xres"]
            hnT, bh = T["hnT"], T["b_hnT"]
            for pr in range(NFC // 2):
                if pr == 4 and blk + 1 < NB:
                    stage_a(blk + 1)
                sl = nsl % NSL
                nsl += 1
                pairs = []
                for h_ in range(2):
                    c0 = h_ * DFF + pr * 256
                    pairs.append((wsl[sl][:, :, h_, :],
                                  w_in_bf[:, c0:c0 + 256].rearrange("(kc p) n -> p kc n", p=128)))
                k.dma("sp", pairs, R=[b_win], W=[bws[sl]])
                for q in range(2):
                    oc = pr * 2 + q
                    pa, pu = (0, 1) if (cnt % 2 == 0) else (2, 3)
                    i2 = cnt % 2
                    cnt += 1
                    for kc in range(8):
                        k.op("pe", lambda e, kc=kc, sl=sl, q=q, pa=pa: e.matmul(
                            P.ps[pa][:], lhsT=wsl[sl][:, kc, 0, q * 128:(q + 1) * 128], rhs=hnT[:, kc, :],
                            start=(kc == 0), stop=(kc == 7)), R=[bws[sl], bh], W=[P.psb[pa]])
                    for kc in range(8):
                        k.op("pe", lambda e, kc=kc, sl=sl, q=q, pu=pu: e.matmul(
                            P.ps[pu][:], lhsT=wsl[sl][:, kc, 1, q * 128:(q + 1) * 128], rhs=hnT[:, kc, :],
                            start=(kc == 0), stop=(kc == 7)), R=[bws[sl], bh], W=[P.psb[pu]])
                    A = a_sb[i2]
                    k.op("pool", lambda e, A=A, oc=oc: e.tensor_copy(out=A[:, 2:4], in_=halo[:, oc, :]),
                         R=[bhalo[oc]], W=[ba[i2]])
                    k.op("act", lambda e, A=A, pa=pa: e.copy(out=A[:, 4:516], in_=P.ps[pa][:]),
                         R=[P.psb[pa]], W=[ba[i2]])
                    k.op("pool", lambda e, A=A, oc=oc: e.tensor_copy(out=halo[:, oc, :], in_=A[:, 514:516]),
                         R=[ba[i2]], W=[bhalo[oc]])
                    C = c_sb[i2]
                    k.op("act", lambda e, C=C, pa=pa, oc=oc: e.activation(
                        out=C[:], in_=P.ps[pa][:], func=AF.Identity, bias=cb[:, oc:oc + 1], scale=cw[:, 2, oc:oc + 1]),
                        R=[P.psb[pa], bc], W=[bcs[i2]])
                    k.op("dve", lambda e, C=C, A=A, oc=oc: e.scalar_tensor_tensor(
                        out=C[:], in0=A[:, 3:515], scalar=cw[:, 1, oc:oc + 1], in1=C[:], op0=ALU.mult, op1=ALU.add),
                        R=[ba[i2], bc, bcs[i2]], W=[bcs[i2]])
                    k.op("dve", lambda e, C=C, A=A, oc=oc: e.scalar_tensor_tensor(
                        out=C[:], in0=A[:, 2:514], scalar=cw[:, 0, oc:oc + 1], in1=C[:], op0=ALU.mult, op1=ALU.add),
                        R=[ba[i2], bc, bcs[i2]], W=[bcs[i2]])
                    Ssb = s_sb[i2]
                    k.op("act", lambda e, C=C, Ssb=Ssb: e.activation(out=Ssb[:], in_=C[:], func=AF.Silu),
                         R=[bcs[i2]], W=[bss[i2]])
                    k.op("dve", lambda e, Ssb=Ssb, pu=pu, oc=oc: e.tensor_tensor(
                        out=gTt[:, oc, :], in0=Ssb[:], in1=P.ps[pu][:], op=ALU.mult),
                        R=[bss[i2], P.psb[pu]], W=[bg[oc]])
            if blk + 1 < NB:
                rms_tr(P, TT[(blk + 1) % 2], gT)
            for j in range(4):
                for n in range(2):
                    pb = 4 + (j * 2 + n) % 2
                    for kc in range(NFC):
                        k.op("pe", lambda e, kc=kc, j=j, n=n, pb=pb: e.matmul(
                            P.ps[pb][:], lhsT=gTt[:, kc, j * 128:(j + 1) * 128],
                            rhs=wout[:, kc, n * 512:(n + 1) * 512], start=(kc == 0), stop=(kc == NFC - 1)),
                            R=[bg[kc], bwo], W=[P.psb[pb]])
                    k.op("dve", lambda e, j=j, n=n, pb=pb: e.tensor_tensor(
                        out=xres[:, j, n * 512:(n + 1) * 512], in0=xres[:, j, n * 512:(n + 1) * 512],
                        in1=P.ps[pb][:], op=ALU.add), R=[bx[j], P.psb[pb]], W=[bx[j]])
                r0 = blk * TB + j * 128
                if final_norm is None:
                    k.dma("sp", [(dst[r0:r0 + 128, :], xres[:, j, :])], R=[bx[j]], W=[b_dst[blk * 4 + j]],
                          key=f"d_xst{blk % 2}_{j}")
                else:
                    ss, bs4 = T["ss"], T["b_ss"]
                    junk, bj = T["junk"], T["b_junk"]
                    o = ostage[j % 2]
                    k.op("act", lambda e, j=j: e.activation(out=junk[:], in_=xres[:, j, :], func=AF.Square,
                                                            accum_out=ss[:, j:j + 1]),
                         R=[bx[j]], W=[bj, bs4[j]])
                    k.op("act", lambda e, j=j: e.activation(out=ss[:, 4 + j:5 + j], in_=ss[:, j:j + 1], func=AF.Sqrt,
                                                            bias=P.epsb[:], scale=1.0 / D),
                         R=[bs4[j], P.b_const], W=[bs4[j]])
                    k.op("dve", lambda e, j=j: e.reciprocal(out=ss[:, 8 + j:9 + j], in_=ss[:, 4 + j:5 + j]),
                         R=[bs4[j]], W=[bs4[j]])
                    k.op("dve", lambda e, j=j, o=o: e.scalar_tensor_tensor(
                        out=o[:], in0=xres[:, j, :], scalar=ss[:, 8 + j:9 + j], in1=fg[:], op0=ALU.mult, op1=ALU.mult),
                        R=[bx[j], bs4[j], bfg], W=[bos[j % 2]])
                    k.dma("sp", [(dst[r0:r0 + 128, :], o[:])], R=[bos[j % 2]], W=[b_dst[blk * 4 + j]], key=f"d_ost{j % 2}")
        for blk in range(NB):
            do_block(blk, TT[blk % 2])
        k.barrier()


HQ = 8
VP = 130
SCALE = 192 ** -0.5
TWO_PI = 2.0 * math.pi
CW1 = 6.28125
CW2 = TWO_PI - 6.28125


def mla_proj_phase(P, src, b_src, W):
    k, nc = P.k, P.nc
    I = P.inp
    QTn = P.dint("QTn", [HQ, 128, S], BF16)
    QTr = P.dint("QTr", [HQ, 64, S], BF16)
    KTn = P.dint("KTn", [HQ, 128, S], BF16)
    KTr = P.dint("KTr", [64, S], BF16)
    Va = P.dint("Va", [S, HQ * VP], BF16)
    P.mla_dram = dict(QTn=QTn, QTr=QTr, KTn=KTn, KTr=KTr, Va=Va)
    P.b_qkv = k.bufs(NB, "qkv")
    stats = P.sb(P.es, "mla_stats", [128, 16], F32)
    P.mla_stats = stats
    P.b_stats = k.buf("mla_stats")
    with ExitStack() as es:
        bc = k.buf("pd_consts")
        gT = load_vec_fm(P, es, "pd_g", I["norm_mix_g"][1], 8, bc)
        qg = load_vec_fm(P, es, "pd_qg", I["mla_q_norm_g"], 3, bc)
        kvg = load_vec_fm(P, es, "pd_kvg", I["mla_kv_norm_g"], 2, bc)
        g5 = P.sb(es, "pd_g5", [128, 5], F32)
        k.op("pool", lambda e: e.tensor_copy(out=g5[:, 0:3], in_=qg[:]), R=[bc], W=[bc])
        k.op("pool", lambda e: e.tensor_copy(out=g5[:, 3:5], in_=kvg[:]), R=[bc], W=[bc])
        win = P.sb(es, "pd_win", [128, 8, 704], BF16)
        k.dma("sp", [(win[:], W["odd_w_in"][0].rearrange("(c p) n -> p c n", p=128))], R=[W["odd_w_in"][1]], W=[bc])
        wuq = P.sb(es, "pd_wuq", [128, 3, 1536], BF16)
        k.dma("sp", [(wuq[:], W["mla_w_uq"][0].rearrange("(c p) n -> p c n", p=128))], R=[W["mla_w_uq"][1]], W=[bc])
        wukv = P.sb(es, "pd_wukv", [128, 2, 2048], BF16)
        k.dma("sp", [(wukv[:], W["mla_w_ukv"][0].rearrange("(c p) n -> p c n", p=128))], R=[W["mla_w_ukv"][1]], W=[bc])
        freq = P.sb(es, "pd_freq", [128, 32], F32)
        k.dma("sp", [(freq[:], I["rope_freq"].partition_broadcast(128))], W=[bc])
        posi = P.sb(es, "pd_posi", [128, 32], I32)
        k.dma("sp", [(posi[:, 8 * i:8 * i + 8], I["positions"][1024 * i:1024 * (i + 1)].rearrange("(j p) -> p j", p=128))
                     for i in range(4)], W=[bc], allow_slow_non_contiguous=True)
        posf = P.sb(es, "pd_posf", [128, 32], F32)
        ang = P.sb(es, "pd_ang", [128, 32, 32], F32)
        tmpf = P.sb(es, "pd_tmpf", [128, 32, 32], F32)
        tmpi = P.sb(es, "pd_tmpi", [128, 32, 32], I32)
        sinT = P.sb(es, "pd_sin", [128, 32, 32], F32)
        cosT = P.sb(es, "pd_cos", [128, 32, 32], F32)
        br = k.buf("pd_rope")
        k.op("dve", lambda e: e.tensor_copy(out=posf[:], in_=posi[:]), R=[bc], W=[br])
        k.op("dve", lambda e: e.tensor_tensor(out=ang[:], in0=posf[:, :].unsqueeze(2).broadcast_to([128, 32, 32]),
                                              in1=freq[:, :].unsqueeze(1).broadcast_to([128, 32, 32]), op=ALU.mult),
             R=[br, bc], W=[br])
        k.op("dve", lambda e: e.tensor_scalar(out=tmpf[:], in0=ang[:], scalar1=1.0 / TWO_PI, scalar2=None, op0=ALU.mult),
             R=[br], W=[br])
        k.op("dve", lambda e: e.tensor_copy(out=tmpi[:], in_=tmpf[:]), R=[br], W=[br])
        k.op("dve", lambda e: e.tensor_copy(out=tmpf[:], in_=tmpi[:]), R=[br], W=[br])
        k.op("dve", lambda e: e.scalar_tensor_tensor(out=ang[:], in0=tmpf[:], scalar=-CW1, in1=ang[:],
                                                     op0=ALU.mult, op1=ALU.add), R=[br], W=[br])
        k.op("dve", lambda e: e.scalar_tensor_tensor(out=ang[:], in0=tmpf[:], scalar=-CW2, in1=ang[:],
                                                     op0=ALU.mult, op1=ALU.add), R=[br], W=[br])
        k.op("dve", lambda e: e.tensor_scalar(out=ang[:], in0=ang[:], scalar1=-math.pi, scalar2=math.pi,
                                              op0=ALU.max, op1=ALU.min), R=[br], W=[br])
        k.op("act", lambda e: e.activation(out=sinT[:], in_=ang[:], func=AF.Sin), R=[br], W=[br])
        k.op("dve", lambda e: e.tensor_scalar(out=tmpf[:], in0=ang[:], scalar1=math.pi / 2, scalar2=math.pi,
                                              op0=ALU.add, op1=ALU.is_gt), R=[br], W=[br])
        k.op("dve", lambda e: e.scalar_tensor_tensor(out=tmpf[:], in0=tmpf[:], scalar=-TWO_PI, in1=ang[:],
                                                     op0=ALU.mult, op1=ALU.add), R=[br], W=[br])
        k.op("dve", lambda e: e.tensor_scalar(out=tmpf[:], in0=tmpf[:], scalar1=math.pi / 2, scalar2=math.pi,
                                              op0=ALU.add, op1=ALU.min), R=[br], W=[br])
        k.op("act", lambda e: e.activation(out=cosT[:], in_=tmpf[:], func=AF.Sin), R=[br], W=[br])
        k.op("pool", lambda e: e.memset(stats[:], 0.0), W=[P.b_stats])

        T = alloc_blockbufs(P, es, "d_", ps_tr=[7])
        T["b_g"] = bc
        SETS = []
        for si in range(2):
            Bd = {}
            def mk(name, shape, dt, Bd=Bd, si=si):
                Bd[name] = P.sb(es, f"pd_{name}{si}", shape, dt)
                Bd["b_" + name] = k.buf(f"pd_{name}{si}")
            mk("junk2", [128, 1536], F32)
            mk("st", [128, 16], F32)
            mk("cn", [128, 640], BF16)
            mk("cT", [128, 5, 128], BF16)
            mk("kr", [128, 64], F32)
            mk("krb", [128, 64], BF16)
            mk("qsb", [128, 1536], F32)
            mk("qbf", [128, 8, 192], BF16)
            mk("kbf", [128, 8, 128], BF16)
            Bd["rt"] = [P.sb(es, f"pd_rt{si}_{i}", [128, 8, 32], F32) for i in range(4)]
            Bd["b_rt"] = k.buf(f"pd_rt{si}")
            SETS.append(Bd)
        vst = [P.sb(es, f"pd_vst{i}", [128, 8, VP], BF16) for i in range(2)]
        bvst = k.bufs(2, "pd_vst")
        for i in range(2):
            k.op("pool", lambda e, i=i: e.memset(vst[i][:], 1.0), W=[bvst[i]])
        qTn_st = P.sb(es, "pd_qTn", [128, 8, 512], BF16)
        qTr_st = P.sb(es, "pd_qTr", [64, 8, 512], BF16)
        kTn_st = P.sb(es, "pd_kTn", [128, 8, 512], BF16)
        kTr_st = P.sb(es, "pd_kTr", [64, 512], BF16)
        bqT = k.buf("pd_qT")
        bkT = k.buf("pd_kT")
        nv = 0
        import os
        LIM = float(os.environ.get("PD_LIMIT", "99"))
        for blk in range(NB):
            if LIM < 1 or (LIM < 90 and blk > 0):
                break
            norm_block(P, T, blk, src, b_src, gT)
            hnT, bh = T["hnT"], T["b_hnT"]
            def do_sub(blk, j, Bd):
                junk2, bj2, st, bst, cn, bcn, cT, bcT = Bd["junk2"], Bd["b_junk2"], Bd["st"], Bd["b_st"], Bd["cn"], Bd["b_cn"], Bd["cT"], Bd["b_cT"]
                kr, bkr, krb, bkrb, qsb, bqsb, qbf, bqbf = Bd["kr"], Bd["b_kr"], Bd["krb"], Bd["b_krb"], Bd["qsb"], Bd["b_qsb"], Bd["qbf"], Bd["b_qbf"]
                kbf, bkbf, rt, brt = Bd["kbf"], Bd["b_kbf"], Bd["rt"], Bd["b_rt"]
                nonlocal nv
                sub = blk * 4 + j
                tok = slice(j * 128, (j + 1) * 128)
                for kc in range(8):
                    k.op("pe", lambda e, kc=kc, tok=tok: e.matmul(P.ps[0][:, 0:384], lhsT=hnT[:, kc, tok], rhs=win[:, kc, 0:384],
                                                                  start=(kc == 0), stop=(kc == 7)), R=[bh, bc], W=[P.psb[0]])
                for kc in range(8):
                    k.op("pe", lambda e, kc=kc, tok=tok: e.matmul(P.ps[1][:, 0:320], lhsT=hnT[:, kc, tok], rhs=win[:, kc, 384:704],
                                                                  start=(kc == 0), stop=(kc == 7)), R=[bh, bc], W=[P.psb[1]])
                k.op("act", lambda e: e.activation(out=junk2[:, 0:384], in_=P.ps[0][:, 0:384], func=AF.Square,
                                                   accum_out=st[:, 0:1]), R=[P.psb[0]], W=[bj2, bst])
                k.op("act", lambda e: e.activation(out=junk2[:, 0:256], in_=P.ps[1][:, 0:256], func=AF.Square,
                                                   accum_out=st[:, 1:2]), R=[P.psb[1]], W=[bj2, bst])
                k.op("act", lambda e: e.activation(out=junk2[:, 0:64], in_=P.ps[1][:, 256:320], func=AF.Square,
                                                   accum_out=st[:, 2:3]), R=[P.psb[1]], W=[bj2, bst])
                k.op("act", lambda e: e.activation(out=st[:, 3:4], in_=st[:, 0:1], func=AF.Sqrt, bias=P.epsb[:],
                                                   scale=1.0 / 384), R=[bst, P.b_const], W=[bst])
                k.op("act", lambda e: e.activation(out=st[:, 4:5], in_=st[:, 1:2], func=AF.Sqrt, bias=P.epsb[:],
                                                   scale=1.0 / 256), R=[bst, P.b_const], W=[bst])
                k.op("dve", lambda e: e.reciprocal(out=st[:, 5:7], in_=st[:, 3:5]), R=[bst], W=[bst])
                k.op("dve", lambda e: e.tensor_scalar(out=cn[:, 0:384], in0=P.ps[0][:, 0:384], scalar1=st[:, 5:6],
                                                      scalar2=None, op0=ALU.mult), R=[P.psb[0], bst], W=[bcn])
                k.op("dve", lambda e: e.tensor_scalar(out=cn[:, 384:640], in0=P.ps[1][:, 0:256], scalar1=st[:, 6:7],
                                                      scalar2=None, op0=ALU.mult), R=[P.psb[1], bst], W=[bcn])
                k.op("act", lambda e: e.copy(out=kr[:], in_=P.ps[1][:, 256:320]), R=[P.psb[1]], W=[bkr])
                psT = P.ps[2][:].bitcast(BF16)
                for c in range(5):
                    k.op("pe", lambda e, c=c, psT=psT: e.transpose(out=psT[:, c * 128:(c + 1) * 128],
                                                                   in_=cn[:, c * 128:(c + 1) * 128], identity=P.ident[:]),
                         R=[bcn, P.b_const], W=[P.psb[2]])
                k.op("dve", lambda e, psT=psT: e.tensor_tensor(out=cT[:], in0=psT[:, 0:640].rearrange("p (c t) -> p c t", c=5),
                                                               in1=g5[:, :].unsqueeze(2).broadcast_to([128, 5, 128]), op=ALU.mult),
                     R=[P.psb[2], bc], W=[bcT])
                if LIM < 3:
                    return
                for n in range(3):
                    for kc in range(3):
                        k.op("pe", lambda e, n=n, kc=kc: e.matmul(P.ps[3 + n][:], lhsT=cT[:, kc, :],
                                                                  rhs=wuq[:, kc, n * 512:(n + 1) * 512],
                                                                  start=(kc == 0), stop=(kc == 2)), R=[bcT, bc], W=[P.psb[3 + n]])
                for n in range(3):
                    k.op("act", lambda e, n=n: e.activation(out=qsb[:, n * 512:(n + 1) * 512], in_=P.ps[3 + n][:],
                                                            func=AF.Copy, scale=SCALE), R=[P.psb[3 + n]], W=[bqsb])
                kvb = [6, 5, 6, 5]
                vs = vst[nv % 2]
                bvs = bvst[nv % 2]
                nv += 1
                for n in range(4):
                    for kc in range(2):
                        k.op("pe", lambda e, n=n, kc=kc: e.matmul(P.ps[kvb[n]][:], lhsT=cT[:, 3 + kc, :],
                                                                  rhs=wukv[:, kc, n * 512:(n + 1) * 512],
                                                                  start=(kc == 0), stop=(kc == 1)), R=[bcT, bc], W=[P.psb[kvb[n]]])
                    pv = P.ps[kvb[n]][:, :].rearrange("p (h c) -> p h c", h=2)
                    if n % 2 == 0:
                        k.op("act", lambda e, n=n, pv=pv: e.copy(out=kbf[:, 2 * n:2 * n + 2, :], in_=pv[:, :, 0:128]),
                             R=[P.psb[kvb[n]]], W=[bkbf])
                        k.op("act", lambda e, n=n, pv=pv, vs=vs: e.copy(out=vs[:, 2 * n:2 * n + 2, 0:128], in_=pv[:, :, 128:256]),
                             R=[P.psb[kvb[n]]], W=[bvs])
                    else:
                        k.op("dve", lambda e, n=n, pv=pv: e.tensor_copy(out=kbf[:, 2 * n:2 * n + 2, :], in_=pv[:, :, 0:128]),
                             R=[P.psb[kvb[n]]], W=[bkbf])
                        k.op("dve", lambda e, n=n, pv=pv, vs=vs: e.tensor_copy(out=vs[:, 2 * n:2 * n + 2, 0:128], in_=pv[:, :, 128:256]),
                             R=[P.psb[kvb[n]]], W=[bvs])
                if LIM < 3.1:
                    return
                k.op("pool", lambda e: e.tensor_tensor(out=junk2[:], in0=qsb[:], in1=qsb[:], op=ALU.mult),
                     R=[bqsb], W=[bj2])
                k.op("dve", lambda e: e.tensor_reduce(out=st[:, 8:16], in_=junk2[:, :].rearrange("p (h d) -> p h d", h=8),
                                                      axis=AX.X, op=ALU.add), R=[bj2], W=[bst])
                k.op("dve", lambda e: e.tensor_tensor(out=stats[:, 0:8], in0=stats[:, 0:8], in1=st[:, 8:16], op=ALU.max),
                     R=[bst, P.b_stats], W=[P.b_stats])
                if LIM < 3.2:
                    return
                q3 = qsb[:, :].rearrange("p (h d) -> p h d", h=8)
                cosb = cosT[:, sub, :].unsqueeze(1).broadcast_to([128, 8, 32])
                sinb = sinT[:, sub, :].unsqueeze(1).broadcast_to([128, 8, 32])
                x1 = q3[:, :, 128:160]
                x2 = q3[:, :, 160:192]
                k.op("pool", lambda e, x1=x1, cosb=cosb: e.tensor_tensor(out=rt[0][:], in0=x1, in1=cosb, op=ALU.mult),
                     R=[bqsb, br], W=[brt])
                k.op("pool", lambda e, x2=x2, sinb=sinb: e.tensor_tensor(out=rt[1][:], in0=x2, in1=sinb, op=ALU.mult),
                     R=[bqsb, br], W=[brt])
                k.op("pool", lambda e, x1=x1, sinb=sinb: e.tensor_tensor(out=rt[2][:], in0=x1, in1=sinb, op=ALU.mult),
                     R=[bqsb, br], W=[brt])
                k.op("pool", lambda e, x2=x2, cosb=cosb: e.tensor_tensor(out=rt[3][:], in0=x2, in1=cosb, op=ALU.mult),
                     R=[bqsb, br], W=[brt])
                k.op("dve", lambda e: e.tensor_tensor(out=qbf[:, :, 128:160], in0=rt[0][:], in1=rt[1][:], op=ALU.subtract),
                     R=[brt], W=[bqbf])
                k.op("dve", lambda e: e.tensor_tensor(out=qbf[:, :, 160:192], in0=rt[2][:], in1=rt[3][:], op=ALU.add),
                     R=[brt], W=[bqbf])
                k.op("act", lambda e, q3=q3: e.copy(out=qbf[:, :, 0:128], in_=q3[:, :, 0:128]), R=[bqsb], W=[bqbf])
                if LIM < 3.3:
                    return
                psQn = P.ps[7][:].bitcast(BF16)
                psQr = P.ps[3][:].bitcast(BF16)
                for h in range(8):
                    k.op("pe", lambda e, h=h, psQn=psQn: e.transpose(out=psQn[:, h * 128:(h + 1) * 128], in_=qbf[:, h, 0:128],
                                                                     identity=P.ident[:]), R=[bqbf, P.b_const], W=[P.psb[7]])
                for h in range(8):
                    k.op("pe", lambda e, h=h, psQr=psQr: e.transpose(out=psQr[0:64, h * 128:(h + 1) * 128], in_=qbf[:, h, 128:192],
                                                                     identity=P.ident[:]), R=[bqbf, P.b_const], W=[P.psb[3]])
                k.op("act", lambda e, psQn=psQn, tok=tok: e.copy(out=qTn_st[:, :, tok], in_=psQn.rearrange("p (h t) -> p h t", h=8)),
                     R=[P.psb[7]], W=[bqT])
                k.op("dve", lambda e, psQr=psQr, tok=tok: e.tensor_copy(out=qTr_st[:, :, tok],
                                                                        in_=psQr[0:64, :].rearrange("p (h t) -> p h t", h=8)),
                     R=[P.psb[3]], W=[bqT])
                if LIM < 4:
                    return
                if LIM < 4.1:
                    return
                if LIM < 4.2:
                    return
                r0 = blk * TB + j * 128
                k.dma("sp", [(P.mla_dram["Va"][r0:r0 + 128, :], vs[:, :, :].rearrange("p h c -> p (h c)"))],
                      R=[bvs], W=[P.b_qkv[blk]], key=f"d_vst{(nv - 1) % 2}")
                if LIM < 4.3:
                    return
                k.op("pool", lambda e: e.tensor_tensor(out=junk2[:, 0:1024], in0=kbf[:, :, :].rearrange("p h c -> p (h c)"),
                                                       in1=kbf[:, :, :].rearrange("p h c -> p (h c)"), op=ALU.mult),
                     R=[bkbf], W=[bj2])
                k.op("dve", lambda e: e.tensor_reduce(out=st[:, 8:16], in_=junk2[:, 0:1024].rearrange("p (h d) -> p h d", h=8),
                                                      axis=AX.X, op=ALU.add), R=[bj2], W=[bst])
                k.op("dve", lambda e: e.tensor_scalar(out=st[:, 8:16], in0=st[:, 8:16], scalar1=st[:, 2:3], scalar2=None,
                                                      op0=ALU.add), R=[bst], W=[bst])
                k.op("dve", lambda e: e.tensor_tensor(out=stats[:, 8:16], in0=stats[:, 8:16], in1=st[:, 8:16], op=ALU.max),
                     R=[bst, P.b_stats], W=[P.b_stats])
                if LIM < 4.4:
                    return
                c1 = cosT[:, sub, :]
                s1 = sinT[:, sub, :]
                k.op("pool", lambda e, c1=c1: e.tensor_tensor(out=rt[0][:, 0, :], in0=kr[:, 0:32], in1=c1, op=ALU.mult),
                     R=[bkr, br], W=[brt])
                k.op("pool", lambda e, s1=s1: e.tensor_tensor(out=rt[1][:, 0, :], in0=kr[:, 32:64], in1=s1, op=ALU.mult),
                     R=[bkr, br], W=[brt])
                k.op("pool", lambda e, s1=s1: e.tensor_tensor(out=rt[2][:, 0, :], in0=kr[:, 0:32], in1=s1, op=ALU.mult),
                     R=[bkr, br], W=[brt])
                k.op("pool", lambda e, c1=c1: e.tensor_tensor(out=rt[3][:, 0, :], in0=kr[:, 32:64], in1=c1, op=ALU.mult),
                     R=[bkr, br], W=[brt])
                k.op("dve", lambda e: e.tensor_tensor(out=krb[:, 0:32], in0=rt[0][:, 0, :], in1=rt[1][:, 0, :], op=ALU.subtract),
                     R=[brt], W=[bkrb])
                k.op("dve", lambda e: e.tensor_tensor(out=krb[:, 32:64], in0=rt[2][:, 0, :], in1=rt[3][:, 0, :], op=ALU.add),
                     R=[brt], W=[bkrb])
                if LIM < 4.5:
                    return
                psKn = P.ps[7][:].bitcast(BF16)
                psKr = P.ps[4][:].bitcast(BF16)
                for h in range(8):
                    k.op("pe", lambda e, h=h, psKn=psKn: e.transpose(out=psKn[:, h * 128:(h + 1) * 128], in_=kbf[:, h, :],
                                                                     identity=P.ident[:]), R=[bkbf, P.b_const], W=[P.psb[7]])
                k.op("pe", lambda e, psKr=psKr: e.transpose(out=psKr[0:64, 0:128], in_=krb[:], identity=P.ident[:]),
                     R=[bkrb, P.b_const], W=[P.psb[4]])
                k.op("act", lambda e, psKn=psKn, tok=tok: e.copy(out=kTn_st[:, :, tok], in_=psKn.rearrange("p (h t) -> p h t", h=8)),
                     R=[P.psb[7]], W=[bkT])
                k.op("dve", lambda e, psKr=psKr, tok=tok: e.tensor_copy(out=kTr_st[:, tok], in_=psKr[0:64, 0:128]),
                     R=[P.psb[4]], W=[bkT])

            for j in range(4):
                if LIM < 2 or (LIM < 90 and j > 0):
                    break
                do_sub(blk, j, SETS[(blk * 4 + j) % 2])
            if LIM < 5:
                break
            c0 = blk * TB
            k.dma("sp", [(P.mla_dram["QTn"][:, :, c0:c0 + TB].rearrange("h d t -> d h t"), qTn_st[:]),
                         (P.mla_dram["QTr"][:, :, c0:c0 + TB].rearrange("h d t -> d h t"), qTr_st[:])],
                  R=[bqT], W=[P.b_qkv[blk]], key="d_qT")
            k.dma("sp", [(P.mla_dram["KTn"][:, :, c0:c0 + TB].rearrange("h d t -> d h t"), kTn_st[:]),
                         (P.mla_dram["KTr"][:, c0:c0 + TB], kTr_st[:])],
                  R=[bkT], W=[P.b_qkv[blk]], key="d_kT")
        k.barrier()


def mla_attn_phase(P, src, b_src, dst, b_dst, W):
    k, nc = P.k, P.nc
    I = P.inp
    Dm = P.mla_dram
    stats = P.mla_stats
    with ExitStack() as es:
        bc = k.buf("pe_consts")
        wo = P.sb(es, "pe_wo", [128, 8, 1024], BF16)
        k.dma("sp", [(wo[:], W["odd_w_out"][0].rearrange("(c p) n -> p c n", p=128))], R=[W["odd_w_out"][1]], W=[bc])
        tps = P.ps[7]
        sm = P.sb(es, "pe_sm", [16, 4], F32)
        dg = P.sb(es, "pe_dg", [8, 8], F32)
        negc = P.sb(es, "pe_negc", [128, 8], F32)
        k.op("pe", lambda e: e.transpose(out=tps[0:16, 0:128], in_=stats[:, 0:16], identity=P.ident_f[:]),
             R=[P.b_stats, P.b_const], W=[P.psb[7]])
        k.op("dve", lambda e: e.tensor_reduce(out=sm[:, 0:1], in_=tps[0:16, 0:128], axis=AX.X, op=ALU.max),
             R=[P.psb[7]], W=[bc])
        k.op("pe", lambda e: e.transpose(out=tps[0:1, 128:144], in_=sm[:, 0:1], identity=P.ident_f[0:16, 0:16]),
             R=[bc, P.b_const], W=[P.psb[7]])
        rowv = P.sb(es, "pe_rowv", [1, 24], F32)
        k.op("dve", lambda e: e.tensor_copy(out=rowv[:, 0:16], in_=tps[0:1, 128:144]), R=[P.psb[7]], W=[bc])
        k.op("dve", lambda e: e.tensor_tensor(out=rowv[:, 16:24], in0=rowv[:, 0:8], in1=rowv[:, 8:16], op=ALU.mult),
             R=[bc], W=[bc])
        k.op("act", lambda e: e.activation(out=rowv[:, 16:24], in_=rowv[:, 16:24], func=AF.Sqrt), R=[bc], W=[bc])
        k.op("dve", lambda e: e.tensor_scalar(out=rowv[:, 16:24], in0=rowv[:, 16:24], scalar1=-1.0, scalar2=None, op0=ALU.mult),
             R=[bc], W=[bc])
        k.op("pe", lambda e: e.matmul(tps[:, 160:168], lhsT=P.ones_f[0:1, :], rhs=rowv[:, 16:24], start=True, stop=True),
             R=[bc, P.b_const], W=[P.psb[7]])
        k.op("dve", lambda e: e.tensor_copy(out=negc[:], in_=tps[:, 160:168]), R=[P.psb[7]], W=[bc])
        tri = P.sb(es, "pe_tri", [128, 128], BF16)
        k.op("pool", lambda e: e.affine_select(out=tri[:], in_=P.ones_f[:], pattern=[[1, 128]], compare_op=ALU.is_ge,
                                               fill=0.0, base=0, channel_multiplier=-1), R=[P.b_const], W=[bc])
        KTn = P.sb(es, "pe_KTn", [128, 8, S], BF16)
        KTr = P.sb(es, "pe_KTr", [128, S], BF16)
        Vs = P.sb(es, "pe_V", [128, 32, HQ * VP], BF16)
        bK = k.bufs(8, "pe_K")
        bV = k.bufs(8, "pe_V")
        k.op("pool", lambda e: e.memset(KTr[64:128, :], 0.0), W=bK)
        for b in range(NB):
            c0 = b * TB
            k.dma("sp", [(KTn[:, :, c0:c0 + TB], Dm["KTn"][:, :, c0:c0 + TB].rearrange("h d t -> d h t")),
                         (KTr[0:64, c0:c0 + TB], Dm["KTr"][:, c0:c0 + TB]),
                         (Vs[:, 4 * b:4 * b + 4, :], Dm["Va"][c0:c0 + TB, :].rearrange("(j p) c -> p j c", p=128))],
                  R=[P.b_qkv[b]], W=[bK[b], bV[b]], key=f"d_peKV{b}")
        NQ = 1
        Qn = [P.sb(es, f"pe_Qn{i}", [128, 8, TB], BF16) for i in range(NQ)]
        Qr = [P.sb(es, f"pe_Qr{i}", [128, 8, TB], BF16) for i in range(NQ)]
        bQ = k.bufs(NQ, "pe_Q")
        for i in range(NQ):
            k.op("pool", lambda e, i=i: e.memset(Qr[i][64:128, :, :], 0.0), W=[bQ[i]])
        NPT = 4
        PT = [P.sb(es, f"pe_PT{i}", [128, TB], BF16) for i in range(NPT)]
        bPT = k.bufs(NPT, "pe_PT")
        att = P.sb(es, "pe_att", [128, 4, 1024], BF16)
        batt = k.bufs(4, "pe_att")
        aT = P.sb(es, "pe_aT", [128, 8, TB], BF16)
        baT = k.buf("pe_aT")
        rden = P.sb(es, "pe_rden", [128, 4], F32)
        brden = k.buf("pe_rden")
        hres = [P.sb(es, f"pe_hres{i}", [128, 1024], F32) for i in range(2)]
        bhres = k.bufs(2, "pe_hres")
        npt = 0
        nsc = 0
        for qb in range(NB):
            qi = qb % NQ
            c0 = qb * TB
            k.dma("sp", [(Qn[qi][:], Dm["QTn"][:, :, c0:c0 + TB].rearrange("h d t -> d h t")),
                         (Qr[qi][0:64], Dm["QTr"][:, :, c0:c0 + TB].rearrange("h d t -> d h t"))],
                  R=[P.b_qkv[qb]], W=[bQ[qi]])
            nkt = 4 * qb + 4
            tiles = [(h, kt) for h in range(8) for kt in range(nkt)]
            SCB = (0, 1, 2, 7)
            info = {}

            def emit_qk(i):
                nonlocal nsc, npt
                h, kt = tiles[i]
                r = kt - 4 * qb
                q0 = max(r, 0) * 128
                sb_ = SCB[nsc % 4]
                nsc += 1
                pi = npt % NPT
                npt += 1
                kb = kt // 4
                info[i] = (sb_, pi, q0, r, kb)
                k.op("pe", lambda e, h=h, kt=kt, q0=q0, sb_=sb_: e.matmul(
                    P.ps[sb_][:, q0:TB], lhsT=KTn[:, h, kt * 128:(kt + 1) * 128], rhs=Qn[qi][:, h, q0:TB],
                    start=True, stop=False), R=[bK[kb], bQ[qi]], W=[P.psb[sb_]])
                k.op("pe", lambda e, h=h, kt=kt, q0=q0, sb_=sb_: e.matmul(
                    P.ps[sb_][:, q0:TB], lhsT=KTr[:, kt * 128:(kt + 1) * 128], rhs=Qr[qi][:, h, q0:TB],
                    start=False, stop=True), R=[bK[kb], bQ[qi]], W=[P.psb[sb_]])
                k.op("act", lambda e, h=h, q0=q0, sb_=sb_, pi=pi: e.activation(
                    out=PT[pi][:, q0:TB], in_=P.ps[sb_][:, q0:TB], func=AF.Exp, bias=negc[:, h:h + 1], scale=1.0),
                    R=[P.psb[sb_], bc], W=[bPT[pi]])
                if r >= 0:
                    k.op("pool", lambda e, q0=q0, pi=pi: e.tensor_tensor(out=PT[pi][:, q0:q0 + 128], in0=PT[pi][:, q0:q0 + 128],
                                                                         in1=tri[:], op=ALU.mult), R=[bPT[pi], bc], W=[bPT[pi]])

            LOOK = 2
            for i in range(min(LOOK, len(tiles))):
                emit_qk(i)
            first = [True, True]
            for i, (h, kt) in enumerate(tiles):
                if i + LOOK < len(tiles):
                    emit_qk(i + LOOK)
                ob = (3, 4) if h % 2 == 0 else (5, 6)
                if kt == 0:
                    first = [True, True]
                sb_, pi, q0, r, kb = info.pop(i)
                for js in range(max(r, 0), 4):
                    bank = ob[js // 2]
                    off = (js % 2) * 129
                    last = (kt == 4 * qb + js)
                    st_ = first[js // 2]
                    first[js // 2] = False
                    k.op("pe", lambda e, js=js, bank=bank, off=off, pi=pi, kt=kt, h=h, st_=st_, last=last: e.matmul(
                        P.ps[bank][:, off:off + 129], lhsT=PT[pi][:, js * 128:(js + 1) * 128],
                        rhs=Vs[:, kt, h * VP:h * VP + 129], start=st_, stop=last, skip_group_check=True),
                        R=[bPT[pi], bV[kb]], W=[P.psb[bank]])
                if kt == nkt - 1:
                    for js in range(4):
                        bank = ob[js // 2]
                        off = (js % 2) * 129
                        k.op("dve", lambda e, js=js, bank=bank, off=off: e.reciprocal(out=rden[:, js:js + 1],
                                                                                      in_=P.ps[bank][:, off + 128:off + 129]),
                             R=[P.psb[bank]], W=[brden])
                        k.op("dve", lambda e, js=js, bank=bank, off=off, h=h: e.tensor_scalar(
                            out=att[:, js, h * 128:(h + 1) * 128], in0=P.ps[bank][:, off:off + 128], scalar1=rden[:, js:js + 1],
                            scalar2=None, op0=ALU.mult), R=[P.psb[bank], brden], W=[batt[js]])
            for js in range(4):
                psA = P.ps[7][:].bitcast(BF16)
                for h in range(8):
                    k.op("pe", lambda e, js=js, h=h, psA=psA: e.transpose(out=psA[:, h * 128:(h + 1) * 128],
                                                                          in_=att[:, js, h * 128:(h + 1) * 128], identity=P.ident[:]),
                         R=[batt[js], P.b_const], W=[P.psb[7]])
                k.op("dve", lambda e, js=js, psA=psA: e.tensor_copy(out=aT[:, :, js * 128:(js + 1) * 128],
                                                                    in_=psA.rearrange("p (h t) -> p h t", h=8)),
                     R=[P.psb[7]], W=[baT])
            for js in range(4):
                r0 = qb * TB + js * 128
                hr = hres[js % 2]
                bhr = bhres[js % 2]
                k.dma("sp", [(hr[:], src[r0:r0 + 128, :])], R=[b_src[qb * 4 + js]], W=[bhr], key=f"d_hres{js % 2}")
                for n in range(2):
                    bank = (3, 5)[n]
                    for h in range(8):
                        k.op("pe", lambda e, js=js, n=n, h=h, bank=bank: e.matmul(
                            P.ps[bank][:], lhsT=aT[:, h, js * 128:(js + 1) * 128], rhs=wo[:, h, n * 512:(n + 1) * 512],
                            start=(h == 0), stop=(h == 7)), R=[baT, bc], W=[P.psb[bank]])
                    k.op("dve", lambda e, n=n, bank=bank, hr=hr: e.tensor_tensor(
                        out=hr[:, n * 512:(n + 1) * 512], in0=hr[:, n * 512:(n + 1) * 512], in1=P.ps[bank][:],
                        op=ALU.add), R=[bhr, P.psb[bank]], W=[bhr])
                k.dma("sp", [(dst[r0:r0 + 128, :], hr[:])], R=[bhr], W=[b_dst[qb * 4 + js]], key=f"d_hst{js % 2}")
        k.barrier()


def hgrn_phase(P, src, b_src, W):
    k, nc = P.k, P.nc
    I = P.inp
    uT_d = P.dint("uT", [512, S], BF16)
    ycatT = P.dint("ycatT", [1024, S], BF16)
    P.uT_d, P.ycatT = uT_d, ycatT
    P.b_uT = k.bufs(NB, "uT")
    P.b_ya = k.bufs(NB, "ya")
    P.b_yb = k.bufs(NB, "yb")
    with ExitStack() as es:
        bc = k.buf("pa_consts")
        gT = load_vec_fm(P, es, "pa_g", I["norm_mix_g"][0], 8, bc)
        win = P.sb(es, "pa_win", [128, 8, 2560], BF16)
        k.dma("sp", [(win[:, 4 * i:4 * i + 4, :], W["even_w_in"][0][512 * i:512 * (i + 1), :].rearrange("(c p) n -> p c n", p=128))
                     for i in range(2)], R=[W["even_w_in"][1]], W=[bc])
        l0 = load_vec_fm(P, es, "pa_l0", I["hgrn_lb_logits"][0], 4, bc)
        l1 = load_vec_fm(P, es, "pa_l1", I["hgrn_lb_logits"][1], 4, bc)
        lb = P.sb(es, "pa_lb", [128, 4], F32)
        oml = P.sb(es, "pa_oml", [128, 4], F32)
        noml = P.sb(es, "pa_noml", [128, 4], F32)
        k.op("dve", lambda e: e.tensor_tensor(out=lb[:], in0=l0[:], in1=l1[:], op=ALU.subtract), R=[bc], W=[bc])
        k.op("act", lambda e: e.activation(out=lb[:], in_=lb[:], func=AF.Sigmoid), R=[bc], W=[bc])
        k.op("dve", lambda e: e.tensor_scalar(out=oml[:], in0=lb[:], scalar1=-1.0, scalar2=1.0, op0=ALU.mult, op1=ALU.add),
             R=[bc], W=[bc])
        k.op("dve", lambda e: e.tensor_scalar(out=noml[:], in0=oml[:], scalar1=-1.0, scalar2=None, op0=ALU.mult),
             R=[bc], W=[bc])
        ng = P.sb(es, "pa_ng", [128, 512], F32)
        k.dma("sp", [(ng[:], I["hgrn_norm_g"].partition_broadcast(128))], W=[bc])
        msk = P.sb(es, "pa_msk", [128, 128], F32)
        k.op("pool", lambda e: e.affine_select(out=msk[:], in_=P.ones_f[:], pattern=[[1, 128]], compare_op=ALU.is_ge,
                                               fill=0.0, base=0, channel_multiplier=-1), R=[P.b_const], W=[bc])
        k.op("pool", lambda e: e.memset(msk[0:64, 64:128], 0.0), R=[bc], W=[bc])
        rst = P.sb(es, "pa_rst", [128, 512], F32)
        k.op("pool", lambda e: e.memset(rst[:], 1.0), W=[bc])
        k.op("pool", lambda e: e.memset(rst[:, :].rearrange("p (n c) -> p n c", c=64)[:, :, 0:1], 0.0), R=[bc], W=[bc])
        mA = P.sb(es, "pa_mA", [128, 512], BF16)
        mB = P.sb(es, "pa_mB", [128, 512], BF16)
        k.op("pool", lambda e: e.memset(mA[:], 1.0), W=[bc])
        k.op("pool", lambda e: e.memset(mA[:, :].rearrange("p (n c) -> p n c", c=128)[:, :, 64:128], 0.0), R=[bc], W=[bc])
        k.op("pool", lambda e: e.memset(mB[:], 1.0), W=[bc])
        k.op("pool", lambda e: e.memset(mB[:, :].rearrange("p (n c) -> p n c", c=128)[:, :, 0:64], 0.0), R=[bc], W=[bc])
        rmA = P.sb(es, "pa_rmA", [128, 1], F32)
        rmB = P.sb(es, "pa_rmB", [128, 1], F32)
        k.op("pool", lambda e: e.memset(rmA[0:64, :], 1.0), W=[bc])
        k.op("pool", lambda e: e.memset(rmA[64:128, :], 0.0), W=[bc])
        k.op("pool", lambda e: e.memset(rmB[0:64, :], 0.0), W=[bc])
        k.op("pool", lambda e: e.memset(rmB[64:128, :], 1.0), W=[bc])
        St = P.sb(es, "pa_S", [128, 4, 128], F32)
        Sbf = [P.sb(es, f"pa_Sbf{i}", [128, 4, 128], BF16) for i in range(2)]
        bS = k.bufs(4, "pa_S")
        bSbf = [k.bufs(4, "pa_SbfA"), k.bufs(4, "pa_SbfB")]
        k.op("pool", lambda e: e.memset(St[:], 0.0), W=bS)
        k.op("pool", lambda e: e.memset(Sbf[0][:], 0.0), W=bSbf[0])
        k.op("pool", lambda e: e.memset(Sbf[1][:], 0.0), W=bSbf[1])

        T = alloc_blockbufs(P, es, "a_", ps_tr=[7])
        T["b_g"] = bc
        sig = P.sb(es, "pa_sig", [128, 512], F32); bsig = k.buf("pa_sig")
        logf = P.sb(es, "pa_logf", [128, 512], F32); blogf = k.buf("pa_logf")
        bb = P.sb(es, "pa_b", [128, 512], F32); bbb = k.buf("pa_b")
        eb = P.sb(es, "pa_eb", [128, 4, 512], F32); beb = k.bufs(4, "pa_eb")
        enb = P.sb(es, "pa_enb", [128, 512], F32); benb = k.buf("pa_enb")
        kk = P.sb(es, "pa_kk", [128, 512], F32); bkk = k.buf("pa_kk")
        kd32 = P.sb(es, "pa_kd32", [128, 512], F32); bkd32 = k.buf("pa_kd32")
        qd = P.sb(es, "pa_qd", [128, 4, 512], BF16); bqd = k.bufs(4, "pa_qd")
        kd = P.sb(es, "pa_kd", [128, 4, 512], BF16); bkd = k.bufs(4, "pa_kd")
        qdA = P.sb(es, "pa_qdA", [128, 4, 512], BF16); qdB = P.sb(es, "pa_qdB", [128, 4, 512], BF16)
        kbT = P.sb(es, "pa_kbT", [128, 4, 512], BF16); bkbT = k.bufs(4, "pa_kbT")
        uTs = P.sb(es, "pa_uT", [128, 4, 512], BF16); buTs = k.buf("pa_uTs")
        kbtm = P.sb(es, "pa_kbtm", [128, 4, 128], BF16); bkbtm = k.buf("pa_kbtm")
        kbtmB = P.sb(es, "pa_kbtmB", [128, 4, 128], BF16)
        vtm = P.sb(es, "pa_vtm", [128, 512], BF16); bvtm = k.buf("pa_vtm")
        ngsg = P.sb(es, "pa_ngsg", [128, 512], F32); bngsg = k.buf("pa_ngsg")
        attm = P.sb(es, "pa_attm", [128, 4, 128], BF16); battm = k.bufs(4, "pa_attm")
        yatm = P.sb(es, "pa_yatm", [128, 512], BF16); byatm = k.buf("pa_yatm")
        yaT = P.sb(es, "pa_yaT", [128, 4, 512], BF16); byaT = k.buf("pa_yaT")
        sst = P.sb(es, "pa_sst", [128, 16], F32); bsst = k.bufs(4, "pa_sst")
        junk = P.sb(es, "pa_junk", [128, 128], F32); bjunk = k.buf("pa_junk")
        import os
        HL = float(os.environ.get("HG_LIMIT", "99"))
        for blk in range(NB):
            if HL < 1 or (HL < 90 and blk > 0):
                break
            norm_block(P, T, blk, src, b_src, gT)
            hnT, bh = T["hnT"], T["b_hnT"]
            c0 = blk * TB
            for c in range(4):
                pb = 4 + c % 2
                for kc in range(8):
                    k.op("pe", lambda e, c=c, kc=kc, pb=pb: e.matmul(P.ps[pb][:], lhsT=win[:, kc, 2048 + c * 128:2048 + (c + 1) * 128],
                                                                     rhs=hnT[:, kc, :], start=(kc == 0), stop=(kc == 7)),
                         R=[bc, bh], W=[P.psb[pb]])
                k.op("act", lambda e, c=c, pb=pb: e.copy(out=uTs[:, c, :], in_=P.ps[pb][:]), R=[P.psb[pb]], W=[buTs])
            k.dma("sp", [(uT_d[:, c0:c0 + TB].rearrange("(c p) t -> p c t", p=128), uTs[:])], R=[buTs], W=[P.b_uT[blk]], key="d_uTst")
            if HL < 1.1:
                break
            for h in range(4):
                if HL < 1.2 and h > 0:
                    break
                pq, pf = 4, 5
                for kc in range(8):
                    k.op("pe", lambda e, h=h, kc=kc: e.matmul(P.ps[pf][:], lhsT=win[:, kc, 512 + h * 128:512 + (h + 1) * 128],
                                                              rhs=hnT[:, kc, :], start=(kc == 0), stop=(kc == 7)),
                         R=[bc, bh], W=[P.psb[pf]])
                for kc in range(8):
                    k.op("pe", lambda e, h=h, kc=kc: e.matmul(P.ps[pq][:], lhsT=win[:, kc, h * 128:(h + 1) * 128],
                                                              rhs=hnT[:, kc, :], start=(kc == 0), stop=(kc == 7)),
                         R=[bc, bh], W=[P.psb[pq]])
                k.op("act", lambda e: e.activation(out=sig[:], in_=P.ps[pf][:], func=AF.Sigmoid), R=[P.psb[pf]], W=[bsig])
                k.op("act", lambda e, h=h: e.activation(out=logf[:], in_=sig[:], func=AF.Ln, bias=lb[:, h:h + 1],
                                                        scale=oml[:, h:h + 1]), R=[bsig, bc], W=[blogf])
                if HL < 1.15:
                    break
                k.op("dve", lambda e: e.tensor_tensor_scan(out=bb[:], data0=rst[:], data1=logf[:], initial=0.0,
                                                           op0=ALU.mult, op1=ALU.add), R=[blogf, bc], W=[bbb])
                if HL < 1.17:
                    break
                k.op("act", lambda e, h=h: e.activation(out=eb[:, h, :], in_=bb[:], func=AF.Exp), R=[bbb], W=[beb[h]])
                k.op("act", lambda e: e.activation(out=enb[:], in_=bb[:], func=AF.Exp, scale=-1.0), R=[bbb], W=[benb])
                k.op("dve", lambda e, h=h: e.tensor_scalar(out=kk[:], in0=sig[:], scalar1=noml[:, h:h + 1], scalar2=oml[:, h:h + 1],
                                                           op0=ALU.mult, op1=ALU.add), R=[bsig, bc], W=[bkk])
                k.op("dve", lambda e, h=h: e.tensor_tensor(out=qd[:, h, :], in0=eb[:, h, :], in1=P.ps[pq][:], op=ALU.mult),
                     R=[beb[h], P.psb[pq]], W=[bqd[h]])
                k.op("pool", lambda e, h=h: e.tensor_tensor(out=qdA[:, h, :], in0=qd[:, h, :], in1=mA[:], op=ALU.mult),
                     R=[bqd[h], bc], W=[bqd[h]])
                k.op("pool", lambda e, h=h: e.tensor_tensor(out=qdB[:, h, :], in0=qd[:, h, :], in1=mB[:], op=ALU.mult),
                     R=[bqd[h], bc], W=[bqd[h]])
                k.op("dve", lambda e: e.tensor_tensor(out=kd32[:], in0=kk[:], in1=enb[:], op=ALU.mult), R=[bkk, benb], W=[bkd32])
                if HL < 1.18:
                    break
                k.op("pool", lambda e, h=h: e.tensor_copy(out=kd[:, h, :], in_=kd32[:]), R=[bkd32], W=[bkd[h]])
                k.op("pool", lambda e, h=h: e.tensor_tensor(
                    out=kbT[:, h, :].rearrange("p (n c) -> p n c", c=64), in0=kd32[:, :].rearrange("p (n c) -> p n c", c=64),
                    in1=eb[:, h, :].rearrange("p (n c) -> p n c", c=64)[:, :, 63:64].broadcast_to([128, 8, 64]), op=ALU.mult),
                    R=[bkd32, beb[h]], W=[bkbT[h]])
            for j in range(4):
                if HL < 2 or (HL < 90 and j > 0):
                    break
                tok = slice(j * 128, (j + 1) * 128)
                for kc in range(8):
                    k.op("pe", lambda e, kc=kc, tok=tok: e.matmul(P.ps[4][:], lhsT=hnT[:, kc, tok], rhs=win[:, kc, 1024:1536],
                                                                  start=(kc == 0), stop=(kc == 7)), R=[bc, bh], W=[P.psb[4]])
                for kc in range(8):
                    k.op("pe", lambda e, kc=kc, tok=tok: e.matmul(P.ps[5][:], lhsT=hnT[:, kc, tok], rhs=win[:, kc, 1536:2048],
                                                                  start=(kc == 0), stop=(kc == 7)), R=[bc, bh], W=[P.psb[5]])
                k.op("act", lambda e: e.copy(out=vtm[:], in_=P.ps[4][:]), R=[P.psb[4]], W=[bvtm])
                k.op("act", lambda e: e.activation(out=ngsg[:], in_=P.ps[5][:], func=AF.Silu), R=[P.psb[5]], W=[bngsg])
                k.op("pool", lambda e: e.tensor_tensor(out=ngsg[:], in0=ngsg[:], in1=ng[:], op=ALU.mult), R=[bngsg, bc], W=[bngsg])
                psK = P.ps[6][:].bitcast(BF16)
                for h in range(4):
                    k.op("pe", lambda e, h=h, tok=tok, psK=psK: e.transpose(out=psK[:, h * 128:(h + 1) * 128], in_=kbT[:, h, tok],
                                                                            identity=P.ident[:]), R=[bkbT[h], P.b_const], W=[P.psb[6]])
                k.op("act", lambda e, psK=psK: e.activation(out=kbtm[:], in_=psK[:, 0:512].rearrange("p (h d) -> p h d", h=4),
                                                            func=AF.Copy, scale=rmA[:, 0:1]), R=[P.psb[6], bc], W=[bkbtm])
                k.op("act", lambda e, psK=psK: e.activation(out=kbtmB[:], in_=psK[:, 0:512].rearrange("p (h d) -> p h d", h=4),
                                                            func=AF.Copy, scale=rmB[:, 0:1]), R=[P.psb[6], bc], W=[bkbtm])
                tA = j * 128 + 63
                tB = j * 128 + 127
                if HL < 3:
                    break
                for h in range(4):
                    pb = h
                    vh = vtm[:, h * 128:(h + 1) * 128]
                    k.op("pe", lambda e, h=h, tok=tok, pb=pb: e.matmul(P.ps[pb][:, 0:128], lhsT=kd[:, h, tok], rhs=qd[:, h, tok],
                                                                       start=True, stop=True), R=[bkd[h], bqd[h]], W=[P.psb[pb]])
                    k.op("pe", lambda e, h=h, pb=pb, vh=vh: e.matmul(P.ps[pb][:, 128:256], lhsT=kbtm[:, h, :], rhs=vh,
                                                                     start=True, stop=True), R=[bkbtm, bvtm], W=[P.psb[pb]])
                    k.op("pe", lambda e, h=h, pb=pb, vh=vh: e.matmul(P.ps[pb][:, 256:384], lhsT=kbtmB[:, h, :], rhs=vh,
                                                                     start=True, stop=True), R=[bkbtm, bvtm], W=[P.psb[pb]])
                    k.op("dve", lambda e, h=h, pb=pb: e.tensor_tensor(out=attm[:, h, :], in0=P.ps[pb][:, 0:128], in1=msk[:], op=ALU.mult),
                         R=[P.psb[pb], bc], W=[battm[h]])
                    k.op("dve", lambda e, h=h, pb=pb, tA=tA: e.scalar_tensor_tensor(
                        out=St[:, h, :], in0=St[:, h, :], scalar=eb[:, h, tA:tA + 1], in1=P.ps[pb][:, 128:256], op0=ALU.mult, op1=ALU.add),
                        R=[bS[h], beb[h], P.psb[pb]], W=[bS[h]])
                    k.op("act", lambda e, h=h: e.copy(out=Sbf[1][:, h, :], in_=St[:, h, :]), R=[bS[h]], W=[bSbf[1][h]])
                if HL < 4:
                    break
                for h in range(4):
                    pb = h
                    po = 4 + h % 2
                    vh = vtm[:, h * 128:(h + 1) * 128]
                    tokA = slice(j * 128, j * 128 + 64)
                    tokB = slice(j * 128 + 64, j * 128 + 128)
                    k.op("pe", lambda e, h=h, po=po, vh=vh: e.matmul(P.ps[po][:, 0:128], lhsT=attm[:, h, :], rhs=vh,
                                                                     start=True, stop=False), R=[battm[h], bvtm], W=[P.psb[po]])
                    k.op("pe", lambda e, h=h, po=po, tok=tok: e.matmul(P.ps[po][:, 0:128], lhsT=qdA[:, h, tok], rhs=Sbf[0][:, h, :],
                                                                       start=False, stop=False),
                         R=[bqd[h], bSbf[0][h]], W=[P.psb[po]])
                    k.op("pe", lambda e, h=h, po=po, tok=tok: e.matmul(P.ps[po][:, 0:128], lhsT=qdB[:, h, tok], rhs=Sbf[1][:, h, :],
                                                                       start=False, stop=True),
                         R=[bqd[h], bSbf[1][h]], W=[P.psb[po]])
                    k.op("dve", lambda e, h=h, pb=pb, tB=tB: e.scalar_tensor_tensor(
                        out=St[:, h, :], in0=St[:, h, :], scalar=eb[:, h, tB:tB + 1], in1=P.ps[pb][:, 256:384], op0=ALU.mult, op1=ALU.add),
                        R=[bS[h], beb[h], P.psb[pb]], W=[bS[h]])
                    k.op("act", lambda e, h=h: e.copy(out=Sbf[0][:, h, :], in_=St[:, h, :]), R=[bS[h]], W=[bSbf[0][h]])
                    k.op("act", lambda e, h=h, po=po: e.activation(out=junk[:], in_=P.ps[po][:, 0:128], func=AF.Square,
                                                                   accum_out=sst[:, h:h + 1]), R=[P.psb[po]], W=[bjunk, bsst[h]])
                    k.op("act", lambda e, h=h: e.activation(out=sst[:, 4 + h:5 + h], in_=sst[:, h:h + 1], func=AF.Sqrt, bias=P.epsb[:],
                                                            scale=1.0 / 128), R=[bsst[h], P.b_const], W=[bsst[h]])
                    k.op("dve", lambda e, h=h: e.reciprocal(out=sst[:, 8 + h:9 + h], in_=sst[:, 4 + h:5 + h]), R=[bsst[h]], W=[bsst[h]])
                    k.op("dve", lambda e, h=h, po=po: e.scalar_tensor_tensor(
                        out=yatm[:, h * 128:(h + 1) * 128], in0=P.ps[po][:, 0:128], scalar=sst[:, 8 + h:9 + h],
                        in1=ngsg[:, h * 128:(h + 1) * 128], op0=ALU.mult, op1=ALU.mult),
                        R=[P.psb[po], bsst[h], bngsg], W=[byatm])
                if HL < 5:
                    break
                psY = P.ps[6][:].bitcast(BF16)
                for h in range(4):
                    k.op("pe", lambda e, h=h, psY=psY: e.transpose(out=psY[:, h * 128:(h + 1) * 128], in_=yatm[:, h * 128:(h + 1) * 128],
                                                                   identity=P.ident[:]), R=[byatm, P.b_const], W=[P.psb[6]])
                k.op("dve", lambda e, psY=psY, tok=tok: e.tensor_copy(out=yaT[:, :, tok], in_=psY[:, 0:512].rearrange("p (h t) -> p h t", h=4)),
                     R=[P.psb[6]], W=[byaT])
            if HL < 6:
                break
            k.dma("sp", [(ycatT[0:512, c0:c0 + TB].rearrange("(c p) t -> p c t", p=128), yaT[:])], R=[byaT], W=[P.b_ya[blk]], key="d_yast")
        k.barrier()


def s5_phase(P, W):
    k, nc = P.k, P.nc
    I = P.inp
    L = 16
    NCH = S // L
    G = 32
    with ExitStack() as es:
        bs = k.buf("s5_tab")

        def dv(fn, R=(), Wb=()):
            k.op("dve", fn, R=[bs] + list(R), W=[bs] + list(Wb))

        def ac(fn, R=(), Wb=()):
            k.op("act", fn, R=[bs] + list(R), W=[bs] + list(Wb))

        def t32(name, shape=(128, 32)):
            return P.sb(es, "s5_" + name, list(shape), F32)

        AR, AI, LDT = t32("AR"), t32("AI"), t32("LDT")
        k.dma("sp", [(AR[0:64, :], I["s5_a_re"].rearrange("g p -> p g")), (AR[64:128, :], I["s5_a_re"].rearrange("g p -> p g")),
                     (AI[0:64, :], I["s5_a_im"].rearrange("g p -> p g")), (AI[64:128, :], I["s5_a_im"].rearrange("g p -> p g")),
                     (LDT[:], I["s5_log_dt"].partition_broadcast(128))], W=[bs], allow_slow_non_contiguous=True)
        B1 = t32("B1", (128, 32, 16))
        B2 = t32("B2", (128, 32, 16))
        k.dma("sp", [(B1[0:64], I["s5_b_re"].rearrange("g p h -> p g h")), (B1[64:128], I["s5_b_im"].rearrange("g p h -> p g h")),
                     (B2[0:64], I["s5_b_im"].rearrange("g p h -> p g h")), (B2[64:128], I["s5_b_re"].rearrange("g p h -> p g h"))],
              W=[bs], key="d_s5_b")
        CN1 = t32("CN1", (128, 4, 128))
        CN2 = t32("CN2", (128, 4, 128))
        cre = I["s5_c_re"].rearrange("(f a) h p -> (a h) f p", f=4)
        cim = I["s5_c_im"].rearrange("(f a) h p -> (a h) f p", f=4)
        k.dma("sp", [(CN1[:, :, 0:64], cre), (CN1[:, :, 64:128], cim), (CN2[:, :, 0:64], cim), (CN2[:, :, 64:128], cre)],
              W=[bs], key="d_s5_c")
        dT = load_vec_fm(P, es, "s5_dT", I["s5_d"], 4, bs)
        bgl = load_vec_fm(P, es, "s5_bglu", I["s5_b_glu"], 4, bs)
        sgn = t32("sgn", (128, 1))
        dv(lambda e: e.memset(sgn[0:64, :], -1.0))
        dv(lambda e: e.memset(sgn[64:128, :], 1.0))
        DT, XR, TH = t32("DT"), t32("XR"), t32("TH")
        ac(lambda e: e.activation(out=DT[:], in_=LDT[:], func=AF.Exp))
        dv(lambda e: e.tensor_tensor(out=XR[:], in0=AR[:], in1=DT[:], op=ALU.mult))
        dv(lambda e: e.tensor_tensor(out=TH[:], in0=AI[:], in1=DT[:], op=ALU.mult))
        MAG = t32("MAG")
        ac(lambda e: e.activation(out=MAG[:], in_=XR[:], func=AF.Exp))
        tf, r1, r2 = t32("tf"), t32("r1"), t32("r2")
        ti_ = P.sb(es, "s5_ti", [128, 32], I32)
        dv(lambda e: e.tensor_scalar(out=tf[:], in0=TH[:], scalar1=1.0 / TWO_PI, scalar2=None, op0=ALU.mult))
        dv(lambda e: e.tensor_copy(out=ti_[:], in_=tf[:]))
        dv(lambda e: e.tensor_copy(out=tf[:], in_=ti_[:]))
        dv(lambda e: e.scalar_tensor_tensor(out=r1[:], in0=tf[:], scalar=-CW1, in1=TH[:], op0=ALU.mult, op1=ALU.add))
        dv(lambda e: e.scalar_tensor_tensor(out=r1[:], in0=tf[:], scalar=-CW2, in1=r1[:], op0=ALU.mult, op1=ALU.add))
        dv(lambda e: e.tensor_scalar(out=r1[:], in0=r1[:], scalar1=-math.pi, scalar2=math.pi, op0=ALU.max, op1=ALU.min))
        SN, CS = t32("SN"), t32("CS")
        ac(lambda e: e.activation(out=SN[:], in_=r1[:], func=AF.Sin))
        dv(lambda e: e.tensor_scalar(out=tf[:], in0=r1[:], scalar1=math.pi / 2, scalar2=math.pi, op0=ALU.add, op1=ALU.is_gt))
        dv(lambda e: e.scalar_tensor_tensor(out=tf[:], in0=tf[:], scalar=-TWO_PI, in1=r1[:], op0=ALU.mult, op1=ALU.add))
        dv(lambda e: e.tensor_scalar(out=tf[:], in0=tf[:], scalar1=math.pi / 2, scalar2=math.pi, op0=ALU.add, op1=ALU.min))
        ac(lambda e: e.activation(out=CS[:], in_=tf[:], func=AF.Sin))
        pows = list(range(0, 17)) + [16 << i for i in range(1, 8)]
        TR = {n: t32(f"TR{n}") for n in pows}
        TI = {n: t32(f"TI{n}") for n in pows}
        dv(lambda e: e.memset(TR[0][:], 1.0))
        dv(lambda e: e.memset(TI[0][:], 0.0))
        dv(lambda e: e.tensor_tensor(out=TR[1][:], in0=MAG[:], in1=CS[:], op=ALU.mult))
        dv(lambda e: e.tensor_tensor(out=TI[1][:], in0=MAG[:], in1=SN[:], op=ALU.mult))
        ta, tb = t32("ta"), t32("tb")

        def cmul(oR, oI, aR, aI, bR, bI):
            dv(lambda e: e.tensor_tensor(out=ta[:], in0=aI[:], in1=bI[:], op=ALU.mult))
            dv(lambda e: e.tensor_tensor(out=tb[:], in0=aR[:], in1=bI[:], op=ALU.mult))
            dv(lambda e: e.tensor_tensor(out=oR[:], in0=aR[:], in1=bR[:], op=ALU.mult))
            dv(lambda e: e.tensor_tensor(out=oR[:], in0=oR[:], in1=ta[:], op=ALU.subtract))
            dv(lambda e: e.tensor_tensor(out=oI[:], in0=aI[:], in1=bR[:], op=ALU.mult))
            dv(lambda e: e.tensor_tensor(out=oI[:], in0=oI[:], in1=tb[:], op=ALU.add))
        for n in range(2, 17):
            cmul(TR[n], TI[n], TR[n - 1], TI[n - 1], TR[1], TI[1])
        for i in range(1, 8):
            n = 16 << i
            cmul(TR[n], TI[n], TR[n // 2], TI[n // 2], TR[n // 2], TI[n // 2])
        den, xr_, cr_, ci_ = t32("den"), t32("xr"), t32("cr"), t32("ci")
        dv(lambda e: e.tensor_tensor(out=den[:], in0=AR[:], in1=AR[:], op=ALU.mult))
        dv(lambda e: e.tensor_tensor(out=ta[:], in0=AI[:], in1=AI[:], op=ALU.mult))
        dv(lambda e: e.tensor_tensor(out=den[:], in0=den[:], in1=ta[:], op=ALU.add))
        dv(lambda e: e.reciprocal(out=den[:], in_=den[:]))
        dv(lambda e: e.tensor_scalar(out=xr_[:], in0=TR[1][:], scalar1=-1.0, scalar2=None, op0=ALU.add))
        dv(lambda e: e.tensor_tensor(out=cr_[:], in0=xr_[:], in1=AR[:], op=ALU.mult))
        dv(lambda e: e.tensor_tensor(out=ta[:], in0=TI[1][:], in1=AI[:], op=ALU.mult))
        dv(lambda e: e.tensor_tensor(out=cr_[:], in0=cr_[:], in1=ta[:], op=ALU.add))
        dv(lambda e: e.tensor_tensor(out=cr_[:], in0=cr_[:], in1=den[:], op=ALU.mult))
        dv(lambda e: e.tensor_tensor(out=ci_[:], in0=TI[1][:], in1=AR[:], op=ALU.mult))
        dv(lambda e: e.tensor_tensor(out=ta[:], in0=xr_[:], in1=AI[:], op=ALU.mult))
        dv(lambda e: e.tensor_tensor(out=ci_[:], in0=ci_[:], in1=ta[:], op=ALU.subtract))
        dv(lambda e: e.tensor_tensor(out=ci_[:], in0=ci_[:], in1=den[:], op=ALU.mult))
        cis = t32("cis")
        dv(lambda e: e.tensor_scalar(out=cis[:], in0=ci_[:], scalar1=sgn[:, 0:1], scalar2=None, op0=ALU.mult))
        ncis = t32("ncis")
        dv(lambda e: e.tensor_scalar(out=ncis[:], in0=cis[:], scalar1=-1.0, scalar2=None, op0=ALU.mult))

        def bc16(t):
            return t[:, :].unsqueeze(2).broadcast_to([128, 32, 16])
        BB = t32("BB", (128, 32, 16))
        BBs = t32("BBs", (128, 32, 16))
        tmp3 = t32("tmp3", (128, 32, 16))
        dv(lambda e: e.tensor_tensor(out=BB[:], in0=B1[:], in1=bc16(cr_), op=ALU.mult))
        dv(lambda e: e.tensor_tensor(out=tmp3[:], in0=B2[:], in1=bc16(cis), op=ALU.mult))
        dv(lambda e: e.tensor_tensor(out=BB[:], in0=BB[:], in1=tmp3[:], op=ALU.add))
        dv(lambda e: e.tensor_tensor(out=BBs[:], in0=B2[:], in1=bc16(cr_), op=ALU.mult))
        dv(lambda e: e.tensor_tensor(out=tmp3[:], in0=B1[:], in1=bc16(ncis), op=ALU.mult))
        dv(lambda e: e.tensor_tensor(out=BBs[:], in0=BBs[:], in1=tmp3[:], op=ALU.add))
        CT1 = t32("CT1", (128, 32, 16))
        CT2 = t32("CT2", (128, 32, 16))
        for f in range(4):
            for (CN, CT) in ((CN1, CT1), (CN2, CT2)):
                k.op("pe", lambda e, f=f, CN=CN: e.transpose(out=P.ps[0][:, 0:128], in_=CN[:, f, :], identity=P.ident_f[:]),
                     R=[bs, P.b_const], W=[P.psb[0]])
                dv(lambda e, f=f, CT=CT: e.tensor_copy(out=CT[:, 8 * f:8 * f + 8, :].rearrange("p g h -> p (g h)"), in_=P.ps[0][:, 0:128]),
                   R=[P.psb[0]])
        CTn = t32("CTn", (128, 32, 16))
        dv(lambda e: e.tensor_scalar(out=CTn[:], in0=CT1[:], scalar1=sgn[:, 0:1], scalar2=-1.0, op0=ALU.mult, op1=ALU.mult))
        pi_ = P.sb(es, "s5_pi", [128, 1], I32)
        k.op("pool", lambda e: e.iota(out=pi_[:], pattern=[[0, 1]], base=0, channel_multiplier=1), R=[bs], W=[bs])
        pg_i = P.sb(es, "s5_pgi", [128, 1], I32)
        om_i = P.sb(es, "s5_omi", [128, 1], I32)
        dv(lambda e: e.tensor_scalar(out=pg_i[:], in0=pi_[:], scalar1=4, scalar2=None, op0=ALU.arith_shift_right))
        dv(lambda e: e.tensor_scalar(out=om_i[:], in0=pg_i[:], scalar1=1, scalar2=None, op0=ALU.bitwise_and))
        pgf, om, em = t32("pgf", (128, 1)), t32("om", (128, 1)), t32("em", (128, 1))
        dv(lambda e: e.tensor_copy(out=pgf[:], in_=pg_i[:]))
        dv(lambda e: e.tensor_copy(out=om[:], in_=om_i[:]))
        dv(lambda e: e.tensor_scalar(out=em[:], in0=om[:], scalar1=-1.0, scalar2=1.0, op0=ALU.mult, op1=ALU.add))
        ji = P.sb(es, "s5_ji", [128, 128], I32)
        k.op("pool", lambda e: e.iota(out=ji[:], pattern=[[1, 128]], base=0, channel_multiplier=0), R=[bs], W=[bs])
        dv(lambda e: e.tensor_scalar(out=ji[:], in0=ji[:], scalar1=4, scalar2=None, op0=ALU.arith_shift_right))
        bmask = t32("bmask", (128, 128))
        dv(lambda e: e.tensor_copy(out=bmask[:], in_=ji[:]))
        dv(lambda e: e.tensor_scalar(out=bmask[:], in0=bmask[:], scalar1=pgf[:, 0:1], scalar2=None, op0=ALU.is_equal))
        esw = t32("esw", (128, 128))
        dv(lambda e: e.memset(esw[:], 0.0))
        dv(lambda e: e.tensor_copy(out=esw[0:64, 64:128], in_=P.ident_f[0:64, 0:64]), R=[P.b_const])
        dv(lambda e: e.tensor_copy(out=esw[64:128, 0:64], in_=P.ident_f[64:128, 64:128]), R=[P.b_const])

        uT = P.sb(es, "s5_uT", [128, 4, S], BF16)
        buT = k.bufs(NB, "s5_uT")
        for b in range(NB):
            c0 = b * TB
            k.dma("sp", [(uT[:, :, c0:c0 + TB], P.uT_d[:, c0:c0 + TB].rearrange("(c p) t -> p c t", p=128))],
                  R=[P.b_uT[b]], W=[buT[b]], key="d_s5uT")
        Xb = P.sb(es, "s5_Xb", [128, G, NCH], BF16)
        bXb = k.bufs(G, "s5_Xb")
        Kb = P.sb(es, "s5_Kb", [128, 4, L, 128], BF16)
        bKb = k.buf("s5_Kb")

        def mb_table(n, MB):
            tis = t32(f"tis_{n}") if False else ta
            dv(lambda e: e.tensor_scalar(out=ta[:], in0=TI[n][:], scalar1=sgn[:, 0:1], scalar2=None, op0=ALU.mult))
            dv(lambda e: e.tensor_tensor(out=MB[:], in0=BB[:], in1=bc16(TR[n]), op=ALU.mult))
            dv(lambda e: e.tensor_tensor(out=tmp3[:], in0=BBs[:], in1=bc16(ta), op=ALU.mult))
            dv(lambda e: e.tensor_tensor(out=MB[:], in0=MB[:], in1=tmp3[:], op=ALU.add))

        with ExitStack() as es2:
            Pm = P.sb(es2, "s5_Pm", [128, 4, L, 2, 128], BF16)
            bPm = k.buf("s5_Pm")
            MB = P.sb(es2, "s5_MB", [128, 32, 16], F32)
            for n in range(L):
                mb_table(n, MB)
                j = L - 1 - n
                for f in range(4):
                    mbf = MB[:, 8 * f:8 * f + 8, :].rearrange("p g h -> p (g h)")
                    k.op("pe", lambda e, mbf=mbf: e.transpose(out=P.ps[0][:, 0:128], in_=mbf, identity=P.ident_f[:]),
                         R=[bs, P.b_const], W=[P.psb[0]])
                    k.op("dve", lambda e, f=f, j=j: e.tensor_scalar(out=Pm[:, f, j, 0, :], in0=P.ps[0][:, 0:128], scalar1=em[:, 0:1],
                                                                    scalar2=None, op0=ALU.mult), R=[P.psb[0], bs], W=[bPm])
                    k.op("dve", lambda e, f=f, j=j: e.tensor_scalar(out=Pm[:, f, j, 1, :], in0=P.ps[0][:, 0:128], scalar1=om[:, 0:1],
                                                                    scalar2=None, op0=ALU.mult), R=[P.psb[0], bs], W=[bPm])
                    ctf = CTn[:, 8 * f:8 * f + 8, :].rearrange("p g h -> p (g h)")
                    k.op("pe", lambda e, mbf=mbf, ctf=ctf: e.matmul(P.ps[1][:, 0:128], lhsT=mbf, rhs=ctf, start=True, stop=True),
                         R=[bs], W=[P.psb[1]])
                    if n == 0:
                        k.op("dve", lambda e: e.tensor_tensor(out=tmpK[:], in0=P.ps[1][:, 0:128], in1=bmask[:], op=ALU.mult),
                             R=[P.psb[1], bs], W=[bs]) if False else None
                    k.op("dve", lambda e, f=f, n=n: e.tensor_tensor(out=Kb[:, f, n, :], in0=P.ps[1][:, 0:128], in1=bmask[:], op=ALU.mult),
                         R=[P.psb[1], bs], W=[bKb])
                    if n == 0:
                        k.op("dve", lambda e, f=f: e.scalar_tensor_tensor(out=Kb[:, f, 0, :], in0=P.ident_f[:], scalar=dT[:, f:f + 1],
                                                                          in1=Kb[:, f, 0, :], op0=ALU.mult, op1=ALU.add),
                             R=[bKb, bs, P.b_const], W=[bKb])
            NBG = 4
            Xs = [P.sb(es2, f"s5_Xs{i}", [128, NCH], F32) for i in range(NBG)]
            bXs = k.bufs(NBG, "s5_Xs")
            NMT = 8
            MT = [P.sb(es2, f"s5_MT{i}", [128, 128], F32) for i in range(NMT)]
            bMT = k.bufs(NMT, "s5_MT")
            tM = [P.sb(es2, f"s5_tM{i}", [128, 128], F32) for i in range(4)]
            btM = k.bufs(4, "s5_tM")
            w2 = P.sb(es2, "s5_w2", [128, 32, 8], F32)
            for i in range(8):
                n = 16 << i
                dv(lambda e, i=i, n=n: e.tensor_scalar(out=w2[:, :, i], in0=TI[n][:], scalar1=sgn[:, 0:1], scalar2=-1.0,
                                                       op0=ALU.mult, op1=ALU.mult))
            nmt = 0
            for g0 in range(0, G, NBG):
                gs = list(range(g0, g0 + NBG))
                for g in gs:
                    f, gq = g // 8, (g % 8) // 2
                    m = g % 2
                    vb = g % NBG
                    for j in range(L):
                        k.op("pe", lambda e, f=f, gq=gq, m=m, j=j, vb=vb: e.matmul(
                            P.ps[vb][:, 0:NCH], lhsT=Pm[32 * gq:32 * gq + 32, f, j, m, :],
                            rhs=uT[32 * gq:32 * gq + 32, f, :].rearrange("p (c l) -> p l c", l=L)[:, j, :],
                            start=(j == 0), stop=(j == L - 1), tile_position=(32 * gq, 0)),
                            R=[bPm] + buT, W=[P.psb[vb]])
                for g in gs:
                    vb = g % NBG
                    X = Xs[g % NBG]
                    k.op("act", lambda e, X=X, vb=vb: e.copy(out=X[:], in_=P.ps[vb][:, 0:NCH]), R=[P.psb[vb]], W=[bXs[g % NBG]])
                for i in range(8):
                    sh = 1 << i
                    n = 16 << i
                    for g in gs:
                        X = Xs[g % NBG]
                        mt = MT[nmt % NMT]
                        bmt = bMT[nmt % NMT]
                        tm_ = tM[nmt % 4]
                        btm = btM[nmt % 4]
                        nmt += 1
                        k.op("act", lambda e, mt=mt, g=g, n=n: e.activation(out=mt[:], in_=P.ident_f[:], func=AF.Copy, scale=TR[n][:, g:g + 1]),
                             R=[bs, P.b_const], W=[bmt])
                        k.op("dve", lambda e, mt=mt, g=g, i=i: e.scalar_tensor_tensor(out=mt[:], in0=esw[:], scalar=w2[:, g, i:i + 1], in1=mt[:],
                                                                                    op0=ALU.mult, op1=ALU.add), R=[bs, bmt], W=[bmt])
                        pb = 4 + g % NBG
                        k.op("pe", lambda e, mt=mt, X=X, sh=sh, pb=pb: e.matmul(P.ps[pb][:, sh:NCH], lhsT=mt[:], rhs=X[:, 0:NCH - sh],
                                                                                start=True, stop=True), R=[bmt, bXs[g % NBG]], W=[P.psb[pb]])
                        k.op("dve", lambda e, X=X, sh=sh, pb=pb: e.tensor_tensor(out=X[:, sh:NCH], in0=X[:, sh:NCH], in1=P.ps[pb][:, sh:NCH],
                                                                                 op=ALU.add), R=[P.psb[pb], bXs[g % NBG]], W=[bXs[g % NBG]])
                for g in gs:
                    X = Xs[g % NBG]
                    k.op("pool", lambda e, g=g: e.memset(Xb[:, g, 0:1], 0.0), W=[bXb[g]])
                    k.op("act", lambda e, g=g, X=X: e.copy(out=Xb[:, g, 1:NCH], in_=X[:, 0:NCH - 1]), R=[bXs[g % NBG]], W=[bXb[g]])
            k.barrier()
        with ExitStack() as es3:
            Qm = P.sb(es3, "s5_Qm", [128, L, 16, 2, 32], BF16)
            bQm = k.buf("s5_Qm")
            Qt = P.sb(es3, "s5_Qt", [128, 32, 16], F32)
            trs = P.sb(es3, "s5_trs", [128, 32], F32)
            nti = P.sb(es3, "s5_nti", [128, 32], F32)
            mpat = P.sb(es3, "s5_mpat", [128, 2, 2, 16], F32)
            dv(lambda e: e.memset(mpat[:], 0.0))
            dv(lambda e: e.memset(mpat[:, 0, 0, :], 1.0))
            dv(lambda e: e.memset(mpat[:, 1, 1, :], 1.0))
            for tau in range(L):
                n = tau + 1
                dv(lambda e, n=n: e.tensor_scalar(out=trs[:], in0=TR[n][:], scalar1=sgn[:, 0:1], scalar2=-1.0, op0=ALU.mult, op1=ALU.mult))
                dv(lambda e, n=n: e.tensor_scalar(out=nti[:], in0=TI[n][:], scalar1=-1.0, scalar2=None, op0=ALU.mult))
                dv(lambda e: e.tensor_tensor(out=Qt[:], in0=CT1[:], in1=bc16(trs), op=ALU.mult))
                dv(lambda e: e.tensor_tensor(out=tmp3[:], in0=CT2[:], in1=bc16(nti), op=ALU.mult))
                dv(lambda e: e.tensor_tensor(out=Qt[:], in0=Qt[:], in1=tmp3[:], op=ALU.add))
                for mem in range(2):
                    k.op("dve", lambda e, tau=tau, mem=mem: e.tensor_tensor(
                        out=Qm[:, tau, :, mem, :], in0=Qt[:, :, :].rearrange("p (q m) h -> p q (m h)", m=2),
                        in1=mpat[:, mem, :, :].rearrange("p m h -> p (m h)").unsqueeze(1).broadcast_to([128, 16, 32]), op=ALU.mult),
                        R=[bs], W=[bQm])
            zT = P.sb(es3, "s5_zT", [128, 4, S], BF16)
            bz = k.bufs(4, "s5_zT")
            ys = [P.sb(es3, f"s5_ys{i}", [128, NCH], F32) for i in range(2)]
            bys = k.bufs(2, "s5_ys")
            yv = [P.sb(es3, f"s5_yv{i}", [128, NCH], F32) for i in range(2)]
            byv = k.bufs(2, "s5_yv")
            y2 = [P.sb(es3, f"s5_y2{i}", [128, NCH], F32) for i in range(2)]
            by2 = k.bufs(2, "s5_y2")
            cnt = 0
            for f in range(4):
                ul = uT[:, f, :].rearrange("p (c l) -> p l c", l=L)
                for tau in range(L):
                    i2 = cnt % 2
                    pb = cnt % 4
                    cnt += 1
                    for q in range(4):
                        for mem in range(2):
                            g = f * 8 + q * 2 + mem
                            k.op("pe", lambda e, q=q, mem=mem, g=g, tau=tau, pb=pb, f=f: e.matmul(
                                P.ps[pb][32 * q:32 * q + 32, 256:512], lhsT=Qm[:, tau, f * 4 + q, mem, :], rhs=Xb[:, g, :],
                                start=(mem == 0), stop=(mem == 1), tile_position=(0, 32 * q)),
                                R=[bQm, bXb[g]], W=[P.psb[pb]])
                    k.op("act", lambda e, pb=pb, i2=i2: e.copy(out=ys[i2][:], in_=P.ps[pb][:, 256:512]), R=[P.psb[pb]], W=[bys[i2]])
                    for j in range(tau + 1):
                        k.op("pe", lambda e, f=f, tau=tau, j=j, pb=pb, ul=ul: e.matmul(
                            P.ps[pb][:, 0:256], lhsT=Kb[:, f, tau - j, :], rhs=ul[:, j, :], start=(j == 0), stop=(j == tau)),
                            R=[bKb] + buT, W=[P.psb[pb]])
                    Y = yv[i2]
                    k.op("dve", lambda e, pb=pb, i2=i2, Y=Y: e.tensor_tensor(out=Y[:], in0=ys[i2][:], in1=P.ps[pb][:, 0:256], op=ALU.add),
                         R=[bys[i2], P.psb[pb]], W=[byv[i2]])
                    Y2 = y2[i2]
                    k.op("pool", lambda e, Y=Y, Y2=Y2: e.tensor_tensor(out=Y2[:], in0=Y[:], in1=Y[:], op=ALU.mult), R=[byv[i2]], W=[by2[i2]])
                    k.op("pool", lambda e, Y2=Y2: e.tensor_scalar(out=Y2[:], in0=Y2[:], scalar1=0.044715, scalar2=1.0, op0=ALU.mult, op1=ALU.add),
                         R=[by2[i2]], W=[by2[i2]])
                    k.op("dve", lambda e, Y=Y, Y2=Y2: e.tensor_tensor(out=Y2[:], in0=Y2[:], in1=Y[:], op=ALU.mult), R=[by2[i2], byv[i2]], W=[by2[i2]])
                    k.op("act", lambda e, Y2=Y2: e.activation(out=Y2[:], in_=Y2[:], func=AF.Sigmoid, scale=1.5957691216057308),
                         R=[by2[i2]], W=[by2[i2]])
                    k.op("dve", lambda e, Y=Y, Y2=Y2, f=f, tau=tau: e.tensor_tensor(
                        out=zT[:, f, :].rearrange("p (c l) -> p l c", l=L)[:, tau, :], in0=Y[:], in1=Y2[:], op=ALU.mult),
                        R=[by2[i2], byv[i2]], W=[bz[f]])
            wgl = P.sb(es3, "s5_wgl", [128, 4, 512], BF16)
            bwgl = k.buf("s5_wgl")
            k.dma("sp", [(wgl[:], W["s5_w_glu"][0].rearrange("(c p) n -> p c n", p=128))], R=[W["s5_w_glu"][1]], W=[bwgl])
            sg = [P.sb(es3, f"s5_sg{i}", [128, TB], F32) for i in range(2)]
            bsg = k.bufs(2, "s5_sg")
            ybT = [P.sb(es3, f"s5_ybT{i}", [128, 4, TB], BF16) for i in range(2)]
            bybT = k.bufs(2, "s5_ybT")
            cnt = 0
            for blk in range(NB):
                c0 = blk * TB
                for fo in range(4):
                    pb = 4 + cnt % 2
                    i2 = cnt % 2
                    cnt += 1
                    for fi in range(4):
                        k.op("pe", lambda e, fo=fo, fi=fi, pb=pb, c0=c0: e.matmul(
                            P.ps[pb][:], lhsT=wgl[:, fi, fo * 128:(fo + 1) * 128], rhs=zT[:, fi, c0:c0 + TB],
                            start=(fi == 0), stop=(fi == 3)), R=[bwgl] + bz, W=[P.psb[pb]])
                    k.op("act", lambda e, fo=fo, pb=pb, i2=i2: e.activation(out=sg[i2][:], in_=P.ps[pb][:], func=AF.Sigmoid,
                                                                           bias=bgl[:, fo:fo + 1], scale=1.0), R=[P.psb[pb], bs], W=[bsg[i2]])
                    k.op("dve", lambda e, fo=fo, i2=i2, blk=blk, c0=c0: e.tensor_tensor(
                        out=ybT[blk % 2][:, fo, :], in0=sg[i2][:], in1=zT[:, fo, c0:c0 + TB], op=ALU.mult),
                        R=[bsg[i2]] + bz, W=[bybT[blk % 2]])
                k.dma("sp", [(P.ycatT[512:1024, c0:c0 + TB].rearrange("(c p) t -> p c t", p=128), ybT[blk % 2][:])],
                      R=[bybT[blk % 2]], W=[P.b_yb[blk]], key=f"d_ybT{blk % 2}")
            k.barrier()


def build(stages=("all",), debug=()):
    P = Prog(debug)
    k = P.k
    I = {}
    P.inp = I
    I["x"] = P.din("x", [S, D])
    I["positions"] = P.din("positions", [S], I32)
    for nm, shp in [("norm_mix_g", [2, D]), ("norm_ffn_g", [2, D]), ("final_norm_g", [D]),
                    ("even_w_in", [D, 2560]), ("hgrn_lb_logits", [2, 512]), ("hgrn_norm_g", [512]),
                    ("s5_a_re", [32, 64]), ("s5_a_im", [32, 64]), ("s5_log_dt", [32]),
                    ("s5_b_re", [32, 64, 16]), ("s5_b_im", [32, 64, 16]),
                    ("s5_c_re", [32, 16, 64]), ("s5_c_im", [32, 16, 64]),
                    ("s5_d", [512]), ("s5_w_glu", [512, 512]), ("s5_b_glu", [512]),
                    ("even_w_out", [D, D]), ("odd_w_in", [D, 704]), ("mla_q_norm_g", [384]),
                    ("mla_w_uq", [384, 1536]), ("mla_kv_norm_g", [256]), ("mla_w_ukv", [256, 2048]),
                    ("odd_w_out", [D, D]), ("ffn_w_in", [2, D, 2 * DFF]), ("ffn_conv_w", [2, 3, DFF]),
                    ("ffn_conv_b", [2, DFF]), ("ffn_w_out", [2, DFF, D])]:
        I[nm] = P.din(nm, shp)
    I["rope_freq"] = P.din("rope_freq", [32])
    out = P.dout("out", [S, D])
    P.b_out = k.bufs(32, "out")
    P.b_x = k.bufs(32, "x")
    setup_common(P)
    if "ffn1" in stages:
        w_in_bf, b1 = cast_weight(P, "w_ffn_in1", I["ffn_w_in"][1], D, 2 * DFF)
        w_out_bf, b2 = cast_weight(P, "w_ffn_out1", I["ffn_w_out"][1], DFF, D)
        ffn_phase(P, 1, I["x"], P.b_x, out, P.b_out, w_in_bf, b1, w_out_bf, b2, final_norm=I["final_norm_g"])
    W = {}
    if "hgrn" in stages:
        W["even_w_in"] = cast_weight(P, "bf_even_w_in", I["even_w_in"], D, 2560)
        hgrn_phase(P, I["x"], P.b_x, W)
    if "s5" in stages:
        W["s5_w_glu"] = cast_weight(P, "bf_s5_w_glu", I["s5_w_glu"], 512, 512)
        s5_phase(P, W)
    if "mla" in stages:
        for nm, r, c in [("odd_w_in", D, 704), ("mla_w_uq", 384, 1536), ("mla_w_ukv", 256, 2048), ("odd_w_out", D, D)]:
            W[nm] = cast_weight(P, "bf_" + nm, I[nm], r, c)
        mla_proj_phase(P, I["x"], P.b_x, W)
        if "mla_pd_only" not in stages:
            mla_attn_phase(P, I["x"], P.b_x, out, P.b_out, W)
    if "all" in stages:
        W["even_w_in"] = cast_weight(P, "bf_even_w_in", I["even_w_in"], D, 2560)
        W["s5_w_glu"] = cast_weight(P, "bf_s5_w_glu", I["s5_w_glu"], 512, 512)
        W["even_w_out"] = cast_weight(P, "bf_even_w_out", I["even_w_out"], D, D)
        W["ffn_in0"] = cast_weight(P, "bf_ffn_in0", I["ffn_w_in"][0], D, 2 * DFF)
        W["ffn_out0"] = cast_weight(P, "bf_ffn_out0", I["ffn_w_out"][0], DFF, D)
        for nm, r, c in [("odd_w_in", D, 704), ("mla_w_uq", 384, 1536), ("mla_w_ukv", 256, 2048), ("odd_w_out", D, D)]:
            W[nm] = cast_weight(P, "bf_" + nm, I[nm], r, c)
        W["ffn_in1"] = cast_weight(P, "bf_ffn_in1", I["ffn_w_in"][1], D, 2 * DFF)
        W["ffn_out1"] = cast_weight(P, "bf_ffn_out1", I["ffn_w_out"][1], DFF, D)
        h2 = P.dint("h2", [S, D], F32)
        h3 = P.dint("h3", [S, D], F32)
        b_h2 = k.bufs(32, "h2")
        b_h3 = k.bufs(32, "h3")
        hgrn_phase(P, I["x"], P.b_x, W)
        s5_phase(P, W)
        ffn_phase(P, 0, I["x"], P.b_x, h2, b_h2, W["ffn_in0"][0], W["ffn_in0"][1], W["ffn_out0"][0], W["ffn_out0"][1],
                  premix=(P.ycatT, P.b_ya, P.b_yb, W["even_w_out"][0], W["even_w_out"][1]))
        mla_proj_phase(P, h2, b_h2, W)
        mla_attn_phase(P, h2, b_h2, h3, b_h3, W)
        ffn_phase(P, 1, h3, b_h3, out, P.b_out, W["ffn_in1"][0], W["ffn_in1"][1], W["ffn_out1"][0], W["ffn_out1"][1],
                  final_norm=I["final_norm_g"])
    k.barrier()
    import os as _os
    k.do_reorder = _os.environ.get("KREORDER", "1") == "1"
    k.finalize(P.es)
    P.es.close()
    return P


INPUT_ORDER = ["x", "positions", "norm_mix_g", "norm_ffn_g", "final_norm_g", "even_w_in", "hgrn_lb_logits",
               "hgrn_norm_g", "s5_a_re", "s5_a_im", "s5_log_dt", "s5_b_re", "s5_b_im", "s5_c_re", "s5_c_im",
               "s5_d", "s5_w_glu", "s5_b_glu", "even_w_out", "odd_w_in", "mla_q_norm_g", "mla_w_uq",
               "mla_kv_norm_g", "mla_w_ukv", "odd_w_out", "ffn_w_in", "ffn_conv_w", "ffn_conv_b", "ffn_w_out"]


def make_in_maps(inputs):
    shared = {}
    for nm in INPUT_ORDER:
        if nm in ("x", "positions"):
            continue
        a = np.ascontiguousarray(np.asarray(inputs[nm]))
        if nm in ("norm_mix_g", "norm_ffn_g", "hgrn_lb_logits", "ffn_w_in", "ffn_conv_w", "ffn_conv_b",
                  "ffn_w_out", "final_norm_g"):
            shared[nm] = a
        else:
            shared[nm] = np.ascontiguousarray(a[0])
    shared["rope_freq"] = (10000.0 ** (-np.arange(0, 64, 2, dtype=np.float32) / np.float32(64))).astype(np.float32)
    x = np.asarray(inputs["x"])
    pos = np.asarray(inputs["positions"])
    maps = []
    for c in range(8):
        m = dict(shared)
        m["x"] = np.ascontiguousarray(x[c])
        m["positions"] = np.ascontiguousarray(pos[c]).astype(np.int32)
        maps.append(m)
    return maps


def run(inputs, stages=("all",), debug=()):
    P = build(stages, debug)
    maps = make_in_maps(inputs)
    res = run_bass_kernel_spmd(P.nc, maps, core_ids=list(range(8)))
    return res


def kernel(**inputs):
    res = run(inputs)
    out = np.stack([np.asarray(r["out"]) for r in res.results], axis=0)
    return out.astype(np.float32)
```
